# Optimizing a Trainium2 kernel written in Bass

```python
import math
import jax, jax.numpy as jnp
from jax import lax
import numpy as np

D_MODEL = 1024
BATCH = 8
SEQ = 2048
DEPTH = 1
DEC_BATCH = 128
DEC_SEQ = 8
PAST_LEN = 16384
PAGE_SIZE = 128

D_MIX = 2 * D_MODEL
SSD_WIDTH = D_MIX // 2
SSD_HEADDIM = 64
SSD_HEADS = SSD_WIDTH // SSD_HEADDIM
SSD_GROUPS = 2
SSD_STATE = 128
SSD_CONV_DIM = SSD_WIDTH + 2 * SSD_GROUPS * SSD_STATE
GDN_WIDTH = D_MIX - SSD_WIDTH
GDN_HEAD_DIM = 128
GDN_HEADS = GDN_WIDTH // GDN_HEAD_DIM
GDN_KEY_DIM = 128
GDN_QK_WIDTH = GDN_HEADS * GDN_KEY_DIM
GDN_CONV_DIM = 2 * GDN_QK_WIDTH + GDN_WIDTH
CONV_WIDTH = 4
CHUNK = 64
D_FF = -(-8 * D_MODEL // (3 * 256)) * 256
IN_SIZES = (SSD_WIDTH, SSD_CONV_DIM, SSD_HEADS, GDN_CONV_DIM, GDN_WIDTH, GDN_HEADS, GDN_HEADS)
IN_DIM = sum(IN_SIZES)
NORM_EPS = 1e-6
L2_EPS = 1e-6

kernel_name = "hymba_ssd_gdn_step"


def _rms_norm(x, w):
    xf = x.astype(jnp.float32)
    y = xf * lax.rsqrt(jnp.mean(xf * xf, axis=-1, keepdims=True) + NORM_EPS)
    return (y * w.astype(jnp.float32)).astype(x.dtype)


def _l2norm(x):
    xf = x.astype(jnp.float32)
    return xf * lax.rsqrt(jnp.sum(xf * xf, axis=-1, keepdims=True) + L2_EPS)


def _causal_dwconv(x, buf, w, b=None):
    xx = jnp.concatenate([buf.astype(x.dtype), x], axis=1)
    y = lax.conv_general_dilated(xx, w[:, None, :].astype(x.dtype), (1,), "VALID",
                                 dimension_numbers=("NWC", "WIO", "NWC"),
                                 feature_group_count=x.shape[-1])
    if b is not None:
        y = y + b.astype(y.dtype)
    return y, xx[:, xx.shape[1] - (CONV_WIDTH - 1):].astype(buf.dtype)


def _pad_time(a, pad):
    return jnp.pad(a, [(0, 0), (0, pad)] + [(0, 0)] * (a.ndim - 2))


def _ssd_chunked(x, dt, A, Bm, Cm, h0):
    b, L, H, P = x.shape
    G, N = Bm.shape[2], Bm.shape[3]
    R = H // G
    cs = min(CHUNK, L)
    nc = -(-L // cs)
    pad = nc * cs - L
    x, dt, Bm, Cm = (_pad_time(t, pad) for t in (x, dt, Bm, Cm))
    xc = x.reshape(b, nc, cs, G, R, P)
    dtc = dt.reshape(b, nc, cs, G, R)
    Bc = Bm.reshape(b, nc, cs, G, N)
    Cc = Cm.reshape(b, nc, cs, G, N)
    acs = jnp.cumsum(dtc * A.reshape(G, R), axis=2)
    xdt = xc * dtc[..., None]
    causal = jnp.tril(jnp.ones((cs, cs), bool))[:, :, None, None]
    seg = acs[:, :, :, None] - acs[:, :, None, :]
    decay_ls = jnp.exp(jnp.where(causal, seg, -jnp.inf))
    cb = jnp.einsum("bclgn,bcsgn->bclsg", Cc, Bc)
    y_diag = jnp.einsum("bclsgr,bcsgrp->bclgrp", cb[..., None] * decay_ls, xdt)
    decay_end = jnp.exp(acs[:, :, -1:] - acs)
    chunk_states = jnp.einsum("bclgn,bclgrp->bcgrpn", Bc, xdt * decay_end[..., None])
    chunk_decay = jnp.exp(acs[:, :, -1])

    def step(h, inp):
        st, dec = inp
        return h * dec[..., None, None] + st, h

    h_last, h_in = lax.scan(step, h0.reshape(b, G, R, P, N),
                            (jnp.moveaxis(chunk_states, 1, 0), jnp.moveaxis(chunk_decay, 1, 0)))
    h_in = jnp.moveaxis(h_in, 0, 1)
    y_off = jnp.einsum("bclgn,bcgrpn->bclgrp", Cc, h_in) * jnp.exp(acs)[..., None]
    y = (y_diag + y_off).reshape(b, nc * cs, H, P)[:, :L]
    return y, h_last.reshape(b, H, P, N)


def _gdn_chunked(q, k, v, g, beta, S0):
    b, L, H, Dk = q.shape
    cs = min(CHUNK, L)
    nc = -(-L // cs)
    pad = nc * cs - L
    q, k, v, g, beta = (_pad_time(t, pad) for t in (q, k, v, g, beta))

    def heads_first(t):
        return jnp.moveaxis(t.reshape((b, nc, cs) + t.shape[2:]), 3, 1)

    q, k, v, g, beta = (heads_first(t) for t in (q, k, v, g, beta))
    gc = jnp.cumsum(g, axis=-1)
    incl = jnp.tril(jnp.ones((cs, cs), bool))
    strict = jnp.tril(jnp.ones((cs, cs), bool), k=-1)
    decay = jnp.exp(jnp.where(incl, gc[..., :, None] - gc[..., None, :], -jnp.inf))
    kb = k * beta[..., None]
    lower = jnp.einsum("bhcld,bhcsd->bhcls", kb, k) * jnp.where(strict, decay, 0.0)
    tri = lower + jnp.eye(cs, dtype=lower.dtype)
    U = lax.linalg.triangular_solve(tri, v * beta[..., None], left_side=True, lower=True, unit_diagonal=True)
    W = lax.linalg.triangular_solve(tri, kb * jnp.exp(gc)[..., None], left_side=True, lower=True, unit_diagonal=True)
    attn = jnp.einsum("bhcld,bhcsd->bhcls", q, k) * decay
    q_in = q * jnp.exp(gc)[..., None]
    k_out = k * jnp.exp(gc[..., -1:] - gc)[..., None]
    g_end = jnp.exp(gc[..., -1])

    def step(S, inp):
        Uc, Wc, attn_c, q_c, k_c, ge = inp
        v_new = Uc - jnp.einsum("bhld,bhde->bhle", Wc, S)
        o = jnp.einsum("bhld,bhde->bhle", q_c, S) + jnp.einsum("bhls,bhse->bhle", attn_c, v_new)
        S = S * ge[..., None, None] + jnp.einsum("bhld,bhle->bhde", k_c, v_new)
        return S, o

    xs = tuple(jnp.moveaxis(t, 2, 0) for t in (U, W, attn, q_in, k_out, g_end))
    S_last, o = lax.scan(step, S0, xs)
    o = jnp.moveaxis(o, 0, 2).reshape(b, H, nc * cs, -1)
    o = jnp.moveaxis(o, 1, 2)[:, :L]
    return o, S_last


def _mixer(h, ssd_h0, ssd_conv_buf, gdn_S0, gdn_conv_buf, w_in, ssd_conv_w, ssd_conv_b, ssd_dt_bias,
           ssd_A_log, ssd_D, ssd_norm_w, gdn_conv_w, gdn_dt_bias, gdn_A_log, gdn_norm_w, w_out):
    f32 = jnp.float32
    b, L, _ = h.shape
    proj = h @ w_in
    z, xbc, dt_raw, qkv, gate, beta_raw, alpha_raw = jnp.split(proj, list(np.cumsum(IN_SIZES)[:-1]), axis=-1)
    xbc, ssd_conv_new = _causal_dwconv(xbc, ssd_conv_buf, ssd_conv_w, ssd_conv_b)
    xbc = jax.nn.silu(xbc.astype(f32))
    xs, Bm, Cm = jnp.split(xbc, [SSD_WIDTH, SSD_WIDTH + SSD_GROUPS * SSD_STATE], axis=-1)
    xs = xs.reshape(b, L, SSD_HEADS, SSD_HEADDIM)
    dt = jax.nn.softplus(dt_raw.astype(f32) + ssd_dt_bias.astype(f32))
    A = -jnp.exp(ssd_A_log.astype(f32))
    y, ssd_h = _ssd_chunked(xs, dt, A, Bm.reshape(b, L, SSD_GROUPS, SSD_STATE),
                            Cm.reshape(b, L, SSD_GROUPS, SSD_STATE), ssd_h0.astype(f32))
    y = y + ssd_D.astype(f32)[:, None] * xs
    gsz = SSD_WIDTH // SSD_GROUPS
    y = y.reshape(b, L, SSD_GROUPS, gsz) * jax.nn.silu(z.astype(f32)).reshape(b, L, SSD_GROUPS, gsz)
    y = _rms_norm(y, ssd_norm_w.reshape(SSD_GROUPS, gsz)).reshape(b, L, SSD_WIDTH)
    qkv, gdn_conv_new = _causal_dwconv(qkv, gdn_conv_buf, gdn_conv_w)
    qkv = jax.nn.silu(qkv.astype(f32))
    q, k, v = jnp.split(qkv, [GDN_QK_WIDTH, 2 * GDN_QK_WIDTH], axis=-1)
    q = _l2norm(q.reshape(b, L, GDN_HEADS, GDN_KEY_DIM)) * (GDN_KEY_DIM ** -0.5)
    k = _l2norm(k.reshape(b, L, GDN_HEADS, GDN_KEY_DIM))
    v = v.reshape(b, L, GDN_HEADS, GDN_HEAD_DIM)
    beta = jax.nn.sigmoid(beta_raw.astype(f32))
    g = -jnp.exp(gdn_A_log.astype(f32)) * jax.nn.softplus(alpha_raw.astype(f32) + gdn_dt_bias.astype(f32))
    o, gdn_S = _gdn_chunked(q, k, v, g, beta, gdn_S0.astype(f32))
    o = _rms_norm(o, gdn_norm_w) * jax.nn.silu(gate.astype(f32)).reshape(b, L, GDN_HEADS, GDN_HEAD_DIM)
    mixed = jnp.concatenate([y, o.reshape(b, L, GDN_WIDTH)], axis=-1).astype(h.dtype)
    return (mixed @ w_out, ssd_h.astype(ssd_h0.dtype), ssd_conv_new,
            gdn_S.astype(gdn_S0.dtype), gdn_conv_new)


def _layer(x, c, ssd_h0, ssd_conv_buf, gdn_S0, gdn_conv_buf, w_ada, b_ada, norm_mix_pre, norm_mix_post,
           norm_ffn_pre, norm_ffn_post, w_in, ssd_conv_w, ssd_conv_b, ssd_dt_bias, ssd_A_log, ssd_D,
           ssd_norm_w, gdn_conv_w, gdn_dt_bias, gdn_A_log, gdn_norm_w, w_out, w_gate_up, w_down):
    mod = jax.nn.silu(c) @ w_ada + b_ada
    sh1, sc1, g1, sh2, sc2, g2 = jnp.split(mod[:, None, :], 6, axis=-1)
    h = _rms_norm(x, norm_mix_pre) * (1 + sc1) + sh1
    m, ssd_h, ssd_conv, gdn_S, gdn_conv = _mixer(
        h, ssd_h0, ssd_conv_buf, gdn_S0, gdn_conv_buf, w_in, ssd_conv_w, ssd_conv_b, ssd_dt_bias,
        ssd_A_log, ssd_D, ssd_norm_w, gdn_conv_w, gdn_dt_bias, gdn_A_log, gdn_norm_w, w_out)
    x = x + g1 * _rms_norm(m, norm_mix_post)
    h = _rms_norm(x, norm_ffn_pre) * (1 + sc2) + sh2
    gu = h @ w_gate_up
    f = (jax.nn.silu(gu[..., :D_FF]) * gu[..., D_FF:]) @ w_down
    x = x + g2 * _rms_norm(f, norm_ffn_post)
    return x, ssd_h, ssd_conv, gdn_S, gdn_conv


def setup_inputs(seed: int = 0) -> dict:
    key = jax.random.key(seed)
    ks = jax.random.split(key, 32)
    f32 = jnp.float32

    def nrm(k, shape, scale):
        return jax.random.normal(k, shape, f32) * scale

    def gain(k, shape):
        return 1.0 + 0.05 * jax.random.normal(k, shape, f32)

    def dt_bias(k, n):
        dt = jnp.exp(jax.random.uniform(k, (DEPTH, n), f32) * (math.log(0.1) - math.log(0.001)) + math.log(0.001))
        return dt + jnp.log(-jnp.expm1(-dt))

    def a_log(k, n):
        return jnp.log(jax.random.uniform(k, (DEPTH, n), f32, minval=1.0, maxval=16.0))

    return {
        "x_prompt": nrm(ks[0], (BATCH, SEQ, D_MODEL), 1.0),
        "x_sample": nrm(ks[1], (DEC_BATCH, DEC_SEQ, D_MODEL), 1.0),
        "c_prompt": nrm(ks[2], (BATCH, D_MODEL), 1.0),
        "c_sample": nrm(ks[3], (DEC_BATCH, D_MODEL), 1.0),
        "state_ssd": nrm(ks[4], (DEPTH, DEC_BATCH, SSD_HEADS, SSD_HEADDIM, SSD_STATE), 0.1),
        "state_ssd_conv": nrm(ks[5], (DEPTH, DEC_BATCH, CONV_WIDTH - 1, SSD_CONV_DIM), 1.0),
        "state_gdn": nrm(ks[6], (DEPTH, DEC_BATCH, GDN_HEADS, GDN_KEY_DIM, GDN_HEAD_DIM), 0.1),
        "state_gdn_conv": nrm(ks[7], (DEPTH, DEC_BATCH, CONV_WIDTH - 1, GDN_CONV_DIM), 1.0),
        "w_ada": nrm(ks[8], (DEPTH, D_MODEL, 6 * D_MODEL), 0.5 * D_MODEL ** -0.5),
        "b_ada": nrm(ks[9], (DEPTH, 6 * D_MODEL), 0.02),
        "norm_mix_pre": gain(ks[10], (DEPTH, D_MODEL)),
        "norm_mix_post": gain(ks[11], (DEPTH, D_MODEL)),
        "norm_ffn_pre": gain(ks[12], (DEPTH, D_MODEL)),
        "norm_ffn_post": gain(ks[13], (DEPTH, D_MODEL)),
        "w_in": nrm(ks[14], (DEPTH, D_MODEL, IN_DIM), D_MODEL ** -0.5),
        "ssd_conv_w": nrm(ks[15], (DEPTH, CONV_WIDTH, SSD_CONV_DIM), CONV_WIDTH ** -0.5),
        "ssd_conv_b": nrm(ks[16], (DEPTH, SSD_CONV_DIM), 0.02),
        "ssd_dt_bias": dt_bias(ks[17], SSD_HEADS),
        "ssd_A_log": a_log(ks[18], SSD_HEADS),
        "ssd_D": gain(ks[19], (DEPTH, SSD_HEADS)),
        "ssd_norm_w": gain(ks[20], (DEPTH, SSD_WIDTH)),
        "gdn_conv_w": nrm(ks[21], (DEPTH, CONV_WIDTH, GDN_CONV_DIM), CONV_WIDTH ** -0.5),
        "gdn_dt_bias": dt_bias(ks[22], GDN_HEADS),
        "gdn_A_log": a_log(ks[23], GDN_HEADS),
        "gdn_norm_w": gain(ks[24], (DEPTH, GDN_HEAD_DIM)),
        "w_out": nrm(ks[25], (DEPTH, D_MIX, D_MODEL), D_MIX ** -0.5),
        "w_gate_up": nrm(ks[26], (DEPTH, D_MODEL, 2 * D_FF), D_MODEL ** -0.5),
        "w_down": nrm(ks[27], (DEPTH, D_FF, D_MODEL), D_FF ** -0.5),
    }


def reference(x_prompt, x_sample, c_prompt, c_sample, state_ssd, state_ssd_conv, state_gdn, state_gdn_conv,
              w_ada, b_ada, norm_mix_pre, norm_mix_post, norm_ffn_pre, norm_ffn_post, w_in, ssd_conv_w,
              ssd_conv_b, ssd_dt_bias, ssd_A_log, ssd_D, ssd_norm_w, gdn_conv_w, gdn_dt_bias, gdn_A_log,
              gdn_norm_w, w_out, w_gate_up, w_down):
    bp = x_prompt.shape[0]
    yp, ys = x_prompt, x_sample
    p_ssd, p_ssd_conv, p_gdn, p_gdn_conv = [], [], [], []
    s_ssd, s_ssd_conv, s_gdn, s_gdn_conv = [], [], [], []
    for l in range(DEPTH):
        lp = (w_ada[l], b_ada[l], norm_mix_pre[l], norm_mix_post[l], norm_ffn_pre[l], norm_ffn_post[l],
              w_in[l], ssd_conv_w[l], ssd_conv_b[l], ssd_dt_bias[l], ssd_A_log[l], ssd_D[l], ssd_norm_w[l],
              gdn_conv_w[l], gdn_dt_bias[l], gdn_A_log[l], gdn_norm_w[l], w_out[l], w_gate_up[l], w_down[l])
        yp, a0, a1, a2, a3 = _layer(
            yp, c_prompt,
            jnp.zeros((bp,) + state_ssd.shape[2:], state_ssd.dtype),
            jnp.zeros((bp,) + state_ssd_conv.shape[2:], state_ssd_conv.dtype),
            jnp.zeros((bp,) + state_gdn.shape[2:], state_gdn.dtype),
            jnp.zeros((bp,) + state_gdn_conv.shape[2:], state_gdn_conv.dtype),
            *lp)
        p_ssd.append(a0); p_ssd_conv.append(a1); p_gdn.append(a2); p_gdn_conv.append(a3)
        ys, b0, b1, b2, b3 = _layer(ys, c_sample, state_ssd[l], state_ssd_conv[l], state_gdn[l],
                                    state_gdn_conv[l], *lp)
        s_ssd.append(b0); s_ssd_conv.append(b1); s_gdn.append(b2); s_gdn_conv.append(b3)
    return (yp, ys, jnp.stack(p_ssd), jnp.stack(p_ssd_conv), jnp.stack(p_gdn), jnp.stack(p_gdn_conv),
            jnp.stack(s_ssd), jnp.stack(s_ssd_conv), jnp.stack(s_gdn), jnp.stack(s_gdn_conv))
```

```python
import contextlib
import os
import numpy as np
import concourse.bass as bass
import concourse.mybir as mybir
from concourse.bass_utils import run_bass_kernel_spmd

F32 = mybir.dt.float32
BF16 = mybir.dt.bfloat16
AF = mybir.ActivationFunctionType
ALU = mybir.AluOpType
AX = mybir.AxisListType

NCORES = 8
D = 1024
NT = 17
DFF = 2816
P2_MODE = 0
ANNOTATE = bool(int(os.environ.get('K_ANN', '0')))
SKIP1 = False
P2_STAGE = 99
P2_HSTEP = 99
P2_SKIP = ()
P2_NL = 4
P2_TILES = 16
STOP_AFTER = int(os.environ.get('K_STOP', '99'))


class Prog:
    ENGS = ("pe", "act", "dve", "pool", "sp")

    def __init__(self, nc, stack):
        self.nc = nc
        self.sems = {}
        for e in self.ENGS:
            self.sems[e] = stack.enter_context(nc.semaphore("s_" + e))
        self.dsems = {"sp": [], "pool": [], "act": []}
        for q, n in (("sp", 8), ("pool", 4), ("act", 2)):
            for j in range(n):
                nm = "d_%s%d" % (q, j)
                self.sems[nm] = stack.enter_context(nc.semaphore(nm))
                self.dsems[q].append(nm)
        self.cnt = {k: 0 for k in self.sems}
        self.known = {e: {} for e in self.ENGS}
        self.lastw = {}
        self.readers = {}
        self.ops = {e: [] for e in self.ENGS}
        self.rr = {"sp": 0, "pool": 0, "act": 0}
        self.nops = 0
        self.tag = "init"

    def sec(self, name):
        self.tag = name

    fine = {}

    def _keys(self, aps):
        ks = []
        for a in aps:
            if a is None or isinstance(a, (int, float)):
                continue
            if isinstance(a, str):
                ks.append(a)
                continue
            nm = a.name
            if nm in self.fine:
                row, gr = self.fine[nm]
                nm = "%s:%d" % (nm, (int(a.offset) % row) // gr)
            ks.append(nm)
        return ks

    def _deps(self, eng, r, w):
        need = {}

        def add(c):
            if c is None:
                return
            s, v = c
            if s == "pe" and eng == "pe":
                return
            if need.get(s, 0) < v:
                need[s] = v

        for k in r:
            add(self.lastw.get(k))
        for k in w:
            add(self.lastw.get(k))
            for c in self.readers.get(k, {}).items():
                add(c)
        waits = []
        kn = self.known[eng]
        for s, v in need.items():
            if kn.get(s, 0) < v:
                kn[s] = v
                waits.append((s, v))
        return waits

    def _commit(self, c, r, w):
        for k in w:
            self.lastw[k] = c
            self.readers[k] = {}
        for k in r:
            d = self.readers.setdefault(k, {})
            if d.get(c[0], 0) < c[1]:
                d[c[0]] = c[1]

    def op(self, eng, fn, ins=(), outs=()):
        r = self._keys(ins)
        w = self._keys(outs)
        w = w + [k for k in r if k.startswith("ps") or k.startswith("pa") or k.startswith("pm")
                 or k.startswith("lpb") or k.startswith("pb")]
        waits = self._deps(eng, r, w)
        self.cnt[eng] += 1
        self.ops[eng].append((waits, fn, eng, 1, self.tag))
        self._commit((eng, self.cnt[eng]), r, w)
        self.nops += 1

    def dma(self, q, out, in_, extra_ins=(), extra_outs=()):
        r = self._keys([in_] + list(extra_ins))
        w = self._keys([out] + list(extra_outs))
        waits = self._deps(q, r, w)
        sems = self.dsems[q]
        j = self.rr[q]
        self.rr[q] = (j + 1) % len(sems)
        nm = sems[j]
        prev = self.cnt[nm]
        if prev > 0 and self.known[q].get(nm, 0) < prev:
            self.known[q][nm] = prev
            waits.append((nm, prev))
        self.cnt[nm] += 16
        self.ops[q].append((waits, lambda e: e.dma_start(out=out, in_=in_), nm, 16, self.tag))
        self._commit((nm, self.cnt[nm]), r, w)
        self.nops += 1

    def barrier(self):
        for e in self.ENGS:
            waits = []
            for s, v in self.cnt.items():
                if v > 0 and self.known[e].get(s, 0) < v:
                    self.known[e][s] = v
                    waits.append((s, v))
            self.ops[e].append((waits, None, None, 0, self.tag))
        self.lastw = {}
        self.readers = {}

    def emit(self):
        nc = self.nc
        with nc.Block() as block:
            for ename, deco in (("sp", block.sync), ("act", block.scalar), ("dve", block.vector),
                                ("pool", block.gpsimd), ("pe", block.tensor)):
                ops = self.ops[ename]

                def body(e, ops=ops):
                    for waits, fn, sname, inc, tag in ops:
                        for s, v in waits:
                            e.wait_ge(self.sems[s], v)
                        if fn is not None:
                            ins = fn(e)
                            ins.then_inc(self.sems[sname], inc)
                            if ANNOTATE:
                                ins.annotate(tag)

                deco(body)
                self.ops[ename] = []

    def mm(self, out, lhsT, rhs, start=True, stop=True):
        self.op("pe", lambda e: e.matmul(out, lhsT=lhsT, rhs=rhs, start=start, stop=stop),
                ins=[lhsT, rhs], outs=[out])

    def mmk(self, out, pairs):
        n = len(pairs)

        def fn(e):
            ins = None
            for i, (l, r) in enumerate(pairs):
                ins = e.matmul(out, lhsT=l, rhs=r, start=(i == 0), stop=(i == n - 1))
            return ins

        self.op("pe", fn, ins=[x for p in pairs for x in p], outs=[out])

    def tr(self, out, in_, ident):
        self.op("pe", lambda e: e.transpose(out, in_, ident), ins=[in_, ident], outs=[out])

    def trs(self, items, ident):
        def fn(e):
            ins = None
            for o, i in items:
                ins = e.transpose(o, i, ident)
            return ins
        self.op("pe", fn, ins=[i for _, i in items] + [ident], outs=[o for o, _ in items])

    def act(self, out, in_, func, bias=None, scale=None):
        kw = {}
        if bias is not None:
            kw["bias"] = bias
        if scale is not None:
            kw["scale"] = scale
        self.op("act", lambda e: e.activation(out=out, in_=in_, func=func, **kw),
                ins=[in_, bias, scale], outs=[out])

    def sqsum(self, junk, in_, accum):
        self.op("act", lambda e: e.activation(out=junk, in_=in_, func=AF.Square, accum_out=accum),
                ins=[in_], outs=[junk, accum])

    def tt(self, eng, out, in0, in1, op):
        self.op(eng, lambda e: e.tensor_tensor(out=out, in0=in0, in1=in1, op=op), ins=[in0, in1], outs=[out])

    def ts(self, eng, out, in0, s1, s2, op0, op1=None):
        if op1 is None:
            self.op(eng, lambda e: e.tensor_scalar(out=out, in0=in0, scalar1=s1, scalar2=None, op0=op0),
                    ins=[in0, s1], outs=[out])
        else:
            self.op(eng, lambda e: e.tensor_scalar(out=out, in0=in0, scalar1=s1, scalar2=s2, op0=op0, op1=op1),
                    ins=[in0, s1, s2], outs=[out])

    def stt(self, eng, out, in0, scalar, in1, op0, op1):
        self.op(eng, lambda e: e.scalar_tensor_tensor(out=out, in0=in0, scalar=scalar, in1=in1, op0=op0, op1=op1),
                ins=[in0, scalar, in1], outs=[out])

    def copy(self, eng, out, in_):
        if eng == "act":
            self.op(eng, lambda e: e.copy(out=out, in_=in_), ins=[in_], outs=[out])
        else:
            self.op(eng, lambda e: e.tensor_copy(out=out, in_=in_), ins=[in_], outs=[out])

    def red(self, eng, out, in_):
        self.op(eng, lambda e: e.tensor_reduce(out=out, in_=in_, axis=AX.X, op=ALU.add), ins=[in_], outs=[out])

    def recip(self, out, in_):
        self.op("dve", lambda e: e.reciprocal(out=out, in_=in_), ins=[in_], outs=[out])

    def memset(self, eng, ap, val):
        self.op(eng, lambda e: e.memset(ap, val), ins=[], outs=[ap])


def bc(ap, axis, n):
    u = ap.unsqueeze(axis)
    shp = list(u.shape)
    shp[axis] = n
    return u.broadcast_to(shp)


def build_program():
    nc = bass.Bass("TRN2", target_bir_lowering=False)

    def din(name, shape, dt=F32):
        return nc.dram_tensor(name, list(shape), dt, kind="ExternalInput").ap()

    def dout(name, shape, dt=F32):
        return nc.dram_tensor(name, list(shape), dt, kind="ExternalOutput").ap()

    def dscr(name, shape, dt=F32):
        return nc.dram_tensor(name, list(shape), dt).ap()

    xs_all = din("xs_all", [NT, 128, D])
    cexp = din("cexp", [2, 128, D])
    w_ada = din("w_ada", [D, 6 * D])
    b_ada_b = din("b_ada_b", [128, 6 * D])
    normvecs = din("normvecs", [128, 4, D])
    w_in = din("w_in", [D, 6688])
    w_out = din("w_out", [2 * D, D])
    w_gu = din("w_gu", [D, 2 * DFF])
    w_dn = din("w_dn", [DFF, D])
    ssd_cw = din("ssd_cw", [128, 12, 5])
    gdn_cw = din("gdn_cw", [128, 24, 4])
    smallv = din("smallv", [128, 64])
    ssd_nw = din("ssd_nw", [128, D])
    gdn_nw = din("gdn_nw", [128, 128])
    st_ssd = din("st_ssd", [16, 1024, 128])
    hist_ssd = din("hist_ssd", [128, 12, 16, 3])
    st_gdn = din("st_gdn", [16, 8, 128, 128])
    hist_gdn = din("hist_gdn", [128, 24, 16, 3])
    ident_d = din("ident", [128, 128])
    masks_d = din("masks", [2, 128, 4, 128])
    blk16_d = din("blk16", [128, 16])

    y_all = dout("y_all", [NT, 128, D])
    o_ssd_p = dout("o_ssd_p", [1024, 128])
    o_ssdc_p = dout("o_ssdc_p", [3, 1536])
    o_gdn_p = dout("o_gdn_p", [1024, 128])
    o_gdnc_p = dout("o_gdnc_p", [3, 3072])
    o_ssd_s = dout("o_ssd_s", [16, 1024, 128])
    o_ssdc_s = dout("o_ssdc_s", [48, 1536])
    o_gdn_s = dout("o_gdn_s", [16, 8, 128, 128])
    o_gdnc_s = dout("o_gdnc_s", [48, 3072])

    modscr = dscr("modscr", [2, 128, 6 * D])
    yssd_scr = dscr("yssd_scr", [NT, 128, D], BF16)
    x1_scr = dscr("x1_scr", [NT, 128, D])

    with contextlib.ExitStack() as gstack:
        P = Prog(nc, gstack)

        def sb(stack, name, shape, dt=F32):
            return stack.enter_context(nc.sbuf_tensor(name, list(shape), dt))

        PT = gstack.enter_context(nc.psum_tensor("pst", [128, 1024], BF16))
        ps1 = contextlib.ExitStack()
        PS = [ps1.enter_context(nc.psum_tensor("ps%d" % i, [128, 512], F32)) for i in range(7)]

        identf = sb(gstack, "identf", [128, 128])
        identb = sb(gstack, "identb", [128, 128], BF16)
        onesf = sb(gstack, "onesf", [128, 128])
        onesb = sb(gstack, "onesb", [128, 128], BF16)
        cst = sb(gstack, "cst", [128, 4])
        masks = [sb(gstack, "masks%d" % t, [128, 4, 128]) for t in range(2)]
        blk16 = sb(gstack, "blk16s", [128, 16])
        smv = sb(gstack, "smv", [128, 64])
        negA = sb(gstack, "negA", [128, 24])
        P.dma("sp", identf[:, :], ident_d[:, :])
        P.dma("pool", identb[:, :], ident_d[:, :])
        for t in range(2):
            P.dma("sp", masks[t][:, :, :], masks_d[t])
        P.dma("sp", blk16[:, :], blk16_d[:, :])
        P.dma("sp", smv[:, :], smallv[:, :])
        P.memset("dve", onesf[:, :], 1.0)
        P.memset("dve", onesb[:, :], 1.0)
        P.memset("dve", cst[:, 0:1], 1e-6)
        P.memset("dve", cst[:, 1:2], 1.0)
        P.memset("dve", cst[:, 2:3], 128e-6)
        P.memset("dve", cst[:, 3:4], 0.0)
        P.act(negA[:, 0:16], smv[:, 16:32], AF.Exp)
        P.act(negA[:, 16:24], smv[:, 56:64], AF.Exp)
        P.ts("dve", negA[:, :], negA[:, :], -1.0, None, ALU.mult)

        MSTT, MINC, MSTR, MBLK = 0, 1, 2, 3

        with contextlib.ExitStack() as st0:
            ct = sb(st0, "ct", [128, D])
            cb = sb(st0, "cb", [128, D], BF16)
            cT = [sb(st0, "cT%d" % t, [128, 8, 128], BF16) for t in range(2)]
            modt = [sb(st0, "modt%d" % t, [128, 6 * D]) for t in range(2)]
            nv = sb(st0, "nv", [128, 4, D])
            wa = [sb(st0, "wa%d" % i, [128, 8, 512], BF16) for i in range(2)]
            bb = [sb(st0, "bb%d" % i, [128, 512]) for i in range(2)]
            P.dma("sp", nv[:, :, :], normvecs[:, :, :])
            for t in range(2):
                P.dma("sp", ct[:, :], cexp[t])
                P.act(cb[:, :], ct[:, :], AF.Silu)
                P.trs([(PT[:, c * 128:(c + 1) * 128], cb[:, c * 128:(c + 1) * 128]) for c in range(8)], identb[:, :])
                P.copy("dve", cT[t][:, :, :], PT[:, :].rearrange("p (c t) -> p c t", c=8))
            wv = w_ada.rearrange("(kc p) n -> p kc n", p=128)
            for j in range(12):
                P.dma("pool", wa[j % 2][:, :, :], wv[:, :, j * 512:(j + 1) * 512])
                P.dma("sp", bb[j % 2][:, :], b_ada_b[:, j * 512:(j + 1) * 512])
                for t in range(2):
                    ps = PS[(2 * j + t) % 4]
                    P.mmk(ps[:, :], [(cT[t][:, kc, :], wa[j % 2][:, kc, :]) for kc in range(8)])
                    P.tt("dve", modt[t][:, j * 512:(j + 1) * 512], ps[:, :], bb[j % 2][:, :], ALU.add)
            for t in range(2):
                m = modt[t]
                P.stt("dve", m[:, D:2 * D], m[:, D:2 * D], 1.0, nv[:, 0, :], ALU.add, ALU.mult)
                P.tt("dve", m[:, 2 * D:3 * D], m[:, 2 * D:3 * D], nv[:, 1, :], ALU.mult)
                P.stt("dve", m[:, 4 * D:5 * D], m[:, 4 * D:5 * D], 1.0, nv[:, 2, :], ALU.add, ALU.mult)
                P.tt("dve", m[:, 5 * D:6 * D], m[:, 5 * D:6 * D], nv[:, 3, :], ALU.mult)
                P.dma("sp", modscr[t], m[:, :])
            P.barrier()
            P.emit()

        def norm_to_T(xt, modv, sidx, hidx, tmpA, hb, hT, stt_):
            P.sec("norm_to_T")
            P.sqsum(tmpA[:, :], xt[:, :], stt_[:, 0:1])
            P.act(stt_[:, 1:2], stt_[:, 0:1], AF.Sqrt, bias=cst[:, 0:1], scale=1.0 / D)
            P.recip(stt_[:, 2:3], stt_[:, 1:2])
            P.stt("dve", tmpA[:, :], xt[:, :], stt_[:, 2:3], modv[:, sidx, :], ALU.mult, ALU.mult)
            P.tt("dve", hb[:, :], tmpA[:, :], modv[:, hidx, :], ALU.add)
            P.trs([(PT[:, c * 128:(c + 1) * 128], hb[:, c * 128:(c + 1) * 128]) for c in range(8)], identb[:, :])
            P.copy("act", hT[:, :, :], PT[:, :].rearrange("p (c t) -> p c t", c=8))

        def norm_to_T_g(xt, modv, sidx, hidx, tmpA, hb, hT, stt_):
            P.sec("norm_to_T")
            P.sqsum(tmpA[:, :], xt[:, :], stt_[:, 0:1])
            yield
            P.act(stt_[:, 1:2], stt_[:, 0:1], AF.Sqrt, bias=cst[:, 0:1], scale=1.0 / D)
            yield
            P.recip(stt_[:, 2:3], stt_[:, 1:2])
            yield
            P.stt("dve", tmpA[:, :], xt[:, :], stt_[:, 2:3], modv[:, sidx, :], ALU.mult, ALU.mult)
            yield
            P.tt("dve", hb[:, :], tmpA[:, :], modv[:, hidx, :], ALU.add)
            yield
            P.trs([(PT[:, c * 128:(c + 1) * 128], hb[:, c * 128:(c + 1) * 128]) for c in range(8)], identb[:, :])
            yield
            P.copy("act", hT[:, :, :], PT[:, :].rearrange("p (c t) -> p c t", c=8))
            yield

        def small_scan_terms(af, nh, typ, pss, ex):
            mk = masks[typ]
            P.mm(pss[:, 0:nh], mk[:, MINC, :], af)
            P.mm(pss[:, nh:2 * nh], mk[:, MSTT, :], af)
            P.mm(pss[:, 2 * nh:3 * nh], mk[:, MBLK, :], af)
            P.act(ex[:, 0:3 * nh], pss[:, 0:3 * nh], AF.Exp)

        def small_scan_terms_g(af, nh, typ, pss, ex):
            mk = masks[typ]
            P.mm(pss[:, 0:nh], mk[:, MINC, :], af)
            yield
            P.mm(pss[:, nh:2 * nh], mk[:, MSTT, :], af)
            yield
            P.mm(pss[:, 2 * nh:3 * nh], mk[:, MBLK, :], af)
            yield
            P.act(ex[:, 0:3 * nh], pss[:, 0:3 * nh], AF.Exp)
            yield

        def proj_conv_gen(typ, ngroups, wt, wcol0, hT, psb, xinS_l, xinP_l, histS, histP, cwt, has_bias, accs, dst_fn):
            items = []

            def slot(i):
                n = len(items)
                if 0 <= i < n:
                    it = items[i]
                    if has_bias:
                        P.act(it[0], it[2][0], AF.Identity, bias=cwt[:, it[4], 4:5], scale=cwt[:, it[4], 0:1])
                    else:
                        P.act(it[0], it[2][0], AF.Identity, scale=cwt[:, it[4], 0:1])
                    yield
                if 0 <= i - 1 < n:
                    it = items[i - 1]
                    for k in range(1, 4):
                        P.stt("dve", it[0], it[2][k], cwt[:, it[4], k:k + 1], it[0], ALU.mult, ALU.add)
                        yield
                if 0 <= i - 2 < n:
                    it = items[i - 2]
                    P.act(it[3], it[1][:, :], AF.Silu)
                    yield

            for g in range(ngroups):
                ps = psb[g % 2]
                for j in range(4):
                    c = 4 * g + j
                    P.mmk(ps[:, j * 128:(j + 1) * 128],
                          [(wt[:, kc, wcol0 + c * 128:wcol0 + (c + 1) * 128], hT[:, kc, :]) for kc in range(8)])
                    yield
                if typ == 0:
                    xin = xinS_l[g % 2]
                    P.copy("pool", xin[:, :, :, 0:3], histS[:, 4 * g:4 * g + 4, :, :])
                    yield
                    P.copy("act", xin[:, :, :, 3:11], ps[:, :].rearrange("p (j s t) -> p j s t", j=4, s=16))
                    yield
                else:
                    xin = xinP_l[g % 2]
                    P.copy("pool", xin[:, :, 0:3], histP[:, 4 * g:4 * g + 4, :])
                    yield
                    P.copy("act", xin[:, :, 3:131], ps[:, :].rearrange("p (j t) -> p j t", j=4))
                    yield
                    P.copy("pool", histP[:, 4 * g:4 * g + 4, :], xin[:, :, 128:131])
                    yield
                for j in range(4):
                    c = 4 * g + j
                    a_ = accs[c % 4]
                    if typ == 0:
                        av = a_[:, :].rearrange("p (s t) -> p s t", s=16)
                        sh = [xin[:, j, :, k:k + 8] for k in range(4)]
                    else:
                        av = a_[:, :]
                        sh = [xin[:, j, k:k + 128] for k in range(4)]
                    items.append((av, a_, sh, dst_fn(c), c))
                    yield from slot(c)
            yield from slot(4 * ngroups)
            yield from slot(4 * ngroups + 1)

        def run_gen(g):
            for _ in g:
                pass

        w_in_v = w_in.rearrange("(kc p) n -> p kc n", p=128)
        if STOP_AFTER >= 1 and not SKIP1:
          with contextlib.ExitStack() as st1:
            wI = sb(st1, "wI", [128, 8, 2576], BF16)
            for kc in range(8):
                P.dma("pool", wI[:, kc, :], w_in_v[:, kc, 0:2576])
            modv = sb(st1, "modv", [128, 3, D])
            cw = sb(st1, "cw", [128, 12, 5])
            nw = sb(st1, "ssdnw", [128, D])
            histS = sb(st1, "histS", [128, 12, 16, 3])
            histP = sb(st1, "histP", [128, 12, 3])
            P.dma("sp", cw[:, :, :], ssd_cw[:, :, :])
            P.dma("sp", nw[:, :], ssd_nw[:, :])
            P.dma("sp", histS[:, :, :, :], hist_ssd[:, :, :, :])
            P.memset("pool", histP[:, :, :], 0.0)
            xt = [sb(st1, "xt%d" % i, [128, D]) for i in range(2)]
            tmpA = sb(st1, "tmpA", [128, D])
            hb = sb(st1, "hb", [128, D], BF16)
            hT = sb(st1, "hT", [128, 8, 128], BF16)
            stt_ = sb(st1, "stt", [128, 4])
            xinS_l = [sb(st1, "xinS%d" % i, [128, 4, 16, 11]) for i in range(2)]
            xinP_l = [sb(st1, "xinP%d" % i, [128, 4, 131]) for i in range(2)]
            acc = [sb(st1, "acc%d" % i, [128, 128]) for i in range(4)]
            xc = sb(st1, "xc", [128, 12, 128], BF16)
            zs = sb(st1, "zs", [128, D], BF16)
            sm = sb(st1, "sm", [128, 48])
            af = sb(st1, "af", [128, 16])
            ex = sb(st1, "ex", [128, 48])
            xtm = sb(st1, "xtm", [128, D], BF16)
            xdt = sb(st1, "xdt", [128, D], BF16)
            xDs = sb(st1, "xDs", [128, D], BF16)
            xw = sb(st1, "xw", [128, D], BF16)
            Btm = sb(st1, "Btm", [128, 2, 128], BF16)
            CBm = sb(st1, "CBm", [128, 2, 128])
            L4 = [sb(st1, "L4_%d" % i, [128, 4, 128]) for i in range(2)]
            E4 = [sb(st1, "E4_%d" % i, [128, 4, 128]) for i in range(2)]
            M4 = [sb(st1, "M4_%d" % i, [128, 4, 128], BF16) for i in range(2)]
            yo = sb(st1, "yo", [128, D])
            yy = sb(st1, "yy", [128, D])
            ss = sb(st1, "ss", [128, 8])
            ysb = [sb(st1, "ysb%d" % i, [128, D], BF16) for i in range(2)]
            hTf = sb(st1, "hTf", [128, D])
            hTb = sb(st1, "hTb", [128, D], BF16)
            lhs3 = sb(st1, "lhs3", [128, 8, 48], BF16)
            cv = sb(st1, "cv", [48, 1536])
            CmT = [sb(st1, "CmT%d" % g, [128, 16 * 128], BF16) for g in range(2)]
            h0 = [sb(st1, "h0_%d" % i, [128, 8, 128]) for i in range(2)]
            h0Tb = sb(st1, "h0Tb", [128, D], BF16)
            xwm = sb(st1, "xwm", [128, D], BF16)
            hn = [sb(st1, "hn%d" % i, [128, 8, 128]) for i in range(2)]
            rsel = sb(st1, "rsel", [128, 2, 128])
            decsel = sb(st1, "decsel", [128, 128])
            P.memset("pool", hTf[:, :], 0.0)
            P.memset("pool", hTb[:, :], 0.0)
            for g in range(2):
                P.memset("pool", CmT[g][:, :], 0.0)

            for ti in range(NT):
                typ = 0 if ti == 0 else 1
                mk = masks[typ]
                if ti <= 1:
                    P.dma("sp", modv[:, :, :], modscr[typ].rearrange("p (s d) -> p s d", s=6)[:, 0:3, :])
                x_ = xt[ti % 2]
                P.dma("sp", x_[:, :], xs_all[ti])
                norm_to_T(x_, modv, 1, 0, tmpA, hb, hT, stt_)
                run_gen(proj_conv_gen(typ, 3, wI, 1024, hT, PS[0:2], xinS_l, xinP_l, histS, histP, cw, True, acc,
                                      lambda c: xc[:, c, :]))
                P.sec("z (token-major) -> silu")
                for nb in range(2):
                    ps = PS[2 + nb]
                    P.mmk(ps[:, :], [(hT[:, kc, :], wI[:, kc, nb * 512:(nb + 1) * 512]) for kc in range(8)])
                    P.act(zs[:, nb * 512:(nb + 1) * 512], ps[:, :], AF.Silu)
                P.sec("dt")
                pd = PS[4]
                P.mmk(pd[:, 0:16], [(hT[:, kc, :], wI[:, kc, 2560:2576]) for kc in range(8)])
                P.tt("dve", sm[:, 0:16], pd[:, 0:16], smv[:, 0:16], ALU.add)
                P.act(sm[:, 16:32], sm[:, 0:16], AF.Exp)
                P.act(sm[:, 32:48], sm[:, 16:32], AF.Ln, bias=cst[:, 1:2], scale=1.0)
                P.tt("dve", af[:, :], sm[:, 32:48], negA[:, 0:16], ALU.mult)
                small_scan_terms(af[:, :], 16, typ, PS[4][:, 64:112], ex)
                P.sec("token-major x, xdt, xD, xw, B")
                P.trs([(PT[:, c * 128:(c + 1) * 128], xc[:, c, :]) for c in range(8)], identb[:, :])
                P.copy("act", xtm[:, :], PT[:, :])
                x3 = xtm[:, :].rearrange("p (h d) -> p h d", h=16)
                P.tt("dve", xdt[:, :].rearrange("p (h d) -> p h d", h=16), x3, bc(sm[:, 32:48], 2, 64), ALU.mult)
                P.tt("pool", xDs[:, :].rearrange("p (h d) -> p h d", h=16), x3, bc(smv[:, 32:48], 2, 64), ALU.mult)
                P.tt("pool", xw[:, :].rearrange("p (h d) -> p h d", h=16),
                     xdt[:, :].rearrange("p (h d) -> p h d", h=16), bc(ex[:, 16:32], 2, 64), ALU.mult)
                P.trs([(PT[:, g * 128:(g + 1) * 128], xc[:, 8 + g, :]) for g in range(2)], identb[:, :])
                P.copy("act", Btm[:, :, :], PT[:, 0:256].rearrange("p (g n) -> p g n", g=2))
                P.sec("CB^T masked")
                pc = PS[5]
                for g in range(2):
                    P.mm(pc[:, g * 128:(g + 1) * 128], xc[:, 8 + g, :], xc[:, 10 + g, :])
                P.tt("dve", CBm[:, :, :], pc[:, 0:256].rearrange("p (g l) -> p g l", g=2), bc(mk[:, MINC, :], 1, 2), ALU.mult)
                P.sec("y_off raw")
                if typ == 1:
                    for g in range(2):
                        P.mm(PS[2 + g][:, :], xc[:, 10 + g, :], hTb[:, g * 512:(g + 1) * 512])
                else:
                    for g in range(2):
                        base = CmT[g][:, :]
                        dst = bass.AP(base.tensor, base.offset, [[16 * 128, 128], [136, 16], [1, 8]])
                        P.op("pool", (lambda e, dst=dst, src=xc[:, 10 + g, :].rearrange("p (s t) -> p s t", s=16):
                                      e.tensor_copy(out=dst, in_=src)), ins=[xc[:, 0, :]], outs=[base])
                    for hh in range(2):
                        P.tt("pool", rsel[:, hh, :].rearrange("p (j s) -> p j s", j=8),
                             bc(af[:, hh:16:2], 2, 16), bc(blk16[:, :], 1, 8), ALU.mult)
                        P.mm(PS[5][:, 256 + hh * 128:256 + (hh + 1) * 128], onesf[:, :], rsel[:, hh, :])
                    P.act(decsel[0:64, :], PS[5][0:64, 256:384], AF.Exp)
                    P.act(decsel[64:128, :], PS[5][64:128, 384:512], AF.Exp)
                    for s in range(16):
                        h0_ = h0[s % 2]
                        hn_ = hn[s % 2]
                        P.dma("sp", h0_[:, :, :], st_ssd[s].rearrange("(j p) n -> p j n", p=128))
                        pa, pb = PS[0], PS[1]
                        P.trs([((pa if j < 4 else pb)[:, (j % 4) * 128:(j % 4 + 1) * 128], h0_[:, j, :]) for j in range(8)],
                              identf[:, :])
                        P.copy("act", h0Tb[:, 0:512], pa[:, :])
                        P.copy("dve", h0Tb[:, 512:1024], pb[:, :])
                        for g in range(2):
                            P.op("pe", (lambda e, o=PS[2 + g][:, :], l=CmT[g][:, s * 128:(s + 1) * 128],
                                        r=h0Tb[:, g * 512:(g + 1) * 512], s=s:
                                        e.matmul(o, lhsT=l, rhs=r, start=(s == 0), stop=(s == 15))),
                                 ins=[CmT[g][:, :], h0Tb[:, :]], outs=[PS[2 + g][:, :]])
                        P.act(xwm[:, :], xw[:, :], AF.Identity, scale=blk16[:, s:s + 1])
                        for half in range(2):
                            pn = PS[half]
                            for jj in range(4):
                                j = half * 4 + jj
                                P.mm(pn[:, jj * 128:(jj + 1) * 128], xwm[:, j * 128:(j + 1) * 128], Btm[:, j // 4, :])
                            for jj in range(4):
                                j = half * 4 + jj
                                P.stt("dve", hn_[:, j, :], h0_[:, j, :], decsel[:, j * 16 + s:j * 16 + s + 1],
                                      pn[:, jj * 128:(jj + 1) * 128], ALU.mult, ALU.add)
                        P.dma("sp", o_ssd_s[s].rearrange("(j p) n -> p j n", p=128), hn_[:, :, :])
                P.sec("per-head intra-chunk")
                for q in range(4):
                    g = q // 2
                    L_, E_, M_ = L4[q % 2], E4[q % 2], M4[q % 2]
                    P.tt("dve", L_[:, :, :], bc(mk[:, MSTT, :], 1, 4), bc(af[:, 4 * q:4 * q + 4], 2, 128), ALU.mult)
                    pg = PS[5 + (q % 2)]
                    for j in range(4):
                        P.mm(pg[:, j * 128:(j + 1) * 128], L_[:, j, :], mk[:, MINC, :])
                    P.act(E_[:, :, :], pg[:, :].rearrange("p (j l) -> p j l", j=4), AF.Exp)
                    P.tt("dve", M_[:, :, :], E_[:, :, :], bc(CBm[:, g, :], 1, 4), ALU.mult)
                    py = PS[0 + g]
                    for j in range(4):
                        h = 4 * q + j
                        hh = h % 8
                        P.mmk(py[:, hh * 64:(hh + 1) * 64],
                              [(identb[:, :], xDs[:, h * 64:(h + 1) * 64]), (M_[:, j, :], xdt[:, h * 64:(h + 1) * 64])])
                P.sec("combine y = y_diag + eacs * y_off")
                for g in range(2):
                    P.copy("act", yo[:, g * 512:(g + 1) * 512], PS[2 + g][:, :])
                P.tt("dve", yo[:, :].rearrange("p (h d) -> p h d", h=16), yo[:, :].rearrange("p (h d) -> p h d", h=16),
                     bc(ex[:, 0:16], 2, 64), ALU.mult)
                for g in range(2):
                    P.tt("dve", yy[:, g * 512:(g + 1) * 512], yo[:, g * 512:(g + 1) * 512], PS[0 + g][:, :], ALU.add)
                P.sec("gate with silu(z), group rmsnorm")
                P.tt("dve", yy[:, :], yy[:, :], zs[:, :], ALU.mult)
                for g in range(2):
                    P.sqsum(yo[:, g * 512:(g + 1) * 512], yy[:, g * 512:(g + 1) * 512], ss[:, g:g + 1])
                P.act(ss[:, 2:4], ss[:, 0:2], AF.Sqrt, bias=cst[:, 0:1], scale=1.0 / 512)
                P.recip(ss[:, 4:6], ss[:, 2:4])
                y_ = ysb[ti % 2]
                for g in range(2):
                    P.stt("dve", y_[:, g * 512:(g + 1) * 512], yy[:, g * 512:(g + 1) * 512], ss[:, 4 + g:5 + g],
                          nw[:, g * 512:(g + 1) * 512], ALU.mult, ALU.mult)
                P.dma("sp", yssd_scr[ti], y_[:, :])
                P.sec("state update (prompt chunks)")
                if typ == 1:
                    for g in range(2):
                        P.mm(PS[2 + g][:, :], Btm[:, g, :], xw[:, g * 512:(g + 1) * 512])
                    h3 = hTf[:, :].rearrange("p (h d) -> p h d", h=16)
                    P.tt("dve", h3, h3, bc(ex[:, 32:48], 2, 64), ALU.mult)
                    for g in range(2):
                        P.tt("dve", hTf[:, g * 512:(g + 1) * 512], hTf[:, g * 512:(g + 1) * 512], PS[2 + g][:, :], ALU.add)
                    P.copy("act", hTb[:, :], hTf[:, :])
                P.sec("conv-state outputs (last 3 raw xbc rows)")
                if ti == 0 or ti == NT - 1:
                    M3 = 48 if typ == 0 else 3
                    if typ == 0:
                        P.copy("pool", lhs3[:, :, :].rearrange("p k (s t) -> p k s t", s=16),
                               hT[:, :, :].rearrange("p k (s t) -> p k s t", s=16)[:, :, :, 5:8])
                    else:
                        P.copy("pool", lhs3[:, :, 0:3], hT[:, :, 125:128])
                    for nb in range(3):
                        ps = PS[5 + (nb % 2)]
                        P.mmk(ps[0:M3, :], [(lhs3[:, kc, 0:M3], wI[:, kc, 1024 + nb * 512:1024 + (nb + 1) * 512])
                                            for kc in range(8)])
                        P.copy("act", cv[0:M3, nb * 512:(nb + 1) * 512], ps[0:M3, :])
                    P.dma("sp", (o_ssdc_s if typ == 0 else o_ssdc_p)[:, :], cv[0:M3, :])
            for half in range(2):
                ps = PS[half]
                P.trs([(ps[:, jj * 128:(jj + 1) * 128], hTf[:, (half * 4 + jj) * 128:(half * 4 + jj + 1) * 128])
                       for jj in range(4)], identf[:, :])
                P.copy("act", hn[half][:, 0:4, :], ps[:, :].rearrange("p (j n) -> p j n", j=4))
                P.dma("sp", o_ssd_p[half * 512:(half + 1) * 512, :].rearrange("(j p) n -> p j n", p=128), hn[half][:, 0:4, :])
            P.barrier()
            P.emit()
        ps1.close()
        if STOP_AFTER >= 2:
          with contextlib.ExitStack() as st2:
            PA = [st2.enter_context(nc.psum_tensor("pa%d" % i, [128, 512], F32)) for i in range(2)]
            PM = st2.enter_context(nc.psum_tensor("pm", [128, 512], F32))
            LPB = [st2.enter_context(nc.psum_tensor("lpb%d" % i, [128, 512], F32)) for i in range(4)]
            LPS = [[LPB[ln][:, i * 128:(i + 1) * 128] for i in range(4)] for ln in range(4)]
            wII = sb(st2, "wII", [128, 8, 4112], BF16)
            for kc in range(8):
                P.dma("pool", wII[:, kc, :], w_in_v[:, kc, 2576:6688])
            wO = sb(st2, "wO", [128, 16, 1024], BF16)
            w_out_v = w_out.rearrange("(kc p) n -> p kc n", p=128)
            for kc in range(16):
                P.dma("pool", wO[:, kc, :], w_out_v[:, kc, :])
            modv = sb(st2, "modv2", [128, 3, D])
            cwg = sb(st2, "cwg", [128, 24, 4])
            gnw = sb(st2, "gnw", [128, 128])
            histP = sb(st2, "histPg", [128, 24, 3])
            P.dma("sp", cwg[:, :, :], gdn_cw[:, :, :])
            P.dma("sp", gnw[:, :], gdn_nw[:, :])
            P.memset("pool", histP[:, :, :], 0.0)
            tmpA = sb(st2, "tmpA2", [128, D])
            hT = sb(st2, "hT2", [128, 8, 128], BF16)
            stt_ = sb(st2, "stt2", [128, 4])
            xinP_l = [sb(st2, "xinP2_%d" % i, [128, 4, 131]) for i in range(2)]
            acc = [sb(st2, "acc2_%d" % i, [128, 128]) for i in range(4)]
            lhs3 = sb(st2, "lhs3g", [128, 8, 48], BF16)
            tmpT = sb(st2, "tmpT2", [128, D])
            sttT = sb(st2, "sttT2", [128, 4])
            ss8 = sb(st2, "ss8", [128, 24])
            otm = sb(st2, "otm", [128, D])
            mixed = sb(st2, "mixed", [128, 2 * D], BF16)
            mixedT = sb(st2, "mixedT", [128, 16, 128], BF16)

            class FB:
                pass

            def make_fb(i, stk):
                fb = FB()
                fb.xt = sb(stk, "f%d_xt" % i, [128, D])
                fb.hb = sb(stk, "f%d_hb" % i, [128, D], BF16)
                fb.qk = sb(stk, "f%d_qk" % i, [128, 16, 128], BF16)
                fb.vfm = sb(stk, "f%d_vfm" % i, [128, 8, 128], BF16)
                fb.ktm = sb(stk, "f%d_ktm" % i, [128, 8, 128], BF16)
                fb.vtm = sb(stk, "f%d_vtm" % i, [128, 8, 128], BF16)
                fb.gs = sb(stk, "f%d_gs" % i, [128, D], BF16)
                fb.sm = sb(stk, "f%d_sm" % i, [128, 64])
                fb.gf = sb(stk, "f%d_gf" % i, [128, 8])
                fb.ex = sb(stk, "f%d_ex" % i, [128, 24])
                return fb

            class Lane:
                pass

            lanes = []

            def make_lane(ln, stk):
                B = Lane()
                B.ps = LPS[ln]
                f = lambda nm, dt=F32: sb(stk, "ln%d_%s" % (ln, nm), [128, 128], dt)
                B.P = [f("P0"), f("P1")]
                B.PT = [f("PT0"), f("PT1")]
                B.X = [f("X0"), f("X1")]
                B.ot = f("ot")
                B.L, B.Dm, B.DmI, B.DmS = B.ot, B.X[1], B.PT[1], B.P[1]
                B.attnT, B.TTb, B.R, B.vn, B.kout = (f("attnT", BF16), f("TTb", BF16), f("R", BF16),
                                                     f("vn", BF16), f("kout", BF16))
                lanes.append(B)

            make_lane(0, st2)
            fbs = [make_fb(0, st2)]

            def front_gen(ti, typ, SB, fb):
                xt, hb, qk, vfm, sm, gf, ex = fb.xt, fb.hb, fb.qk, fb.vfm, fb.sm, fb.gf, fb.ex
                P.sec("F:load+norm")
                if ti <= 1:
                    P.dma("sp", modv[:, :, :], modscr[typ].rearrange("p (s d) -> p s d", s=6)[:, 0:3, :])
                    yield
                P.dma("sp", xt[:, :], xs_all[ti])
                yield
                yield from norm_to_T_g(xt, modv, 1, 0, tmpA, hb, hT, stt_)
                yield
                yield from proj_conv_gen(typ, 6, wII, 0, hT, PA, (SB.xinS_l if typ == 0 else None), xinP_l,
                                         (SB.histS if typ == 0 else None), histP, cwg, False, acc,
                                         lambda c: (qk[:, c, :] if c < 16 else vfm[:, c - 16, :]))
                if ti == 0 or ti == NT - 1:
                    P.sec("F:convstate")
                    M3 = 48 if typ == 0 else 3
                    if typ == 0:
                        P.copy("pool", lhs3[:, :, :].rearrange("p k (s t) -> p k s t", s=16),
                               hT[:, :, :].rearrange("p k (s t) -> p k s t", s=16)[:, :, :, 5:8])
                        yield
                    else:
                        P.copy("pool", lhs3[:, :, 0:3], hT[:, :, 125:128])
                        yield
                    for cc in range(3):
                        for nb in range(2):
                            c0 = cc * 1024 + nb * 512
                            P.mmk(PA[nb][0:M3, :], [(lhs3[:, kc, 0:M3], wII[:, kc, c0:c0 + 512]) for kc in range(8)])
                            yield
                            P.copy("act", tmpA[0:M3, nb * 512:(nb + 1) * 512], PA[nb][0:M3, :])
                            yield
                        P.dma("sp", (o_gdnc_s if typ == 0 else o_gdnc_p)[:, cc * 1024:(cc + 1) * 1024], tmpA[0:M3, :])
                        yield
                    yield
                P.sec("F:gate")
                for nb in range(2):
                    P.mmk(PA[nb][:, :], [(hT[:, kc, :], wII[:, kc, 3072 + nb * 512:3072 + (nb + 1) * 512])
                                         for kc in range(8)])
                    yield
                    P.act(fb.gs[:, nb * 512:(nb + 1) * 512], PA[nb][:, :], AF.Silu)
                    yield
                yield
                P.sec("F:beta/g")
                P.mmk(PM[:, 0:16], [(hT[:, kc, :], wII[:, kc, 4096:4112]) for kc in range(8)])
                yield
                P.act(sm[:, 0:8], PM[:, 0:8], AF.Exp, scale=-1.0)
                yield
                P.ts("dve", sm[:, 0:8], sm[:, 0:8], 1.0, None, ALU.add)
                yield
                P.recip(sm[:, 8:16], sm[:, 0:8])
                yield
                P.ts("dve", sm[:, 16:24], sm[:, 8:16], -1.0, None, ALU.mult)
                yield
                P.tt("dve", sm[:, 24:32], PM[:, 8:16], smv[:, 48:56], ALU.add)
                yield
                P.act(sm[:, 32:40], sm[:, 24:32], AF.Exp)
                yield
                P.act(sm[:, 40:48], sm[:, 32:40], AF.Ln, bias=cst[:, 1:2], scale=1.0)
                yield
                P.tt("dve", gf[:, :], sm[:, 40:48], negA[:, 16:24], ALU.mult)
                yield
                yield
                P.sec("F:beta/g")
                yield from small_scan_terms_g(gf[:, :], 8, typ, PM[:, 64:88], ex)
                P.ts("dve", sm[:, 48:56], ex[:, 0:8], -1.0, None, ALU.mult)
                yield
                yield
                for half in range(2):
                    P.sec("F:l2norm")
                    src = qk[:, half * 8:(half + 1) * 8, :]
                    P.tt("pool", hb[:, :].rearrange("p (c t) -> p c t", c=8), src, src, ALU.mult)
                    yield
                    for i in range(2):
                        rq = tmpA[:, i * 512:(i + 1) * 512]
                        P.mm(PA[i][:, :], onesb[:, :], hb[:, i * 512:(i + 1) * 512])
                        yield
                        if half == 0:
                            P.act(rq, PA[i][:, :], AF.Sqrt, bias=cst[:, 2:3], scale=128.0)
                            yield
                        else:
                            P.act(rq, PA[i][:, :], AF.Sqrt, bias=cst[:, 0:1], scale=1.0)
                            yield
                        P.recip(rq, rq)
                        yield
                        dst = qk[:, half * 8 + 4 * i:half * 8 + 4 * i + 4, :]
                        P.tt("dve", dst, dst, rq.rearrange("p (c t) -> p c t", c=4), ALU.mult)
                        yield
                    yield
                P.sec("F:transposes")
                P.copy("pool", hb[:, :].rearrange("p (c t) -> p c t", c=8), qk[:, 8:16, :])
                yield
                P.trs([(PT[:, j * 128:(j + 1) * 128], qk[:, 8 + j, :]) for j in range(8)], identb[:, :])
                yield
                P.copy("act", fb.ktm[:, :, :], PT[:, :].rearrange("p (h d) -> p h d", h=8))
                yield
                yield
                P.sec("F:transposes")
                P.trs([(PT[:, j * 128:(j + 1) * 128], vfm[:, j, :]) for j in range(8)], identb[:, :])
                yield
                P.copy("act", fb.vtm[:, :, :], PT[:, :].rearrange("p (h d) -> p h d", h=8))
                yield
                if typ == 0:
                    P.tt("pool", SB.rselg[:, :].rearrange("p (h s) -> p h s", h=8),
                         bc(gf[:, :], 2, 16), bc(blk16[:, :], 1, 8), ALU.mult)
                    yield
                    P.mm(PM[:, 128:256], onesf[:, :], SB.rselg[:, :])
                    yield
                    P.act(SB.gendsel[:, :], PM[:, 128:256], AF.Exp)
                    yield
                yield

            def head_gen(h, B, typ, SB, fb):
                mk = masks[typ]
                m_lev = 3 if typ == 0 else 7
                qk, hb, sm, gf, ex, ktm, vtm = fb.qk, fb.hb, fb.sm, fb.gf, fb.ex, fb.ktm, fb.vtm
                kT = qk[:, 8 + h, :]
                qT = qk[:, h, :]
                _st = ["H:decay"]
                P.sec(_st[0])
                P.act(B.L[:, :], mk[:, MSTT, :], AF.Identity, scale=gf[:, h:h + 1])
                yield
                P.mm(B.ps[3][:, :], B.L[:, :], mk[:, MINC, :])
                yield
                P.act(B.Dm[:, :], B.ps[3][:, :], AF.Exp)
                yield
                P.tt("dve", B.DmI[:, :], B.Dm[:, :], mk[:, MINC, :], ALU.mult)
                yield
                P.tt("dve", B.DmS[:, :], B.Dm[:, :], mk[:, MSTR, :], ALU.mult)
                yield
                P.mm(B.ps[0][:, :], kT, hb[:, h * 128:(h + 1) * 128])
                yield
                P.mm(B.ps[1][:, :], kT, qT)
                yield
                P.stt("dve", B.P[0][:, :], B.ps[0][:, :], sm[:, 16 + h:17 + h], B.DmS[:, :], ALU.mult, ALU.mult)
                yield
                P.tt("dve", B.attnT[:, :], B.ps[1][:, :], B.DmI[:, :], ALU.mult)
                yield
                yield
                _st[0] = "H:dbl"
                P.sec(_st[0])
                P.tr(B.ps[2][:, :], B.P[0][:, :], identf[:, :])
                yield
                P.copy("act", B.PT[0][:, :], B.ps[2][:, :])
                yield
                P.tt("dve", B.X[0][:, :], B.P[0][:, :], identf[:, :], ALU.add)
                yield
                yield
                P.sec(_st[0])
                if m_lev > 1:
                    P.mm(B.ps[0][:, :], B.P[0][:, :], B.PT[0][:, :])
                    yield
                    if m_lev > 2:
                        P.mm(B.ps[1][:, :], B.PT[0][:, :], B.P[0][:, :])
                        yield
                    P.copy("act", B.PT[1][:, :], B.ps[0][:, :])
                    yield
                    if m_lev > 2:
                        P.copy("dve", B.P[1][:, :], B.ps[1][:, :])
                        yield
                    yield
                    P.sec(_st[0])
                xi = 0
                for j in range(1, m_lev):
                    cur, nx = j % 2, (j + 1) % 2
                    if j + 1 < m_lev:
                        P.mm(B.ps[0][:, :], B.P[cur][:, :], B.PT[cur][:, :])
                        yield
                        if j + 2 < m_lev:
                            P.mm(B.ps[1][:, :], B.PT[cur][:, :], B.P[cur][:, :])
                            yield
                    P.mm(B.ps[2][:, :], B.PT[cur][:, :], B.X[xi][:, :])
                    yield
                    if j + 1 < m_lev:
                        P.copy("act", B.PT[nx][:, :], B.ps[0][:, :])
                        yield
                        if j + 2 < m_lev:
                            P.copy("dve", B.P[nx][:, :], B.ps[1][:, :])
                            yield
                    P.tt("dve", B.X[1 - xi][:, :], B.X[xi][:, :], B.ps[2][:, :], ALU.add)
                    yield
                    xi = 1 - xi
                    yield
                    P.sec(_st[0])
                _st[0] = "H:state"
                P.sec(_st[0])
                P.copy("act", B.TTb[:, :], B.X[xi][:, :])
                yield
                if typ == 1:
                    Sf_h, Sb_h = SB.Sf[h], SB.Sb[h]
                    P.mm(B.ps[3][:, :], kT, Sb_h[:, :])
                    yield
                else:
                    P.dma("sp", SB.S0f[:, :, :], st_gdn[:, h, :, :].rearrange("s d e -> d s e"))
                    yield
                    P.copy("act", SB.S0b[:, :, :], SB.S0f[:, :, :])
                    yield
                    for (dstT, srcT) in ((SB.kTm, kT), (SB.qTm, qT)):
                        base = dstT[:, :]
                        dap = bass.AP(base.tensor, base.offset, [[16 * 128, 128], [136, 16], [1, 8]])
                        P.op("pool", (lambda e, dap=dap, src=srcT.rearrange("p (s t) -> p s t", s=16):
                                      e.tensor_copy(out=dap, in_=src)), ins=[srcT], outs=[base])
                        yield
                    P.mmk(B.ps[3][:, :], [(SB.kTm[:, s * 128:(s + 1) * 128], SB.S0b[:, s, :]) for s in range(16)])
                    yield
                P.stt("dve", B.R[:, :], B.ps[3][:, :], sm[:, 48 + h:49 + h], vtm[:, h, :], ALU.mult, ALU.add)
                yield
                P.mm(B.ps[0][:, :], B.TTb[:, :], B.R[:, :])
                yield
                P.act(B.vn[:, :], B.ps[0][:, :], AF.Identity, scale=sm[:, 8 + h:9 + h])
                yield
                yield
                P.sec(_st[0])
                if typ == 1:
                    P.mm(B.ps[1][:, :], qT, Sb_h[:, :])
                    yield
                else:
                    P.mmk(B.ps[1][:, :], [(SB.qTm[:, s * 128:(s + 1) * 128], SB.S0b[:, s, :]) for s in range(16)])
                    yield
                P.mm(B.ps[2][:, :], B.attnT[:, :], B.vn[:, :])
                yield
                P.act(B.ot[:, :], B.ps[1][:, :], AF.Identity, scale=ex[:, h:h + 1])
                yield
                P.tt("dve", otm[:, h * 128:(h + 1) * 128], B.ot[:, :], B.ps[2][:, :], ALU.add)
                yield
                P.act(B.kout[:, :], ktm[:, h, :], AF.Identity, scale=ex[:, 8 + h:9 + h])
                yield
                yield
                P.sec(_st[0])
                if typ == 1:
                    P.mm(B.ps[3][:, :], B.kout[:, :], B.vn[:, :])
                    yield
                    P.stt("dve", Sf_h[:, :], Sf_h[:, :], ex[:, 16 + h:17 + h], B.ps[3][:, :], ALU.mult, ALU.add)
                    yield
                    P.copy("pool", Sb_h[:, :], Sf_h[:, :])
                    yield
                else:
                    for s in range(16):
                        km = SB.koutm[s % 2]
                        P.act(km[:, :], B.kout[:, :], AF.Identity, scale=blk16[:, s:s + 1])
                        yield
                        pss = B.ps[3]
                        P.mm(pss[:, :], km[:, :], B.vn[:, :])
                        yield
                        P.stt("dve", SB.S0f[:, s, :], SB.S0f[:, s, :], SB.gendsel[:, h * 16 + s:h * 16 + s + 1],
                              pss[:, :], ALU.mult, ALU.add)
                        yield
                    P.dma("sp", o_gdn_s[:, h, :, :].rearrange("s d e -> d s e"), SB.S0f[:, :, :])
                    yield
                yield

            def tail_gen(ti, fb):
                xt = fb.xt
                P.sec("T:onorm")
                P.dma("sp", mixed[:, 0:D], yssd_scr[ti])
                yield
                P.tt("pool", tmpT[:, :], otm[:, :], otm[:, :], ALU.mult)
                yield
                P.red("dve", ss8[:, 0:8], tmpT[:, :].rearrange("p (h d) -> p h d", h=8))
                yield
                P.act(ss8[:, 8:16], ss8[:, 0:8], AF.Sqrt, bias=cst[:, 0:1], scale=1.0 / 128)
                yield
                P.recip(ss8[:, 16:24], ss8[:, 8:16])
                yield
                o3 = otm[:, :].rearrange("p (h d) -> p h d", h=8)
                P.tt("dve", o3, o3, bc(ss8[:, 16:24], 2, 128), ALU.mult)
                yield
                P.tt("dve", o3, o3, bc(gnw[:, :], 1, 8), ALU.mult)
                yield
                P.tt("dve", mixed[:, D:2 * D], otm[:, :], fb.gs[:, :], ALU.mult)
                yield
                yield
                P.sec("T:outproj")
                for half in range(2):
                    P.trs([(PT[:, j * 128:(j + 1) * 128], mixed[:, (half * 8 + j) * 128:(half * 8 + j + 1) * 128])
                           for j in range(8)], identb[:, :])
                    yield
                    P.copy("act", mixedT[:, half * 8:(half + 1) * 8, :], PT[:, :].rearrange("p (c t) -> p c t", c=8))
                    yield
                yield
                P.sec("T:outproj")
                for nb in range(2):
                    P.mmk(PA[nb][:, :], [(mixedT[:, kc, :], wO[:, kc, nb * 512:(nb + 1) * 512]) for kc in range(16)])
                    yield
                    P.copy("act", tmpT[:, nb * 512:(nb + 1) * 512], PA[nb][:, :])
                    yield
                yield
                P.sec("T:resid")
                P.sqsum(otm[:, :], tmpT[:, :], sttT[:, 0:1])
                yield
                P.act(sttT[:, 1:2], sttT[:, 0:1], AF.Sqrt, bias=cst[:, 0:1], scale=1.0 / D)
                yield
                P.recip(sttT[:, 2:3], sttT[:, 1:2])
                yield
                P.stt("dve", tmpT[:, :], tmpT[:, :], sttT[:, 2:3], modv[:, 2, :], ALU.mult, ALU.mult)
                yield
                P.tt("dve", xt[:, :], tmpT[:, :], xt[:, :], ALU.add)
                yield
                P.dma("sp", x1_scr[ti], xt[:, :])
                yield
                yield

            def speed(g, k):
                while True:
                    for _ in range(k):
                        try:
                            next(g)
                        except StopIteration:
                            return
                    yield

            def run_rr(gens):
                alive = list(gens)
                while alive:
                    nxt = []
                    for gn in alive:
                        try:
                            next(gn)
                            nxt.append(gn)
                        except StopIteration:
                            pass
                    alive = nxt

            def back(ti, typ, SB, fb, nl, side):
                side = [side] if side is not None else []
                for h0_ in range(0, 8, nl):
                    gens = [head_gen(h0_ + i, lanes[i], typ, SB, fb) for i in range(min(nl, 8 - h0_))]
                    alive = gens + side
                    while any(g in alive for g in gens):
                        nxt = []
                        for gn in alive:
                            try:
                                next(gn)
                                nxt.append(gn)
                            except StopIteration:
                                if gn in side:
                                    side = []
                        alive = nxt
                run_rr([tail_gen(ti, fb)] + side)

            class SBufs:
                pass

            with contextlib.ExitStack() as st2s:
                SB = SBufs()
                SB.histS = sb(st2s, "histSg", [128, 24, 16, 3])
                SB.xinS_l = [sb(st2s, "xinS2_%d" % i, [128, 4, 16, 11]) for i in range(2)]
                SB.S0f = sb(st2s, "S0f", [128, 16, 128])
                SB.S0b = sb(st2s, "S0b", [128, 16, 128], BF16)
                SB.kTm = sb(st2s, "kTm", [128, 16 * 128], BF16)
                SB.qTm = sb(st2s, "qTm", [128, 16 * 128], BF16)
                SB.koutm = [sb(st2s, "koutm%d" % i, [128, 128], BF16) for i in range(2)]
                SB.rselg = sb(st2s, "rselg", [128, 128])
                SB.gendsel = sb(st2s, "gendsel", [128, 128])
                P.dma("sp", SB.histS[:, :, :, :], hist_gdn[:, :, :, :])
                P.memset("pool", SB.kTm[:, :], 0.0)
                P.memset("pool", SB.qTm[:, :], 0.0)
                run_rr([front_gen(0, 0, SB, fbs[0])])
                back(0, 0, SB, fbs[0], 1, None)
                P.barrier()
                P.emit()
            with contextlib.ExitStack() as st2p:
                SB = SBufs()
                SB.Sf = [sb(st2p, "Sf%d" % h, [128, 128]) for h in range(8)]
                SB.Sb = [sb(st2p, "Sb%d" % h, [128, 128], BF16) for h in range(8)]
                for ln in range(1, P2_NL):
                    make_lane(ln, st2p)
                fbs.append(make_fb(1, st2p))
                for h in range(8):
                    P.memset("pool", SB.Sf[h][:, :], 0.0)
                    P.memset("pool", SB.Sb[h][:, :], 0.0)
                run_rr([front_gen(1, 1, SB, fbs[1])])
                for ti in range(1, NT):
                    side = speed(front_gen(ti + 1, 1, SB, fbs[(ti + 1) % 2]), 2) if ti + 1 < NT else None
                    back(ti, 1, SB, fbs[ti % 2], P2_NL, side)
                for h in range(8):
                    P.dma("sp", o_gdn_p[h * 128:(h + 1) * 128, :], SB.Sf[h][:, :])
                P.barrier()
                P.emit()
        if STOP_AFTER >= 3:
          with contextlib.ExitStack() as st3:
            PB = [st3.enter_context(nc.psum_tensor("pb%d" % i, [128, 512], F32)) for i in range(7)]
            wG = sb(st3, "wG", [128, 8, 2 * DFF], BF16)
            w_gu_v = w_gu.rearrange("(kc p) n -> p kc n", p=128)
            for kc in range(8):
                P.dma("pool", wG[:, kc, :], w_gu_v[:, kc, :])
            wD = sb(st3, "wD", [128, 22, D], BF16)
            w_dn_v = w_dn.rearrange("(kc p) n -> p kc n", p=128)
            for kc in range(22):
                P.dma("pool", wD[:, kc, :], w_dn_v[:, kc, :])
            modv = sb(st3, "modv3", [128, 3, D])
            xt = [sb(st3, "xt3_%d" % i, [128, D]) for i in range(2)]
            tmpA = [sb(st3, "tmpA3_%d" % i, [128, D]) for i in range(2)]
            tmpB = sb(st3, "tmpB3", [128, D])
            hb = [sb(st3, "hb3_%d" % i, [128, D], BF16) for i in range(2)]
            hT = [sb(st3, "hT3_%d" % i, [128, 8, 128], BF16) for i in range(2)]
            stt_ = [sb(st3, "stt3_%d" % i, [128, 4]) for i in range(2)]
            sg = [sb(st3, "sg%d" % i, [128, 512]) for i in range(2)]
            hid = sb(st3, "hid", [128, DFF], BF16)
            hidT = sb(st3, "hidT", [128, 22, 128], BF16)

            def ffn_gen(ti):
                typ = 0 if ti == 0 else 1
                pr = ti % 2
                x_, tA, hb_, hT_, st_ = xt[pr], tmpA[pr], hb[pr], hT[pr], stt_[pr]
                if ti <= 1:
                    P.dma("sp", modv[:, :, :], modscr[typ].rearrange("p (s d) -> p s d", s=6)[:, 3:6, :])
                P.dma("sp", x_[:, :], x1_scr[ti])
                yield
                yield from norm_to_T_g(x_, modv, 1, 0, tA, hb_, hT_, st_)
                for i in range(6):
                    w = 512 if i < 5 else 256
                    pg_, pu_ = PB[(2 * i) % 6], PB[(2 * i + 1) % 6]
                    P.mmk(pg_[:, 0:w], [(hT_[:, kc, :], wG[:, kc, i * 512:i * 512 + w]) for kc in range(8)])
                    yield
                    P.mmk(pu_[:, 0:w], [(hT_[:, kc, :], wG[:, kc, DFF + i * 512:DFF + i * 512 + w]) for kc in range(8)])
                    yield
                    s_ = sg[i % 2]
                    P.act(s_[:, 0:w], pg_[:, 0:w], AF.Silu)
                    yield
                    P.tt("dve", hid[:, i * 512:i * 512 + w], s_[:, 0:w], pu_[:, 0:w], ALU.mult)
                    yield
                yield "HALF"
                for grp in range(3):
                    n = 8 if grp < 2 else 6
                    P.trs([(PT[:, j * 128:(j + 1) * 128], hid[:, (grp * 8 + j) * 128:(grp * 8 + j + 1) * 128])
                           for j in range(n)], identb[:, :])
                    yield
                    P.copy("act", hidT[:, grp * 8:grp * 8 + n, :],
                           PT[:, 0:n * 128].rearrange("p (c t) -> p c t", c=n))
                    yield
                for nb in range(2):
                    P.mmk(PB[6][:, :], [(hidT[:, kc, :], wD[:, kc, nb * 512:(nb + 1) * 512]) for kc in range(22)])
                    yield
                    P.copy("act", tA[:, nb * 512:(nb + 1) * 512], PB[6][:, :])
                    yield
                P.sqsum(tmpB[:, :], tA[:, :], st_[:, 0:1])
                yield
                P.act(st_[:, 1:2], st_[:, 0:1], AF.Sqrt, bias=cst[:, 0:1], scale=1.0 / D)
                yield
                P.recip(st_[:, 2:3], st_[:, 1:2])
                yield
                P.stt("dve", tA[:, :], tA[:, :], st_[:, 2:3], modv[:, 2, :], ALU.mult, ALU.mult)
                yield
                P.tt("dve", x_[:, :], tA[:, :], x_[:, :], ALU.add)
                yield
                P.dma("sp", y_all[ti], x_[:, :])
                yield

            run_gen(ffn_gen(0))
            nxt = 2
            cur = ffn_gen(1)
            young = None
            while cur is not None:
                try:
                    v = next(cur)
                    if v == "HALF" and young is None and nxt < NT:
                        young = ffn_gen(nxt)
                        nxt += 1
                except StopIteration:
                    cur, young = young, None
                    if cur is None and nxt < NT:
                        cur = ffn_gen(nxt)
                        nxt += 1
                    continue
                if young is not None:
                    try:
                        v2 = next(young)
                        if v2 == "HALF":
                            pass
                    except StopIteration:
                        young = None
            P.barrier()
            P.emit()
    return nc


def _prep_inputs(inp):
    f = lambda a: np.ascontiguousarray(np.asarray(a, dtype=np.float32))
    xp, xs = f(inp["x_prompt"]), f(inp["x_sample"])
    cp, cs = f(inp["c_prompt"]), f(inp["c_sample"])
    bcast = lambda v: np.ascontiguousarray(np.broadcast_to(np.asarray(v, np.float32).reshape(1, -1), (128, v.size)))
    shared = {}
    shared["w_ada"] = f(inp["w_ada"][0])
    shared["b_ada_b"] = bcast(inp["b_ada"][0])
    shared["normvecs"] = np.ascontiguousarray(np.stack(
        [bcast(inp[k][0]) for k in ("norm_mix_pre", "norm_mix_post", "norm_ffn_pre", "norm_ffn_post")], axis=1))
    shared["w_in"] = f(inp["w_in"][0])
    shared["w_out"] = f(inp["w_out"][0])
    shared["w_gu"] = f(inp["w_gate_up"][0])
    shared["w_dn"] = f(inp["w_down"][0])
    scw = np.concatenate([f(inp["ssd_conv_w"][0]), f(inp["ssd_conv_b"][0])[None, :]], axis=0)
    shared["ssd_cw"] = np.ascontiguousarray(scw.reshape(5, 12, 128).transpose(2, 1, 0))
    shared["gdn_cw"] = np.ascontiguousarray(f(inp["gdn_conv_w"][0]).reshape(4, 24, 128).transpose(2, 1, 0))
    sv = np.concatenate([f(inp["ssd_dt_bias"][0]), f(inp["ssd_A_log"][0]), f(inp["ssd_D"][0]),
                         f(inp["gdn_dt_bias"][0]), f(inp["gdn_A_log"][0])])
    shared["smallv"] = bcast(sv)
    shared["ssd_nw"] = bcast(inp["ssd_norm_w"][0])
    shared["gdn_nw"] = bcast(inp["gdn_norm_w"][0])
    shared["ident"] = np.eye(128, dtype=np.float32)
    idx = np.arange(128)
    m = np.zeros((2, 128, 4, 128), np.float32)
    for typ, bs in ((0, 8), (1, 128)):
        same = (idx[:, None] // bs) == (idx[None, :] // bs)
        m[typ, :, 0, :] = same & (idx[:, None] > idx[None, :])
        m[typ, :, 1, :] = same & (idx[:, None] <= idx[None, :])
        m[typ, :, 2, :] = same & (idx[:, None] < idx[None, :])
        m[typ, :, 3, :] = same
    shared["masks"] = m
    shared["blk16"] = np.ascontiguousarray(((idx[:, None] // 8) == np.arange(16)[None, :]).astype(np.float32))
    maps = []
    for i in range(NCORES):
        d = dict(shared)
        sl = slice(16 * i, 16 * (i + 1))
        d["xs_all"] = np.ascontiguousarray(np.concatenate(
            [xs[sl].reshape(1, 128, D), xp[i].reshape(16, 128, D)], axis=0))
        d["cexp"] = np.ascontiguousarray(np.stack(
            [np.repeat(cs[sl], 8, axis=0), np.broadcast_to(cp[i][None, :], (128, D))], axis=0))
        d["st_ssd"] = np.ascontiguousarray(f(inp["state_ssd"][0, sl]).reshape(16, 1024, 128))
        hs = f(inp["state_ssd_conv"][0, sl])
        d["hist_ssd"] = np.ascontiguousarray(hs.reshape(16, 3, 12, 128).transpose(3, 2, 0, 1))
        d["st_gdn"] = np.ascontiguousarray(f(inp["state_gdn"][0, sl]))
        hg = f(inp["state_gdn_conv"][0, sl])
        d["hist_gdn"] = np.ascontiguousarray(hg.reshape(16, 3, 24, 128).transpose(3, 2, 0, 1))
        maps.append(d)
    return maps


def kernel(**inp):
    maps = _prep_inputs(inp)
    nc = build_program()
    res = run_bass_kernel_spmd(nc, maps, core_ids=list(range(NCORES)))
    R = res.results
    cat = lambda k: np.stack([np.asarray(r[k]) for r in R], axis=0)
    y_all = cat("y_all")
    y_prompt = y_all[:, 1:].reshape(8, 2048, D)
    y_sample = y_all[:, 0].reshape(128, 8, D)
    ssd_p = cat("o_ssd_p").reshape(1, 8, 16, 64, 128)
    ssdc_p = cat("o_ssdc_p").reshape(1, 8, 3, 1536)
    gdn_p = cat("o_gdn_p").reshape(1, 8, 8, 128, 128)
    gdnc_p = cat("o_gdnc_p").reshape(1, 8, 3, 3072)
    ssd_s = cat("o_ssd_s").reshape(1, 128, 16, 64, 128)
    ssdc_s = cat("o_ssdc_s").reshape(1, 128, 3, 1536)
    gdn_s = cat("o_gdn_s").reshape(1, 128, 8, 128, 128)
    gdnc_s = cat("o_gdnc_s").reshape(1, 128, 3, 3072)
    outs = (y_prompt, y_sample, ssd_p, ssdc_p, gdn_p, gdnc_p, ssd_s, ssdc_s, gdn_s, gdnc_s)
    return tuple(np.ascontiguousarray(o, dtype=np.float32) for o in outs)
```

```python
import contextlib
import os
import numpy as np
import concourse.bass as bass
import concourse.mybir as mybir
from concourse.bass_utils import run_bass_kernel_spmd

F32 = mybir.dt.float32
BF16 = mybir.dt.bfloat16
AF = mybir.ActivationFunctionType
ALU = mybir.AluOpType
AX = mybir.AxisListType

NCORES = 8
D = 1024
NT = 17
DFF = 2816
P2_MODE = 0
ANNOTATE = bool(int(os.environ.get('K_ANN', '0')))
SKIP1 = False
P2_STAGE = 99
P2_HSTEP = 99
P2_SKIP = ()
P2_NL = 4
P2_TILES = 16
STOP_AFTER = int(os.environ.get('K_STOP', '99'))


class Prog:
    ENGS = ("pe", "act", "dve", "pool", "sp")

    def __init__(self, nc, stack):
        self.nc = nc
        self.sems = {}
        for e in self.ENGS:
            self.sems[e] = stack.enter_context(nc.semaphore("s_" + e))
        self.dsems = {"sp": [], "pool": [], "act": []}
        for q, n in (("sp", 8), ("pool", 4), ("act", 2)):
            for j in range(n):
                nm = "d_%s%d" % (q, j)
                self.sems[nm] = stack.enter_context(nc.semaphore(nm))
                self.dsems[q].append(nm)
        self.cnt = {k: 0 for k in self.sems}
        self.known = {e: {} for e in self.ENGS}
        self.lastw = {}
        self.readers = {}
        self.ops = {e: [] for e in self.ENGS}
        self.rr = {"sp": 0, "pool": 0, "act": 0}
        self.nops = 0
        self.tag = "init"

    def sec(self, name):
        self.tag = name

    fine = {}

    def _keys(self, aps):
        ks = []
        for a in aps:
            if a is None or isinstance(a, (int, float)):
                continue
            if isinstance(a, str):
                ks.append(a)
                continue
            nm = a.name
            if nm in self.fine:
                row, gr = self.fine[nm]
                nm = "%s:%d" % (nm, (int(a.offset) % row) // gr)
            ks.append(nm)
        return ks

    def _deps(self, eng, r, w):
        need = {}

        def add(c):
            if c is None:
                return
            s, v = c
            if s == "pe" and eng == "pe":
                return
            if need.get(s, 0) < v:
                need[s] = v

        for k in r:
            add(self.lastw.get(k))
        for k in w:
            add(self.lastw.get(k))
            for c in self.readers.get(k, {}).items():
                add(c)
        waits = []
        kn = self.known[eng]
        for s, v in need.items():
            if kn.get(s, 0) < v:
                kn[s] = v
                waits.append((s, v))
        return waits

    def _commit(self, c, r, w):
        for k in w:
            self.lastw[k] = c
            self.readers[k] = {}
        for k in r:
            d = self.readers.setdefault(k, {})
            if d.get(c[0], 0) < c[1]:
                d[c[0]] = c[1]

    def op(self, eng, fn, ins=(), outs=()):
        r = self._keys(ins)
        w = self._keys(outs)
        w = w + [k for k in r if k.startswith("ps") or k.startswith("pa") or k.startswith("pm")
                 or k.startswith("lpb") or k.startswith("pb")]
        waits = self._deps(eng, r, w)
        self.cnt[eng] += 1
        self.ops[eng].append((waits, fn, eng, 1, self.tag))
        self._commit((eng, self.cnt[eng]), r, w)
        self.nops += 1

    def dma(self, q, out, in_, extra_ins=(), extra_outs=()):
        r = self._keys([in_] + list(extra_ins))
        w = self._keys([out] + list(extra_outs))
        waits = self._deps(q, r, w)
        sems = self.dsems[q]
        j = self.rr[q]
        self.rr[q] = (j + 1) % len(sems)
        nm = sems[j]
        prev = self.cnt[nm]
        if prev > 0 and self.known[q].get(nm, 0) < prev:
            self.known[q][nm] = prev
            waits.append((nm, prev))
        self.cnt[nm] += 16
        self.ops[q].append((waits, lambda e: e.dma_start(out=out, in_=in_), nm, 16, self.tag))
        self._commit((nm, self.cnt[nm]), r, w)
        self.nops += 1

    def barrier(self):
        for e in self.ENGS:
            waits = []
            for s, v in self.cnt.items():
                if v > 0 and self.known[e].get(s, 0) < v:
                    self.known[e][s] = v
                    waits.append((s, v))
            self.ops[e].append((waits, None, None, 0, self.tag))
        self.lastw = {}
        self.readers = {}

    def emit(self):
        nc = self.nc
        with nc.Block() as block:
            for ename, deco in (("sp", block.sync), ("act", block.scalar), ("dve", block.vector),
                                ("pool", block.gpsimd), ("pe", block.tensor)):
                ops = self.ops[ename]

                def body(e, ops=ops):
                    for waits, fn, sname, inc, tag in ops:
                        for s, v in waits:
                            e.wait_ge(self.sems[s], v)
                        if fn is not None:
                            ins = fn(e)
                            ins.then_inc(self.sems[sname], inc)
                            if ANNOTATE:
                                ins.annotate(tag)

                deco(body)
                self.ops[ename] = []

    def mm(self, out, lhsT, rhs, start=True, stop=True):
        self.op("pe", lambda e: e.matmul(out, lhsT=lhsT, rhs=rhs, start=start, stop=stop),
                ins=[lhsT, rhs], outs=[out])

    def mmk(self, out, pairs):
        n = len(pairs)

        def fn(e):
            ins = None
            for i, (l, r) in enumerate(pairs):
                ins = e.matmul(out, lhsT=l, rhs=r, start=(i == 0), stop=(i == n - 1))
            return ins

        self.op("pe", fn, ins=[x for p in pairs for x in p], outs=[out])

    def tr(self, out, in_, ident):
        self.op("pe", lambda e: e.transpose(out, in_, ident), ins=[in_, ident], outs=[out])

    def trs(self, items, ident):
        def fn(e):
            ins = None
            for o, i in items:
                ins = e.transpose(o, i, ident)
            return ins
        self.op("pe", fn, ins=[i for _, i in items] + [ident], outs=[o for o, _ in items])

    def act(self, out, in_, func, bias=None, scale=None):
        kw = {}
        if bias is not None:
            kw["bias"] = bias
        if scale is not None:
            kw["scale"] = scale
        self.op("act", lambda e: e.activation(out=out, in_=in_, func=func, **kw),
                ins=[in_, bias, scale], outs=[out])

    def sqsum(self, junk, in_, accum):
        self.op("act", lambda e: e.activation(out=junk, in_=in_, func=AF.Square, accum_out=accum),
                ins=[in_], outs=[junk, accum])

    def tt(self, eng, out, in0, in1, op):
        self.op(eng, lambda e: e.tensor_tensor(out=out, in0=in0, in1=in1, op=op), ins=[in0, in1], outs=[out])

    def ts(self, eng, out, in0, s1, s2, op0, op1=None):
        if op1 is None:
            self.op(eng, lambda e: e.tensor_scalar(out=out, in0=in0, scalar1=s1, scalar2=None, op0=op0),
                    ins=[in0, s1], outs=[out])
        else:
            self.op(eng, lambda e: e.tensor_scalar(out=out, in0=in0, scalar1=s1, scalar2=s2, op0=op0, op1=op1),
                    ins=[in0, s1, s2], outs=[out])

    def stt(self, eng, out, in0, scalar, in1, op0, op1):
        self.op(eng, lambda e: e.scalar_tensor_tensor(out=out, in0=in0, scalar=scalar, in1=in1, op0=op0, op1=op1),
                ins=[in0, scalar, in1], outs=[out])

    def copy(self, eng, out, in_):
        if eng == "act":
            self.op(eng, lambda e: e.copy(out=out, in_=in_), ins=[in_], outs=[out])
        else:
            self.op(eng, lambda e: e.tensor_copy(out=out, in_=in_), ins=[in_], outs=[out])

    def red(self, eng, out, in_):
        self.op(eng, lambda e: e.tensor_reduce(out=out, in_=in_, axis=AX.X, op=ALU.add), ins=[in_], outs=[out])

    def recip(self, out, in_):
        self.op("dve", lambda e: e.reciprocal(out=out, in_=in_), ins=[in_], outs=[out])

    def memset(self, eng, ap, val):
        self.op(eng, lambda e: e.memset(ap, val), ins=[], outs=[ap])


def bc(ap, axis, n):
    u = ap.unsqueeze(axis)
    shp = list(u.shape)
    shp[axis] = n
    return u.broadcast_to(shp)


def build_program():
    nc = bass.Bass("TRN2", target_bir_lowering=False)

    def din(name, shape, dt=F32):
        return nc.dram_tensor(name, list(shape), dt, kind="ExternalInput").ap()

    def dout(name, shape, dt=F32):
        return nc.dram_tensor(name, list(shape), dt, kind="ExternalOutput").ap()

    def dscr(name, shape, dt=F32):
        return nc.dram_tensor(name, list(shape), dt).ap()

    xs_all = din("xs_all", [NT, 128, D])
    cexp = din("cexp", [2, 128, D])
    w_ada = din("w_ada", [D, 6 * D])
    b_ada_b = din("b_ada_b", [128, 6 * D])
    normvecs = din("normvecs", [128, 4, D])
    w_in = din("w_in", [D, 6688])
    w_out = din("w_out", [2 * D, D])
    w_gu = din("w_gu", [D, 2 * DFF])
    w_dn = din("w_dn", [DFF, D])
    ssd_cw = din("ssd_cw", [128, 12, 5])
    gdn_cw = din("gdn_cw", [128, 24, 4])
    smallv = din("smallv", [128, 64])
    ssd_nw = din("ssd_nw", [128, D])
    gdn_nw = din("gdn_nw", [128, 128])
    st_ssd = din("st_ssd", [16, 1024, 128])
    hist_ssd = din("hist_ssd", [128, 12, 16, 3])
    st_gdn = din("st_gdn", [16, 8, 128, 128])
    hist_gdn = din("hist_gdn", [128, 24, 16, 3])
    ident_d = din("ident", [128, 128])
    masks_d = din("masks", [2, 128, 4, 128])
    blk16_d = din("blk16", [128, 16])

    y_all = dout("y_all", [NT, 128, D])
    o_ssd_p = dout("o_ssd_p", [1024, 128])
    o_ssdc_p = dout("o_ssdc_p", [3, 1536])
    o_gdn_p = dout("o_gdn_p", [1024, 128])
    o_gdnc_p = dout("o_gdnc_p", [3, 3072])
    o_ssd_s = dout("o_ssd_s", [16, 1024, 128])
    o_ssdc_s = dout("o_ssdc_s", [48, 1536])
    o_gdn_s = dout("o_gdn_s", [16, 8, 128, 128])
    o_gdnc_s = dout("o_gdnc_s", [48, 3072])

    modscr = dscr("modscr", [2, 128, 6 * D])
    yssd_scr = dscr("yssd_scr", [NT, 128, D], BF16)
    x1_scr = dscr("x1_scr", [NT, 128, D])

    with contextlib.ExitStack() as gstack:
        P = Prog(nc, gstack)

        def sb(stack, name, shape, dt=F32):
            return stack.enter_context(nc.sbuf_tensor(name, list(shape), dt))

        PT = gstack.enter_context(nc.psum_tensor("pst", [128, 1024], BF16))
        ps1 = contextlib.ExitStack()
        PS = [ps1.enter_context(nc.psum_tensor("ps%d" % i, [128, 512], F32)) for i in range(7)]

        identf = sb(gstack, "identf", [128, 128])
        identb = sb(gstack, "identb", [128, 128], BF16)
        onesf = sb(gstack, "onesf", [128, 128])
        onesb = sb(gstack, "onesb", [128, 128], BF16)
        cst = sb(gstack, "cst", [128, 4])
        masks = [sb(gstack, "masks%d" % t, [128, 4, 128]) for t in range(2)]
        blk16 = sb(gstack, "blk16s", [128, 16])
        smv = sb(gstack, "smv", [128, 64])
        negA = sb(gstack, "negA", [128, 24])
        P.dma("sp", identf[:, :], ident_d[:, :])
        P.dma("pool", identb[:, :], ident_d[:, :])
        for t in range(2):
            P.dma("sp", masks[t][:, :, :], masks_d[t])
        P.dma("sp", blk16[:, :], blk16_d[:, :])
        P.dma("sp", smv[:, :], smallv[:, :])
        P.memset("dve", onesf[:, :], 1.0)
        P.memset("dve", onesb[:, :], 1.0)
        P.memset("dve", cst[:, 0:1], 1e-6)
        P.memset("dve", cst[:, 1:2], 1.0)
        P.memset("dve", cst[:, 2:3], 128e-6)
        P.memset("dve", cst[:, 3:4], 0.0)
        P.act(negA[:, 0:16], smv[:, 16:32], AF.Exp)
        P.act(negA[:, 16:24], smv[:, 56:64], AF.Exp)
        P.ts("dve", negA[:, :], negA[:, :], -1.0, None, ALU.mult)

        MSTT, MINC, MSTR, MBLK = 0, 1, 2, 3

        with contextlib.ExitStack() as st0:
            ct = sb(st0, "ct", [128, D])
            cb = sb(st0, "cb", [128, D], BF16)
            cT = [sb(st0, "cT%d" % t, [128, 8, 128], BF16) for t in range(2)]
            modt = [sb(st0, "modt%d" % t, [128, 6 * D]) for t in range(2)]
            nv = sb(st0, "nv", [128, 4, D])
            wa = [sb(st0, "wa%d" % i, [128, 8, 512], BF16) for i in range(2)]
            bb = [sb(st0, "bb%d" % i, [128, 512]) for i in range(2)]
            P.dma("sp", nv[:, :, :], normvecs[:, :, :])
            for t in range(2):
                P.dma("sp", ct[:, :], cexp[t])
                P.act(cb[:, :], ct[:, :], AF.Silu)
                P.trs([(PT[:, c * 128:(c + 1) * 128], cb[:, c * 128:(c + 1) * 128]) for c in range(8)], identb[:, :])
                P.copy("dve", cT[t][:, :, :], PT[:, :].rearrange("p (c t) -> p c t", c=8))
            wv = w_ada.rearrange("(kc p) n -> p kc n", p=128)
            for j in range(12):
                P.dma("pool", wa[j % 2][:, :, :], wv[:, :, j * 512:(j + 1) * 512])
                P.dma("sp", bb[j % 2][:, :], b_ada_b[:, j * 512:(j + 1) * 512])
                for t in range(2):
                    ps = PS[(2 * j + t) % 4]
                    P.mmk(ps[:, :], [(cT[t][:, kc, :], wa[j % 2][:, kc, :]) for kc in range(8)])
                    P.tt("dve", modt[t][:, j * 512:(j + 1) * 512], ps[:, :], bb[j % 2][:, :], ALU.add)
            for t in range(2):
                m = modt[t]
                P.stt("dve", m[:, D:2 * D], m[:, D:2 * D], 1.0, nv[:, 0, :], ALU.add, ALU.mult)
                P.tt("dve", m[:, 2 * D:3 * D], m[:, 2 * D:3 * D], nv[:, 1, :], ALU.mult)
                P.stt("dve", m[:, 4 * D:5 * D], m[:, 4 * D:5 * D], 1.0, nv[:, 2, :], ALU.add, ALU.mult)
                P.tt("dve", m[:, 5 * D:6 * D], m[:, 5 * D:6 * D], nv[:, 3, :], ALU.mult)
                P.dma("sp", modscr[t], m[:, :])
            P.barrier()
            P.emit()

        def norm_to_T(xt, modv, sidx, hidx, tmpA, hb, hT, stt_):
            P.sec("norm_to_T")
            P.sqsum(tmpA[:, :], xt[:, :], stt_[:, 0:1])
            P.act(stt_[:, 1:2], stt_[:, 0:1], AF.Sqrt, bias=cst[:, 0:1], scale=1.0 / D)
            P.recip(stt_[:, 2:3], stt_[:, 1:2])
            P.stt("dve", tmpA[:, :], xt[:, :], stt_[:, 2:3], modv[:, sidx, :], ALU.mult, ALU.mult)
            P.tt("dve", hb[:, :], tmpA[:, :], modv[:, hidx, :], ALU.add)
            P.trs([(PT[:, c * 128:(c + 1) * 128], hb[:, c * 128:(c + 1) * 128]) for c in range(8)], identb[:, :])
            P.copy("act", hT[:, :, :], PT[:, :].rearrange("p (c t) -> p c t", c=8))

        def norm_to_T_g(xt, modv, sidx, hidx, tmpA, hb, hT, stt_):
            P.sec("norm_to_T")
            P.sqsum(tmpA[:, :], xt[:, :], stt_[:, 0:1])
            yield
            P.act(stt_[:, 1:2], stt_[:, 0:1], AF.Sqrt, bias=cst[:, 0:1], scale=1.0 / D)
            yield
            P.recip(stt_[:, 2:3], stt_[:, 1:2])
            yield
            P.stt("dve", tmpA[:, :], xt[:, :], stt_[:, 2:3], modv[:, sidx, :], ALU.mult, ALU.mult)
            yield
            P.tt("dve", hb[:, :], tmpA[:, :], modv[:, hidx, :], ALU.add)
            yield
            P.trs([(PT[:, c * 128:(c + 1) * 128], hb[:, c * 128:(c + 1) * 128]) for c in range(8)], identb[:, :])
            yield
            P.copy("act", hT[:, :, :], PT[:, :].rearrange("p (c t) -> p c t", c=8))
            yield

        def small_scan_terms(af, nh, typ, pss, ex):
            mk = masks[typ]
            P.mm(pss[:, 0:nh], mk[:, MINC, :], af)
            P.mm(pss[:, nh:2 * nh], mk[:, MSTT, :], af)
            P.mm(pss[:, 2 * nh:3 * nh], mk[:, MBLK, :], af)
            P.act(ex[:, 0:3 * nh], pss[:, 0:3 * nh], AF.Exp)

        def small_scan_terms_g(af, nh, typ, pss, ex):
            mk = masks[typ]
            P.mm(pss[:, 0:nh], mk[:, MINC, :], af)
            yield
            P.mm(pss[:, nh:2 * nh], mk[:, MSTT, :], af)
            yield
            P.mm(pss[:, 2 * nh:3 * nh], mk[:, MBLK, :], af)
            yield
            P.act(ex[:, 0:3 * nh], pss[:, 0:3 * nh], AF.Exp)
            yield

        def proj_conv_gen(typ, ngroups, wt, wcol0, hT, psb, xinS_l, xinP_l, histS, histP, cwt, has_bias, accs, dst_fn):
            items = []

            def slot(i):
                n = len(items)
                if 0 <= i < n:
                    it = items[i]
                    if has_bias:
                        P.act(it[0], it[2][0], AF.Identity, bias=cwt[:, it[4], 4:5], scale=cwt[:, it[4], 0:1])
                    else:
                        P.act(it[0], it[2][0], AF.Identity, scale=cwt[:, it[4], 0:1])
                    yield
                if 0 <= i - 1 < n:
                    it = items[i - 1]
                    for k in range(1, 4):
                        P.stt("dve", it[0], it[2][k], cwt[:, it[4], k:k + 1], it[0], ALU.mult, ALU.add)
                        yield
                if 0 <= i - 2 < n:
                    it = items[i - 2]
                    P.act(it[3], it[1][:, :], AF.Silu)
                    yield

            for g in range(ngroups):
                ps = psb[g % 2]
                for j in range(4):
                    c = 4 * g + j
                    P.mmk(ps[:, j * 128:(j + 1) * 128],
                          [(wt[:, kc, wcol0 + c * 128:wcol0 + (c + 1) * 128], hT[:, kc, :]) for kc in range(8)])
                    yield
                if typ == 0:
                    xin = xinS_l[g % 2]
                    P.copy("pool", xin[:, :, :, 0:3], histS[:, 4 * g:4 * g + 4, :, :])
                    yield
                    P.copy("act", xin[:, :, :, 3:11], ps[:, :].rearrange("p (j s t) -> p j s t", j=4, s=16))
                    yield
                else:
                    xin = xinP_l[g % 2]
                    P.copy("pool", xin[:, :, 0:3], histP[:, 4 * g:4 * g + 4, :])
                    yield
                    P.copy("act", xin[:, :, 3:131], ps[:, :].rearrange("p (j t) -> p j t", j=4))
                    yield
                    P.copy("pool", histP[:, 4 * g:4 * g + 4, :], xin[:, :, 128:131])
                    yield
                for j in range(4):
                    c = 4 * g + j
                    a_ = accs[c % 4]
                    if typ == 0:
                        av = a_[:, :].rearrange("p (s t) -> p s t", s=16)
                        sh = [xin[:, j, :, k:k + 8] for k in range(4)]
                    else:
                        av = a_[:, :]
                        sh = [xin[:, j, k:k + 128] for k in range(4)]
                    items.append((av, a_, sh, dst_fn(c), c))
                    yield from slot(c)
            yield from slot(4 * ngroups)
            yield from slot(4 * ngroups + 1)

        def run_gen(g):
            for _ in g:
                pass

        w_in_v = w_in.rearrange("(kc p) n -> p kc n", p=128)
        if STOP_AFTER >= 1 and not SKIP1:
          with contextlib.ExitStack() as st1:
            wI = sb(st1, "wI", [128, 8, 2576], BF16)
            for kc in range(8):
                P.dma("pool", wI[:, kc, :], w_in_v[:, kc, 0:2576])
            modv = sb(st1, "modv", [128, 3, D])
            cw = sb(st1, "cw", [128, 12, 5])
            nw = sb(st1, "ssdnw", [128, D])
            histS = sb(st1, "histS", [128, 12, 16, 3])
            histP = sb(st1, "histP", [128, 12, 3])
            P.dma("sp", cw[:, :, :], ssd_cw[:, :, :])
            P.dma("sp", nw[:, :], ssd_nw[:, :])
            P.dma("sp", histS[:, :, :, :], hist_ssd[:, :, :, :])
            P.memset("pool", histP[:, :, :], 0.0)
            xt = [sb(st1, "xt%d" % i, [128, D]) for i in range(2)]
            tmpA = sb(st1, "tmpA", [128, D])
            hb = sb(st1, "hb", [128, D], BF16)
            hT = sb(st1, "hT", [128, 8, 128], BF16)
            stt_ = sb(st1, "stt", [128, 4])
            xinS_l = [sb(st1, "xinS%d" % i, [128, 4, 16, 11]) for i in range(2)]
            xinP_l = [sb(st1, "xinP%d" % i, [128, 4, 131]) for i in range(2)]
            acc = [sb(st1, "acc%d" % i, [128, 128]) for i in range(4)]
            xc = sb(st1, "xc", [128, 12, 128], BF16)
            zs = sb(st1, "zs", [128, D], BF16)
            sm = sb(st1, "sm", [128, 48])
            af = sb(st1, "af", [128, 16])
            ex = sb(st1, "ex", [128, 48])
            xtm = sb(st1, "xtm", [128, D], BF16)
            xdt = sb(st1, "xdt", [128, D], BF16)
            xDs = sb(st1, "xDs", [128, D], BF16)
            xw = sb(st1, "xw", [128, D], BF16)
            Btm = sb(st1, "Btm", [128, 2, 128], BF16)
            CBm = sb(st1, "CBm", [128, 2, 128])
            L4 = [sb(st1, "L4_%d" % i, [128, 4, 128]) for i in range(2)]
            E4 = [sb(st1, "E4_%d" % i, [128, 4, 128]) for i in range(2)]
            M4 = [sb(st1, "M4_%d" % i, [128, 4, 128], BF16) for i in range(2)]
            yo = sb(st1, "yo", [128, D])
            yy = sb(st1, "yy", [128, D])
            ss = sb(st1, "ss", [128, 8])
            ysb = [sb(st1, "ysb%d" % i, [128, D], BF16) for i in range(2)]
            hTf = sb(st1, "hTf", [128, D])
            hTb = sb(st1, "hTb", [128, D], BF16)
            lhs3 = sb(st1, "lhs3", [128, 8, 48], BF16)
            cv = sb(st1, "cv", [48, 1536])
            CmT = [sb(st1, "CmT%d" % g, [128, 16 * 128], BF16) for g in range(2)]
            h0 = [sb(st1, "h0_%d" % i, [128, 8, 128]) for i in range(2)]
            h0Tb = sb(st1, "h0Tb", [128, D], BF16)
            xwm = sb(st1, "xwm", [128, D], BF16)
            hn = [sb(st1, "hn%d" % i, [128, 8, 128]) for i in range(2)]
            rsel = sb(st1, "rsel", [128, 2, 128])
            decsel = sb(st1, "decsel", [128, 128])
            P.memset("pool", hTf[:, :], 0.0)
            P.memset("pool", hTb[:, :], 0.0)
            for g in range(2):
                P.memset("pool", CmT[g][:, :], 0.0)

            for ti in range(NT):
                typ = 0 if ti == 0 else 1
                mk = masks[typ]
                if ti <= 1:
                    P.dma("sp", modv[:, :, :], modscr[typ].rearrange("p (s d) -> p s d", s=6)[:, 0:3, :])
                x_ = xt[ti % 2]
                P.dma("sp", x_[:, :], xs_all[ti])
                norm_to_T(x_, modv, 1, 0, tmpA, hb, hT, stt_)
                run_gen(proj_conv_gen(typ, 3, wI, 1024, hT, PS[0:2], xinS_l, xinP_l, histS, histP, cw, True, acc,
                                      lambda c: xc[:, c, :]))
                P.sec("z (token-major) -> silu")
                for nb in range(2):
                    ps = PS[2 + nb]
                    P.mmk(ps[:, :], [(hT[:, kc, :], wI[:, kc, nb * 512:(nb + 1) * 512]) for kc in range(8)])
                    P.act(zs[:, nb * 512:(nb + 1) * 512], ps[:, :], AF.Silu)
                P.sec("dt")
                pd = PS[4]
                P.mmk(pd[:, 0:16], [(hT[:, kc, :], wI[:, kc, 2560:2576]) for kc in range(8)])
                P.tt("dve", sm[:, 0:16], pd[:, 0:16], smv[:, 0:16], ALU.add)
                P.act(sm[:, 16:32], sm[:, 0:16], AF.Exp)
                P.act(sm[:, 32:48], sm[:, 16:32], AF.Ln, bias=cst[:, 1:2], scale=1.0)
                P.tt("dve", af[:, :], sm[:, 32:48], negA[:, 0:16], ALU.mult)
                small_scan_terms(af[:, :], 16, typ, PS[4][:, 64:112], ex)
                P.sec("token-major x, xdt, xD, xw, B")
                P.trs([(PT[:, c * 128:(c + 1) * 128], xc[:, c, :]) for c in range(8)], identb[:, :])
                P.copy("act", xtm[:, :], PT[:, :])
                x3 = xtm[:, :].rearrange("p (h d) -> p h d", h=16)
                P.tt("dve", xdt[:, :].rearrange("p (h d) -> p h d", h=16), x3, bc(sm[:, 32:48], 2, 64), ALU.mult)
                P.tt("pool", xDs[:, :].rearrange("p (h d) -> p h d", h=16), x3, bc(smv[:, 32:48], 2, 64), ALU.mult)
                P.tt("pool", xw[:, :].rearrange("p (h d) -> p h d", h=16),
                     xdt[:, :].rearrange("p (h d) -> p h d", h=16), bc(ex[:, 16:32], 2, 64), ALU.mult)
                P.trs([(PT[:, g * 128:(g + 1) * 128], xc[:, 8 + g, :]) for g in range(2)], identb[:, :])
                P.copy("act", Btm[:, :, :], PT[:, 0:256].rearrange("p (g n) -> p g n", g=2))
                P.sec("CB^T masked")
                pc = PS[5]
                for g in range(2):
                    P.mm(pc[:, g * 128:(g + 1) * 128], xc[:, 8 + g, :], xc[:, 10 + g, :])
                P.tt("dve", CBm[:, :, :], pc[:, 0:256].rearrange("p (g l) -> p g l", g=2), bc(mk[:, MINC, :], 1, 2), ALU.mult)
                P.sec("y_off raw")
                if typ == 1:
                    for g in range(2):
                        P.mm(PS[2 + g][:, :], xc[:, 10 + g, :], hTb[:, g * 512:(g + 1) * 512])
                else:
                    for g in range(2):
                        base = CmT[g][:, :]
                        dst = bass.AP(base.tensor, base.offset, [[16 * 128, 128], [136, 16], [1, 8]])
                        P.op("pool", (lambda e, dst=dst, src=xc[:, 10 + g, :].rearrange("p (s t) -> p s t", s=16):
                                      e.tensor_copy(out=dst, in_=src)), ins=[xc[:, 0, :]], outs=[base])
                    for hh in range(2):
                        P.tt("pool", rsel[:, hh, :].rearrange("p (j s) -> p j s", j=8),
                             bc(af[:, hh:16:2], 2, 16), bc(blk16[:, :], 1, 8), ALU.mult)
                        P.mm(PS[5][:, 256 + hh * 128:256 + (hh + 1) * 128], onesf[:, :], rsel[:, hh, :])
                    P.act(decsel[0:64, :], PS[5][0:64, 256:384], AF.Exp)
                    P.act(decsel[64:128, :], PS[5][64:128, 384:512], AF.Exp)
                    for s in range(16):
                        h0_ = h0[s % 2]
                        hn_ = hn[s % 2]
                        P.dma("sp", h0_[:, :, :], st_ssd[s].rearrange("(j p) n -> p j n", p=128))
                        pa, pb = PS[0], PS[1]
                        P.trs([((pa if j < 4 else pb)[:, (j % 4) * 128:(j % 4 + 1) * 128], h0_[:, j, :]) for j in range(8)],
                              identf[:, :])
                        P.copy("act", h0Tb[:, 0:512], pa[:, :])
                        P.copy("dve", h0Tb[:, 512:1024], pb[:, :])
                        for g in range(2):
                            P.op("pe", (lambda e, o=PS[2 + g][:, :], l=CmT[g][:, s * 128:(s + 1) * 128],
                                        r=h0Tb[:, g * 512:(g + 1) * 512], s=s:
                                        e.matmul(o, lhsT=l, rhs=r, start=(s == 0), stop=(s == 15))),
                                 ins=[CmT[g][:, :], h0Tb[:, :]], outs=[PS[2 + g][:, :]])
                        P.act(xwm[:, :], xw[:, :], AF.Identity, scale=blk16[:, s:s + 1])
                        for half in range(2):
                            pn = PS[half]
                            for jj in range(4):
                                j = half * 4 + jj
                                P.mm(pn[:, jj * 128:(jj + 1) * 128], xwm[:, j * 128:(j + 1) * 128], Btm[:, j // 4, :])
                            for jj in range(4):
                                j = half * 4 + jj
                                P.stt("dve", hn_[:, j, :], h0_[:, j, :], decsel[:, j * 16 + s:j * 16 + s + 1],
                                      pn[:, jj * 128:(jj + 1) * 128], ALU.mult, ALU.add)
                        P.dma("sp", o_ssd_s[s].rearrange("(j p) n -> p j n", p=128), hn_[:, :, :])
                P.sec("per-head intra-chunk")
                for q in range(4):
                    g = q // 2
                    L_, E_, M_ = L4[q % 2], E4[q % 2], M4[q % 2]
                    P.tt("dve", L_[:, :, :], bc(mk[:, MSTT, :], 1, 4), bc(af[:, 4 * q:4 * q + 4], 2, 128), ALU.mult)
                    pg = PS[5 + (q % 2)]
                    for j in range(4):
                        P.mm(pg[:, j * 128:(j + 1) * 128], L_[:, j, :], mk[:, MINC, :])
                    P.act(E_[:, :, :], pg[:, :].rearrange("p (j l) -> p j l", j=4), AF.Exp)
                    P.tt("dve", M_[:, :, :], E_[:, :, :], bc(CBm[:, g, :], 1, 4), ALU.mult)
                    py = PS[0 + g]
                    for j in range(4):
                        h = 4 * q + j
                        hh = h % 8
                        P.mmk(py[:, hh * 64:(hh + 1) * 64],
                              [(identb[:, :], xDs[:, h * 64:(h + 1) * 64]), (M_[:, j, :], xdt[:, h * 64:(h + 1) * 64])])
                P.sec("combine y = y_diag + eacs * y_off")
                for g in range(2):
                    P.copy("act", yo[:, g * 512:(g + 1) * 512], PS[2 + g][:, :])
                P.tt("dve", yo[:, :].rearrange("p (h d) -> p h d", h=16), yo[:, :].rearrange("p (h d) -> p h d", h=16),
                     bc(ex[:, 0:16], 2, 64), ALU.mult)
                for g in range(2):
                    P.tt("dve", yy[:, g * 512:(g + 1) * 512], yo[:, g * 512:(g + 1) * 512], PS[0 + g][:, :], ALU.add)
                P.sec("gate with silu(z), group rmsnorm")
                P.tt("dve", yy[:, :], yy[:, :], zs[:, :], ALU.mult)
                for g in range(2):
                    P.sqsum(yo[:, g * 512:(g + 1) * 512], yy[:, g * 512:(g + 1) * 512], ss[:, g:g + 1])
                P.act(ss[:, 2:4], ss[:, 0:2], AF.Sqrt, bias=cst[:, 0:1], scale=1.0 / 512)
                P.recip(ss[:, 4:6], ss[:, 2:4])
                y_ = ysb[ti % 2]
                for g in range(2):
                    P.stt("dve", y_[:, g * 512:(g + 1) * 512], yy[:, g * 512:(g + 1) * 512], ss[:, 4 + g:5 + g],
                          nw[:, g * 512:(g + 1) * 512], ALU.mult, ALU.mult)
                P.dma("sp", yssd_scr[ti], y_[:, :])
                P.sec("state update (prompt chunks)")
                if typ == 1:
                    for g in range(2):
                        P.mm(PS[2 + g][:, :], Btm[:, g, :], xw[:, g * 512:(g + 1) * 512])
                    h3 = hTf[:, :].rearrange("p (h d) -> p h d", h=16)
                    P.tt("dve", h3, h3, bc(ex[:, 32:48], 2, 64), ALU.mult)
                    for g in range(2):
                        P.tt("dve", hTf[:, g * 512:(g + 1) * 512], hTf[:, g * 512:(g + 1) * 512], PS[2 + g][:, :], ALU.add)
                    P.copy("act", hTb[:, :], hTf[:, :])
                P.sec("conv-state outputs (last 3 raw xbc rows)")
                if ti == 0 or ti == NT - 1:
                    M3 = 48 if typ == 0 else 3
                    if typ == 0:
                        P.copy("pool", lhs3[:, :, :].rearrange("p k (s t) -> p k s t", s=16),
                               hT[:, :, :].rearrange("p k (s t) -> p k s t", s=16)[:, :, :, 5:8])
                    else:
                        P.copy("pool", lhs3[:, :, 0:3], hT[:, :, 125:128])
                    for nb in range(3):
                        ps = PS[5 + (nb % 2)]
                        P.mmk(ps[0:M3, :], [(lhs3[:, kc, 0:M3], wI[:, kc, 1024 + nb * 512:1024 + (nb + 1) * 512])
                                            for kc in range(8)])
                        P.copy("act", cv[0:M3, nb * 512:(nb + 1) * 512], ps[0:M3, :])
                    P.dma("sp", (o_ssdc_s if typ == 0 else o_ssdc_p)[:, :], cv[0:M3, :])
            for half in range(2):
                ps = PS[half]
                P.trs([(ps[:, jj * 128:(jj + 1) * 128], hTf[:, (half * 4 + jj) * 128:(half * 4 + jj + 1) * 128])
                       for jj in range(4)], identf[:, :])
                P.copy("act", hn[half][:, 0:4, :], ps[:, :].rearrange("p (j n) -> p j n", j=4))
                P.dma("sp", o_ssd_p[half * 512:(half + 1) * 512, :].rearrange("(j p) n -> p j n", p=128), hn[half][:, 0:4, :])
            P.barrier()
            P.emit()
        ps1.close()
        if STOP_AFTER >= 2:
          with contextlib.ExitStack() as st2:
            PA = [st2.enter_context(nc.psum_tensor("pa%d" % i, [128, 512], F32)) for i in range(2)]
            PM = st2.enter_context(nc.psum_tensor("pm", [128, 512], F32))
            LPB = [st2.enter_context(nc.psum_tensor("lpb%d" % i, [128, 512], F32)) for i in range(4)]
            LPS = [[LPB[ln][:, i * 128:(i + 1) * 128] for i in range(4)] for ln in range(4)]
            wII = sb(st2, "wII", [128, 8, 4112], BF16)
            for kc in range(8):
                P.dma("pool", wII[:, kc, :], w_in_v[:, kc, 2576:6688])
            wO = sb(st2, "wO", [128, 16, 1024], BF16)
            w_out_v = w_out.rearrange("(kc p) n -> p kc n", p=128)
            for kc in range(16):
                P.dma("pool", wO[:, kc, :], w_out_v[:, kc, :])
            modv = sb(st2, "modv2", [128, 3, D])
            cwg = sb(st2, "cwg", [128, 24, 4])
            gnw = sb(st2, "gnw", [128, 128])
            histP = sb(st2, "histPg", [128, 24, 3])
            P.dma("sp", cwg[:, :, :], gdn_cw[:, :, :])
            P.dma("sp", gnw[:, :], gdn_nw[:, :])
            P.memset("pool", histP[:, :, :], 0.0)
            tmpA = sb(st2, "tmpA2", [128, D])
            hT = sb(st2, "hT2", [128, 8, 128], BF16)
            stt_ = sb(st2, "stt2", [128, 4])
            xinP_l = [sb(st2, "xinP2_%d" % i, [128, 4, 131]) for i in range(2)]
            acc = [sb(st2, "acc2_%d" % i, [128, 128]) for i in range(4)]
            lhs3 = sb(st2, "lhs3g", [128, 8, 48], BF16)
            tmpT = sb(st2, "tmpT2", [128, D])
            sttT = sb(st2, "sttT2", [128, 4])
            ss8 = sb(st2, "ss8", [128, 24])
            otm = sb(st2, "otm", [128, D])
            mixed = sb(st2, "mixed", [128, 2 * D], BF16)
            mixedT = sb(st2, "mixedT", [128, 16, 128], BF16)

            class FB:
                pass

            def make_fb(i, stk):
                fb = FB()
                fb.xt = sb(stk, "f%d_xt" % i, [128, D])
                fb.hb = sb(stk, "f%d_hb" % i, [128, D], BF16)
                fb.qk = sb(stk, "f%d_qk" % i, [128, 16, 128], BF16)
                fb.vfm = sb(stk, "f%d_vfm" % i, [128, 8, 128], BF16)
                fb.ktm = sb(stk, "f%d_ktm" % i, [128, 8, 128], BF16)
                fb.vtm = sb(stk, "f%d_vtm" % i, [128, 8, 128], BF16)
                fb.gs = sb(stk, "f%d_gs" % i, [128, D], BF16)
                fb.sm = sb(stk, "f%d_sm" % i, [128, 64])
                fb.gf = sb(stk, "f%d_gf" % i, [128, 8])
                fb.ex = sb(stk, "f%d_ex" % i, [128, 24])
                return fb

            class Lane:
                pass

            lanes = []

            def make_lane(ln, stk):
                B = Lane()
                B.ps = LPS[ln]
                f = lambda nm, dt=F32: sb(stk, "ln%d_%s" % (ln, nm), [128, 128], dt)
                B.P = [f("P0"), f("P1")]
                B.PT = [f("PT0"), f("PT1")]
                B.X = [f("X0"), f("X1")]
                B.ot = f("ot")
                B.L, B.Dm, B.DmI, B.DmS = B.ot, B.X[1], B.PT[1], B.P[1]
                B.attnT, B.TTb, B.R, B.vn, B.kout = (f("attnT", BF16), f("TTb", BF16), f("R", BF16),
                                                     f("vn", BF16), f("kout", BF16))
                lanes.append(B)

            make_lane(0, st2)
            fbs = [make_fb(0, st2)]

            def front_gen(ti, typ, SB, fb):
                xt, hb, qk, vfm, sm, gf, ex = fb.xt, fb.hb, fb.qk, fb.vfm, fb.sm, fb.gf, fb.ex
                P.sec("F:load+norm")
                if ti <= 1:
                    P.dma("sp", modv[:, :, :], modscr[typ].rearrange("p (s d) -> p s d", s=6)[:, 0:3, :])
                    yield
                P.dma("sp", xt[:, :], xs_all[ti])
                yield
                yield from norm_to_T_g(xt, modv, 1, 0, tmpA, hb, hT, stt_)
                yield
                yield from proj_conv_gen(typ, 6, wII, 0, hT, PA, (SB.xinS_l if typ == 0 else None), xinP_l,
                                         (SB.histS if typ == 0 else None), histP, cwg, False, acc,
                                         lambda c: (qk[:, c, :] if c < 16 else vfm[:, c - 16, :]))
                if ti == 0 or ti == NT - 1:
                    P.sec("F:convstate")
                    M3 = 48 if typ == 0 else 3
                    if typ == 0:
                        P.copy("pool", lhs3[:, :, :].rearrange("p k (s t) -> p k s t", s=16),
                               hT[:, :, :].rearrange("p k (s t) -> p k s t", s=16)[:, :, :, 5:8])
                        yield
                    else:
                        P.copy("pool", lhs3[:, :, 0:3], hT[:, :, 125:128])
                        yield
                    for cc in range(3):
                        for nb in range(2):
                            c0 = cc * 1024 + nb * 512
                            P.mmk(PA[nb][0:M3, :], [(lhs3[:, kc, 0:M3], wII[:, kc, c0:c0 + 512]) for kc in range(8)])
                            yield
                            P.copy("act", tmpA[0:M3, nb * 512:(nb + 1) * 512], PA[nb][0:M3, :])
                            yield
                        P.dma("sp", (o_gdnc_s if typ == 0 else o_gdnc_p)[:, cc * 1024:(cc + 1) * 1024], tmpA[0:M3, :])
                        yield
                    yield
                P.sec("F:gate")
                for nb in range(2):
                    P.mmk(PA[nb][:, :], [(hT[:, kc, :], wII[:, kc, 3072 + nb * 512:3072 + (nb + 1) * 512])
                                         for kc in range(8)])
                    yield
                    P.act(fb.gs[:, nb * 512:(nb + 1) * 512], PA[nb][:, :], AF.Silu)
                    yield
                yield
                P.sec("F:beta/g")
                P.mmk(PM[:, 0:16], [(hT[:, kc, :], wII[:, kc, 4096:4112]) for kc in range(8)])
                yield
                P.act(sm[:, 0:8], PM[:, 0:8], AF.Exp, scale=-1.0)
                yield
                P.ts("dve", sm[:, 0:8], sm[:, 0:8], 1.0, None, ALU.add)
                yield
                P.recip(sm[:, 8:16], sm[:, 0:8])
                yield
                P.ts("dve", sm[:, 16:24], sm[:, 8:16], -1.0, None, ALU.mult)
                yield
                P.tt("dve", sm[:, 24:32], PM[:, 8:16], smv[:, 48:56], ALU.add)
                yield
                P.act(sm[:, 32:40], sm[:, 24:32], AF.Exp)
                yield
                P.act(sm[:, 40:48], sm[:, 32:40], AF.Ln, bias=cst[:, 1:2], scale=1.0)
                yield
                P.tt("dve", gf[:, :], sm[:, 40:48], negA[:, 16:24], ALU.mult)
                yield
                yield
                P.sec("F:beta/g")
                yield from small_scan_terms_g(gf[:, :], 8, typ, PM[:, 64:88], ex)
                P.ts("dve", sm[:, 48:56], ex[:, 0:8], -1.0, None, ALU.mult)
                yield
                yield
                for half in range(2):
                    P.sec("F:l2norm")
                    src = qk[:, half * 8:(half + 1) * 8, :]
                    P.tt("pool", hb[:, :].rearrange("p (c t) -> p c t", c=8), src, src, ALU.mult)
                    yield
                    for i in range(2):
                        rq = tmpA[:, i * 512:(i + 1) * 512]
                        P.mm(PA[i][:, :], onesb[:, :], hb[:, i * 512:(i + 1) * 512])
                        yield
                        if half == 0:
                            P.act(rq, PA[i][:, :], AF.Sqrt, bias=cst[:, 2:3], scale=128.0)
                            yield
                        else:
                            P.act(rq, PA[i][:, :], AF.Sqrt, bias=cst[:, 0:1], scale=1.0)
                            yield
                        P.recip(rq, rq)
                        yield
                        dst = qk[:, half * 8 + 4 * i:half * 8 + 4 * i + 4, :]
                        P.tt("dve", dst, dst, rq.rearrange("p (c t) -> p c t", c=4), ALU.mult)
                        yield
                    yield
                P.sec("F:transposes")
                P.copy("pool", hb[:, :].rearrange("p (c t) -> p c t", c=8), qk[:, 8:16, :])
                yield
                P.trs([(PT[:, j * 128:(j + 1) * 128], qk[:, 8 + j, :]) for j in range(8)], identb[:, :])
                yield
                P.copy("act", fb.ktm[:, :, :], PT[:, :].rearrange("p (h d) -> p h d", h=8))
                yield
                yield
                P.sec("F:transposes")
                P.trs([(PT[:, j * 128:(j + 1) * 128], vfm[:, j, :]) for j in range(8)], identb[:, :])
                yield
                P.copy("act", fb.vtm[:, :, :], PT[:, :].rearrange("p (h d) -> p h d", h=8))
                yield
                if typ == 0:
                    P.tt("pool", SB.rselg[:, :].rearrange("p (h s) -> p h s", h=8),
                         bc(gf[:, :], 2, 16), bc(blk16[:, :], 1, 8), ALU.mult)
                    yield
                    P.mm(PM[:, 128:256], onesf[:, :], SB.rselg[:, :])
                    yield
                    P.act(SB.gendsel[:, :], PM[:, 128:256], AF.Exp)
                    yield
                yield

            def head_gen(h, B, typ, SB, fb):
                mk = masks[typ]
                m_lev = 3 if typ == 0 else 7
                qk, hb, sm, gf, ex, ktm, vtm = fb.qk, fb.hb, fb.sm, fb.gf, fb.ex, fb.ktm, fb.vtm
                kT = qk[:, 8 + h, :]
                qT = qk[:, h, :]
                _st = ["H:decay"]
                P.sec(_st[0])
                P.act(B.L[:, :], mk[:, MSTT, :], AF.Identity, scale=gf[:, h:h + 1])
                yield
                P.mm(B.ps[3][:, :], B.L[:, :], mk[:, MINC, :])
                yield
                P.act(B.Dm[:, :], B.ps[3][:, :], AF.Exp)
                yield
                P.tt("dve", B.DmI[:, :], B.Dm[:, :], mk[:, MINC, :], ALU.mult)
                yield
                P.tt("dve", B.DmS[:, :], B.Dm[:, :], mk[:, MSTR, :], ALU.mult)
                yield
                P.mm(B.ps[0][:, :], kT, hb[:, h * 128:(h + 1) * 128])
                yield
                P.mm(B.ps[1][:, :], kT, qT)
                yield
                P.stt("dve", B.P[0][:, :], B.ps[0][:, :], sm[:, 16 + h:17 + h], B.DmS[:, :], ALU.mult, ALU.mult)
                yield
                P.tt("dve", B.attnT[:, :], B.ps[1][:, :], B.DmI[:, :], ALU.mult)
                yield
                yield
                _st[0] = "H:dbl"
                P.sec(_st[0])
                P.tr(B.ps[2][:, :], B.P[0][:, :], identf[:, :])
                yield
                P.copy("act", B.PT[0][:, :], B.ps[2][:, :])
                yield
                P.tt("dve", B.X[0][:, :], B.P[0][:, :], identf[:, :], ALU.add)
                yield
                yield
                P.sec(_st[0])
                if m_lev > 1:
                    P.mm(B.ps[0][:, :], B.P[0][:, :], B.PT[0][:, :])
                    yield
                    if m_lev > 2:
                        P.mm(B.ps[1][:, :], B.PT[0][:, :], B.P[0][:, :])
                        yield
                    P.copy("act", B.PT[1][:, :], B.ps[0][:, :])
                    yield
                    if m_lev > 2:
                        P.copy("act", B.P[1][:, :], B.ps[1][:, :])
                        yield
                    yield
                    P.sec(_st[0])
                xi = 0
                for j in range(1, m_lev):
                    cur, nx = j % 2, (j + 1) % 2
                    if j + 1 < m_lev:
                        P.mm(B.ps[0][:, :], B.P[cur][:, :], B.PT[cur][:, :])
                        yield
                        if j + 2 < m_lev:
                            P.mm(B.ps[1][:, :], B.PT[cur][:, :], B.P[cur][:, :])
                            yield
                    P.mm(B.ps[2][:, :], B.PT[cur][:, :], B.X[xi][:, :])
                    yield
                    if j + 1 < m_lev:
                        P.copy("act", B.PT[nx][:, :], B.ps[0][:, :])
                        yield
                        if j + 2 < m_lev:
                            P.copy("act", B.P[nx][:, :], B.ps[1][:, :])
                            yield
                    P.tt("dve", B.X[1 - xi][:, :], B.X[xi][:, :], B.ps[2][:, :], ALU.add)
                    yield
                    xi = 1 - xi
                    yield
                    P.sec(_st[0])
                _st[0] = "H:state"
                P.sec(_st[0])
                P.copy("act", B.TTb[:, :], B.X[xi][:, :])
                yield
                if typ == 1:
                    Sf_h, Sb_h = SB.Sf[h], SB.Sb[h]
                    P.mm(B.ps[3][:, :], kT, Sb_h[:, :])
                    yield
                else:
                    P.dma("sp", SB.S0f[:, :, :], st_gdn[:, h, :, :].rearrange("s d e -> d s e"))
                    yield
                    P.copy("act", SB.S0b[:, :, :], SB.S0f[:, :, :])
                    yield
                    for (dstT, srcT) in ((SB.kTm, kT), (SB.qTm, qT)):
                        base = dstT[:, :]
                        dap = bass.AP(base.tensor, base.offset, [[16 * 128, 128], [136, 16], [1, 8]])
                        P.op("pool", (lambda e, dap=dap, src=srcT.rearrange("p (s t) -> p s t", s=16):
                                      e.tensor_copy(out=dap, in_=src)), ins=[srcT], outs=[base])
                        yield
                    P.mmk(B.ps[3][:, :], [(SB.kTm[:, s * 128:(s + 1) * 128], SB.S0b[:, s, :]) for s in range(16)])
                    yield
                P.stt("dve", B.R[:, :], B.ps[3][:, :], sm[:, 48 + h:49 + h], vtm[:, h, :], ALU.mult, ALU.add)
                yield
                P.mm(B.ps[0][:, :], B.TTb[:, :], B.R[:, :])
                yield
                P.act(B.vn[:, :], B.ps[0][:, :], AF.Identity, scale=sm[:, 8 + h:9 + h])
                yield
                yield
                P.sec(_st[0])
                if typ == 1:
                    P.mm(B.ps[1][:, :], qT, Sb_h[:, :])
                    yield
                else:
                    P.mmk(B.ps[1][:, :], [(SB.qTm[:, s * 128:(s + 1) * 128], SB.S0b[:, s, :]) for s in range(16)])
                    yield
                P.mm(B.ps[2][:, :], B.attnT[:, :], B.vn[:, :])
                yield
                P.act(B.ot[:, :], B.ps[1][:, :], AF.Identity, scale=ex[:, h:h + 1])
                yield
                P.tt("dve", otm[:, h * 128:(h + 1) * 128], B.ot[:, :], B.ps[2][:, :], ALU.add)
                yield
                P.act(B.kout[:, :], ktm[:, h, :], AF.Identity, scale=ex[:, 8 + h:9 + h])
                yield
                yield
                P.sec(_st[0])
                if typ == 1:
                    P.mm(B.ps[3][:, :], B.kout[:, :], B.vn[:, :])
                    yield
                    P.stt("dve", Sf_h[:, :], Sf_h[:, :], ex[:, 16 + h:17 + h], B.ps[3][:, :], ALU.mult, ALU.add)
                    yield
                    P.copy("act", Sb_h[:, :], Sf_h[:, :])
                    yield
                else:
                    P.tt("dve", SB.koutm_all[:, :, :], bc(B.kout[:, :], 1, 16), bc(blk16[:, :], 2, 128), ALU.mult)
                    yield
                    for s in range(16):
                        P.mm(LPB[s // 4][:, (s % 4) * 128:(s % 4 + 1) * 128], SB.koutm_all[:, s, :], B.vn[:, :])
                    yield
                    for j4 in range(4):
                        S4 = SB.S0f[:, 4 * j4:4 * j4 + 4, :]
                        gsel = SB.gendsel[:, h * 16 + 4 * j4:h * 16 + 4 * j4 + 4]
                        P.tt("dve", S4, S4, bc(gsel, 2, 128), ALU.mult)
                        yield
                        P.tt("dve", S4, S4, LPB[j4][:, :].rearrange("p (s e) -> p s e", s=4), ALU.add)
                        yield
                    P.dma("sp", o_gdn_s[:, h, :, :].rearrange("s d e -> d s e"), SB.S0f[:, :, :])
                    yield
                yield

            def tail_gen(ti, fb):
                xt = fb.xt
                P.sec("T:onorm")
                P.dma("sp", mixed[:, 0:D], yssd_scr[ti])
                yield
                P.tt("pool", tmpT[:, :], otm[:, :], otm[:, :], ALU.mult)
                yield
                P.red("dve", ss8[:, 0:8], tmpT[:, :].rearrange("p (h d) -> p h d", h=8))
                yield
                P.act(ss8[:, 8:16], ss8[:, 0:8], AF.Sqrt, bias=cst[:, 0:1], scale=1.0 / 128)
                yield
                P.recip(ss8[:, 16:24], ss8[:, 8:16])
                yield
                o3 = otm[:, :].rearrange("p (h d) -> p h d", h=8)
                P.tt("dve", o3, o3, bc(ss8[:, 16:24], 2, 128), ALU.mult)
                yield
                P.tt("dve", o3, o3, bc(gnw[:, :], 1, 8), ALU.mult)
                yield
                P.tt("dve", mixed[:, D:2 * D], otm[:, :], fb.gs[:, :], ALU.mult)
                yield
                yield
                P.sec("T:outproj")
                for half in range(2):
                    P.trs([(PT[:, j * 128:(j + 1) * 128], mixed[:, (half * 8 + j) * 128:(half * 8 + j + 1) * 128])
                           for j in range(8)], identb[:, :])
                    yield
                    P.copy("act", mixedT[:, half * 8:(half + 1) * 8, :], PT[:, :].rearrange("p (c t) -> p c t", c=8))
                    yield
                yield
                P.sec("T:outproj")
                for nb in range(2):
                    P.mmk(PA[nb][:, :], [(mixedT[:, kc, :], wO[:, kc, nb * 512:(nb + 1) * 512]) for kc in range(16)])
                    yield
                    P.copy("act", tmpT[:, nb * 512:(nb + 1) * 512], PA[nb][:, :])
                    yield
                yield
                P.sec("T:resid")
                P.sqsum(otm[:, :], tmpT[:, :], sttT[:, 0:1])
                yield
                P.act(sttT[:, 1:2], sttT[:, 0:1], AF.Sqrt, bias=cst[:, 0:1], scale=1.0 / D)
                yield
                P.recip(sttT[:, 2:3], sttT[:, 1:2])
                yield
                P.stt("dve", tmpT[:, :], tmpT[:, :], sttT[:, 2:3], modv[:, 2, :], ALU.mult, ALU.mult)
                yield
                P.tt("dve", xt[:, :], tmpT[:, :], xt[:, :], ALU.add)
                yield
                P.dma("sp", x1_scr[ti], xt[:, :])
                yield
                yield

            def speed(g, k):
                while True:
                    for _ in range(k):
                        try:
                            next(g)
                        except StopIteration:
                            return
                    yield

            def run_rr(gens):
                alive = list(gens)
                while alive:
                    nxt = []
                    for gn in alive:
                        try:
                            next(gn)
                            nxt.append(gn)
                        except StopIteration:
                            pass
                    alive = nxt

            def back(ti, typ, SB, fb, nl, side):
                side = [side] if side is not None else []
                for h0_ in range(0, 8, nl):
                    gens = [head_gen(h0_ + i, lanes[i], typ, SB, fb) for i in range(min(nl, 8 - h0_))]
                    alive = gens + side
                    while any(g in alive for g in gens):
                        nxt = []
                        for gn in alive:
                            try:
                                next(gn)
                                nxt.append(gn)
                            except StopIteration:
                                if gn in side:
                                    side = []
                        alive = nxt
                run_rr([tail_gen(ti, fb)] + side)

            class SBufs:
                pass

            with contextlib.ExitStack() as st2s:
                SB = SBufs()
                SB.histS = sb(st2s, "histSg", [128, 24, 16, 3])
                SB.xinS_l = [sb(st2s, "xinS2_%d" % i, [128, 4, 16, 11]) for i in range(2)]
                SB.S0f = sb(st2s, "S0f", [128, 16, 128])
                SB.S0b = sb(st2s, "S0b", [128, 16, 128], BF16)
                SB.kTm = sb(st2s, "kTm", [128, 16 * 128], BF16)
                SB.qTm = sb(st2s, "qTm", [128, 16 * 128], BF16)
                SB.koutm_all = sb(st2s, "koutm_all", [128, 16, 128], BF16)
                SB.rselg = sb(st2s, "rselg", [128, 128])
                SB.gendsel = sb(st2s, "gendsel", [128, 128])
                P.dma("sp", SB.histS[:, :, :, :], hist_gdn[:, :, :, :])
                P.memset("pool", SB.kTm[:, :], 0.0)
                P.memset("pool", SB.qTm[:, :], 0.0)
                run_rr([front_gen(0, 0, SB, fbs[0])])
                back(0, 0, SB, fbs[0], 1, None)
                P.barrier()
                P.emit()
            with contextlib.ExitStack() as st2p:
                SB = SBufs()
                SB.Sf = [sb(st2p, "Sf%d" % h, [128, 128]) for h in range(8)]
                SB.Sb = [sb(st2p, "Sb%d" % h, [128, 128], BF16) for h in range(8)]
                for ln in range(1, P2_NL):
                    make_lane(ln, st2p)
                fbs.append(make_fb(1, st2p))
                for h in range(8):
                    P.memset("pool", SB.Sf[h][:, :], 0.0)
                    P.memset("pool", SB.Sb[h][:, :], 0.0)
                run_rr([front_gen(1, 1, SB, fbs[1])])
                for ti in range(1, NT):
                    side = speed(front_gen(ti + 1, 1, SB, fbs[(ti + 1) % 2]), 2) if ti + 1 < NT else None
                    back(ti, 1, SB, fbs[ti % 2], P2_NL, side)
                for h in range(8):
                    P.dma("sp", o_gdn_p[h * 128:(h + 1) * 128, :], SB.Sf[h][:, :])
                P.barrier()
                P.emit()
        if STOP_AFTER >= 3:
          with contextlib.ExitStack() as st3:
            PB = [st3.enter_context(nc.psum_tensor("pb%d" % i, [128, 512], F32)) for i in range(7)]
            wG = sb(st3, "wG", [128, 8, 2 * DFF], BF16)
            w_gu_v = w_gu.rearrange("(kc p) n -> p kc n", p=128)
            for kc in range(8):
                P.dma("pool", wG[:, kc, :], w_gu_v[:, kc, :])
            wD = sb(st3, "wD", [128, 22, D], BF16)
            w_dn_v = w_dn.rearrange("(kc p) n -> p kc n", p=128)
            for kc in range(22):
                P.dma("pool", wD[:, kc, :], w_dn_v[:, kc, :])
            modv = sb(st3, "modv3", [128, 3, D])
            xt = [sb(st3, "xt3_%d" % i, [128, D]) for i in range(2)]
            tmpA = [sb(st3, "tmpA3_%d" % i, [128, D]) for i in range(2)]
            tmpB = sb(st3, "tmpB3", [128, D])
            hb = [sb(st3, "hb3_%d" % i, [128, D], BF16) for i in range(2)]
            hT = [sb(st3, "hT3_%d" % i, [128, 8, 128], BF16) for i in range(2)]
            stt_ = [sb(st3, "stt3_%d" % i, [128, 4]) for i in range(2)]
            sg = [sb(st3, "sg%d" % i, [128, 512]) for i in range(2)]
            hid = sb(st3, "hid", [128, DFF], BF16)
            hidT = sb(st3, "hidT", [128, 22, 128], BF16)

            def ffn_gen(ti):
                typ = 0 if ti == 0 else 1
                pr = ti % 2
                x_, tA, hb_, hT_, st_ = xt[pr], tmpA[pr], hb[pr], hT[pr], stt_[pr]
                if ti <= 1:
                    P.dma("sp", modv[:, :, :], modscr[typ].rearrange("p (s d) -> p s d", s=6)[:, 3:6, :])
                P.dma("sp", x_[:, :], x1_scr[ti])
                yield
                yield from norm_to_T_g(x_, modv, 1, 0, tA, hb_, hT_, st_)
                for i in range(6):
                    w = 512 if i < 5 else 256
                    pg_, pu_ = PB[(2 * i) % 6], PB[(2 * i + 1) % 6]
                    P.mmk(pg_[:, 0:w], [(hT_[:, kc, :], wG[:, kc, i * 512:i * 512 + w]) for kc in range(8)])
                    yield
                    P.mmk(pu_[:, 0:w], [(hT_[:, kc, :], wG[:, kc, DFF + i * 512:DFF + i * 512 + w]) for kc in range(8)])
                    yield
                    s_ = sg[i % 2]
                    P.act(s_[:, 0:w], pg_[:, 0:w], AF.Silu)
                    yield
                    P.tt("dve", hid[:, i * 512:i * 512 + w], s_[:, 0:w], pu_[:, 0:w], ALU.mult)
                    yield
                yield "HALF"
                for grp in range(3):
                    n = 8 if grp < 2 else 6
                    P.trs([(PT[:, j * 128:(j + 1) * 128], hid[:, (grp * 8 + j) * 128:(grp * 8 + j + 1) * 128])
                           for j in range(n)], identb[:, :])
                    yield
                    P.copy("act", hidT[:, grp * 8:grp * 8 + n, :],
                           PT[:, 0:n * 128].rearrange("p (c t) -> p c t", c=n))
                    yield
                for nb in range(2):
                    P.mmk(PB[6][:, :], [(hidT[:, kc, :], wD[:, kc, nb * 512:(nb + 1) * 512]) for kc in range(22)])
                    yield
                    P.copy("act", tA[:, nb * 512:(nb + 1) * 512], PB[6][:, :])
                    yield
                P.sqsum(tmpB[:, :], tA[:, :], st_[:, 0:1])
                yield
                P.act(st_[:, 1:2], st_[:, 0:1], AF.Sqrt, bias=cst[:, 0:1], scale=1.0 / D)
                yield
                P.recip(st_[:, 2:3], st_[:, 1:2])
                yield
                P.stt("dve", tA[:, :], tA[:, :], st_[:, 2:3], modv[:, 2, :], ALU.mult, ALU.mult)
                yield
                P.tt("dve", x_[:, :], tA[:, :], x_[:, :], ALU.add)
                yield
                P.dma("sp", y_all[ti], x_[:, :])
                yield

            run_gen(ffn_gen(0))
            nxt = 2
            cur = ffn_gen(1)
            young = None
            while cur is not None:
                try:
                    v = next(cur)
                    if v == "HALF" and young is None and nxt < NT:
                        young = ffn_gen(nxt)
                        nxt += 1
                except StopIteration:
                    cur, young = young, None
                    if cur is None and nxt < NT:
                        cur = ffn_gen(nxt)
                        nxt += 1
                    continue
                if young is not None:
                    try:
                        v2 = next(young)
                        if v2 == "HALF":
                            pass
                    except StopIteration:
                        young = None
            P.barrier()
            P.emit()
    return nc


def _prep_inputs(inp):
    f = lambda a: np.ascontiguousarray(np.asarray(a, dtype=np.float32))
    xp, xs = f(inp["x_prompt"]), f(inp["x_sample"])
    cp, cs = f(inp["c_prompt"]), f(inp["c_sample"])
    bcast = lambda v: np.ascontiguousarray(np.broadcast_to(np.asarray(v, np.float32).reshape(1, -1), (128, v.size)))
    shared = {}
    shared["w_ada"] = f(inp["w_ada"][0])
    shared["b_ada_b"] = bcast(inp["b_ada"][0])
    shared["normvecs"] = np.ascontiguousarray(np.stack(
        [bcast(inp[k][0]) for k in ("norm_mix_pre", "norm_mix_post", "norm_ffn_pre", "norm_ffn_post")], axis=1))
    shared["w_in"] = f(inp["w_in"][0])
    shared["w_out"] = f(inp["w_out"][0])
    shared["w_gu"] = f(inp["w_gate_up"][0])
    shared["w_dn"] = f(inp["w_down"][0])
    scw = np.concatenate([f(inp["ssd_conv_w"][0]), f(inp["ssd_conv_b"][0])[None, :]], axis=0)
    shared["ssd_cw"] = np.ascontiguousarray(scw.reshape(5, 12, 128).transpose(2, 1, 0))
    shared["gdn_cw"] = np.ascontiguousarray(f(inp["gdn_conv_w"][0]).reshape(4, 24, 128).transpose(2, 1, 0))
    sv = np.concatenate([f(inp["ssd_dt_bias"][0]), f(inp["ssd_A_log"][0]), f(inp["ssd_D"][0]),
                         f(inp["gdn_dt_bias"][0]), f(inp["gdn_A_log"][0])])
    shared["smallv"] = bcast(sv)
    shared["ssd_nw"] = bcast(inp["ssd_norm_w"][0])
    shared["gdn_nw"] = bcast(inp["gdn_norm_w"][0])
    shared["ident"] = np.eye(128, dtype=np.float32)
    idx = np.arange(128)
    m = np.zeros((2, 128, 4, 128), np.float32)
    for typ, bs in ((0, 8), (1, 128)):
        same = (idx[:, None] // bs) == (idx[None, :] // bs)
        m[typ, :, 0, :] = same & (idx[:, None] > idx[None, :])
        m[typ, :, 1, :] = same & (idx[:, None] <= idx[None, :])
        m[typ, :, 2, :] = same & (idx[:, None] < idx[None, :])
        m[typ, :, 3, :] = same
    shared["masks"] = m
    shared["blk16"] = np.ascontiguousarray(((idx[:, None] // 8) == np.arange(16)[None, :]).astype(np.float32))
    maps = []
    for i in range(NCORES):
        d = dict(shared)
        sl = slice(16 * i, 16 * (i + 1))
        d["xs_all"] = np.ascontiguousarray(np.concatenate(
            [xs[sl].reshape(1, 128, D), xp[i].reshape(16, 128, D)], axis=0))
        d["cexp"] = np.ascontiguousarray(np.stack(
            [np.repeat(cs[sl], 8, axis=0), np.broadcast_to(cp[i][None, :], (128, D))], axis=0))
        d["st_ssd"] = np.ascontiguousarray(f(inp["state_ssd"][0, sl]).reshape(16, 1024, 128))
        hs = f(inp["state_ssd_conv"][0, sl])
        d["hist_ssd"] = np.ascontiguousarray(hs.reshape(16, 3, 12, 128).transpose(3, 2, 0, 1))
        d["st_gdn"] = np.ascontiguousarray(f(inp["state_gdn"][0, sl]))
        hg = f(inp["state_gdn_conv"][0, sl])
        d["hist_gdn"] = np.ascontiguousarray(hg.reshape(16, 3, 24, 128).transpose(3, 2, 0, 1))
        maps.append(d)
    return maps


def kernel(**inp):
    maps = _prep_inputs(inp)
    nc = build_program()
    res = run_bass_kernel_spmd(nc, maps, core_ids=list(range(NCORES)))
    R = res.results
    cat = lambda k: np.stack([np.asarray(r[k]) for r in R], axis=0)
    y_all = cat("y_all")
    y_prompt = y_all[:, 1:].reshape(8, 2048, D)
    y_sample = y_all[:, 0].reshape(128, 8, D)
    ssd_p = cat("o_ssd_p").reshape(1, 8, 16, 64, 128)
    ssdc_p = cat("o_ssdc_p").reshape(1, 8, 3, 1536)
    gdn_p = cat("o_gdn_p").reshape(1, 8, 8, 128, 128)
    gdnc_p = cat("o_gdnc_p").reshape(1, 8, 3, 3072)
    ssd_s = cat("o_ssd_s").reshape(1, 128, 16, 64, 128)
    ssdc_s = cat("o_ssdc_s").reshape(1, 128, 3, 1536)
    gdn_s = cat("o_gdn_s").reshape(1, 128, 8, 128, 128)
    gdnc_s = cat("o_gdnc_s").reshape(1, 128, 3, 3072)
    outs = (y_prompt, y_sample, ssd_p, ssdc_p, gdn_p, gdnc_p, ssd_s, ssdc_s, gdn_s, gdnc_s)
    return tuple(np.ascontiguousarray(o, dtype=np.float32) for o in outs)
```

```python
import contextlib
import itertools
import os
import numpy as np
import concourse.bass as bass
import concourse.mybir as mybir
from concourse.bass_utils import run_bass_kernel_spmd

F32 = mybir.dt.float32
BF16 = mybir.dt.bfloat16
AF = mybir.ActivationFunctionType
ALU = mybir.AluOpType
AX = mybir.AxisListType

NCORES = 8
D = 1024
NT = 17
DFF = 2816
P2_MODE = 0
ANNOTATE = bool(int(os.environ.get('K_ANN', '0')))
SKIP1 = False
P2_STAGE = 99
P2_HSTEP = 99
P2_SKIP = ()
P2_NL = 4
P2_TILES = 16
STOP_AFTER = int(os.environ.get('K_STOP', '99'))


class Prog:
    ENGS = ("pe", "act", "dve", "pool", "sp")

    def __init__(self, nc, stack):
        self.nc = nc
        self.sems = {}
        for e in self.ENGS:
            self.sems[e] = stack.enter_context(nc.semaphore("s_" + e))
        self.dsems = {"sp": [], "pool": [], "act": []}
        for q, n in (("sp", 8), ("pool", 4), ("act", 2)):
            for j in range(n):
                nm = "d_%s%d" % (q, j)
                self.sems[nm] = stack.enter_context(nc.semaphore(nm))
                self.dsems[q].append(nm)
        self.cnt = {k: 0 for k in self.sems}
        self.known = {e: {} for e in self.ENGS}
        self.lastw = {}
        self.readers = {}
        self.ops = {e: [] for e in self.ENGS}
        self.rr = {"sp": 0, "pool": 0, "act": 0}
        self.nops = 0
        self.tag = "init"

    def sec(self, name):
        self.tag = name

    fine = {}

    def _keys(self, aps):
        ks = []
        for a in aps:
            if a is None or isinstance(a, (int, float)):
                continue
            if isinstance(a, str):
                ks.append(a)
                continue
            nm = a.name
            if nm in self.fine:
                row, gr = self.fine[nm]
                nm = "%s:%d" % (nm, (int(a.offset) % row) // gr)
            ks.append(nm)
        return ks

    def _deps(self, eng, r, w):
        need = {}

        def add(c):
            if c is None:
                return
            s, v = c
            if s == "pe" and eng == "pe":
                return
            if need.get(s, 0) < v:
                need[s] = v

        for k in r:
            add(self.lastw.get(k))
        for k in w:
            add(self.lastw.get(k))
            for c in self.readers.get(k, {}).items():
                add(c)
        waits = []
        kn = self.known[eng]
        for s, v in need.items():
            if kn.get(s, 0) < v:
                kn[s] = v
                waits.append((s, v))
        return waits

    def _commit(self, c, r, w):
        for k in w:
            self.lastw[k] = c
            self.readers[k] = {}
        for k in r:
            d = self.readers.setdefault(k, {})
            if d.get(c[0], 0) < c[1]:
                d[c[0]] = c[1]

    def op(self, eng, fn, ins=(), outs=()):
        r = self._keys(ins)
        w = self._keys(outs)
        w = w + [k for k in r if k.startswith("ps") or k.startswith("pa") or k.startswith("pm")
                 or k.startswith("lpb") or k.startswith("pb")]
        waits = self._deps(eng, r, w)
        self.cnt[eng] += 1
        self.ops[eng].append((waits, fn, eng, 1, self.tag))
        self._commit((eng, self.cnt[eng]), r, w)
        self.nops += 1

    def dma(self, q, out, in_, extra_ins=(), extra_outs=()):
        r = self._keys([in_] + list(extra_ins))
        w = self._keys([out] + list(extra_outs))
        waits = self._deps(q, r, w)
        sems = self.dsems[q]
        j = self.rr[q]
        self.rr[q] = (j + 1) % len(sems)
        nm = sems[j]
        prev = self.cnt[nm]
        if prev > 0 and self.known[q].get(nm, 0) < prev:
            self.known[q][nm] = prev
            waits.append((nm, prev))
        self.cnt[nm] += 16
        self.ops[q].append((waits, lambda e: e.dma_start(out=out, in_=in_), nm, 16, self.tag))
        self._commit((nm, self.cnt[nm]), r, w)
        self.nops += 1

    def barrier(self):
        for e in self.ENGS:
            waits = []
            for s, v in self.cnt.items():
                if v > 0 and self.known[e].get(s, 0) < v:
                    self.known[e][s] = v
                    waits.append((s, v))
            self.ops[e].append((waits, None, None, 0, self.tag))
        self.lastw = {}
        self.readers = {}

    def emit(self):
        nc = self.nc
        with nc.Block() as block:
            for ename, deco in (("sp", block.sync), ("act", block.scalar), ("dve", block.vector),
                                ("pool", block.gpsimd), ("pe", block.tensor)):
                ops = self.ops[ename]

                def body(e, ops=ops):
                    for waits, fn, sname, inc, tag in ops:
                        for s, v in waits:
                            e.wait_ge(self.sems[s], v)
                        if fn is not None:
                            ins = fn(e)
                            ins.then_inc(self.sems[sname], inc)
                            if ANNOTATE:
                                ins.annotate(tag)

                deco(body)
                self.ops[ename] = []

    def mm(self, out, lhsT, rhs, start=True, stop=True):
        self.op("pe", lambda e: e.matmul(out, lhsT=lhsT, rhs=rhs, start=start, stop=stop),
                ins=[lhsT, rhs], outs=[out])

    def mmk(self, out, pairs):
        n = len(pairs)

        def fn(e):
            ins = None
            for i, (l, r) in enumerate(pairs):
                ins = e.matmul(out, lhsT=l, rhs=r, start=(i == 0), stop=(i == n - 1))
            return ins

        self.op("pe", fn, ins=[x for p in pairs for x in p], outs=[out])

    def tr(self, out, in_, ident):
        self.op("pe", lambda e: e.transpose(out, in_, ident), ins=[in_, ident], outs=[out])

    def trs(self, items, ident):
        def fn(e):
            ins = None
            for o, i in items:
                ins = e.transpose(o, i, ident)
            return ins
        self.op("pe", fn, ins=[i for _, i in items] + [ident], outs=[o for o, _ in items])

    def act(self, out, in_, func, bias=None, scale=None):
        kw = {}
        if bias is not None:
            kw["bias"] = bias
        if scale is not None:
            kw["scale"] = scale
        self.op("act", lambda e: e.activation(out=out, in_=in_, func=func, **kw),
                ins=[in_, bias, scale], outs=[out])

    def sqsum(self, junk, in_, accum):
        self.op("act", lambda e: e.activation(out=junk, in_=in_, func=AF.Square, accum_out=accum),
                ins=[in_], outs=[junk, accum])

    def tt(self, eng, out, in0, in1, op):
        self.op(eng, lambda e: e.tensor_tensor(out=out, in0=in0, in1=in1, op=op), ins=[in0, in1], outs=[out])

    def ts(self, eng, out, in0, s1, s2, op0, op1=None):
        if op1 is None:
            self.op(eng, lambda e: e.tensor_scalar(out=out, in0=in0, scalar1=s1, scalar2=None, op0=op0),
                    ins=[in0, s1], outs=[out])
        else:
            self.op(eng, lambda e: e.tensor_scalar(out=out, in0=in0, scalar1=s1, scalar2=s2, op0=op0, op1=op1),
                    ins=[in0, s1, s2], outs=[out])

    def stt(self, eng, out, in0, scalar, in1, op0, op1):
        self.op(eng, lambda e: e.scalar_tensor_tensor(out=out, in0=in0, scalar=scalar, in1=in1, op0=op0, op1=op1),
                ins=[in0, scalar, in1], outs=[out])

    def copy(self, eng, out, in_):
        if eng == "act":
            self.op(eng, lambda e: e.copy(out=out, in_=in_), ins=[in_], outs=[out])
        else:
            self.op(eng, lambda e: e.tensor_copy(out=out, in_=in_), ins=[in_], outs=[out])

    def red(self, eng, out, in_):
        self.op(eng, lambda e: e.tensor_reduce(out=out, in_=in_, axis=AX.X, op=ALU.add), ins=[in_], outs=[out])

    def recip(self, out, in_):
        self.op("dve", lambda e: e.reciprocal(out=out, in_=in_), ins=[in_], outs=[out])

    def memset(self, eng, ap, val):
        self.op(eng, lambda e: e.memset(ap, val), ins=[], outs=[ap])


def bc(ap, axis, n):
    u = ap.unsqueeze(axis)
    shp = list(u.shape)
    shp[axis] = n
    return u.broadcast_to(shp)


def build_program():
    nc = bass.Bass("TRN2", target_bir_lowering=False)

    def din(name, shape, dt=F32):
        return nc.dram_tensor(name, list(shape), dt, kind="ExternalInput").ap()

    def dout(name, shape, dt=F32):
        return nc.dram_tensor(name, list(shape), dt, kind="ExternalOutput").ap()

    def dscr(name, shape, dt=F32):
        return nc.dram_tensor(name, list(shape), dt).ap()

    xs_all = din("xs_all", [NT, 128, D])
    cexp = din("cexp", [2, 128, D])
    w_ada = din("w_ada", [D, 6 * D])
    b_ada_b = din("b_ada_b", [128, 6 * D])
    normvecs = din("normvecs", [128, 4, D])
    w_in = din("w_in", [D, 6688])
    w_out = din("w_out", [2 * D, D])
    w_gu = din("w_gu", [D, 2 * DFF])
    w_dn = din("w_dn", [DFF, D])
    ssd_cw = din("ssd_cw", [128, 12, 5])
    gdn_cw = din("gdn_cw", [128, 24, 4])
    smallv = din("smallv", [128, 64])
    ssd_nw = din("ssd_nw", [128, D])
    gdn_nw = din("gdn_nw", [128, 128])
    st_ssd = din("st_ssd", [16, 1024, 128])
    hist_ssd = din("hist_ssd", [128, 12, 16, 3])
    st_gdn = din("st_gdn", [16, 8, 128, 128])
    hist_gdn = din("hist_gdn", [128, 24, 16, 3])
    ident_d = din("ident", [128, 128])
    masks_d = din("masks", [2, 128, 4, 128])
    blk16_d = din("blk16", [128, 16])

    y_all = dout("y_all", [NT, 128, D])
    o_ssd_p = dout("o_ssd_p", [1024, 128])
    o_ssdc_p = dout("o_ssdc_p", [3, 1536])
    o_gdn_p = dout("o_gdn_p", [1024, 128])
    o_gdnc_p = dout("o_gdnc_p", [3, 3072])
    o_ssd_s = dout("o_ssd_s", [16, 1024, 128])
    o_ssdc_s = dout("o_ssdc_s", [48, 1536])
    o_gdn_s = dout("o_gdn_s", [16, 8, 128, 128])
    o_gdnc_s = dout("o_gdnc_s", [48, 3072])

    modscr = dscr("modscr", [2, 128, 6 * D])
    yssd_scr = dscr("yssd_scr", [NT, 128, D], BF16)
    x1_scr = dscr("x1_scr", [NT, 128, D])

    with contextlib.ExitStack() as gstack:
        P = Prog(nc, gstack)

        def sb(stack, name, shape, dt=F32):
            return stack.enter_context(nc.sbuf_tensor(name, list(shape), dt))

        PT = gstack.enter_context(nc.psum_tensor("pst", [128, 1024], BF16))
        ps1 = contextlib.ExitStack()
        PS = [ps1.enter_context(nc.psum_tensor("ps%d" % i, [128, 512], F32)) for i in range(7)]

        identf = sb(gstack, "identf", [128, 128])
        identb = sb(gstack, "identb", [128, 128], BF16)
        onesf = sb(gstack, "onesf", [128, 128])
        onesb = sb(gstack, "onesb", [128, 128], BF16)
        cst = sb(gstack, "cst", [128, 4])
        masks = [sb(gstack, "masks%d" % t, [128, 4, 128]) for t in range(2)]
        blk16 = sb(gstack, "blk16s", [128, 16])
        smv = sb(gstack, "smv", [128, 64])
        negA = sb(gstack, "negA", [128, 24])
        P.dma("sp", identf[:, :], ident_d[:, :])
        P.dma("pool", identb[:, :], ident_d[:, :])
        for t in range(2):
            P.dma("sp", masks[t][:, :, :], masks_d[t])
        P.dma("sp", blk16[:, :], blk16_d[:, :])
        P.dma("sp", smv[:, :], smallv[:, :])
        P.memset("dve", onesf[:, :], 1.0)
        P.memset("dve", onesb[:, :], 1.0)
        P.memset("dve", cst[:, 0:1], 1e-6)
        P.memset("dve", cst[:, 1:2], 1.0)
        P.memset("dve", cst[:, 2:3], 128e-6)
        P.memset("dve", cst[:, 3:4], 0.0)
        P.act(negA[:, 0:16], smv[:, 16:32], AF.Exp)
        P.act(negA[:, 16:24], smv[:, 56:64], AF.Exp)
        P.ts("dve", negA[:, :], negA[:, :], -1.0, None, ALU.mult)

        MSTT, MINC, MSTR, MBLK = 0, 1, 2, 3

        with contextlib.ExitStack() as st0:
            ct = sb(st0, "ct", [128, D])
            cb = sb(st0, "cb", [128, D], BF16)
            cT = [sb(st0, "cT%d" % t, [128, 8, 128], BF16) for t in range(2)]
            modt = [sb(st0, "modt%d" % t, [128, 6 * D]) for t in range(2)]
            nv = sb(st0, "nv", [128, 4, D])
            wa = [sb(st0, "wa%d" % i, [128, 8, 512], BF16) for i in range(2)]
            bb = [sb(st0, "bb%d" % i, [128, 512]) for i in range(2)]
            P.dma("sp", nv[:, :, :], normvecs[:, :, :])
            for t in range(2):
                P.dma("sp", ct[:, :], cexp[t])
                P.act(cb[:, :], ct[:, :], AF.Silu)
                P.trs([(PT[:, c * 128:(c + 1) * 128], cb[:, c * 128:(c + 1) * 128]) for c in range(8)], identb[:, :])
                P.copy("dve", cT[t][:, :, :], PT[:, :].rearrange("p (c t) -> p c t", c=8))
            wv = w_ada.rearrange("(kc p) n -> p kc n", p=128)
            for j in range(12):
                P.dma("pool", wa[j % 2][:, :, :], wv[:, :, j * 512:(j + 1) * 512])
                P.dma("sp", bb[j % 2][:, :], b_ada_b[:, j * 512:(j + 1) * 512])
                for t in range(2):
                    ps = PS[(2 * j + t) % 4]
                    P.mmk(ps[:, :], [(cT[t][:, kc, :], wa[j % 2][:, kc, :]) for kc in range(8)])
                    P.tt("dve", modt[t][:, j * 512:(j + 1) * 512], ps[:, :], bb[j % 2][:, :], ALU.add)
            for t in range(2):
                m = modt[t]
                P.stt("dve", m[:, D:2 * D], m[:, D:2 * D], 1.0, nv[:, 0, :], ALU.add, ALU.mult)
                P.tt("dve", m[:, 2 * D:3 * D], m[:, 2 * D:3 * D], nv[:, 1, :], ALU.mult)
                P.stt("dve", m[:, 4 * D:5 * D], m[:, 4 * D:5 * D], 1.0, nv[:, 2, :], ALU.add, ALU.mult)
                P.tt("dve", m[:, 5 * D:6 * D], m[:, 5 * D:6 * D], nv[:, 3, :], ALU.mult)
                P.dma("sp", modscr[t], m[:, :])
            P.barrier()
            P.emit()

        def norm_to_T(xt, modv, sidx, hidx, tmpA, hb, hT, stt_):
            P.sec("norm_to_T")
            P.sqsum(tmpA[:, :], xt[:, :], stt_[:, 0:1])
            P.act(stt_[:, 1:2], stt_[:, 0:1], AF.Sqrt, bias=cst[:, 0:1], scale=1.0 / D)
            P.recip(stt_[:, 2:3], stt_[:, 1:2])
            P.stt("dve", tmpA[:, :], xt[:, :], stt_[:, 2:3], modv[:, sidx, :], ALU.mult, ALU.mult)
            P.tt("dve", hb[:, :], tmpA[:, :], modv[:, hidx, :], ALU.add)
            P.trs([(PT[:, c * 128:(c + 1) * 128], hb[:, c * 128:(c + 1) * 128]) for c in range(8)], identb[:, :])
            P.copy("act", hT[:, :, :], PT[:, :].rearrange("p (c t) -> p c t", c=8))

        def norm_to_T_g(xt, modv, sidx, hidx, tmpA, hb, hT, stt_):
            P.sec("norm_to_T")
            P.sqsum(tmpA[:, :], xt[:, :], stt_[:, 0:1])
            yield
            P.act(stt_[:, 1:2], stt_[:, 0:1], AF.Sqrt, bias=cst[:, 0:1], scale=1.0 / D)
            yield
            P.recip(stt_[:, 2:3], stt_[:, 1:2])
            yield
            P.stt("dve", tmpA[:, :], xt[:, :], stt_[:, 2:3], modv[:, sidx, :], ALU.mult, ALU.mult)
            yield
            P.tt("dve", hb[:, :], tmpA[:, :], modv[:, hidx, :], ALU.add)
            yield
            P.trs([(PT[:, c * 128:(c + 1) * 128], hb[:, c * 128:(c + 1) * 128]) for c in range(8)], identb[:, :])
            yield
            P.copy("act", hT[:, :, :], PT[:, :].rearrange("p (c t) -> p c t", c=8))
            yield

        def small_scan_terms(af, nh, typ, pss, ex):
            mk = masks[typ]
            P.mm(pss[:, 0:nh], mk[:, MINC, :], af)
            P.mm(pss[:, nh:2 * nh], mk[:, MSTT, :], af)
            P.mm(pss[:, 2 * nh:3 * nh], mk[:, MBLK, :], af)
            P.act(ex[:, 0:3 * nh], pss[:, 0:3 * nh], AF.Exp)

        def small_scan_terms_g(af, nh, typ, pss, ex):
            mk = masks[typ]
            P.mm(pss[:, 0:nh], mk[:, MINC, :], af)
            yield
            P.mm(pss[:, nh:2 * nh], mk[:, MSTT, :], af)
            yield
            P.mm(pss[:, 2 * nh:3 * nh], mk[:, MBLK, :], af)
            yield
            P.act(ex[:, 0:3 * nh], pss[:, 0:3 * nh], AF.Exp)
            yield

        def proj_conv_gen(typ, ngroups, wt, wcol0, hT, psb, xinS_l, xinP_l, histS, histP, cwt, has_bias, accs, dst_fn):
            items = []

            def slot(i):
                n = len(items)
                if 0 <= i < n:
                    it = items[i]
                    if has_bias:
                        P.act(it[0], it[2][0], AF.Identity, bias=cwt[:, it[4], 4:5], scale=cwt[:, it[4], 0:1])
                    else:
                        P.act(it[0], it[2][0], AF.Identity, scale=cwt[:, it[4], 0:1])
                    yield
                if 0 <= i - 1 < n:
                    it = items[i - 1]
                    for k in range(1, 4):
                        P.stt("dve", it[0], it[2][k], cwt[:, it[4], k:k + 1], it[0], ALU.mult, ALU.add)
                        yield
                if 0 <= i - 2 < n:
                    it = items[i - 2]
                    P.act(it[3], it[1][:, :], AF.Silu)
                    yield

            for g in range(ngroups):
                ps = psb[g % 2]
                for j in range(4):
                    c = 4 * g + j
                    P.mmk(ps[:, j * 128:(j + 1) * 128],
                          [(wt[:, kc, wcol0 + c * 128:wcol0 + (c + 1) * 128], hT[:, kc, :]) for kc in range(8)])
                    yield
                if typ == 0:
                    xin = xinS_l[g % 2]
                    P.copy("pool", xin[:, :, :, 0:3], histS[:, 4 * g:4 * g + 4, :, :])
                    yield
                    P.copy("act", xin[:, :, :, 3:11], ps[:, :].rearrange("p (j s t) -> p j s t", j=4, s=16))
                    yield
                else:
                    xin = xinP_l[g % 2]
                    P.copy("pool", xin[:, :, 0:3], histP[:, 4 * g:4 * g + 4, :])
                    yield
                    P.copy("act", xin[:, :, 3:131], ps[:, :].rearrange("p (j t) -> p j t", j=4))
                    yield
                    P.copy("pool", histP[:, 4 * g:4 * g + 4, :], xin[:, :, 128:131])
                    yield
                for j in range(4):
                    c = 4 * g + j
                    a_ = accs[c % 4]
                    if typ == 0:
                        av = a_[:, :].rearrange("p (s t) -> p s t", s=16)
                        sh = [xin[:, j, :, k:k + 8] for k in range(4)]
                    else:
                        av = a_[:, :]
                        sh = [xin[:, j, k:k + 128] for k in range(4)]
                    items.append((av, a_, sh, dst_fn(c), c))
                    yield from slot(c)
            yield from slot(4 * ngroups)
            yield from slot(4 * ngroups + 1)

        def run_gen(g):
            for _ in g:
                pass

        w_in_v = w_in.rearrange("(kc p) n -> p kc n", p=128)
        if STOP_AFTER >= 1 and not SKIP1:
          with contextlib.ExitStack() as st1:
            wI = sb(st1, "wI", [128, 8, 2576], BF16)
            for kc in range(8):
                P.dma("pool", wI[:, kc, :], w_in_v[:, kc, 0:2576])
            modv = sb(st1, "modv", [128, 3, D])
            cw = sb(st1, "cw", [128, 12, 5])
            nw = sb(st1, "ssdnw", [128, D])
            histS = sb(st1, "histS", [128, 12, 16, 3])
            histP = sb(st1, "histP", [128, 12, 3])
            P.dma("sp", cw[:, :, :], ssd_cw[:, :, :])
            P.dma("sp", nw[:, :], ssd_nw[:, :])
            P.dma("sp", histS[:, :, :, :], hist_ssd[:, :, :, :])
            P.memset("pool", histP[:, :, :], 0.0)
            xt = [sb(st1, "xt%d" % i, [128, D]) for i in range(2)]
            tmpA = sb(st1, "tmpA", [128, D])
            hb = sb(st1, "hb", [128, D], BF16)
            hT = sb(st1, "hT", [128, 8, 128], BF16)
            stt_ = sb(st1, "stt", [128, 4])
            xinS_l = [sb(st1, "xinS%d" % i, [128, 4, 16, 11]) for i in range(2)]
            xinP_l = [sb(st1, "xinP%d" % i, [128, 4, 131]) for i in range(2)]
            acc = [sb(st1, "acc%d" % i, [128, 128]) for i in range(4)]
            xc = sb(st1, "xc", [128, 12, 128], BF16)
            zs = sb(st1, "zs", [128, D], BF16)
            sm = sb(st1, "sm", [128, 48])
            af = sb(st1, "af", [128, 16])
            ex = sb(st1, "ex", [128, 48])
            xtm = sb(st1, "xtm", [128, D], BF16)
            xdt = sb(st1, "xdt", [128, D], BF16)
            xDs = sb(st1, "xDs", [128, D], BF16)
            xw = sb(st1, "xw", [128, D], BF16)
            Btm = sb(st1, "Btm", [128, 2, 128], BF16)
            CBm = sb(st1, "CBm", [128, 2, 128])
            L4 = [sb(st1, "L4_%d" % i, [128, 4, 128]) for i in range(2)]
            E4 = [sb(st1, "E4_%d" % i, [128, 4, 128]) for i in range(2)]
            M4 = [sb(st1, "M4_%d" % i, [128, 4, 128], BF16) for i in range(2)]
            yo = sb(st1, "yo", [128, D])
            yy = sb(st1, "yy", [128, D])
            ss = sb(st1, "ss", [128, 8])
            ysb = [sb(st1, "ysb%d" % i, [128, D], BF16) for i in range(2)]
            hTf = sb(st1, "hTf", [128, D])
            hTb = sb(st1, "hTb", [128, D], BF16)
            lhs3 = sb(st1, "lhs3", [128, 8, 48], BF16)
            cv = sb(st1, "cv", [48, 1536])
            CmT = [sb(st1, "CmT%d" % g, [128, 16 * 128], BF16) for g in range(2)]
            h0 = [sb(st1, "h0_%d" % i, [128, 8, 128]) for i in range(2)]
            h0Tb = sb(st1, "h0Tb", [128, D], BF16)
            xwm = sb(st1, "xwm", [128, D], BF16)
            hn = [sb(st1, "hn%d" % i, [128, 8, 128]) for i in range(2)]
            rsel = sb(st1, "rsel", [128, 2, 128])
            decsel = sb(st1, "decsel", [128, 128])
            P.memset("pool", hTf[:, :], 0.0)
            P.memset("pool", hTb[:, :], 0.0)
            for g in range(2):
                P.memset("pool", CmT[g][:, :], 0.0)

            for ti in range(NT):
                typ = 0 if ti == 0 else 1
                mk = masks[typ]
                if ti <= 1:
                    P.dma("sp", modv[:, :, :], modscr[typ].rearrange("p (s d) -> p s d", s=6)[:, 0:3, :])
                x_ = xt[ti % 2]
                P.dma("sp", x_[:, :], xs_all[ti])
                norm_to_T(x_, modv, 1, 0, tmpA, hb, hT, stt_)
                run_gen(proj_conv_gen(typ, 3, wI, 1024, hT, PS[0:2], xinS_l, xinP_l, histS, histP, cw, True, acc,
                                      lambda c: xc[:, c, :]))
                P.sec("z (token-major) -> silu")
                for nb in range(2):
                    ps = PS[2 + nb]
                    P.mmk(ps[:, :], [(hT[:, kc, :], wI[:, kc, nb * 512:(nb + 1) * 512]) for kc in range(8)])
                    P.act(zs[:, nb * 512:(nb + 1) * 512], ps[:, :], AF.Silu)
                P.sec("dt")
                pd = PS[4]
                P.mmk(pd[:, 0:16], [(hT[:, kc, :], wI[:, kc, 2560:2576]) for kc in range(8)])
                P.tt("dve", sm[:, 0:16], pd[:, 0:16], smv[:, 0:16], ALU.add)
                P.act(sm[:, 16:32], sm[:, 0:16], AF.Exp)
                P.act(sm[:, 32:48], sm[:, 16:32], AF.Ln, bias=cst[:, 1:2], scale=1.0)
                P.tt("dve", af[:, :], sm[:, 32:48], negA[:, 0:16], ALU.mult)
                small_scan_terms(af[:, :], 16, typ, PS[4][:, 64:112], ex)
                P.sec("token-major x, xdt, xD, xw, B")
                P.trs([(PT[:, c * 128:(c + 1) * 128], xc[:, c, :]) for c in range(8)], identb[:, :])
                P.copy("act", xtm[:, :], PT[:, :])
                x3 = xtm[:, :].rearrange("p (h d) -> p h d", h=16)
                P.tt("dve", xdt[:, :].rearrange("p (h d) -> p h d", h=16), x3, bc(sm[:, 32:48], 2, 64), ALU.mult)
                P.tt("pool", xDs[:, :].rearrange("p (h d) -> p h d", h=16), x3, bc(smv[:, 32:48], 2, 64), ALU.mult)
                P.tt("pool", xw[:, :].rearrange("p (h d) -> p h d", h=16),
                     xdt[:, :].rearrange("p (h d) -> p h d", h=16), bc(ex[:, 16:32], 2, 64), ALU.mult)
                P.trs([(PT[:, g * 128:(g + 1) * 128], xc[:, 8 + g, :]) for g in range(2)], identb[:, :])
                P.copy("act", Btm[:, :, :], PT[:, 0:256].rearrange("p (g n) -> p g n", g=2))
                P.sec("CB^T masked")
                pc = PS[5]
                for g in range(2):
                    P.mm(pc[:, g * 128:(g + 1) * 128], xc[:, 8 + g, :], xc[:, 10 + g, :])
                P.tt("dve", CBm[:, :, :], pc[:, 0:256].rearrange("p (g l) -> p g l", g=2), bc(mk[:, MINC, :], 1, 2), ALU.mult)
                P.sec("y_off raw")
                if typ == 1:
                    for g in range(2):
                        P.mm(PS[2 + g][:, :], xc[:, 10 + g, :], hTb[:, g * 512:(g + 1) * 512])
                else:
                    for g in range(2):
                        base = CmT[g][:, :]
                        dst = bass.AP(base.tensor, base.offset, [[16 * 128, 128], [136, 16], [1, 8]])
                        P.op("pool", (lambda e, dst=dst, src=xc[:, 10 + g, :].rearrange("p (s t) -> p s t", s=16):
                                      e.tensor_copy(out=dst, in_=src)), ins=[xc[:, 0, :]], outs=[base])
                    for hh in range(2):
                        P.tt("pool", rsel[:, hh, :].rearrange("p (j s) -> p j s", j=8),
                             bc(af[:, hh:16:2], 2, 16), bc(blk16[:, :], 1, 8), ALU.mult)
                        P.mm(PS[5][:, 256 + hh * 128:256 + (hh + 1) * 128], onesf[:, :], rsel[:, hh, :])
                    P.act(decsel[0:64, :], PS[5][0:64, 256:384], AF.Exp)
                    P.act(decsel[64:128, :], PS[5][64:128, 384:512], AF.Exp)
                    for s in range(16):
                        h0_ = h0[s % 2]
                        hn_ = hn[s % 2]
                        P.dma("sp", h0_[:, :, :], st_ssd[s].rearrange("(j p) n -> p j n", p=128))
                        pa, pb = PS[0], PS[1]
                        P.trs([((pa if j < 4 else pb)[:, (j % 4) * 128:(j % 4 + 1) * 128], h0_[:, j, :]) for j in range(8)],
                              identf[:, :])
                        P.copy("act", h0Tb[:, 0:512], pa[:, :])
                        P.copy("dve", h0Tb[:, 512:1024], pb[:, :])
                        for g in range(2):
                            P.op("pe", (lambda e, o=PS[2 + g][:, :], l=CmT[g][:, s * 128:(s + 1) * 128],
                                        r=h0Tb[:, g * 512:(g + 1) * 512], s=s:
                                        e.matmul(o, lhsT=l, rhs=r, start=(s == 0), stop=(s == 15))),
                                 ins=[CmT[g][:, :], h0Tb[:, :]], outs=[PS[2 + g][:, :]])
                        P.act(xwm[:, :], xw[:, :], AF.Identity, scale=blk16[:, s:s + 1])
                        for half in range(2):
                            pn = PS[half]
                            for jj in range(4):
                                j = half * 4 + jj
                                P.mm(pn[:, jj * 128:(jj + 1) * 128], xwm[:, j * 128:(j + 1) * 128], Btm[:, j // 4, :])
                            for jj in range(4):
                                j = half * 4 + jj
                                P.stt("dve", hn_[:, j, :], h0_[:, j, :], decsel[:, j * 16 + s:j * 16 + s + 1],
                                      pn[:, jj * 128:(jj + 1) * 128], ALU.mult, ALU.add)
                        P.dma("sp", o_ssd_s[s].rearrange("(j p) n -> p j n", p=128), hn_[:, :, :])
                P.sec("per-head intra-chunk")
                for q in range(4):
                    g = q // 2
                    L_, E_, M_ = L4[q % 2], E4[q % 2], M4[q % 2]
                    P.tt("dve", L_[:, :, :], bc(mk[:, MSTT, :], 1, 4), bc(af[:, 4 * q:4 * q + 4], 2, 128), ALU.mult)
                    pg = PS[5 + (q % 2)]
                    for j in range(4):
                        P.mm(pg[:, j * 128:(j + 1) * 128], L_[:, j, :], mk[:, MINC, :])
                    P.act(E_[:, :, :], pg[:, :].rearrange("p (j l) -> p j l", j=4), AF.Exp)
                    P.tt("dve", M_[:, :, :], E_[:, :, :], bc(CBm[:, g, :], 1, 4), ALU.mult)
                    py = PS[0 + g]
                    for j in range(4):
                        h = 4 * q + j
                        hh = h % 8
                        P.mmk(py[:, hh * 64:(hh + 1) * 64],
                              [(identb[:, :], xDs[:, h * 64:(h + 1) * 64]), (M_[:, j, :], xdt[:, h * 64:(h + 1) * 64])])
                P.sec("combine y = y_diag + eacs * y_off")
                for g in range(2):
                    P.copy("act", yo[:, g * 512:(g + 1) * 512], PS[2 + g][:, :])
                P.tt("dve", yo[:, :].rearrange("p (h d) -> p h d", h=16), yo[:, :].rearrange("p (h d) -> p h d", h=16),
                     bc(ex[:, 0:16], 2, 64), ALU.mult)
                for g in range(2):
                    P.tt("dve", yy[:, g * 512:(g + 1) * 512], yo[:, g * 512:(g + 1) * 512], PS[0 + g][:, :], ALU.add)
                P.sec("gate with silu(z), group rmsnorm")
                P.tt("dve", yy[:, :], yy[:, :], zs[:, :], ALU.mult)
                for g in range(2):
                    P.sqsum(yo[:, g * 512:(g + 1) * 512], yy[:, g * 512:(g + 1) * 512], ss[:, g:g + 1])
                P.act(ss[:, 2:4], ss[:, 0:2], AF.Sqrt, bias=cst[:, 0:1], scale=1.0 / 512)
                P.recip(ss[:, 4:6], ss[:, 2:4])
                y_ = ysb[ti % 2]
                for g in range(2):
                    P.stt("dve", y_[:, g * 512:(g + 1) * 512], yy[:, g * 512:(g + 1) * 512], ss[:, 4 + g:5 + g],
                          nw[:, g * 512:(g + 1) * 512], ALU.mult, ALU.mult)
                P.dma("sp", yssd_scr[ti], y_[:, :])
                P.sec("state update (prompt chunks)")
                if typ == 1:
                    for g in range(2):
                        P.mm(PS[2 + g][:, :], Btm[:, g, :], xw[:, g * 512:(g + 1) * 512])
                    h3 = hTf[:, :].rearrange("p (h d) -> p h d", h=16)
                    P.tt("dve", h3, h3, bc(ex[:, 32:48], 2, 64), ALU.mult)
                    for g in range(2):
                        P.tt("dve", hTf[:, g * 512:(g + 1) * 512], hTf[:, g * 512:(g + 1) * 512], PS[2 + g][:, :], ALU.add)
                    P.copy("act", hTb[:, :], hTf[:, :])
                P.sec("conv-state outputs (last 3 raw xbc rows)")
                if ti == 0 or ti == NT - 1:
                    M3 = 48 if typ == 0 else 3
                    if typ == 0:
                        P.copy("pool", lhs3[:, :, :].rearrange("p k (s t) -> p k s t", s=16),
                               hT[:, :, :].rearrange("p k (s t) -> p k s t", s=16)[:, :, :, 5:8])
                    else:
                        P.copy("pool", lhs3[:, :, 0:3], hT[:, :, 125:128])
                    for nb in range(3):
                        ps = PS[5 + (nb % 2)]
                        P.mmk(ps[0:M3, :], [(lhs3[:, kc, 0:M3], wI[:, kc, 1024 + nb * 512:1024 + (nb + 1) * 512])
                                            for kc in range(8)])
                        P.copy("act", cv[0:M3, nb * 512:(nb + 1) * 512], ps[0:M3, :])
                    P.dma("sp", (o_ssdc_s if typ == 0 else o_ssdc_p)[:, :], cv[0:M3, :])
            for half in range(2):
                ps = PS[half]
                P.trs([(ps[:, jj * 128:(jj + 1) * 128], hTf[:, (half * 4 + jj) * 128:(half * 4 + jj + 1) * 128])
                       for jj in range(4)], identf[:, :])
                P.copy("act", hn[half][:, 0:4, :], ps[:, :].rearrange("p (j n) -> p j n", j=4))
                P.dma("sp", o_ssd_p[half * 512:(half + 1) * 512, :].rearrange("(j p) n -> p j n", p=128), hn[half][:, 0:4, :])
            P.barrier()
            P.emit()
        ps1.close()
        if STOP_AFTER >= 2:
          with contextlib.ExitStack() as st2:
            PA = [st2.enter_context(nc.psum_tensor("pa%d" % i, [128, 512], F32)) for i in range(2)]
            PM = st2.enter_context(nc.psum_tensor("pm", [128, 512], F32))
            LPB = [st2.enter_context(nc.psum_tensor("lpb%d" % i, [128, 512], F32)) for i in range(4)]
            LPS = [[LPB[ln][:, i * 128:(i + 1) * 128] for i in range(4)] for ln in range(4)]
            wII = sb(st2, "wII", [128, 8, 4112], BF16)
            for kc in range(8):
                P.dma("pool", wII[:, kc, :], w_in_v[:, kc, 2576:6688])
            wO = sb(st2, "wO", [128, 16, 1024], BF16)
            w_out_v = w_out.rearrange("(kc p) n -> p kc n", p=128)
            for kc in range(16):
                P.dma("pool", wO[:, kc, :], w_out_v[:, kc, :])
            modv = sb(st2, "modv2", [128, 3, D])
            cwg = sb(st2, "cwg", [128, 24, 4])
            gnw = sb(st2, "gnw", [128, 128])
            histP = sb(st2, "histPg", [128, 24, 3])
            P.dma("sp", cwg[:, :, :], gdn_cw[:, :, :])
            P.dma("sp", gnw[:, :], gdn_nw[:, :])
            P.memset("pool", histP[:, :, :], 0.0)
            tmpA = sb(st2, "tmpA2", [128, D])
            hT = sb(st2, "hT2", [128, 8, 128], BF16)
            stt_ = sb(st2, "stt2", [128, 4])
            xinP_l = [sb(st2, "xinP2_%d" % i, [128, 4, 131]) for i in range(2)]
            acc = [sb(st2, "acc2_%d" % i, [128, 128]) for i in range(4)]
            lhs3 = sb(st2, "lhs3g", [128, 8, 48], BF16)
            tmpT = sb(st2, "tmpT2", [128, D])
            sttT = sb(st2, "sttT2", [128, 4])
            ss8 = sb(st2, "ss8", [128, 24])
            otm = sb(st2, "otm", [128, D])
            mixed = sb(st2, "mixed", [128, 2 * D], BF16)
            mixedT = sb(st2, "mixedT", [128, 16, 128], BF16)

            class FB:
                pass

            def make_fb(i, stk):
                fb = FB()
                fb.xt = sb(stk, "f%d_xt" % i, [128, D])
                fb.hb = sb(stk, "f%d_hb" % i, [128, D], BF16)
                fb.qk = sb(stk, "f%d_qk" % i, [128, 16, 128], BF16)
                fb.vfm = sb(stk, "f%d_vfm" % i, [128, 8, 128], BF16)
                fb.ktm = sb(stk, "f%d_ktm" % i, [128, 8, 128], BF16)
                fb.vtm = sb(stk, "f%d_vtm" % i, [128, 8, 128], BF16)
                fb.gs = sb(stk, "f%d_gs" % i, [128, D], BF16)
                fb.sm = sb(stk, "f%d_sm" % i, [128, 64])
                fb.gf = sb(stk, "f%d_gf" % i, [128, 8])
                fb.ex = sb(stk, "f%d_ex" % i, [128, 24])
                return fb

            class Lane:
                pass

            lanes = []

            def make_lane(ln, stk):
                B = Lane()
                B.ps = LPS[ln]
                f = lambda nm, dt=F32: sb(stk, "ln%d_%s" % (ln, nm), [128, 128], dt)
                B.P = [f("P0"), f("P1")]
                B.PT = [f("PT0"), f("PT1")]
                B.X = [f("X0"), f("X1")]
                B.ot = f("ot")
                B.L, B.Dm, B.DmI, B.DmS = B.ot, B.X[1], B.PT[1], B.P[1]
                B.attnT, B.TTb, B.R, B.vn, B.kout = (f("attnT", BF16), f("TTb", BF16), f("R", BF16),
                                                     f("vn", BF16), f("kout", BF16))
                lanes.append(B)

            make_lane(0, st2)
            fbs = [make_fb(0, st2)]

            def front_gen(ti, typ, SB, fb):
                xt, hb, qk, vfm, sm, gf, ex = fb.xt, fb.hb, fb.qk, fb.vfm, fb.sm, fb.gf, fb.ex
                P.sec("F:load+norm")
                if ti <= 1:
                    P.dma("sp", modv[:, :, :], modscr[typ].rearrange("p (s d) -> p s d", s=6)[:, 0:3, :])
                    yield
                P.dma("sp", xt[:, :], xs_all[ti])
                yield
                yield from norm_to_T_g(xt, modv, 1, 0, tmpA, hb, hT, stt_)
                yield
                yield from proj_conv_gen(typ, 6, wII, 0, hT, PA, (SB.xinS_l if typ == 0 else None), xinP_l,
                                         (SB.histS if typ == 0 else None), histP, cwg, False, acc,
                                         lambda c: (qk[:, c, :] if c < 16 else vfm[:, c - 16, :]))
                if ti == 0 or ti == NT - 1:
                    P.sec("F:convstate")
                    M3 = 48 if typ == 0 else 3
                    if typ == 0:
                        P.copy("pool", lhs3[:, :, :].rearrange("p k (s t) -> p k s t", s=16),
                               hT[:, :, :].rearrange("p k (s t) -> p k s t", s=16)[:, :, :, 5:8])
                        yield
                    else:
                        P.copy("pool", lhs3[:, :, 0:3], hT[:, :, 125:128])
                        yield
                    for cc in range(3):
                        for nb in range(2):
                            c0 = cc * 1024 + nb * 512
                            P.mmk(PA[nb][0:M3, :], [(lhs3[:, kc, 0:M3], wII[:, kc, c0:c0 + 512]) for kc in range(8)])
                            yield
                            P.copy("act", tmpA[0:M3, nb * 512:(nb + 1) * 512], PA[nb][0:M3, :])
                            yield
                        P.dma("sp", (o_gdnc_s if typ == 0 else o_gdnc_p)[:, cc * 1024:(cc + 1) * 1024], tmpA[0:M3, :])
                        yield
                    yield
                P.sec("F:gate")
                for nb in range(2):
                    P.mmk(PA[nb][:, :], [(hT[:, kc, :], wII[:, kc, 3072 + nb * 512:3072 + (nb + 1) * 512])
                                         for kc in range(8)])
                    yield
                    P.act(fb.gs[:, nb * 512:(nb + 1) * 512], PA[nb][:, :], AF.Silu)
                    yield
                yield
                P.sec("F:beta/g")
                P.mmk(PM[:, 0:16], [(hT[:, kc, :], wII[:, kc, 4096:4112]) for kc in range(8)])
                yield
                P.act(sm[:, 0:8], PM[:, 0:8], AF.Exp, scale=-1.0)
                yield
                P.ts("dve", sm[:, 0:8], sm[:, 0:8], 1.0, None, ALU.add)
                yield
                P.recip(sm[:, 8:16], sm[:, 0:8])
                yield
                P.ts("dve", sm[:, 16:24], sm[:, 8:16], -1.0, None, ALU.mult)
                yield
                P.tt("dve", sm[:, 24:32], PM[:, 8:16], smv[:, 48:56], ALU.add)
                yield
                P.act(sm[:, 32:40], sm[:, 24:32], AF.Exp)
                yield
                P.act(sm[:, 40:48], sm[:, 32:40], AF.Ln, bias=cst[:, 1:2], scale=1.0)
                yield
                P.tt("dve", gf[:, :], sm[:, 40:48], negA[:, 16:24], ALU.mult)
                yield
                yield
                P.sec("F:beta/g")
                yield from small_scan_terms_g(gf[:, :], 8, typ, PM[:, 64:88], ex)
                P.ts("dve", sm[:, 48:56], ex[:, 0:8], -1.0, None, ALU.mult)
                yield
                yield
                for half in range(2):
                    P.sec("F:l2norm")
                    src = qk[:, half * 8:(half + 1) * 8, :]
                    P.tt("pool", hb[:, :].rearrange("p (c t) -> p c t", c=8), src, src, ALU.mult)
                    yield
                    for i in range(2):
                        rq = tmpA[:, i * 512:(i + 1) * 512]
                        P.mm(PA[i][:, :], onesb[:, :], hb[:, i * 512:(i + 1) * 512])
                        yield
                        if half == 0:
                            P.act(rq, PA[i][:, :], AF.Sqrt, bias=cst[:, 2:3], scale=128.0)
                            yield
                        else:
                            P.act(rq, PA[i][:, :], AF.Sqrt, bias=cst[:, 0:1], scale=1.0)
                            yield
                        P.recip(rq, rq)
                        yield
                        dst = qk[:, half * 8 + 4 * i:half * 8 + 4 * i + 4, :]
                        P.tt("dve", dst, dst, rq.rearrange("p (c t) -> p c t", c=4), ALU.mult)
                        yield
                    yield
                P.sec("F:transposes")
                P.copy("pool", hb[:, :].rearrange("p (c t) -> p c t", c=8), qk[:, 8:16, :])
                yield
                P.trs([(PT[:, j * 128:(j + 1) * 128], qk[:, 8 + j, :]) for j in range(8)], identb[:, :])
                yield
                P.copy("act", fb.ktm[:, :, :], PT[:, :].rearrange("p (h d) -> p h d", h=8))
                yield
                yield
                P.sec("F:transposes")
                P.trs([(PT[:, j * 128:(j + 1) * 128], vfm[:, j, :]) for j in range(8)], identb[:, :])
                yield
                P.copy("act", fb.vtm[:, :, :], PT[:, :].rearrange("p (h d) -> p h d", h=8))
                yield
                if typ == 0:
                    P.tt("pool", SB.rselg[:, :].rearrange("p (h s) -> p h s", h=8),
                         bc(gf[:, :], 2, 16), bc(blk16[:, :], 1, 8), ALU.mult)
                    yield
                    P.mm(PM[:, 128:256], onesf[:, :], SB.rselg[:, :])
                    yield
                    P.act(SB.gendsel[:, :], PM[:, 128:256], AF.Exp)
                    yield
                yield

            def head_gen(h, B, typ, SB, fb):
                mk = masks[typ]
                m_lev = 3 if typ == 0 else 7
                qk, hb, sm, gf, ex, ktm, vtm = fb.qk, fb.hb, fb.sm, fb.gf, fb.ex, fb.ktm, fb.vtm
                kT = qk[:, 8 + h, :]
                qT = qk[:, h, :]
                _st = ["H:decay"]
                P.sec(_st[0])
                P.act(B.L[:, :], mk[:, MSTT, :], AF.Identity, scale=gf[:, h:h + 1])
                yield
                P.mm(B.ps[3][:, :], B.L[:, :], mk[:, MINC, :])
                yield
                P.act(B.Dm[:, :], B.ps[3][:, :], AF.Exp)
                yield
                P.tt("dve", B.DmI[:, :], B.Dm[:, :], mk[:, MINC, :], ALU.mult)
                yield
                P.tt("dve", B.DmS[:, :], B.Dm[:, :], mk[:, MSTR, :], ALU.mult)
                yield
                P.mm(B.ps[0][:, :], kT, hb[:, h * 128:(h + 1) * 128])
                yield
                P.mm(B.ps[1][:, :], kT, qT)
                yield
                P.stt("dve", B.P[0][:, :], B.ps[0][:, :], sm[:, 16 + h:17 + h], B.DmS[:, :], ALU.mult, ALU.mult)
                yield
                P.tt("dve", B.attnT[:, :], B.ps[1][:, :], B.DmI[:, :], ALU.mult)
                yield
                yield
                _st[0] = "H:dbl"
                P.sec(_st[0])
                P.tr(B.ps[2][:, :], B.P[0][:, :], identf[:, :])
                yield
                P.copy("act", B.PT[0][:, :], B.ps[2][:, :])
                yield
                P.tt("dve", B.X[0][:, :], B.P[0][:, :], identf[:, :], ALU.add)
                yield
                yield
                P.sec(_st[0])
                if m_lev > 1:
                    P.mm(B.ps[0][:, :], B.P[0][:, :], B.PT[0][:, :])
                    yield
                    if m_lev > 2:
                        P.mm(B.ps[1][:, :], B.PT[0][:, :], B.P[0][:, :])
                        yield
                    P.copy("act", B.PT[1][:, :], B.ps[0][:, :])
                    yield
                    if m_lev > 2:
                        P.copy("act", B.P[1][:, :], B.ps[1][:, :])
                        yield
                    yield
                    P.sec(_st[0])
                xi = 0
                for j in range(1, m_lev):
                    cur, nx = j % 2, (j + 1) % 2
                    if j + 1 < m_lev:
                        P.mm(B.ps[0][:, :], B.P[cur][:, :], B.PT[cur][:, :])
                        yield
                        if j + 2 < m_lev:
                            P.mm(B.ps[1][:, :], B.PT[cur][:, :], B.P[cur][:, :])
                            yield
                    P.mm(B.ps[2][:, :], B.PT[cur][:, :], B.X[xi][:, :])
                    yield
                    if j + 1 < m_lev:
                        P.copy("act", B.PT[nx][:, :], B.ps[0][:, :])
                        yield
                        if j + 2 < m_lev:
                            P.copy("act", B.P[nx][:, :], B.ps[1][:, :])
                            yield
                    P.tt("dve", B.X[1 - xi][:, :], B.X[xi][:, :], B.ps[2][:, :], ALU.add)
                    yield
                    xi = 1 - xi
                    yield
                    P.sec(_st[0])
                _st[0] = "H:state"
                P.sec(_st[0])
                P.copy("act", B.TTb[:, :], B.X[xi][:, :])
                yield
                if typ == 1:
                    Sf_h, Sb_h = SB.Sf[h], SB.Sb[h]
                    P.mm(B.ps[3][:, :], kT, Sb_h[:, :])
                    yield
                else:
                    P.dma("sp", SB.S0f[:, :, :], st_gdn[:, h, :, :].rearrange("s d e -> d s e"))
                    yield
                    P.copy("act", SB.S0b[:, :, :], SB.S0f[:, :, :])
                    yield
                    for (dstT, srcT) in ((SB.kTm, kT), (SB.qTm, qT)):
                        base = dstT[:, :]
                        dap = bass.AP(base.tensor, base.offset, [[16 * 128, 128], [136, 16], [1, 8]])
                        P.op("pool", (lambda e, dap=dap, src=srcT.rearrange("p (s t) -> p s t", s=16):
                                      e.tensor_copy(out=dap, in_=src)), ins=[srcT], outs=[base])
                        yield
                    P.mmk(B.ps[3][:, :], [(SB.kTm[:, s * 128:(s + 1) * 128], SB.S0b[:, s, :]) for s in range(16)])
                    yield
                P.stt("dve", B.R[:, :], B.ps[3][:, :], sm[:, 48 + h:49 + h], vtm[:, h, :], ALU.mult, ALU.add)
                yield
                P.mm(B.ps[0][:, :], B.TTb[:, :], B.R[:, :])
                yield
                P.act(B.vn[:, :], B.ps[0][:, :], AF.Identity, scale=sm[:, 8 + h:9 + h])
                yield
                yield
                P.sec(_st[0])
                if typ == 1:
                    P.mm(B.ps[1][:, :], qT, Sb_h[:, :])
                    yield
                else:
                    P.mmk(B.ps[1][:, :], [(SB.qTm[:, s * 128:(s + 1) * 128], SB.S0b[:, s, :]) for s in range(16)])
                    yield
                P.mm(B.ps[2][:, :], B.attnT[:, :], B.vn[:, :])
                yield
                P.act(B.ot[:, :], B.ps[1][:, :], AF.Identity, scale=ex[:, h:h + 1])
                yield
                P.tt("dve", otm[:, h * 128:(h + 1) * 128], B.ot[:, :], B.ps[2][:, :], ALU.add)
                yield
                P.act(B.kout[:, :], ktm[:, h, :], AF.Identity, scale=ex[:, 8 + h:9 + h])
                yield
                yield
                P.sec(_st[0])
                if typ == 1:
                    P.mm(B.ps[3][:, :], B.kout[:, :], B.vn[:, :])
                    yield
                    P.stt("dve", Sf_h[:, :], Sf_h[:, :], ex[:, 16 + h:17 + h], B.ps[3][:, :], ALU.mult, ALU.add)
                    yield
                    P.copy("act", Sb_h[:, :], Sf_h[:, :])
                    yield
                else:
                    P.tt("dve", SB.koutm_all[:, :, :], bc(B.kout[:, :], 1, 16), bc(blk16[:, :], 2, 128), ALU.mult)
                    yield
                    for s in range(16):
                        P.mm(LPB[s // 4][:, (s % 4) * 128:(s % 4 + 1) * 128], SB.koutm_all[:, s, :], B.vn[:, :])
                    yield
                    for j4 in range(4):
                        S4 = SB.S0f[:, 4 * j4:4 * j4 + 4, :]
                        gsel = SB.gendsel[:, h * 16 + 4 * j4:h * 16 + 4 * j4 + 4]
                        P.tt("dve", S4, S4, bc(gsel, 2, 128), ALU.mult)
                        yield
                        P.tt("dve", S4, S4, LPB[j4][:, :].rearrange("p (s e) -> p s e", s=4), ALU.add)
                        yield
                    P.dma("sp", o_gdn_s[:, h, :, :].rearrange("s d e -> d s e"), SB.S0f[:, :, :])
                    yield
                yield

            def tail_gen(ti, fb):
                xt = fb.xt
                P.sec("T:onorm")
                P.dma("sp", mixed[:, 0:D], yssd_scr[ti])
                yield
                P.tt("pool", tmpT[:, :], otm[:, :], otm[:, :], ALU.mult)
                yield
                P.red("dve", ss8[:, 0:8], tmpT[:, :].rearrange("p (h d) -> p h d", h=8))
                yield
                P.act(ss8[:, 8:16], ss8[:, 0:8], AF.Sqrt, bias=cst[:, 0:1], scale=1.0 / 128)
                yield
                P.recip(ss8[:, 16:24], ss8[:, 8:16])
                yield
                o3 = otm[:, :].rearrange("p (h d) -> p h d", h=8)
                P.tt("dve", o3, o3, bc(ss8[:, 16:24], 2, 128), ALU.mult)
                yield
                P.tt("dve", o3, o3, bc(gnw[:, :], 1, 8), ALU.mult)
                yield
                P.tt("dve", mixed[:, D:2 * D], otm[:, :], fb.gs[:, :], ALU.mult)
                yield
                yield
                P.sec("T:outproj")
                for half in range(2):
                    P.trs([(PT[:, j * 128:(j + 1) * 128], mixed[:, (half * 8 + j) * 128:(half * 8 + j + 1) * 128])
                           for j in range(8)], identb[:, :])
                    yield
                    P.copy("act", mixedT[:, half * 8:(half + 1) * 8, :], PT[:, :].rearrange("p (c t) -> p c t", c=8))
                    yield
                yield
                P.sec("T:outproj")
                for nb in range(2):
                    P.mmk(PA[nb][:, :], [(mixedT[:, kc, :], wO[:, kc, nb * 512:(nb + 1) * 512]) for kc in range(16)])
                    yield
                    P.copy("act", tmpT[:, nb * 512:(nb + 1) * 512], PA[nb][:, :])
                    yield
                yield
                P.sec("T:resid")
                P.sqsum(otm[:, :], tmpT[:, :], sttT[:, 0:1])
                yield
                P.act(sttT[:, 1:2], sttT[:, 0:1], AF.Sqrt, bias=cst[:, 0:1], scale=1.0 / D)
                yield
                P.recip(sttT[:, 2:3], sttT[:, 1:2])
                yield
                P.stt("dve", tmpT[:, :], tmpT[:, :], sttT[:, 2:3], modv[:, 2, :], ALU.mult, ALU.mult)
                yield
                P.tt("dve", xt[:, :], tmpT[:, :], xt[:, :], ALU.add)
                yield
                P.dma("sp", x1_scr[ti], xt[:, :])
                yield
                yield

            def speed(g, k):
                while True:
                    for _ in range(k):
                        try:
                            next(g)
                        except StopIteration:
                            return
                    yield

            def run_rr(gens):
                alive = list(gens)
                while alive:
                    nxt = []
                    for gn in alive:
                        try:
                            next(gn)
                            nxt.append(gn)
                        except StopIteration:
                            pass
                    alive = nxt

            def back(ti, typ, SB, fb, nl, side, with_tail=True):
                side = [side] if side is not None else []
                for h0_ in range(0, 8, nl):
                    gens = [head_gen(h0_ + i, lanes[i], typ, SB, fb) for i in range(min(nl, 8 - h0_))]
                    alive = gens + side
                    while any(g in alive for g in gens):
                        nxt = []
                        for gn in alive:
                            try:
                                next(gn)
                                nxt.append(gn)
                            except StopIteration:
                                if gn in side:
                                    side = []
                        alive = nxt
                if with_tail:
                    run_rr([tail_gen(ti, fb)] + side)
                else:
                    run_rr(side)

            class SBufs:
                pass

            with contextlib.ExitStack() as st2s:
                SB = SBufs()
                SB.histS = sb(st2s, "histSg", [128, 24, 16, 3])
                SB.xinS_l = [sb(st2s, "xinS2_%d" % i, [128, 4, 16, 11]) for i in range(2)]
                SB.S0f = sb(st2s, "S0f", [128, 16, 128])
                SB.S0b = sb(st2s, "S0b", [128, 16, 128], BF16)
                SB.kTm = sb(st2s, "kTm", [128, 16 * 128], BF16)
                SB.qTm = sb(st2s, "qTm", [128, 16 * 128], BF16)
                SB.koutm_all = sb(st2s, "koutm_all", [128, 16, 128], BF16)
                SB.rselg = sb(st2s, "rselg", [128, 128])
                SB.gendsel = sb(st2s, "gendsel", [128, 128])
                P.dma("sp", SB.histS[:, :, :, :], hist_gdn[:, :, :, :])
                P.memset("pool", SB.kTm[:, :], 0.0)
                P.memset("pool", SB.qTm[:, :], 0.0)
                run_rr([front_gen(0, 0, SB, fbs[0])])
                back(0, 0, SB, fbs[0], 1, None)
                P.barrier()
                P.emit()
            with contextlib.ExitStack() as st2p:
                SB = SBufs()
                SB.Sf = [sb(st2p, "Sf%d" % h, [128, 128]) for h in range(8)]
                SB.Sb = [sb(st2p, "Sb%d" % h, [128, 128], BF16) for h in range(8)]
                for ln in range(1, P2_NL):
                    make_lane(ln, st2p)
                fbs.append(make_fb(1, st2p))
                for h in range(8):
                    P.memset("pool", SB.Sf[h][:, :], 0.0)
                    P.memset("pool", SB.Sb[h][:, :], 0.0)
                run_rr([front_gen(1, 1, SB, fbs[1])])
                pending_tail = None
                for ti in range(1, NT):
                    parts = []
                    if pending_tail is not None:
                        parts.append(pending_tail)
                    if ti + 1 < NT:
                        parts.append(speed(front_gen(ti + 1, 1, SB, fbs[(ti + 1) % 2]), 2))
                    side = itertools.chain(*parts) if parts else None
                    back(ti, 1, SB, fbs[ti % 2], P2_NL, side, with_tail=False)
                    pending_tail = tail_gen(ti, fbs[ti % 2])
                run_rr([pending_tail])
                for h in range(8):
                    P.dma("sp", o_gdn_p[h * 128:(h + 1) * 128, :], SB.Sf[h][:, :])
                P.barrier()
                P.emit()
        if STOP_AFTER >= 3:
          with contextlib.ExitStack() as st3:
            PB = [st3.enter_context(nc.psum_tensor("pb%d" % i, [128, 512], F32)) for i in range(7)]
            wG = sb(st3, "wG", [128, 8, 2 * DFF], BF16)
            w_gu_v = w_gu.rearrange("(kc p) n -> p kc n", p=128)
            for kc in range(8):
                P.dma("pool", wG[:, kc, :], w_gu_v[:, kc, :])
            wD = sb(st3, "wD", [128, 22, D], BF16)
            w_dn_v = w_dn.rearrange("(kc p) n -> p kc n", p=128)
            for kc in range(22):
                P.dma("pool", wD[:, kc, :], w_dn_v[:, kc, :])
            modv = sb(st3, "modv3", [128, 3, D])
            xt = [sb(st3, "xt3_%d" % i, [128, D]) for i in range(2)]
            tmpA = [sb(st3, "tmpA3_%d" % i, [128, D]) for i in range(2)]
            tmpB = sb(st3, "tmpB3", [128, D])
            hb = [sb(st3, "hb3_%d" % i, [128, D], BF16) for i in range(2)]
            hT = [sb(st3, "hT3_%d" % i, [128, 8, 128], BF16) for i in range(2)]
            stt_ = [sb(st3, "stt3_%d" % i, [128, 4]) for i in range(2)]
            sg = [sb(st3, "sg%d" % i, [128, 512]) for i in range(2)]
            hid = sb(st3, "hid", [128, DFF], BF16)
            hidT = sb(st3, "hidT", [128, 22, 128], BF16)

            def ffn_gen(ti):
                typ = 0 if ti == 0 else 1
                pr = ti % 2
                x_, tA, hb_, hT_, st_ = xt[pr], tmpA[pr], hb[pr], hT[pr], stt_[pr]
                if ti <= 1:
                    P.dma("sp", modv[:, :, :], modscr[typ].rearrange("p (s d) -> p s d", s=6)[:, 3:6, :])
                P.dma("sp", x_[:, :], x1_scr[ti])
                yield
                yield from norm_to_T_g(x_, modv, 1, 0, tA, hb_, hT_, st_)
                for i in range(6):
                    w = 512 if i < 5 else 256
                    pg_, pu_ = PB[(2 * i) % 6], PB[(2 * i + 1) % 6]
                    P.mmk(pg_[:, 0:w], [(hT_[:, kc, :], wG[:, kc, i * 512:i * 512 + w]) for kc in range(8)])
                    yield
                    P.mmk(pu_[:, 0:w], [(hT_[:, kc, :], wG[:, kc, DFF + i * 512:DFF + i * 512 + w]) for kc in range(8)])
                    yield
                    s_ = sg[i % 2]
                    P.act(s_[:, 0:w], pg_[:, 0:w], AF.Silu)
                    yield
                    P.tt("dve", hid[:, i * 512:i * 512 + w], s_[:, 0:w], pu_[:, 0:w], ALU.mult)
                    yield
                yield "HALF"
                for grp in range(3):
                    n = 8 if grp < 2 else 6
                    P.trs([(PT[:, j * 128:(j + 1) * 128], hid[:, (grp * 8 + j) * 128:(grp * 8 + j + 1) * 128])
                           for j in range(n)], identb[:, :])
                    yield
                    P.copy("act", hidT[:, grp * 8:grp * 8 + n, :],
                           PT[:, 0:n * 128].rearrange("p (c t) -> p c t", c=n))
                    yield
                for nb in range(2):
                    P.mmk(PB[6][:, :], [(hidT[:, kc, :], wD[:, kc, nb * 512:(nb + 1) * 512]) for kc in range(22)])
                    yield
                    P.copy("act", tA[:, nb * 512:(nb + 1) * 512], PB[6][:, :])
                    yield
                P.sqsum(tmpB[:, :], tA[:, :], st_[:, 0:1])
                yield
                P.act(st_[:, 1:2], st_[:, 0:1], AF.Sqrt, bias=cst[:, 0:1], scale=1.0 / D)
                yield
                P.recip(st_[:, 2:3], st_[:, 1:2])
                yield
                P.stt("dve", tA[:, :], tA[:, :], st_[:, 2:3], modv[:, 2, :], ALU.mult, ALU.mult)
                yield
                P.tt("dve", x_[:, :], tA[:, :], x_[:, :], ALU.add)
                yield
                P.dma("sp", y_all[ti], x_[:, :])
                yield

            run_gen(ffn_gen(0))
            nxt = 2
            cur = ffn_gen(1)
            young = None
            while cur is not None:
                try:
                    v = next(cur)
                    if v == "HALF" and young is None and nxt < NT:
                        young = ffn_gen(nxt)
                        nxt += 1
                except StopIteration:
                    cur, young = young, None
                    if cur is None and nxt < NT:
                        cur = ffn_gen(nxt)
                        nxt += 1
                    continue
                if young is not None:
                    try:
                        v2 = next(young)
                        if v2 == "HALF":
                            pass
                    except StopIteration:
                        young = None
            P.barrier()
            P.emit()
    return nc


def _prep_inputs(inp):
    f = lambda a: np.ascontiguousarray(np.asarray(a, dtype=np.float32))
    xp, xs = f(inp["x_prompt"]), f(inp["x_sample"])
    cp, cs = f(inp["c_prompt"]), f(inp["c_sample"])
    bcast = lambda v: np.ascontiguousarray(np.broadcast_to(np.asarray(v, np.float32).reshape(1, -1), (128, v.size)))
    shared = {}
    shared["w_ada"] = f(inp["w_ada"][0])
    shared["b_ada_b"] = bcast(inp["b_ada"][0])
    shared["normvecs"] = np.ascontiguousarray(np.stack(
        [bcast(inp[k][0]) for k in ("norm_mix_pre", "norm_mix_post", "norm_ffn_pre", "norm_ffn_post")], axis=1))
    shared["w_in"] = f(inp["w_in"][0])
    shared["w_out"] = f(inp["w_out"][0])
    shared["w_gu"] = f(inp["w_gate_up"][0])
    shared["w_dn"] = f(inp["w_down"][0])
    scw = np.concatenate([f(inp["ssd_conv_w"][0]), f(inp["ssd_conv_b"][0])[None, :]], axis=0)
    shared["ssd_cw"] = np.ascontiguousarray(scw.reshape(5, 12, 128).transpose(2, 1, 0))
    shared["gdn_cw"] = np.ascontiguousarray(f(inp["gdn_conv_w"][0]).reshape(4, 24, 128).transpose(2, 1, 0))
    sv = np.concatenate([f(inp["ssd_dt_bias"][0]), f(inp["ssd_A_log"][0]), f(inp["ssd_D"][0]),
                         f(inp["gdn_dt_bias"][0]), f(inp["gdn_A_log"][0])])
    shared["smallv"] = bcast(sv)
    shared["ssd_nw"] = bcast(inp["ssd_norm_w"][0])
    shared["gdn_nw"] = bcast(inp["gdn_norm_w"][0])
    shared["ident"] = np.eye(128, dtype=np.float32)
    idx = np.arange(128)
    m = np.zeros((2, 128, 4, 128), np.float32)
    for typ, bs in ((0, 8), (1, 128)):
        same = (idx[:, None] // bs) == (idx[None, :] // bs)
        m[typ, :, 0, :] = same & (idx[:, None] > idx[None, :])
        m[typ, :, 1, :] = same & (idx[:, None] <= idx[None, :])
        m[typ, :, 2, :] = same & (idx[:, None] < idx[None, :])
        m[typ, :, 3, :] = same
    shared["masks"] = m
    shared["blk16"] = np.ascontiguousarray(((idx[:, None] // 8) == np.arange(16)[None, :]).astype(np.float32))
    maps = []
    for i in range(NCORES):
        d = dict(shared)
        sl = slice(16 * i, 16 * (i + 1))
        d["xs_all"] = np.ascontiguousarray(np.concatenate(
            [xs[sl].reshape(1, 128, D), xp[i].reshape(16, 128, D)], axis=0))
        d["cexp"] = np.ascontiguousarray(np.stack(
            [np.repeat(cs[sl], 8, axis=0), np.broadcast_to(cp[i][None, :], (128, D))], axis=0))
        d["st_ssd"] = np.ascontiguousarray(f(inp["state_ssd"][0, sl]).reshape(16, 1024, 128))
        hs = f(inp["state_ssd_conv"][0, sl])
        d["hist_ssd"] = np.ascontiguousarray(hs.reshape(16, 3, 12, 128).transpose(3, 2, 0, 1))
        d["st_gdn"] = np.ascontiguousarray(f(inp["state_gdn"][0, sl]))
        hg = f(inp["state_gdn_conv"][0, sl])
        d["hist_gdn"] = np.ascontiguousarray(hg.reshape(16, 3, 24, 128).transpose(3, 2, 0, 1))
        maps.append(d)
    return maps


def kernel(**inp):
    maps = _prep_inputs(inp)
    nc = build_program()
    res = run_bass_kernel_spmd(nc, maps, core_ids=list(range(NCORES)))
    R = res.results
    cat = lambda k: np.stack([np.asarray(r[k]) for r in R], axis=0)
    y_all = cat("y_all")
    y_prompt = y_all[:, 1:].reshape(8, 2048, D)
    y_sample = y_all[:, 0].reshape(128, 8, D)
    ssd_p = cat("o_ssd_p").reshape(1, 8, 16, 64, 128)
    ssdc_p = cat("o_ssdc_p").reshape(1, 8, 3, 1536)
    gdn_p = cat("o_gdn_p").reshape(1, 8, 8, 128, 128)
    gdnc_p = cat("o_gdnc_p").reshape(1, 8, 3, 3072)
    ssd_s = cat("o_ssd_s").reshape(1, 128, 16, 64, 128)
    ssdc_s = cat("o_ssdc_s").reshape(1, 128, 3, 1536)
    gdn_s = cat("o_gdn_s").reshape(1, 128, 8, 128, 128)
    gdnc_s = cat("o_gdnc_s").reshape(1, 128, 3, 3072)
    outs = (y_prompt, y_sample, ssd_p, ssdc_p, gdn_p, gdnc_p, ssd_s, ssdc_s, gdn_s, gdnc_s)
    return tuple(np.ascontiguousarray(o, dtype=np.float32) for o in outs)
```

```python
import contextlib
import itertools
import os
import numpy as np
import concourse.bass as bass
import concourse.mybir as mybir
from concourse.bass_utils import run_bass_kernel_spmd

F32 = mybir.dt.float32
BF16 = mybir.dt.bfloat16
AF = mybir.ActivationFunctionType
ALU = mybir.AluOpType
AX = mybir.AxisListType

NCORES = 8
D = 1024
NT = 17
DFF = 2816
P2_MODE = 0
ANNOTATE = bool(int(os.environ.get('K_ANN', '0')))
SKIP1 = False
P2_STAGE = 99
P2_HSTEP = 99
P2_SKIP = ()
P2_NL = 4
P2_TILES = 16
STOP_AFTER = int(os.environ.get('K_STOP', '99'))


class Prog:
    ENGS = ("pe", "act", "dve", "pool", "sp")

    def __init__(self, nc, stack):
        self.nc = nc
        self.sems = {}
        for e in self.ENGS:
            self.sems[e] = stack.enter_context(nc.semaphore("s_" + e))
        self.dsems = {"sp": [], "pool": [], "act": []}
        for q, n in (("sp", 8), ("pool", 4), ("act", 2)):
            for j in range(n):
                nm = "d_%s%d" % (q, j)
                self.sems[nm] = stack.enter_context(nc.semaphore(nm))
                self.dsems[q].append(nm)
        self.cnt = {k: 0 for k in self.sems}
        self.known = {e: {} for e in self.ENGS}
        self.lastw = {}
        self.readers = {}
        self.ops = {e: [] for e in self.ENGS}
        self.rr = {"sp": 0, "pool": 0, "act": 0}
        self.nops = 0
        self.tag = "init"

    def sec(self, name):
        self.tag = name

    fine = {}

    def _keys(self, aps):
        ks = []
        for a in aps:
            if a is None or isinstance(a, (int, float)):
                continue
            if isinstance(a, str):
                ks.append(a)
                continue
            nm = a.name
            if nm in self.fine:
                row, gr = self.fine[nm]
                nm = "%s:%d" % (nm, (int(a.offset) % row) // gr)
            ks.append(nm)
        return ks

    def _deps(self, eng, r, w):
        need = {}

        def add(c):
            if c is None:
                return
            s, v = c
            if s == "pe" and eng == "pe":
                return
            if need.get(s, 0) < v:
                need[s] = v

        for k in r:
            add(self.lastw.get(k))
        for k in w:
            add(self.lastw.get(k))
            for c in self.readers.get(k, {}).items():
                add(c)
        waits = []
        kn = self.known[eng]
        for s, v in need.items():
            if kn.get(s, 0) < v:
                kn[s] = v
                waits.append((s, v))
        return waits

    def _commit(self, c, r, w):
        for k in w:
            self.lastw[k] = c
            self.readers[k] = {}
        for k in r:
            d = self.readers.setdefault(k, {})
            if d.get(c[0], 0) < c[1]:
                d[c[0]] = c[1]

    def op(self, eng, fn, ins=(), outs=()):
        r = self._keys(ins)
        w = self._keys(outs)
        w = w + [k for k in r if k.startswith("ps") or k.startswith("pa") or k.startswith("pm")
                 or k.startswith("lpb") or k.startswith("pb")]
        waits = self._deps(eng, r, w)
        self.cnt[eng] += 1
        self.ops[eng].append((waits, fn, eng, 1, self.tag))
        self._commit((eng, self.cnt[eng]), r, w)
        self.nops += 1

    def dma(self, q, out, in_, extra_ins=(), extra_outs=()):
        r = self._keys([in_] + list(extra_ins))
        w = self._keys([out] + list(extra_outs))
        waits = self._deps(q, r, w)
        sems = self.dsems[q]
        j = self.rr[q]
        self.rr[q] = (j + 1) % len(sems)
        nm = sems[j]
        prev = self.cnt[nm]
        if prev > 0 and self.known[q].get(nm, 0) < prev:
            self.known[q][nm] = prev
            waits.append((nm, prev))
        self.cnt[nm] += 16
        self.ops[q].append((waits, lambda e: e.dma_start(out=out, in_=in_), nm, 16, self.tag))
        self._commit((nm, self.cnt[nm]), r, w)
        self.nops += 1

    def barrier(self):
        for e in self.ENGS:
            waits = []
            for s, v in self.cnt.items():
                if v > 0 and self.known[e].get(s, 0) < v:
                    self.known[e][s] = v
                    waits.append((s, v))
            self.ops[e].append((waits, None, None, 0, self.tag))
        self.lastw = {}
        self.readers = {}

    def emit(self):
        nc = self.nc
        with nc.Block() as block:
            for ename, deco in (("sp", block.sync), ("act", block.scalar), ("dve", block.vector),
                                ("pool", block.gpsimd), ("pe", block.tensor)):
                ops = self.ops[ename]

                def body(e, ops=ops):
                    for waits, fn, sname, inc, tag in ops:
                        for s, v in waits:
                            e.wait_ge(self.sems[s], v)
                        if fn is not None:
                            ins = fn(e)
                            ins.then_inc(self.sems[sname], inc)
                            if ANNOTATE:
                                ins.annotate(tag)

                deco(body)
                self.ops[ename] = []

    def mm(self, out, lhsT, rhs, start=True, stop=True):
        self.op("pe", lambda e: e.matmul(out, lhsT=lhsT, rhs=rhs, start=start, stop=stop),
                ins=[lhsT, rhs], outs=[out])

    def mmk(self, out, pairs):
        n = len(pairs)

        def fn(e):
            ins = None
            for i, (l, r) in enumerate(pairs):
                ins = e.matmul(out, lhsT=l, rhs=r, start=(i == 0), stop=(i == n - 1))
            return ins

        self.op("pe", fn, ins=[x for p in pairs for x in p], outs=[out])

    def tr(self, out, in_, ident):
        self.op("pe", lambda e: e.transpose(out, in_, ident), ins=[in_, ident], outs=[out])

    def trs(self, items, ident):
        def fn(e):
            ins = None
            for o, i in items:
                ins = e.transpose(o, i, ident)
            return ins
        self.op("pe", fn, ins=[i for _, i in items] + [ident], outs=[o for o, _ in items])

    def act(self, out, in_, func, bias=None, scale=None):
        kw = {}
        if bias is not None:
            kw["bias"] = bias
        if scale is not None:
            kw["scale"] = scale
        self.op("act", lambda e: e.activation(out=out, in_=in_, func=func, **kw),
                ins=[in_, bias, scale], outs=[out])

    def sqsum(self, junk, in_, accum):
        self.op("act", lambda e: e.activation(out=junk, in_=in_, func=AF.Square, accum_out=accum),
                ins=[in_], outs=[junk, accum])

    def tt(self, eng, out, in0, in1, op):
        self.op(eng, lambda e: e.tensor_tensor(out=out, in0=in0, in1=in1, op=op), ins=[in0, in1], outs=[out])

    def ts(self, eng, out, in0, s1, s2, op0, op1=None):
        if op1 is None:
            self.op(eng, lambda e: e.tensor_scalar(out=out, in0=in0, scalar1=s1, scalar2=None, op0=op0),
                    ins=[in0, s1], outs=[out])
        else:
            self.op(eng, lambda e: e.tensor_scalar(out=out, in0=in0, scalar1=s1, scalar2=s2, op0=op0, op1=op1),
                    ins=[in0, s1, s2], outs=[out])

    def stt(self, eng, out, in0, scalar, in1, op0, op1):
        self.op(eng, lambda e: e.scalar_tensor_tensor(out=out, in0=in0, scalar=scalar, in1=in1, op0=op0, op1=op1),
                ins=[in0, scalar, in1], outs=[out])

    def copy(self, eng, out, in_):
        if eng == "act":
            self.op(eng, lambda e: e.copy(out=out, in_=in_), ins=[in_], outs=[out])
        else:
            self.op(eng, lambda e: e.tensor_copy(out=out, in_=in_), ins=[in_], outs=[out])

    def red(self, eng, out, in_):
        self.op(eng, lambda e: e.tensor_reduce(out=out, in_=in_, axis=AX.X, op=ALU.add), ins=[in_], outs=[out])

    def recip(self, out, in_):
        self.op("dve", lambda e: e.reciprocal(out=out, in_=in_), ins=[in_], outs=[out])

    def memset(self, eng, ap, val):
        self.op(eng, lambda e: e.memset(ap, val), ins=[], outs=[ap])


def bc(ap, axis, n):
    u = ap.unsqueeze(axis)
    shp = list(u.shape)
    shp[axis] = n
    return u.broadcast_to(shp)


def build_program():
    nc = bass.Bass("TRN2", target_bir_lowering=False)

    def din(name, shape, dt=F32):
        return nc.dram_tensor(name, list(shape), dt, kind="ExternalInput").ap()

    def dout(name, shape, dt=F32):
        return nc.dram_tensor(name, list(shape), dt, kind="ExternalOutput").ap()

    def dscr(name, shape, dt=F32):
        return nc.dram_tensor(name, list(shape), dt).ap()

    xs_all = din("xs_all", [NT, 128, D])
    cexp = din("cexp", [2, 128, D])
    w_ada = din("w_ada", [D, 6 * D])
    b_ada_b = din("b_ada_b", [128, 6 * D])
    normvecs = din("normvecs", [128, 4, D])
    w_in = din("w_in", [D, 6688])
    w_out = din("w_out", [2 * D, D])
    w_gu = din("w_gu", [D, 2 * DFF])
    w_dn = din("w_dn", [DFF, D])
    ssd_cw = din("ssd_cw", [128, 12, 5])
    gdn_cw = din("gdn_cw", [128, 24, 4])
    smallv = din("smallv", [128, 64])
    ssd_nw = din("ssd_nw", [128, D])
    gdn_nw = din("gdn_nw", [128, 128])
    st_ssd = din("st_ssd", [16, 1024, 128])
    hist_ssd = din("hist_ssd", [128, 12, 16, 3])
    st_gdn = din("st_gdn", [16, 8, 128, 128])
    hist_gdn = din("hist_gdn", [128, 24, 16, 3])
    ident_d = din("ident", [128, 128])
    masks_d = din("masks", [2, 128, 4, 128])
    blk16_d = din("blk16", [128, 16])

    y_all = dout("y_all", [NT, 128, D])
    o_ssd_p = dout("o_ssd_p", [1024, 128])
    o_ssdc_p = dout("o_ssdc_p", [3, 1536])
    o_gdn_p = dout("o_gdn_p", [1024, 128])
    o_gdnc_p = dout("o_gdnc_p", [3, 3072])
    o_ssd_s = dout("o_ssd_s", [16, 1024, 128])
    o_ssdc_s = dout("o_ssdc_s", [48, 1536])
    o_gdn_s = dout("o_gdn_s", [16, 8, 128, 128])
    o_gdnc_s = dout("o_gdnc_s", [48, 3072])

    modscr = dscr("modscr", [2, 128, 6 * D])
    yssd_scr = dscr("yssd_scr", [NT, 128, D], BF16)
    x1_scr = dscr("x1_scr", [NT, 128, D])

    with contextlib.ExitStack() as gstack:
        P = Prog(nc, gstack)

        def sb(stack, name, shape, dt=F32):
            return stack.enter_context(nc.sbuf_tensor(name, list(shape), dt))

        class WBlocks:
            def __init__(self, stk, name, nk, bounds):
                self.bounds = bounds
                self.t = [sb(stk, "%s_%d" % (name, j), [128, nk, bounds[j + 1] - bounds[j]], BF16)
                          for j in range(len(bounds) - 1)]

            def load(self, dview, col_off, order=None):
                for j in (order if order is not None else range(len(self.t))):
                    b0, b1 = self.bounds[j], self.bounds[j + 1]
                    P.dma("pool", self.t[j][:, :, :], dview[:, :, col_off + b0:col_off + b1])

            def col(self, kc, c0, w):
                for j in range(len(self.t)):
                    if self.bounds[j] <= c0 and c0 + w <= self.bounds[j + 1]:
                        return self.t[j][:, kc, c0 - self.bounds[j]:c0 - self.bounds[j] + w]
                raise ValueError("column range straddles weight blocks")

        PT = gstack.enter_context(nc.psum_tensor("pst", [128, 1024], BF16))
        ps1 = contextlib.ExitStack()
        PS = [ps1.enter_context(nc.psum_tensor("ps%d" % i, [128, 512], F32)) for i in range(7)]

        identf = sb(gstack, "identf", [128, 128])
        identb = sb(gstack, "identb", [128, 128], BF16)
        onesf = sb(gstack, "onesf", [128, 128])
        onesb = sb(gstack, "onesb", [128, 128], BF16)
        cst = sb(gstack, "cst", [128, 4])
        masks = [sb(gstack, "masks%d" % t, [128, 4, 128]) for t in range(2)]
        blk16 = sb(gstack, "blk16s", [128, 16])
        smv = sb(gstack, "smv", [128, 64])
        negA = sb(gstack, "negA", [128, 24])
        P.dma("sp", identf[:, :], ident_d[:, :])
        P.dma("pool", identb[:, :], ident_d[:, :])
        for t in range(2):
            P.dma("sp", masks[t][:, :, :], masks_d[t])
        P.dma("sp", blk16[:, :], blk16_d[:, :])
        P.dma("sp", smv[:, :], smallv[:, :])
        P.memset("dve", onesf[:, :], 1.0)
        P.memset("dve", onesb[:, :], 1.0)
        P.memset("dve", cst[:, 0:1], 1e-6)
        P.memset("dve", cst[:, 1:2], 1.0)
        P.memset("dve", cst[:, 2:3], 128e-6)
        P.memset("dve", cst[:, 3:4], 0.0)
        P.act(negA[:, 0:16], smv[:, 16:32], AF.Exp)
        P.act(negA[:, 16:24], smv[:, 56:64], AF.Exp)
        P.ts("dve", negA[:, :], negA[:, :], -1.0, None, ALU.mult)

        MSTT, MINC, MSTR, MBLK = 0, 1, 2, 3

        with contextlib.ExitStack() as st0:
            ct = sb(st0, "ct", [128, D])
            cb = sb(st0, "cb", [128, D], BF16)
            cT = [sb(st0, "cT%d" % t, [128, 8, 128], BF16) for t in range(2)]
            modt = [sb(st0, "modt%d" % t, [128, 6 * D]) for t in range(2)]
            nv = sb(st0, "nv", [128, 4, D])
            wa = [sb(st0, "wa%d" % i, [128, 8, 512], BF16) for i in range(2)]
            bb = [sb(st0, "bb%d" % i, [128, 512]) for i in range(2)]
            P.dma("sp", nv[:, :, :], normvecs[:, :, :])
            for t in range(2):
                P.dma("sp", ct[:, :], cexp[t])
                P.act(cb[:, :], ct[:, :], AF.Silu)
                P.trs([(PT[:, c * 128:(c + 1) * 128], cb[:, c * 128:(c + 1) * 128]) for c in range(8)], identb[:, :])
                P.copy("dve", cT[t][:, :, :], PT[:, :].rearrange("p (c t) -> p c t", c=8))
            wv = w_ada.rearrange("(kc p) n -> p kc n", p=128)
            for j in range(12):
                P.dma("pool", wa[j % 2][:, :, :], wv[:, :, j * 512:(j + 1) * 512])
                P.dma("sp", bb[j % 2][:, :], b_ada_b[:, j * 512:(j + 1) * 512])
                for t in range(2):
                    ps = PS[(2 * j + t) % 4]
                    P.mmk(ps[:, :], [(cT[t][:, kc, :], wa[j % 2][:, kc, :]) for kc in range(8)])
                    P.tt("dve", modt[t][:, j * 512:(j + 1) * 512], ps[:, :], bb[j % 2][:, :], ALU.add)
            for t in range(2):
                m = modt[t]
                P.stt("dve", m[:, D:2 * D], m[:, D:2 * D], 1.0, nv[:, 0, :], ALU.add, ALU.mult)
                P.tt("dve", m[:, 2 * D:3 * D], m[:, 2 * D:3 * D], nv[:, 1, :], ALU.mult)
                P.stt("dve", m[:, 4 * D:5 * D], m[:, 4 * D:5 * D], 1.0, nv[:, 2, :], ALU.add, ALU.mult)
                P.tt("dve", m[:, 5 * D:6 * D], m[:, 5 * D:6 * D], nv[:, 3, :], ALU.mult)
                P.dma("sp", modscr[t], m[:, :])
            P.barrier()
            P.emit()

        def norm_to_T(xt, modv, sidx, hidx, tmpA, hb, hT, stt_):
            P.sec("norm_to_T")
            P.sqsum(tmpA[:, :], xt[:, :], stt_[:, 0:1])
            P.act(stt_[:, 1:2], stt_[:, 0:1], AF.Sqrt, bias=cst[:, 0:1], scale=1.0 / D)
            P.recip(stt_[:, 2:3], stt_[:, 1:2])
            P.stt("dve", tmpA[:, :], xt[:, :], stt_[:, 2:3], modv[:, sidx, :], ALU.mult, ALU.mult)
            P.tt("dve", hb[:, :], tmpA[:, :], modv[:, hidx, :], ALU.add)
            P.trs([(PT[:, c * 128:(c + 1) * 128], hb[:, c * 128:(c + 1) * 128]) for c in range(8)], identb[:, :])
            P.copy("act", hT[:, :, :], PT[:, :].rearrange("p (c t) -> p c t", c=8))

        def norm_to_T_g(xt, modv, sidx, hidx, tmpA, hb, hT, stt_):
            P.sec("norm_to_T")
            P.sqsum(tmpA[:, :], xt[:, :], stt_[:, 0:1])
            yield
            P.act(stt_[:, 1:2], stt_[:, 0:1], AF.Sqrt, bias=cst[:, 0:1], scale=1.0 / D)
            yield
            P.recip(stt_[:, 2:3], stt_[:, 1:2])
            yield
            P.stt("dve", tmpA[:, :], xt[:, :], stt_[:, 2:3], modv[:, sidx, :], ALU.mult, ALU.mult)
            yield
            P.tt("dve", hb[:, :], tmpA[:, :], modv[:, hidx, :], ALU.add)
            yield
            P.trs([(PT[:, c * 128:(c + 1) * 128], hb[:, c * 128:(c + 1) * 128]) for c in range(8)], identb[:, :])
            yield
            P.copy("act", hT[:, :, :], PT[:, :].rearrange("p (c t) -> p c t", c=8))
            yield

        def small_scan_terms(af, nh, typ, pss, ex):
            mk = masks[typ]
            P.mm(pss[:, 0:nh], mk[:, MINC, :], af)
            P.mm(pss[:, nh:2 * nh], mk[:, MSTT, :], af)
            P.mm(pss[:, 2 * nh:3 * nh], mk[:, MBLK, :], af)
            P.act(ex[:, 0:3 * nh], pss[:, 0:3 * nh], AF.Exp)

        def small_scan_terms_g(af, nh, typ, pss, ex):
            mk = masks[typ]
            P.mm(pss[:, 0:nh], mk[:, MINC, :], af)
            yield
            P.mm(pss[:, nh:2 * nh], mk[:, MSTT, :], af)
            yield
            P.mm(pss[:, 2 * nh:3 * nh], mk[:, MBLK, :], af)
            yield
            P.act(ex[:, 0:3 * nh], pss[:, 0:3 * nh], AF.Exp)
            yield

        def proj_conv_gen(typ, ngroups, wt, wcol0, hT, psb, xinS_l, xinP_l, histS, histP, cwt, has_bias, accs, dst_fn):
            items = []

            def slot(i):
                n = len(items)
                if 0 <= i < n:
                    it = items[i]
                    if has_bias:
                        P.act(it[0], it[2][0], AF.Identity, bias=cwt[:, it[4], 4:5], scale=cwt[:, it[4], 0:1])
                    else:
                        P.act(it[0], it[2][0], AF.Identity, scale=cwt[:, it[4], 0:1])
                    yield
                if 0 <= i - 1 < n:
                    it = items[i - 1]
                    for k in range(1, 4):
                        P.stt("dve", it[0], it[2][k], cwt[:, it[4], k:k + 1], it[0], ALU.mult, ALU.add)
                        yield
                if 0 <= i - 2 < n:
                    it = items[i - 2]
                    P.act(it[3], it[1][:, :], AF.Silu)
                    yield

            for g in range(ngroups):
                ps = psb[g % 2]
                for j in range(4):
                    c = 4 * g + j
                    P.mmk(ps[:, j * 128:(j + 1) * 128],
                          [((wt.col(kc, wcol0 + c * 128, 128) if hasattr(wt, "col")
                             else wt[:, kc, wcol0 + c * 128:wcol0 + (c + 1) * 128]), hT[:, kc, :]) for kc in range(8)])
                    yield
                if typ == 0:
                    xin = xinS_l[g % 2]
                    P.copy("pool", xin[:, :, :, 0:3], histS[:, 4 * g:4 * g + 4, :, :])
                    yield
                    P.copy("act", xin[:, :, :, 3:11], ps[:, :].rearrange("p (j s t) -> p j s t", j=4, s=16))
                    yield
                else:
                    xin = xinP_l[g % 2]
                    P.copy("pool", xin[:, :, 0:3], histP[:, 4 * g:4 * g + 4, :])
                    yield
                    P.copy("act", xin[:, :, 3:131], ps[:, :].rearrange("p (j t) -> p j t", j=4))
                    yield
                    P.copy("pool", histP[:, 4 * g:4 * g + 4, :], xin[:, :, 128:131])
                    yield
                for j in range(4):
                    c = 4 * g + j
                    a_ = accs[c % 4]
                    if typ == 0:
                        av = a_[:, :].rearrange("p (s t) -> p s t", s=16)
                        sh = [xin[:, j, :, k:k + 8] for k in range(4)]
                    else:
                        av = a_[:, :]
                        sh = [xin[:, j, k:k + 128] for k in range(4)]
                    items.append((av, a_, sh, dst_fn(c), c))
                    yield from slot(c)
            yield from slot(4 * ngroups)
            yield from slot(4 * ngroups + 1)

        def run_gen(g):
            for _ in g:
                pass

        w_in_v = w_in.rearrange("(kc p) n -> p kc n", p=128)
        if STOP_AFTER >= 1 and not SKIP1:
          with contextlib.ExitStack() as st1:
            wI = sb(st1, "wI", [128, 8, 2576], BF16)
            for kc in range(8):
                P.dma("pool", wI[:, kc, :], w_in_v[:, kc, 0:2576])
            modv = sb(st1, "modv", [128, 3, D])
            cw = sb(st1, "cw", [128, 12, 5])
            nw = sb(st1, "ssdnw", [128, D])
            histS = sb(st1, "histS", [128, 12, 16, 3])
            histP = sb(st1, "histP", [128, 12, 3])
            P.dma("sp", cw[:, :, :], ssd_cw[:, :, :])
            P.dma("sp", nw[:, :], ssd_nw[:, :])
            P.dma("sp", histS[:, :, :, :], hist_ssd[:, :, :, :])
            P.memset("pool", histP[:, :, :], 0.0)
            xt = [sb(st1, "xt%d" % i, [128, D]) for i in range(2)]
            tmpA = sb(st1, "tmpA", [128, D])
            hb = sb(st1, "hb", [128, D], BF16)
            hT = sb(st1, "hT", [128, 8, 128], BF16)
            stt_ = sb(st1, "stt", [128, 4])
            xinS_l = [sb(st1, "xinS%d" % i, [128, 4, 16, 11]) for i in range(2)]
            xinP_l = [sb(st1, "xinP%d" % i, [128, 4, 131]) for i in range(2)]
            acc = [sb(st1, "acc%d" % i, [128, 128]) for i in range(4)]
            xc = sb(st1, "xc", [128, 12, 128], BF16)
            zs = sb(st1, "zs", [128, D], BF16)
            sm = sb(st1, "sm", [128, 48])
            af = sb(st1, "af", [128, 16])
            ex = sb(st1, "ex", [128, 48])
            xtm = sb(st1, "xtm", [128, D], BF16)
            xdt = sb(st1, "xdt", [128, D], BF16)
            xDs = sb(st1, "xDs", [128, D], BF16)
            xw = sb(st1, "xw", [128, D], BF16)
            Btm = sb(st1, "Btm", [128, 2, 128], BF16)
            CBm = sb(st1, "CBm", [128, 2, 128])
            L4 = [sb(st1, "L4_%d" % i, [128, 4, 128]) for i in range(2)]
            E4 = [sb(st1, "E4_%d" % i, [128, 4, 128]) for i in range(2)]
            M4 = [sb(st1, "M4_%d" % i, [128, 4, 128], BF16) for i in range(2)]
            yo = sb(st1, "yo", [128, D])
            yy = sb(st1, "yy", [128, D])
            ss = sb(st1, "ss", [128, 8])
            ysb = [sb(st1, "ysb%d" % i, [128, D], BF16) for i in range(2)]
            hTf = sb(st1, "hTf", [128, D])
            hTb = sb(st1, "hTb", [128, D], BF16)
            lhs3 = sb(st1, "lhs3", [128, 8, 48], BF16)
            cv = sb(st1, "cv", [48, 1536])
            CmT = [sb(st1, "CmT%d" % g, [128, 16 * 128], BF16) for g in range(2)]
            h0 = [sb(st1, "h0_%d" % i, [128, 8, 128]) for i in range(2)]
            h0Tb = sb(st1, "h0Tb", [128, D], BF16)
            xwm = sb(st1, "xwm", [128, D], BF16)
            hn = [sb(st1, "hn%d" % i, [128, 8, 128]) for i in range(2)]
            rsel = sb(st1, "rsel", [128, 2, 128])
            decsel = sb(st1, "decsel", [128, 128])
            P.memset("pool", hTf[:, :], 0.0)
            P.memset("pool", hTb[:, :], 0.0)
            for g in range(2):
                P.memset("pool", CmT[g][:, :], 0.0)

            for ti in range(NT):
                typ = 0 if ti == 0 else 1
                mk = masks[typ]
                if ti <= 1:
                    P.dma("sp", modv[:, :, :], modscr[typ].rearrange("p (s d) -> p s d", s=6)[:, 0:3, :])
                x_ = xt[ti % 2]
                P.dma("sp", x_[:, :], xs_all[ti])
                norm_to_T(x_, modv, 1, 0, tmpA, hb, hT, stt_)
                run_gen(proj_conv_gen(typ, 3, wI, 1024, hT, PS[0:2], xinS_l, xinP_l, histS, histP, cw, True, acc,
                                      lambda c: xc[:, c, :]))
                P.sec("z (token-major) -> silu")
                for nb in range(2):
                    ps = PS[2 + nb]
                    P.mmk(ps[:, :], [(hT[:, kc, :], wI[:, kc, nb * 512:(nb + 1) * 512]) for kc in range(8)])
                    P.act(zs[:, nb * 512:(nb + 1) * 512], ps[:, :], AF.Silu)
                P.sec("dt")
                pd = PS[4]
                P.mmk(pd[:, 0:16], [(hT[:, kc, :], wI[:, kc, 2560:2576]) for kc in range(8)])
                P.tt("dve", sm[:, 0:16], pd[:, 0:16], smv[:, 0:16], ALU.add)
                P.act(sm[:, 16:32], sm[:, 0:16], AF.Exp)
                P.act(sm[:, 32:48], sm[:, 16:32], AF.Ln, bias=cst[:, 1:2], scale=1.0)
                P.tt("dve", af[:, :], sm[:, 32:48], negA[:, 0:16], ALU.mult)
                small_scan_terms(af[:, :], 16, typ, PS[4][:, 64:112], ex)
                P.sec("token-major x, xdt, xD, xw, B")
                P.trs([(PT[:, c * 128:(c + 1) * 128], xc[:, c, :]) for c in range(8)], identb[:, :])
                P.copy("act", xtm[:, :], PT[:, :])
                x3 = xtm[:, :].rearrange("p (h d) -> p h d", h=16)
                P.tt("dve", xdt[:, :].rearrange("p (h d) -> p h d", h=16), x3, bc(sm[:, 32:48], 2, 64), ALU.mult)
                P.tt("pool", xDs[:, :].rearrange("p (h d) -> p h d", h=16), x3, bc(smv[:, 32:48], 2, 64), ALU.mult)
                P.tt("pool", xw[:, :].rearrange("p (h d) -> p h d", h=16),
                     xdt[:, :].rearrange("p (h d) -> p h d", h=16), bc(ex[:, 16:32], 2, 64), ALU.mult)
                P.trs([(PT[:, g * 128:(g + 1) * 128], xc[:, 8 + g, :]) for g in range(2)], identb[:, :])
                P.copy("act", Btm[:, :, :], PT[:, 0:256].rearrange("p (g n) -> p g n", g=2))
                P.sec("CB^T masked")
                pc = PS[5]
                for g in range(2):
                    P.mm(pc[:, g * 128:(g + 1) * 128], xc[:, 8 + g, :], xc[:, 10 + g, :])
                P.tt("dve", CBm[:, :, :], pc[:, 0:256].rearrange("p (g l) -> p g l", g=2), bc(mk[:, MINC, :], 1, 2), ALU.mult)
                P.sec("y_off raw")
                if typ == 1:
                    for g in range(2):
                        P.mm(PS[2 + g][:, :], xc[:, 10 + g, :], hTb[:, g * 512:(g + 1) * 512])
                else:
                    for g in range(2):
                        base = CmT[g][:, :]
                        dst = bass.AP(base.tensor, base.offset, [[16 * 128, 128], [136, 16], [1, 8]])
                        P.op("pool", (lambda e, dst=dst, src=xc[:, 10 + g, :].rearrange("p (s t) -> p s t", s=16):
                                      e.tensor_copy(out=dst, in_=src)), ins=[xc[:, 0, :]], outs=[base])
                    for hh in range(2):
                        P.tt("pool", rsel[:, hh, :].rearrange("p (j s) -> p j s", j=8),
                             bc(af[:, hh:16:2], 2, 16), bc(blk16[:, :], 1, 8), ALU.mult)
                        P.mm(PS[5][:, 256 + hh * 128:256 + (hh + 1) * 128], onesf[:, :], rsel[:, hh, :])
                    P.act(decsel[0:64, :], PS[5][0:64, 256:384], AF.Exp)
                    P.act(decsel[64:128, :], PS[5][64:128, 384:512], AF.Exp)
                    for s in range(16):
                        h0_ = h0[s % 2]
                        hn_ = hn[s % 2]
                        P.dma("sp", h0_[:, :, :], st_ssd[s].rearrange("(j p) n -> p j n", p=128))
                        pa, pb = PS[0], PS[1]
                        P.trs([((pa if j < 4 else pb)[:, (j % 4) * 128:(j % 4 + 1) * 128], h0_[:, j, :]) for j in range(8)],
                              identf[:, :])
                        P.copy("act", h0Tb[:, 0:512], pa[:, :])
                        P.copy("dve", h0Tb[:, 512:1024], pb[:, :])
                        for g in range(2):
                            P.op("pe", (lambda e, o=PS[2 + g][:, :], l=CmT[g][:, s * 128:(s + 1) * 128],
                                        r=h0Tb[:, g * 512:(g + 1) * 512], s=s:
                                        e.matmul(o, lhsT=l, rhs=r, start=(s == 0), stop=(s == 15))),
                                 ins=[CmT[g][:, :], h0Tb[:, :]], outs=[PS[2 + g][:, :]])
                        P.act(xwm[:, :], xw[:, :], AF.Identity, scale=blk16[:, s:s + 1])
                        for half in range(2):
                            pn = PS[half]
                            for jj in range(4):
                                j = half * 4 + jj
                                P.mm(pn[:, jj * 128:(jj + 1) * 128], xwm[:, j * 128:(j + 1) * 128], Btm[:, j // 4, :])
                            for jj in range(4):
                                j = half * 4 + jj
                                P.stt("dve", hn_[:, j, :], h0_[:, j, :], decsel[:, j * 16 + s:j * 16 + s + 1],
                                      pn[:, jj * 128:(jj + 1) * 128], ALU.mult, ALU.add)
                        P.dma("sp", o_ssd_s[s].rearrange("(j p) n -> p j n", p=128), hn_[:, :, :])
                P.sec("per-head intra-chunk")
                for q in range(4):
                    g = q // 2
                    L_, E_, M_ = L4[q % 2], E4[q % 2], M4[q % 2]
                    P.tt("dve", L_[:, :, :], bc(mk[:, MSTT, :], 1, 4), bc(af[:, 4 * q:4 * q + 4], 2, 128), ALU.mult)
                    pg = PS[5 + (q % 2)]
                    for j in range(4):
                        P.mm(pg[:, j * 128:(j + 1) * 128], L_[:, j, :], mk[:, MINC, :])
                    P.act(E_[:, :, :], pg[:, :].rearrange("p (j l) -> p j l", j=4), AF.Exp)
                    P.tt("dve", M_[:, :, :], E_[:, :, :], bc(CBm[:, g, :], 1, 4), ALU.mult)
                    py = PS[0 + g]
                    for j in range(4):
                        h = 4 * q + j
                        hh = h % 8
                        P.mmk(py[:, hh * 64:(hh + 1) * 64],
                              [(identb[:, :], xDs[:, h * 64:(h + 1) * 64]), (M_[:, j, :], xdt[:, h * 64:(h + 1) * 64])])
                P.sec("combine y = y_diag + eacs * y_off")
                for g in range(2):
                    P.copy("act", yo[:, g * 512:(g + 1) * 512], PS[2 + g][:, :])
                P.tt("dve", yo[:, :].rearrange("p (h d) -> p h d", h=16), yo[:, :].rearrange("p (h d) -> p h d", h=16),
                     bc(ex[:, 0:16], 2, 64), ALU.mult)
                for g in range(2):
                    P.tt("dve", yy[:, g * 512:(g + 1) * 512], yo[:, g * 512:(g + 1) * 512], PS[0 + g][:, :], ALU.add)
                P.sec("gate with silu(z), group rmsnorm")
                P.tt("dve", yy[:, :], yy[:, :], zs[:, :], ALU.mult)
                for g in range(2):
                    P.sqsum(yo[:, g * 512:(g + 1) * 512], yy[:, g * 512:(g + 1) * 512], ss[:, g:g + 1])
                P.act(ss[:, 2:4], ss[:, 0:2], AF.Sqrt, bias=cst[:, 0:1], scale=1.0 / 512)
                P.recip(ss[:, 4:6], ss[:, 2:4])
                y_ = ysb[ti % 2]
                for g in range(2):
                    P.stt("dve", y_[:, g * 512:(g + 1) * 512], yy[:, g * 512:(g + 1) * 512], ss[:, 4 + g:5 + g],
                          nw[:, g * 512:(g + 1) * 512], ALU.mult, ALU.mult)
                P.dma("sp", yssd_scr[ti], y_[:, :])
                P.sec("state update (prompt chunks)")
                if typ == 1:
                    for g in range(2):
                        P.mm(PS[2 + g][:, :], Btm[:, g, :], xw[:, g * 512:(g + 1) * 512])
                    h3 = hTf[:, :].rearrange("p (h d) -> p h d", h=16)
                    P.tt("dve", h3, h3, bc(ex[:, 32:48], 2, 64), ALU.mult)
                    for g in range(2):
                        P.tt("dve", hTf[:, g * 512:(g + 1) * 512], hTf[:, g * 512:(g + 1) * 512], PS[2 + g][:, :], ALU.add)
                    P.copy("act", hTb[:, :], hTf[:, :])
                P.sec("conv-state outputs (last 3 raw xbc rows)")
                if ti == 0 or ti == NT - 1:
                    M3 = 48 if typ == 0 else 3
                    if typ == 0:
                        P.copy("pool", lhs3[:, :, :].rearrange("p k (s t) -> p k s t", s=16),
                               hT[:, :, :].rearrange("p k (s t) -> p k s t", s=16)[:, :, :, 5:8])
                    else:
                        P.copy("pool", lhs3[:, :, 0:3], hT[:, :, 125:128])
                    for nb in range(3):
                        ps = PS[5 + (nb % 2)]
                        P.mmk(ps[0:M3, :], [(lhs3[:, kc, 0:M3], wI[:, kc, 1024 + nb * 512:1024 + (nb + 1) * 512])
                                            for kc in range(8)])
                        P.copy("act", cv[0:M3, nb * 512:(nb + 1) * 512], ps[0:M3, :])
                    P.dma("sp", (o_ssdc_s if typ == 0 else o_ssdc_p)[:, :], cv[0:M3, :])
            for half in range(2):
                ps = PS[half]
                P.trs([(ps[:, jj * 128:(jj + 1) * 128], hTf[:, (half * 4 + jj) * 128:(half * 4 + jj + 1) * 128])
                       for jj in range(4)], identf[:, :])
                P.copy("act", hn[half][:, 0:4, :], ps[:, :].rearrange("p (j n) -> p j n", j=4))
                P.dma("sp", o_ssd_p[half * 512:(half + 1) * 512, :].rearrange("(j p) n -> p j n", p=128), hn[half][:, 0:4, :])
            P.barrier()
            P.emit()
        ps1.close()
        if STOP_AFTER >= 2:
          with contextlib.ExitStack() as st2:
            PA = [st2.enter_context(nc.psum_tensor("pa%d" % i, [128, 512], F32)) for i in range(2)]
            PM = st2.enter_context(nc.psum_tensor("pm", [128, 512], F32))
            LPB = [st2.enter_context(nc.psum_tensor("lpb%d" % i, [128, 512], F32)) for i in range(4)]
            LPS = [[LPB[ln][:, i * 128:(i + 1) * 128] for i in range(4)] for ln in range(4)]
            wII = WBlocks(st2, "wII", 8, [0, 512, 1024, 1536, 2048, 2560, 3072, 3584, 4096, 4112])
            wII.load(w_in_v, 2576)
            wO = WBlocks(st2, "wO", 16, [0, 512, 1024])
            w_out_v = w_out.rearrange("(kc p) n -> p kc n", p=128)
            wO.load(w_out_v, 0)
            modv = sb(st2, "modv2", [128, 3, D])
            cwg = sb(st2, "cwg", [128, 24, 4])
            gnw = sb(st2, "gnw", [128, 128])
            histP = sb(st2, "histPg", [128, 24, 3])
            P.dma("sp", cwg[:, :, :], gdn_cw[:, :, :])
            P.dma("sp", gnw[:, :], gdn_nw[:, :])
            P.memset("pool", histP[:, :, :], 0.0)
            tmpA = sb(st2, "tmpA2", [128, D])
            hT = sb(st2, "hT2", [128, 8, 128], BF16)
            stt_ = sb(st2, "stt2", [128, 4])
            xinP_l = [sb(st2, "xinP2_%d" % i, [128, 4, 131]) for i in range(2)]
            acc = [sb(st2, "acc2_%d" % i, [128, 128]) for i in range(4)]
            lhs3 = sb(st2, "lhs3g", [128, 8, 48], BF16)
            tmpT = sb(st2, "tmpT2", [128, D])
            sttT = sb(st2, "sttT2", [128, 4])
            ss8 = sb(st2, "ss8", [128, 24])
            otm = sb(st2, "otm", [128, D])
            mixed = sb(st2, "mixed", [128, 2 * D], BF16)
            mixedT = sb(st2, "mixedT", [128, 16, 128], BF16)

            class FB:
                pass

            def make_fb(i, stk):
                fb = FB()
                fb.xt = sb(stk, "f%d_xt" % i, [128, D])
                fb.hb = sb(stk, "f%d_hb" % i, [128, D], BF16)
                fb.qk = sb(stk, "f%d_qk" % i, [128, 16, 128], BF16)
                fb.vfm = sb(stk, "f%d_vfm" % i, [128, 8, 128], BF16)
                fb.ktm = sb(stk, "f%d_ktm" % i, [128, 8, 128], BF16)
                fb.vtm = sb(stk, "f%d_vtm" % i, [128, 8, 128], BF16)
                fb.gs = sb(stk, "f%d_gs" % i, [128, D], BF16)
                fb.sm = sb(stk, "f%d_sm" % i, [128, 64])
                fb.gf = sb(stk, "f%d_gf" % i, [128, 8])
                fb.ex = sb(stk, "f%d_ex" % i, [128, 24])
                return fb

            class Lane:
                pass

            lanes = []

            def make_lane(ln, stk):
                B = Lane()
                B.ps = LPS[ln]
                f = lambda nm, dt=F32: sb(stk, "ln%d_%s" % (ln, nm), [128, 128], dt)
                B.P = [f("P0"), f("P1")]
                B.PT = [f("PT0"), f("PT1")]
                B.X = [f("X0"), f("X1")]
                B.ot = f("ot")
                B.L, B.Dm, B.DmI, B.DmS = B.ot, B.X[1], B.PT[1], B.P[1]
                B.attnT, B.TTb, B.R, B.vn, B.kout = (f("attnT", BF16), f("TTb", BF16), f("R", BF16),
                                                     f("vn", BF16), f("kout", BF16))
                lanes.append(B)

            make_lane(0, st2)
            fbs = [make_fb(0, st2)]

            def front_gen(ti, typ, SB, fb):
                xt, hb, qk, vfm, sm, gf, ex = fb.xt, fb.hb, fb.qk, fb.vfm, fb.sm, fb.gf, fb.ex
                P.sec("F:load+norm")
                if ti <= 1:
                    P.dma("sp", modv[:, :, :], modscr[typ].rearrange("p (s d) -> p s d", s=6)[:, 0:3, :])
                    yield
                P.dma("sp", xt[:, :], xs_all[ti])
                yield
                yield from norm_to_T_g(xt, modv, 1, 0, tmpA, hb, hT, stt_)
                yield
                yield from proj_conv_gen(typ, 6, wII, 0, hT, PA, (SB.xinS_l if typ == 0 else None), xinP_l,
                                         (SB.histS if typ == 0 else None), histP, cwg, False, acc,
                                         lambda c: (qk[:, c, :] if c < 16 else vfm[:, c - 16, :]))
                if ti == 0 or ti == NT - 1:
                    P.sec("F:convstate")
                    M3 = 48 if typ == 0 else 3
                    if typ == 0:
                        P.copy("pool", lhs3[:, :, :].rearrange("p k (s t) -> p k s t", s=16),
                               hT[:, :, :].rearrange("p k (s t) -> p k s t", s=16)[:, :, :, 5:8])
                        yield
                    else:
                        P.copy("pool", lhs3[:, :, 0:3], hT[:, :, 125:128])
                        yield
                    for cc in range(3):
                        for nb in range(2):
                            c0 = cc * 1024 + nb * 512
                            P.mmk(PA[nb][0:M3, :], [(lhs3[:, kc, 0:M3], wII.col(kc, c0, 512)) for kc in range(8)])
                            yield
                            P.copy("act", tmpA[0:M3, nb * 512:(nb + 1) * 512], PA[nb][0:M3, :])
                            yield
                        P.dma("sp", (o_gdnc_s if typ == 0 else o_gdnc_p)[:, cc * 1024:(cc + 1) * 1024], tmpA[0:M3, :])
                        yield
                    yield
                P.sec("F:gate")
                for nb in range(2):
                    P.mmk(PA[nb][:, :], [(hT[:, kc, :], wII.col(kc, 3072 + nb * 512, 512))
                                         for kc in range(8)])
                    yield
                    P.act(fb.gs[:, nb * 512:(nb + 1) * 512], PA[nb][:, :], AF.Silu)
                    yield
                yield
                P.sec("F:beta/g")
                P.mmk(PM[:, 0:16], [(hT[:, kc, :], wII.col(kc, 4096, 16)) for kc in range(8)])
                yield
                P.act(sm[:, 0:8], PM[:, 0:8], AF.Exp, scale=-1.0)
                yield
                P.ts("dve", sm[:, 0:8], sm[:, 0:8], 1.0, None, ALU.add)
                yield
                P.recip(sm[:, 8:16], sm[:, 0:8])
                yield
                P.ts("dve", sm[:, 16:24], sm[:, 8:16], -1.0, None, ALU.mult)
                yield
                P.tt("dve", sm[:, 24:32], PM[:, 8:16], smv[:, 48:56], ALU.add)
                yield
                P.act(sm[:, 32:40], sm[:, 24:32], AF.Exp)
                yield
                P.act(sm[:, 40:48], sm[:, 32:40], AF.Ln, bias=cst[:, 1:2], scale=1.0)
                yield
                P.tt("dve", gf[:, :], sm[:, 40:48], negA[:, 16:24], ALU.mult)
                yield
                yield
                P.sec("F:beta/g")
                yield from small_scan_terms_g(gf[:, :], 8, typ, PM[:, 64:88], ex)
                P.ts("dve", sm[:, 48:56], ex[:, 0:8], -1.0, None, ALU.mult)
                yield
                yield
                for half in range(2):
                    P.sec("F:l2norm")
                    src = qk[:, half * 8:(half + 1) * 8, :]
                    P.tt("pool", hb[:, :].rearrange("p (c t) -> p c t", c=8), src, src, ALU.mult)
                    yield
                    for i in range(2):
                        rq = tmpA[:, i * 512:(i + 1) * 512]
                        P.mm(PA[i][:, :], onesb[:, :], hb[:, i * 512:(i + 1) * 512])
                        yield
                        if half == 0:
                            P.act(rq, PA[i][:, :], AF.Sqrt, bias=cst[:, 2:3], scale=128.0)
                            yield
                        else:
                            P.act(rq, PA[i][:, :], AF.Sqrt, bias=cst[:, 0:1], scale=1.0)
                            yield
                        P.recip(rq, rq)
                        yield
                        dst = qk[:, half * 8 + 4 * i:half * 8 + 4 * i + 4, :]
                        P.tt("dve", dst, dst, rq.rearrange("p (c t) -> p c t", c=4), ALU.mult)
                        yield
                    yield
                P.sec("F:transposes")
                P.copy("pool", hb[:, :].rearrange("p (c t) -> p c t", c=8), qk[:, 8:16, :])
                yield
                P.trs([(PT[:, j * 128:(j + 1) * 128], qk[:, 8 + j, :]) for j in range(8)], identb[:, :])
                yield
                P.copy("act", fb.ktm[:, :, :], PT[:, :].rearrange("p (h d) -> p h d", h=8))
                yield
                yield
                P.sec("F:transposes")
                P.trs([(PT[:, j * 128:(j + 1) * 128], vfm[:, j, :]) for j in range(8)], identb[:, :])
                yield
                P.copy("act", fb.vtm[:, :, :], PT[:, :].rearrange("p (h d) -> p h d", h=8))
                yield
                if typ == 0:
                    P.tt("pool", SB.rselg[:, :].rearrange("p (h s) -> p h s", h=8),
                         bc(gf[:, :], 2, 16), bc(blk16[:, :], 1, 8), ALU.mult)
                    yield
                    P.mm(PM[:, 128:256], onesf[:, :], SB.rselg[:, :])
                    yield
                    P.act(SB.gendsel[:, :], PM[:, 128:256], AF.Exp)
                    yield
                yield

            def head_gen(h, B, typ, SB, fb):
                mk = masks[typ]
                m_lev = 3 if typ == 0 else 7
                qk, hb, sm, gf, ex, ktm, vtm = fb.qk, fb.hb, fb.sm, fb.gf, fb.ex, fb.ktm, fb.vtm
                kT = qk[:, 8 + h, :]
                qT = qk[:, h, :]
                _st = ["H:decay"]
                P.sec(_st[0])
                P.act(B.L[:, :], mk[:, MSTT, :], AF.Identity, scale=gf[:, h:h + 1])
                yield
                P.mm(B.ps[3][:, :], B.L[:, :], mk[:, MINC, :])
                yield
                P.act(B.Dm[:, :], B.ps[3][:, :], AF.Exp)
                yield
                P.tt("dve", B.DmI[:, :], B.Dm[:, :], mk[:, MINC, :], ALU.mult)
                yield
                P.tt("dve", B.DmS[:, :], B.Dm[:, :], mk[:, MSTR, :], ALU.mult)
                yield
                P.mm(B.ps[0][:, :], kT, hb[:, h * 128:(h + 1) * 128])
                yield
                P.mm(B.ps[1][:, :], kT, qT)
                yield
                P.stt("dve", B.P[0][:, :], B.ps[0][:, :], sm[:, 16 + h:17 + h], B.DmS[:, :], ALU.mult, ALU.mult)
                yield
                P.tt("dve", B.attnT[:, :], B.ps[1][:, :], B.DmI[:, :], ALU.mult)
                yield
                yield
                _st[0] = "H:dbl"
                P.sec(_st[0])
                P.tr(B.ps[2][:, :], B.P[0][:, :], identf[:, :])
                yield
                P.copy("act", B.PT[0][:, :], B.ps[2][:, :])
                yield
                P.tt("dve", B.X[0][:, :], B.P[0][:, :], identf[:, :], ALU.add)
                yield
                yield
                P.sec(_st[0])
                if m_lev > 1:
                    P.mm(B.ps[0][:, :], B.P[0][:, :], B.PT[0][:, :])
                    yield
                    if m_lev > 2:
                        P.mm(B.ps[1][:, :], B.PT[0][:, :], B.P[0][:, :])
                        yield
                    P.copy("act", B.PT[1][:, :], B.ps[0][:, :])
                    yield
                    if m_lev > 2:
                        P.copy("act", B.P[1][:, :], B.ps[1][:, :])
                        yield
                    yield
                    P.sec(_st[0])
                xi = 0
                for j in range(1, m_lev):
                    cur, nx = j % 2, (j + 1) % 2
                    if j + 1 < m_lev:
                        P.mm(B.ps[0][:, :], B.P[cur][:, :], B.PT[cur][:, :])
                        yield
                        if j + 2 < m_lev:
                            P.mm(B.ps[1][:, :], B.PT[cur][:, :], B.P[cur][:, :])
                            yield
                    P.mm(B.ps[2][:, :], B.PT[cur][:, :], B.X[xi][:, :])
                    yield
                    if j + 1 < m_lev:
                        P.copy("act", B.PT[nx][:, :], B.ps[0][:, :])
                        yield
                        if j + 2 < m_lev:
                            P.copy("act", B.P[nx][:, :], B.ps[1][:, :])
                            yield
                    P.tt("dve", B.X[1 - xi][:, :], B.X[xi][:, :], B.ps[2][:, :], ALU.add)
                    yield
                    xi = 1 - xi
                    yield
                    P.sec(_st[0])
                _st[0] = "H:state"
                P.sec(_st[0])
                P.copy("act", B.TTb[:, :], B.X[xi][:, :])
                yield
                if typ == 1:
                    Sf_h, Sb_h = SB.Sf[h], SB.Sb[h]
                    P.mm(B.ps[3][:, :], kT, Sb_h[:, :])
                    yield
                else:
                    P.dma("sp", SB.S0f[:, :, :], st_gdn[:, h, :, :].rearrange("s d e -> d s e"))
                    yield
                    P.copy("act", SB.S0b[:, :, :], SB.S0f[:, :, :])
                    yield
                    for (dstT, srcT) in ((SB.kTm, kT), (SB.qTm, qT)):
                        base = dstT[:, :]
                        dap = bass.AP(base.tensor, base.offset, [[16 * 128, 128], [136, 16], [1, 8]])
                        P.op("pool", (lambda e, dap=dap, src=srcT.rearrange("p (s t) -> p s t", s=16):
                                      e.tensor_copy(out=dap, in_=src)), ins=[srcT], outs=[base])
                        yield
                    P.mmk(B.ps[3][:, :], [(SB.kTm[:, s * 128:(s + 1) * 128], SB.S0b[:, s, :]) for s in range(16)])
                    yield
                P.stt("dve", B.R[:, :], B.ps[3][:, :], sm[:, 48 + h:49 + h], vtm[:, h, :], ALU.mult, ALU.add)
                yield
                P.mm(B.ps[0][:, :], B.TTb[:, :], B.R[:, :])
                yield
                P.act(B.vn[:, :], B.ps[0][:, :], AF.Identity, scale=sm[:, 8 + h:9 + h])
                yield
                yield
                P.sec(_st[0])
                if typ == 1:
                    P.mm(B.ps[1][:, :], qT, Sb_h[:, :])
                    yield
                else:
                    P.mmk(B.ps[1][:, :], [(SB.qTm[:, s * 128:(s + 1) * 128], SB.S0b[:, s, :]) for s in range(16)])
                    yield
                P.mm(B.ps[2][:, :], B.attnT[:, :], B.vn[:, :])
                yield
                P.act(B.ot[:, :], B.ps[1][:, :], AF.Identity, scale=ex[:, h:h + 1])
                yield
                P.tt("dve", otm[:, h * 128:(h + 1) * 128], B.ot[:, :], B.ps[2][:, :], ALU.add)
                yield
                P.act(B.kout[:, :], ktm[:, h, :], AF.Identity, scale=ex[:, 8 + h:9 + h])
                yield
                yield
                P.sec(_st[0])
                if typ == 1:
                    P.mm(B.ps[3][:, :], B.kout[:, :], B.vn[:, :])
                    yield
                    P.stt("dve", Sf_h[:, :], Sf_h[:, :], ex[:, 16 + h:17 + h], B.ps[3][:, :], ALU.mult, ALU.add)
                    yield
                    P.copy("act", Sb_h[:, :], Sf_h[:, :])
                    yield
                else:
                    P.tt("dve", SB.koutm_all[:, :, :], bc(B.kout[:, :], 1, 16), bc(blk16[:, :], 2, 128), ALU.mult)
                    yield
                    for s in range(16):
                        P.mm(LPB[s // 4][:, (s % 4) * 128:(s % 4 + 1) * 128], SB.koutm_all[:, s, :], B.vn[:, :])
                    yield
                    for j4 in range(4):
                        S4 = SB.S0f[:, 4 * j4:4 * j4 + 4, :]
                        gsel = SB.gendsel[:, h * 16 + 4 * j4:h * 16 + 4 * j4 + 4]
                        P.tt("dve", S4, S4, bc(gsel, 2, 128), ALU.mult)
                        yield
                        P.tt("dve", S4, S4, LPB[j4][:, :].rearrange("p (s e) -> p s e", s=4), ALU.add)
                        yield
                    P.dma("sp", o_gdn_s[:, h, :, :].rearrange("s d e -> d s e"), SB.S0f[:, :, :])
                    yield
                yield

            def tail_gen(ti, fb):
                xt = fb.xt
                P.sec("T:onorm")
                P.dma("sp", mixed[:, 0:D], yssd_scr[ti])
                yield
                P.tt("pool", tmpT[:, :], otm[:, :], otm[:, :], ALU.mult)
                yield
                P.red("dve", ss8[:, 0:8], tmpT[:, :].rearrange("p (h d) -> p h d", h=8))
                yield
                P.act(ss8[:, 8:16], ss8[:, 0:8], AF.Sqrt, bias=cst[:, 0:1], scale=1.0 / 128)
                yield
                P.recip(ss8[:, 16:24], ss8[:, 8:16])
                yield
                o3 = otm[:, :].rearrange("p (h d) -> p h d", h=8)
                P.tt("dve", o3, o3, bc(ss8[:, 16:24], 2, 128), ALU.mult)
                yield
                P.tt("dve", o3, o3, bc(gnw[:, :], 1, 8), ALU.mult)
                yield
                P.tt("dve", mixed[:, D:2 * D], otm[:, :], fb.gs[:, :], ALU.mult)
                yield
                yield
                P.sec("T:outproj")
                for half in range(2):
                    P.trs([(PT[:, j * 128:(j + 1) * 128], mixed[:, (half * 8 + j) * 128:(half * 8 + j + 1) * 128])
                           for j in range(8)], identb[:, :])
                    yield
                    P.copy("act", mixedT[:, half * 8:(half + 1) * 8, :], PT[:, :].rearrange("p (c t) -> p c t", c=8))
                    yield
                yield
                P.sec("T:outproj")
                for nb in range(2):
                    P.mmk(PA[nb][:, :], [(mixedT[:, kc, :], wO.col(kc, nb * 512, 512)) for kc in range(16)])
                    yield
                    P.copy("act", tmpT[:, nb * 512:(nb + 1) * 512], PA[nb][:, :])
                    yield
                yield
                P.sec("T:resid")
                P.sqsum(otm[:, :], tmpT[:, :], sttT[:, 0:1])
                yield
                P.act(sttT[:, 1:2], sttT[:, 0:1], AF.Sqrt, bias=cst[:, 0:1], scale=1.0 / D)
                yield
                P.recip(sttT[:, 2:3], sttT[:, 1:2])
                yield
                P.stt("dve", tmpT[:, :], tmpT[:, :], sttT[:, 2:3], modv[:, 2, :], ALU.mult, ALU.mult)
                yield
                P.tt("dve", xt[:, :], tmpT[:, :], xt[:, :], ALU.add)
                yield
                P.dma("sp", x1_scr[ti], xt[:, :])
                yield
                yield

            def speed(g, k):
                while True:
                    for _ in range(k):
                        try:
                            next(g)
                        except StopIteration:
                            return
                    yield

            def run_rr(gens):
                alive = list(gens)
                while alive:
                    nxt = []
                    for gn in alive:
                        try:
                            next(gn)
                            nxt.append(gn)
                        except StopIteration:
                            pass
                    alive = nxt

            def back(ti, typ, SB, fb, nl, side, with_tail=True):
                side = [side] if side is not None else []
                for h0_ in range(0, 8, nl):
                    gens = [head_gen(h0_ + i, lanes[i], typ, SB, fb) for i in range(min(nl, 8 - h0_))]
                    alive = gens + side
                    while any(g in alive for g in gens):
                        nxt = []
                        for gn in alive:
                            try:
                                next(gn)
                                nxt.append(gn)
                            except StopIteration:
                                if gn in side:
                                    side = []
                        alive = nxt
                if with_tail:
                    run_rr([tail_gen(ti, fb)] + side)
                else:
                    run_rr(side)

            class SBufs:
                pass

            with contextlib.ExitStack() as st2s:
                SB = SBufs()
                SB.histS = sb(st2s, "histSg", [128, 24, 16, 3])
                SB.xinS_l = [sb(st2s, "xinS2_%d" % i, [128, 4, 16, 11]) for i in range(2)]
                SB.S0f = sb(st2s, "S0f", [128, 16, 128])
                SB.S0b = sb(st2s, "S0b", [128, 16, 128], BF16)
                SB.kTm = sb(st2s, "kTm", [128, 16 * 128], BF16)
                SB.qTm = sb(st2s, "qTm", [128, 16 * 128], BF16)
                SB.koutm_all = sb(st2s, "koutm_all", [128, 16, 128], BF16)
                SB.rselg = sb(st2s, "rselg", [128, 128])
                SB.gendsel = sb(st2s, "gendsel", [128, 128])
                P.dma("sp", SB.histS[:, :, :, :], hist_gdn[:, :, :, :])
                P.memset("pool", SB.kTm[:, :], 0.0)
                P.memset("pool", SB.qTm[:, :], 0.0)
                run_rr([front_gen(0, 0, SB, fbs[0])])
                back(0, 0, SB, fbs[0], 1, None)
                P.barrier()
                P.emit()
            with contextlib.ExitStack() as st2p:
                SB = SBufs()
                SB.Sf = [sb(st2p, "Sf%d" % h, [128, 128]) for h in range(8)]
                SB.Sb = [sb(st2p, "Sb%d" % h, [128, 128], BF16) for h in range(8)]
                for ln in range(1, P2_NL):
                    make_lane(ln, st2p)
                fbs.append(make_fb(1, st2p))
                for h in range(8):
                    P.memset("pool", SB.Sf[h][:, :], 0.0)
                    P.memset("pool", SB.Sb[h][:, :], 0.0)
                run_rr([front_gen(1, 1, SB, fbs[1])])
                pending_tail = None
                for ti in range(1, NT):
                    parts = []
                    if pending_tail is not None:
                        parts.append(pending_tail)
                    if ti + 1 < NT:
                        parts.append(speed(front_gen(ti + 1, 1, SB, fbs[(ti + 1) % 2]), 2))
                    side = itertools.chain(*parts) if parts else None
                    back(ti, 1, SB, fbs[ti % 2], P2_NL, side, with_tail=False)
                    pending_tail = tail_gen(ti, fbs[ti % 2])
                run_rr([pending_tail])
                for h in range(8):
                    P.dma("sp", o_gdn_p[h * 128:(h + 1) * 128, :], SB.Sf[h][:, :])
                P.barrier()
                P.emit()
        if STOP_AFTER >= 3:
          with contextlib.ExitStack() as st3:
            PB = [st3.enter_context(nc.psum_tensor("pb%d" % i, [128, 512], F32)) for i in range(7)]
            gb = [0, 512, 1024, 1536, 2048, 2560, 2816]
            wG = WBlocks(st3, "wG", 8, gb + [DFF + b for b in gb[1:]])
            w_gu_v = w_gu.rearrange("(kc p) n -> p kc n", p=128)
            wG.load(w_gu_v, 0, order=[j for i in range(6) for j in (i, 6 + i)])
            wD = WBlocks(st3, "wD", 22, [0, 512, 1024])
            w_dn_v = w_dn.rearrange("(kc p) n -> p kc n", p=128)
            wD.load(w_dn_v, 0)
            modv = sb(st3, "modv3", [128, 3, D])
            xt = [sb(st3, "xt3_%d" % i, [128, D]) for i in range(2)]
            tmpA = [sb(st3, "tmpA3_%d" % i, [128, D]) for i in range(2)]
            tmpB = sb(st3, "tmpB3", [128, D])
            hb = [sb(st3, "hb3_%d" % i, [128, D], BF16) for i in range(2)]
            hT = [sb(st3, "hT3_%d" % i, [128, 8, 128], BF16) for i in range(2)]
            stt_ = [sb(st3, "stt3_%d" % i, [128, 4]) for i in range(2)]
            sg = [sb(st3, "sg%d" % i, [128, 512]) for i in range(2)]
            hid = sb(st3, "hid", [128, DFF], BF16)
            hidT = sb(st3, "hidT", [128, 22, 128], BF16)

            def ffn_gen(ti):
                typ = 0 if ti == 0 else 1
                pr = ti % 2
                x_, tA, hb_, hT_, st_ = xt[pr], tmpA[pr], hb[pr], hT[pr], stt_[pr]
                if ti <= 1:
                    P.dma("sp", modv[:, :, :], modscr[typ].rearrange("p (s d) -> p s d", s=6)[:, 3:6, :])
                P.dma("sp", x_[:, :], x1_scr[ti])
                yield
                yield from norm_to_T_g(x_, modv, 1, 0, tA, hb_, hT_, st_)
                for i in range(6):
                    w = 512 if i < 5 else 256
                    pg_, pu_ = PB[(2 * i) % 6], PB[(2 * i + 1) % 6]
                    P.mmk(pg_[:, 0:w], [(hT_[:, kc, :], wG.col(kc, i * 512, w)) for kc in range(8)])
                    yield
                    P.mmk(pu_[:, 0:w], [(hT_[:, kc, :], wG.col(kc, DFF + i * 512, w)) for kc in range(8)])
                    yield
                    s_ = sg[i % 2]
                    P.act(s_[:, 0:w], pg_[:, 0:w], AF.Silu)
                    yield
                    P.tt("dve", hid[:, i * 512:i * 512 + w], s_[:, 0:w], pu_[:, 0:w], ALU.mult)
                    yield
                yield "HALF"
                for grp in range(3):
                    n = 8 if grp < 2 else 6
                    P.trs([(PT[:, j * 128:(j + 1) * 128], hid[:, (grp * 8 + j) * 128:(grp * 8 + j + 1) * 128])
                           for j in range(n)], identb[:, :])
                    yield
                    P.copy("act", hidT[:, grp * 8:grp * 8 + n, :],
                           PT[:, 0:n * 128].rearrange("p (c t) -> p c t", c=n))
                    yield
                for nb in range(2):
                    P.mmk(PB[6][:, :], [(hidT[:, kc, :], wD.col(kc, nb * 512, 512)) for kc in range(22)])
                    yield
                    P.copy("act", tA[:, nb * 512:(nb + 1) * 512], PB[6][:, :])
                    yield
                P.sqsum(tmpB[:, :], tA[:, :], st_[:, 0:1])
                yield
                P.act(st_[:, 1:2], st_[:, 0:1], AF.Sqrt, bias=cst[:, 0:1], scale=1.0 / D)
                yield
                P.recip(st_[:, 2:3], st_[:, 1:2])
                yield
                P.stt("dve", tA[:, :], tA[:, :], st_[:, 2:3], modv[:, 2, :], ALU.mult, ALU.mult)
                yield
                P.tt("dve", x_[:, :], tA[:, :], x_[:, :], ALU.add)
                yield
                P.dma("sp", y_all[ti], x_[:, :])
                yield

            run_gen(ffn_gen(0))
            nxt = 2
            cur = ffn_gen(1)
            young = None
            while cur is not None:
                try:
                    v = next(cur)
                    if v == "HALF" and young is None and nxt < NT:
                        young = ffn_gen(nxt)
                        nxt += 1
                except StopIteration:
                    cur, young = young, None
                    if cur is None and nxt < NT:
                        cur = ffn_gen(nxt)
                        nxt += 1
                    continue
                if young is not None:
                    try:
                        v2 = next(young)
                        if v2 == "HALF":
                            pass
                    except StopIteration:
                        young = None
            P.barrier()
            P.emit()
    return nc


def _prep_inputs(inp):
    f = lambda a: np.ascontiguousarray(np.asarray(a, dtype=np.float32))
    xp, xs = f(inp["x_prompt"]), f(inp["x_sample"])
    cp, cs = f(inp["c_prompt"]), f(inp["c_sample"])
    bcast = lambda v: np.ascontiguousarray(np.broadcast_to(np.asarray(v, np.float32).reshape(1, -1), (128, v.size)))
    shared = {}
    shared["w_ada"] = f(inp["w_ada"][0])
    shared["b_ada_b"] = bcast(inp["b_ada"][0])
    shared["normvecs"] = np.ascontiguousarray(np.stack(
        [bcast(inp[k][0]) for k in ("norm_mix_pre", "norm_mix_post", "norm_ffn_pre", "norm_ffn_post")], axis=1))
    shared["w_in"] = f(inp["w_in"][0])
    shared["w_out"] = f(inp["w_out"][0])
    shared["w_gu"] = f(inp["w_gate_up"][0])
    shared["w_dn"] = f(inp["w_down"][0])
    scw = np.concatenate([f(inp["ssd_conv_w"][0]), f(inp["ssd_conv_b"][0])[None, :]], axis=0)
    shared["ssd_cw"] = np.ascontiguousarray(scw.reshape(5, 12, 128).transpose(2, 1, 0))
    shared["gdn_cw"] = np.ascontiguousarray(f(inp["gdn_conv_w"][0]).reshape(4, 24, 128).transpose(2, 1, 0))
    sv = np.concatenate([f(inp["ssd_dt_bias"][0]), f(inp["ssd_A_log"][0]), f(inp["ssd_D"][0]),
                         f(inp["gdn_dt_bias"][0]), f(inp["gdn_A_log"][0])])
    shared["smallv"] = bcast(sv)
    shared["ssd_nw"] = bcast(inp["ssd_norm_w"][0])
    shared["gdn_nw"] = bcast(inp["gdn_norm_w"][0])
    shared["ident"] = np.eye(128, dtype=np.float32)
    idx = np.arange(128)
    m = np.zeros((2, 128, 4, 128), np.float32)
    for typ, bs in ((0, 8), (1, 128)):
        same = (idx[:, None] // bs) == (idx[None, :] // bs)
        m[typ, :, 0, :] = same & (idx[:, None] > idx[None, :])
        m[typ, :, 1, :] = same & (idx[:, None] <= idx[None, :])
        m[typ, :, 2, :] = same & (idx[:, None] < idx[None, :])
        m[typ, :, 3, :] = same
    shared["masks"] = m
    shared["blk16"] = np.ascontiguousarray(((idx[:, None] // 8) == np.arange(16)[None, :]).astype(np.float32))
    maps = []
    for i in range(NCORES):
        d = dict(shared)
        sl = slice(16 * i, 16 * (i + 1))
        d["xs_all"] = np.ascontiguousarray(np.concatenate(
            [xs[sl].reshape(1, 128, D), xp[i].reshape(16, 128, D)], axis=0))
        d["cexp"] = np.ascontiguousarray(np.stack(
            [np.repeat(cs[sl], 8, axis=0), np.broadcast_to(cp[i][None, :], (128, D))], axis=0))
        d["st_ssd"] = np.ascontiguousarray(f(inp["state_ssd"][0, sl]).reshape(16, 1024, 128))
        hs = f(inp["state_ssd_conv"][0, sl])
        d["hist_ssd"] = np.ascontiguousarray(hs.reshape(16, 3, 12, 128).transpose(3, 2, 0, 1))
        d["st_gdn"] = np.ascontiguousarray(f(inp["state_gdn"][0, sl]))
        hg = f(inp["state_gdn_conv"][0, sl])
        d["hist_gdn"] = np.ascontiguousarray(hg.reshape(16, 3, 24, 128).transpose(3, 2, 0, 1))
        maps.append(d)
    return maps


def kernel(**inp):
    maps = _prep_inputs(inp)
    nc = build_program()
    res = run_bass_kernel_spmd(nc, maps, core_ids=list(range(NCORES)))
    R = res.results
    cat = lambda k: np.stack([np.asarray(r[k]) for r in R], axis=0)
    y_all = cat("y_all")
    y_prompt = y_all[:, 1:].reshape(8, 2048, D)
    y_sample = y_all[:, 0].reshape(128, 8, D)
    ssd_p = cat("o_ssd_p").reshape(1, 8, 16, 64, 128)
    ssdc_p = cat("o_ssdc_p").reshape(1, 8, 3, 1536)
    gdn_p = cat("o_gdn_p").reshape(1, 8, 8, 128, 128)
    gdnc_p = cat("o_gdnc_p").reshape(1, 8, 3, 3072)
    ssd_s = cat("o_ssd_s").reshape(1, 128, 16, 64, 128)
    ssdc_s = cat("o_ssdc_s").reshape(1, 128, 3, 1536)
    gdn_s = cat("o_gdn_s").reshape(1, 128, 8, 128, 128)
    gdnc_s = cat("o_gdnc_s").reshape(1, 128, 3, 3072)
    outs = (y_prompt, y_sample, ssd_p, ssdc_p, gdn_p, gdnc_p, ssd_s, ssdc_s, gdn_s, gdnc_s)
    return tuple(np.ascontiguousarray(o, dtype=np.float32) for o in outs)
```

```python
import contextlib
import itertools
import os
import numpy as np
import concourse.bass as bass
import concourse.mybir as mybir
from concourse.bass_utils import run_bass_kernel_spmd

F32 = mybir.dt.float32
BF16 = mybir.dt.bfloat16
AF = mybir.ActivationFunctionType
ALU = mybir.AluOpType
AX = mybir.AxisListType

NCORES = 8
D = 1024
NT = 17
DFF = 2816
P2_MODE = 0
ANNOTATE = bool(int(os.environ.get('K_ANN', '0')))
SKIP1 = False
P2_STAGE = 99
P2_HSTEP = 99
P2_SKIP = ()
P2_NL = 4
P2_TILES = 16
STOP_AFTER = int(os.environ.get('K_STOP', '99'))


class Prog:
    ENGS = ("pe", "act", "dve", "pool", "sp")

    def __init__(self, nc, stack):
        self.nc = nc
        self.sems = {}
        for e in self.ENGS:
            self.sems[e] = stack.enter_context(nc.semaphore("s_" + e))
        self.dsems = {"sp": [], "pool": [], "act": []}
        for q, n in (("sp", 8), ("pool", 4), ("act", 2)):
            for j in range(n):
                nm = "d_%s%d" % (q, j)
                self.sems[nm] = stack.enter_context(nc.semaphore(nm))
                self.dsems[q].append(nm)
        self.cnt = {k: 0 for k in self.sems}
        self.known = {e: {} for e in self.ENGS}
        self.lastw = {}
        self.readers = {}
        self.ops = {e: [] for e in self.ENGS}
        self.rr = {"sp": 0, "pool": 0, "act": 0}
        self.nops = 0
        self.tag = "init"

    def sec(self, name):
        self.tag = name

    fine = {}

    def _keys(self, aps):
        ks = []
        for a in aps:
            if a is None or isinstance(a, (int, float)):
                continue
            if isinstance(a, str):
                ks.append(a)
                continue
            nm = a.name
            if nm in self.fine:
                row, gr = self.fine[nm]
                nm = "%s:%d" % (nm, (int(a.offset) % row) // gr)
            ks.append(nm)
        return ks

    def _deps(self, eng, r, w):
        need = {}

        def add(c):
            if c is None:
                return
            s, v = c
            if s == "pe" and eng == "pe":
                return
            if need.get(s, 0) < v:
                need[s] = v

        for k in r:
            add(self.lastw.get(k))
        for k in w:
            add(self.lastw.get(k))
            for c in self.readers.get(k, {}).items():
                add(c)
        waits = []
        kn = self.known[eng]
        for s, v in need.items():
            if kn.get(s, 0) < v:
                kn[s] = v
                waits.append((s, v))
        return waits

    def _commit(self, c, r, w):
        for k in w:
            self.lastw[k] = c
            self.readers[k] = {}
        for k in r:
            d = self.readers.setdefault(k, {})
            if d.get(c[0], 0) < c[1]:
                d[c[0]] = c[1]

    def op(self, eng, fn, ins=(), outs=()):
        r = self._keys(ins)
        w = self._keys(outs)
        w = w + [k for k in r if k.startswith("ps") or k.startswith("pa") or k.startswith("pm")
                 or k.startswith("lpb") or k.startswith("pb")]
        waits = self._deps(eng, r, w)
        self.cnt[eng] += 1
        self.ops[eng].append((waits, fn, eng, 1, self.tag))
        self._commit((eng, self.cnt[eng]), r, w)
        self.nops += 1

    def dma(self, q, out, in_, extra_ins=(), extra_outs=()):
        r = self._keys([in_] + list(extra_ins))
        w = self._keys([out] + list(extra_outs))
        waits = self._deps(q, r, w)
        sems = self.dsems[q]
        j = self.rr[q]
        self.rr[q] = (j + 1) % len(sems)
        nm = sems[j]
        prev = self.cnt[nm]
        if prev > 0 and self.known[q].get(nm, 0) < prev:
            self.known[q][nm] = prev
            waits.append((nm, prev))
        self.cnt[nm] += 16
        self.ops[q].append((waits, lambda e: e.dma_start(out=out, in_=in_), nm, 16, self.tag))
        self._commit((nm, self.cnt[nm]), r, w)
        self.nops += 1

    def barrier(self):
        for e in self.ENGS:
            waits = []
            for s, v in self.cnt.items():
                if v > 0 and self.known[e].get(s, 0) < v:
                    self.known[e][s] = v
                    waits.append((s, v))
            self.ops[e].append((waits, None, None, 0, self.tag))
        self.lastw = {}
        self.readers = {}

    def emit(self):
        nc = self.nc
        with nc.Block() as block:
            for ename, deco in (("sp", block.sync), ("act", block.scalar), ("dve", block.vector),
                                ("pool", block.gpsimd), ("pe", block.tensor)):
                ops = self.ops[ename]

                def body(e, ops=ops):
                    for waits, fn, sname, inc, tag in ops:
                        for s, v in waits:
                            e.wait_ge(self.sems[s], v)
                        if fn is not None:
                            ins = fn(e)
                            ins.then_inc(self.sems[sname], inc)
                            if ANNOTATE:
                                ins.annotate(tag)

                deco(body)
                self.ops[ename] = []

    def mm(self, out, lhsT, rhs, start=True, stop=True):
        self.op("pe", lambda e: e.matmul(out, lhsT=lhsT, rhs=rhs, start=start, stop=stop),
                ins=[lhsT, rhs], outs=[out])

    def mmk(self, out, pairs):
        n = len(pairs)

        def fn(e):
            ins = None
            for i, (l, r) in enumerate(pairs):
                ins = e.matmul(out, lhsT=l, rhs=r, start=(i == 0), stop=(i == n - 1))
            return ins

        self.op("pe", fn, ins=[x for p in pairs for x in p], outs=[out])

    def tr(self, out, in_, ident):
        self.op("pe", lambda e: e.transpose(out, in_, ident), ins=[in_, ident], outs=[out])

    def trs(self, items, ident):
        def fn(e):
            ins = None
            for o, i in items:
                ins = e.transpose(o, i, ident)
            return ins
        self.op("pe", fn, ins=[i for _, i in items] + [ident], outs=[o for o, _ in items])

    def act(self, out, in_, func, bias=None, scale=None):
        kw = {}
        if bias is not None:
            kw["bias"] = bias
        if scale is not None:
            kw["scale"] = scale
        self.op("act", lambda e: e.activation(out=out, in_=in_, func=func, **kw),
                ins=[in_, bias, scale], outs=[out])

    def sqsum(self, junk, in_, accum):
        self.op("act", lambda e: e.activation(out=junk, in_=in_, func=AF.Square, accum_out=accum),
                ins=[in_], outs=[junk, accum])

    def tt(self, eng, out, in0, in1, op):
        self.op(eng, lambda e: e.tensor_tensor(out=out, in0=in0, in1=in1, op=op), ins=[in0, in1], outs=[out])

    def ts(self, eng, out, in0, s1, s2, op0, op1=None):
        if op1 is None:
            self.op(eng, lambda e: e.tensor_scalar(out=out, in0=in0, scalar1=s1, scalar2=None, op0=op0),
                    ins=[in0, s1], outs=[out])
        else:
            self.op(eng, lambda e: e.tensor_scalar(out=out, in0=in0, scalar1=s1, scalar2=s2, op0=op0, op1=op1),
                    ins=[in0, s1, s2], outs=[out])

    def stt(self, eng, out, in0, scalar, in1, op0, op1):
        self.op(eng, lambda e: e.scalar_tensor_tensor(out=out, in0=in0, scalar=scalar, in1=in1, op0=op0, op1=op1),
                ins=[in0, scalar, in1], outs=[out])

    def copy(self, eng, out, in_):
        if eng == "act":
            self.op(eng, lambda e: e.copy(out=out, in_=in_), ins=[in_], outs=[out])
        else:
            self.op(eng, lambda e: e.tensor_copy(out=out, in_=in_), ins=[in_], outs=[out])

    def red(self, eng, out, in_):
        self.op(eng, lambda e: e.tensor_reduce(out=out, in_=in_, axis=AX.X, op=ALU.add), ins=[in_], outs=[out])

    def recip(self, out, in_):
        self.op("dve", lambda e: e.reciprocal(out=out, in_=in_), ins=[in_], outs=[out])

    def memset(self, eng, ap, val):
        self.op(eng, lambda e: e.memset(ap, val), ins=[], outs=[ap])


def bc(ap, axis, n):
    u = ap.unsqueeze(axis)
    shp = list(u.shape)
    shp[axis] = n
    return u.broadcast_to(shp)


def build_program():
    nc = bass.Bass("TRN2", target_bir_lowering=False)

    def din(name, shape, dt=F32):
        return nc.dram_tensor(name, list(shape), dt, kind="ExternalInput").ap()

    def dout(name, shape, dt=F32):
        return nc.dram_tensor(name, list(shape), dt, kind="ExternalOutput").ap()

    def dscr(name, shape, dt=F32):
        return nc.dram_tensor(name, list(shape), dt).ap()

    xs_all = din("xs_all", [NT, 128, D])
    cexp = din("cexp", [2, 128, D])
    w_ada = din("w_ada", [D, 6 * D])
    b_ada_b = din("b_ada_b", [128, 6 * D])
    normvecs = din("normvecs", [128, 4, D])
    w_in = din("w_in", [D, 6688])
    w_out = din("w_out", [2 * D, D])
    w_gu = din("w_gu", [D, 2 * DFF])
    w_dn = din("w_dn", [DFF, D])
    ssd_cw = din("ssd_cw", [128, 12, 5])
    gdn_cw = din("gdn_cw", [128, 24, 4])
    smallv = din("smallv", [128, 64])
    ssd_nw = din("ssd_nw", [128, D])
    gdn_nw = din("gdn_nw", [128, 128])
    st_ssd = din("st_ssd", [16, 1024, 128])
    hist_ssd = din("hist_ssd", [128, 12, 16, 3])
    st_gdn = din("st_gdn", [16, 8, 128, 128])
    hist_gdn = din("hist_gdn", [128, 24, 16, 3])
    ident_d = din("ident", [128, 128])
    masks_d = din("masks", [2, 128, 4, 128])
    blk16_d = din("blk16", [128, 16])

    y_all = dout("y_all", [NT, 128, D])
    o_ssd_p = dout("o_ssd_p", [1024, 128])
    o_ssdc_p = dout("o_ssdc_p", [3, 1536])
    o_gdn_p = dout("o_gdn_p", [1024, 128])
    o_gdnc_p = dout("o_gdnc_p", [3, 3072])
    o_ssd_s = dout("o_ssd_s", [16, 1024, 128])
    o_ssdc_s = dout("o_ssdc_s", [48, 1536])
    o_gdn_s = dout("o_gdn_s", [16, 8, 128, 128])
    o_gdnc_s = dout("o_gdnc_s", [48, 3072])

    modscr = dscr("modscr", [2, 128, 6 * D])
    yssd_scr = dscr("yssd_scr", [NT, 128, D], BF16)
    x1_scr = dscr("x1_scr", [NT, 128, D])

    with contextlib.ExitStack() as gstack:
        P = Prog(nc, gstack)

        def sb(stack, name, shape, dt=F32):
            return stack.enter_context(nc.sbuf_tensor(name, list(shape), dt))

        class WBlocks:
            def __init__(self, stk, name, nk, bounds):
                self.bounds = bounds
                self.t = [sb(stk, "%s_%d" % (name, j), [128, nk, bounds[j + 1] - bounds[j]], BF16)
                          for j in range(len(bounds) - 1)]

            def load(self, dview, col_off, order=None):
                for j in (order if order is not None else range(len(self.t))):
                    b0, b1 = self.bounds[j], self.bounds[j + 1]
                    P.dma("pool", self.t[j][:, :, :], dview[:, :, col_off + b0:col_off + b1])

            def col(self, kc, c0, w):
                for j in range(len(self.t)):
                    if self.bounds[j] <= c0 and c0 + w <= self.bounds[j + 1]:
                        return self.t[j][:, kc, c0 - self.bounds[j]:c0 - self.bounds[j] + w]
                raise ValueError("column range straddles weight blocks")

        PT = gstack.enter_context(nc.psum_tensor("pst", [128, 1024], BF16))
        ps1 = contextlib.ExitStack()
        PS = [ps1.enter_context(nc.psum_tensor("ps%d" % i, [128, 512], F32)) for i in range(7)]

        identf = sb(gstack, "identf", [128, 128])
        identb = sb(gstack, "identb", [128, 128], BF16)
        onesf = sb(gstack, "onesf", [128, 128])
        onesb = sb(gstack, "onesb", [128, 128], BF16)
        cst = sb(gstack, "cst", [128, 4])
        masks = [sb(gstack, "masks%d" % t, [128, 4, 128]) for t in range(2)]
        blk16 = sb(gstack, "blk16s", [128, 16])
        smv = sb(gstack, "smv", [128, 64])
        negA = sb(gstack, "negA", [128, 24])
        P.dma("sp", identf[:, :], ident_d[:, :])
        P.dma("pool", identb[:, :], ident_d[:, :])
        for t in range(2):
            P.dma("sp", masks[t][:, :, :], masks_d[t])
        P.dma("sp", blk16[:, :], blk16_d[:, :])
        P.dma("sp", smv[:, :], smallv[:, :])
        P.memset("dve", onesf[:, :], 1.0)
        P.memset("dve", onesb[:, :], 1.0)
        P.memset("dve", cst[:, 0:1], 1e-6)
        P.memset("dve", cst[:, 1:2], 1.0)
        P.memset("dve", cst[:, 2:3], 128e-6)
        P.memset("dve", cst[:, 3:4], 0.0)
        P.act(negA[:, 0:16], smv[:, 16:32], AF.Exp)
        P.act(negA[:, 16:24], smv[:, 56:64], AF.Exp)
        P.ts("dve", negA[:, :], negA[:, :], -1.0, None, ALU.mult)

        MSTT, MINC, MSTR, MBLK = 0, 1, 2, 3

        with contextlib.ExitStack() as st0:
            ct = sb(st0, "ct", [128, D])
            cb = sb(st0, "cb", [128, D], BF16)
            cT = [sb(st0, "cT%d" % t, [128, 8, 128], BF16) for t in range(2)]
            modt = [sb(st0, "modt%d" % t, [128, 6 * D]) for t in range(2)]
            nv = sb(st0, "nv", [128, 4, D])
            wa = [sb(st0, "wa%d" % i, [128, 8, 512], BF16) for i in range(2)]
            bb = [sb(st0, "bb%d" % i, [128, 512]) for i in range(2)]
            P.dma("sp", nv[:, :, :], normvecs[:, :, :])
            for t in range(2):
                P.dma("sp", ct[:, :], cexp[t])
                P.act(cb[:, :], ct[:, :], AF.Silu)
                P.trs([(PT[:, c * 128:(c + 1) * 128], cb[:, c * 128:(c + 1) * 128]) for c in range(8)], identb[:, :])
                P.copy("dve", cT[t][:, :, :], PT[:, :].rearrange("p (c t) -> p c t", c=8))
            wv = w_ada.rearrange("(kc p) n -> p kc n", p=128)
            for j in range(12):
                P.dma("pool", wa[j % 2][:, :, :], wv[:, :, j * 512:(j + 1) * 512])
                P.dma("sp", bb[j % 2][:, :], b_ada_b[:, j * 512:(j + 1) * 512])
                for t in range(2):
                    ps = PS[(2 * j + t) % 4]
                    P.mmk(ps[:, :], [(cT[t][:, kc, :], wa[j % 2][:, kc, :]) for kc in range(8)])
                    P.tt("dve", modt[t][:, j * 512:(j + 1) * 512], ps[:, :], bb[j % 2][:, :], ALU.add)
            for t in range(2):
                m = modt[t]
                P.stt("dve", m[:, D:2 * D], m[:, D:2 * D], 1.0, nv[:, 0, :], ALU.add, ALU.mult)
                P.tt("dve", m[:, 2 * D:3 * D], m[:, 2 * D:3 * D], nv[:, 1, :], ALU.mult)
                P.stt("dve", m[:, 4 * D:5 * D], m[:, 4 * D:5 * D], 1.0, nv[:, 2, :], ALU.add, ALU.mult)
                P.tt("dve", m[:, 5 * D:6 * D], m[:, 5 * D:6 * D], nv[:, 3, :], ALU.mult)
                P.dma("sp", modscr[t], m[:, :])
            P.barrier()
            P.emit()

        def norm_to_T(xt, modv, sidx, hidx, tmpA, hb, hT, stt_):
            P.sec("norm_to_T")
            P.sqsum(tmpA[:, :], xt[:, :], stt_[:, 0:1])
            P.act(stt_[:, 1:2], stt_[:, 0:1], AF.Sqrt, bias=cst[:, 0:1], scale=1.0 / D)
            P.recip(stt_[:, 2:3], stt_[:, 1:2])
            P.stt("dve", tmpA[:, :], xt[:, :], stt_[:, 2:3], modv[:, sidx, :], ALU.mult, ALU.mult)
            P.tt("dve", hb[:, :], tmpA[:, :], modv[:, hidx, :], ALU.add)
            P.trs([(PT[:, c * 128:(c + 1) * 128], hb[:, c * 128:(c + 1) * 128]) for c in range(8)], identb[:, :])
            P.copy("act", hT[:, :, :], PT[:, :].rearrange("p (c t) -> p c t", c=8))

        def norm_to_T_g(xt, modv, sidx, hidx, tmpA, hb, hT, stt_):
            P.sec("norm_to_T")
            P.sqsum(tmpA[:, :], xt[:, :], stt_[:, 0:1])
            yield
            P.act(stt_[:, 1:2], stt_[:, 0:1], AF.Sqrt, bias=cst[:, 0:1], scale=1.0 / D)
            yield
            P.recip(stt_[:, 2:3], stt_[:, 1:2])
            yield
            P.stt("dve", tmpA[:, :], xt[:, :], stt_[:, 2:3], modv[:, sidx, :], ALU.mult, ALU.mult)
            yield
            P.tt("dve", hb[:, :], tmpA[:, :], modv[:, hidx, :], ALU.add)
            yield
            P.trs([(PT[:, c * 128:(c + 1) * 128], hb[:, c * 128:(c + 1) * 128]) for c in range(8)], identb[:, :])
            yield
            P.copy("act", hT[:, :, :], PT[:, :].rearrange("p (c t) -> p c t", c=8))
            yield

        def small_scan_terms(af, nh, typ, pss, ex):
            mk = masks[typ]
            P.mm(pss[:, 0:nh], mk[:, MINC, :], af)
            P.mm(pss[:, nh:2 * nh], mk[:, MSTT, :], af)
            P.mm(pss[:, 2 * nh:3 * nh], mk[:, MBLK, :], af)
            P.act(ex[:, 0:3 * nh], pss[:, 0:3 * nh], AF.Exp)

        def small_scan_terms_g(af, nh, typ, pss, ex):
            mk = masks[typ]
            P.mm(pss[:, 0:nh], mk[:, MINC, :], af)
            yield
            P.mm(pss[:, nh:2 * nh], mk[:, MSTT, :], af)
            yield
            P.mm(pss[:, 2 * nh:3 * nh], mk[:, MBLK, :], af)
            yield
            P.act(ex[:, 0:3 * nh], pss[:, 0:3 * nh], AF.Exp)
            yield

        def proj_conv_gen(typ, ngroups, wt, wcol0, hT, psb, xinS_l, xinP_l, histS, histP, cwt, has_bias, accs, dst_fn):
            items = []

            def slot(i):
                n = len(items)
                if 0 <= i < n:
                    it = items[i]
                    if has_bias:
                        P.act(it[0], it[2][0], AF.Identity, bias=cwt[:, it[4], 4:5], scale=cwt[:, it[4], 0:1])
                    else:
                        P.act(it[0], it[2][0], AF.Identity, scale=cwt[:, it[4], 0:1])
                    yield
                if 0 <= i - 1 < n:
                    it = items[i - 1]
                    for k in range(1, 4):
                        P.stt("dve", it[0], it[2][k], cwt[:, it[4], k:k + 1], it[0], ALU.mult, ALU.add)
                        yield
                if 0 <= i - 2 < n:
                    it = items[i - 2]
                    P.act(it[3], it[1][:, :], AF.Silu)
                    yield

            for g in range(ngroups):
                ps = psb[g % 2]
                for j in range(4):
                    c = 4 * g + j
                    P.mmk(ps[:, j * 128:(j + 1) * 128],
                          [((wt.col(kc, wcol0 + c * 128, 128) if hasattr(wt, "col")
                             else wt[:, kc, wcol0 + c * 128:wcol0 + (c + 1) * 128]), hT[:, kc, :]) for kc in range(8)])
                    yield
                if typ == 0:
                    xin = xinS_l[g % 2]
                    P.copy("pool", xin[:, :, :, 0:3], histS[:, 4 * g:4 * g + 4, :, :])
                    yield
                    P.copy("act", xin[:, :, :, 3:11], ps[:, :].rearrange("p (j s t) -> p j s t", j=4, s=16))
                    yield
                else:
                    xin = xinP_l[g % 2]
                    P.copy("pool", xin[:, :, 0:3], histP[:, 4 * g:4 * g + 4, :])
                    yield
                    P.copy("act", xin[:, :, 3:131], ps[:, :].rearrange("p (j t) -> p j t", j=4))
                    yield
                    P.copy("pool", histP[:, 4 * g:4 * g + 4, :], xin[:, :, 128:131])
                    yield
                for j in range(4):
                    c = 4 * g + j
                    a_ = accs[c % 4]
                    if typ == 0:
                        av = a_[:, :].rearrange("p (s t) -> p s t", s=16)
                        sh = [xin[:, j, :, k:k + 8] for k in range(4)]
                    else:
                        av = a_[:, :]
                        sh = [xin[:, j, k:k + 128] for k in range(4)]
                    items.append((av, a_, sh, dst_fn(c), c))
                    yield from slot(c)
            yield from slot(4 * ngroups)
            yield from slot(4 * ngroups + 1)

        def run_gen(g):
            for _ in g:
                pass

        w_in_v = w_in.rearrange("(kc p) n -> p kc n", p=128)
        if STOP_AFTER >= 1 and not SKIP1:
          with contextlib.ExitStack() as st1:
            wI = WBlocks(st1, "wI", 8, [0, 512, 1024, 1536, 2048, 2560, 2576])
            wI.load(w_in_v, 0, order=[2, 3, 4, 0, 1, 5])
            modv = sb(st1, "modv", [128, 3, D])
            cw = sb(st1, "cw", [128, 12, 5])
            nw = sb(st1, "ssdnw", [128, D])
            histS = sb(st1, "histS", [128, 12, 16, 3])
            histP = sb(st1, "histP", [128, 12, 3])
            P.dma("sp", cw[:, :, :], ssd_cw[:, :, :])
            P.dma("sp", nw[:, :], ssd_nw[:, :])
            P.dma("sp", histS[:, :, :, :], hist_ssd[:, :, :, :])
            P.memset("pool", histP[:, :, :], 0.0)
            xt = [sb(st1, "xt%d" % i, [128, D]) for i in range(2)]
            tmpA = sb(st1, "tmpA", [128, D])
            hb = sb(st1, "hb", [128, D], BF16)
            hT = sb(st1, "hT", [128, 8, 128], BF16)
            stt_ = sb(st1, "stt", [128, 4])
            xinS_l = [sb(st1, "xinS%d" % i, [128, 4, 16, 11]) for i in range(2)]
            xinP_l = [sb(st1, "xinP%d" % i, [128, 4, 131]) for i in range(2)]
            acc = [sb(st1, "acc%d" % i, [128, 128]) for i in range(4)]
            xc = sb(st1, "xc", [128, 12, 128], BF16)
            zs = sb(st1, "zs", [128, D], BF16)
            sm = sb(st1, "sm", [128, 48])
            af = sb(st1, "af", [128, 16])
            ex = sb(st1, "ex", [128, 48])
            xtm = sb(st1, "xtm", [128, D], BF16)
            xdt = sb(st1, "xdt", [128, D], BF16)
            xDs = sb(st1, "xDs", [128, D], BF16)
            xw = sb(st1, "xw", [128, D], BF16)
            Btm = sb(st1, "Btm", [128, 2, 128], BF16)
            CBm = sb(st1, "CBm", [128, 2, 128])
            L4 = [sb(st1, "L4_%d" % i, [128, 4, 128]) for i in range(2)]
            E4 = [sb(st1, "E4_%d" % i, [128, 4, 128]) for i in range(2)]
            M4 = [sb(st1, "M4_%d" % i, [128, 4, 128], BF16) for i in range(2)]
            yo = sb(st1, "yo", [128, D])
            yy = sb(st1, "yy", [128, D])
            ss = sb(st1, "ss", [128, 8])
            ysb = [sb(st1, "ysb%d" % i, [128, D], BF16) for i in range(2)]
            hTf = sb(st1, "hTf", [128, D])
            hTb = sb(st1, "hTb", [128, D], BF16)
            lhs3 = sb(st1, "lhs3", [128, 8, 48], BF16)
            cv = sb(st1, "cv", [48, 1536])
            CmT = [sb(st1, "CmT%d" % g, [128, 16 * 128], BF16) for g in range(2)]
            h0 = [sb(st1, "h0_%d" % i, [128, 8, 128]) for i in range(2)]
            h0Tb = sb(st1, "h0Tb", [128, D], BF16)
            xwm = sb(st1, "xwm", [128, D], BF16)
            hn = [sb(st1, "hn%d" % i, [128, 8, 128]) for i in range(2)]
            rsel = sb(st1, "rsel", [128, 2, 128])
            decsel = sb(st1, "decsel", [128, 128])
            P.memset("pool", hTf[:, :], 0.0)
            P.memset("pool", hTb[:, :], 0.0)
            for g in range(2):
                P.memset("pool", CmT[g][:, :], 0.0)

            for ti in range(NT):
                typ = 0 if ti == 0 else 1
                mk = masks[typ]
                if ti <= 1:
                    P.dma("sp", modv[:, :, :], modscr[typ].rearrange("p (s d) -> p s d", s=6)[:, 0:3, :])
                x_ = xt[ti % 2]
                P.dma("sp", x_[:, :], xs_all[ti])
                norm_to_T(x_, modv, 1, 0, tmpA, hb, hT, stt_)
                run_gen(proj_conv_gen(typ, 3, wI, 1024, hT, PS[0:2], xinS_l, xinP_l, histS, histP, cw, True, acc,
                                      lambda c: xc[:, c, :]))
                P.sec("z (token-major) -> silu")
                for nb in range(2):
                    ps = PS[2 + nb]
                    P.mmk(ps[:, :], [(hT[:, kc, :], wI.col(kc, nb * 512, 512)) for kc in range(8)])
                    P.act(zs[:, nb * 512:(nb + 1) * 512], ps[:, :], AF.Silu)
                P.sec("dt")
                pd = PS[4]
                P.mmk(pd[:, 0:16], [(hT[:, kc, :], wI.col(kc, 2560, 16)) for kc in range(8)])
                P.tt("dve", sm[:, 0:16], pd[:, 0:16], smv[:, 0:16], ALU.add)
                P.act(sm[:, 16:32], sm[:, 0:16], AF.Exp)
                P.act(sm[:, 32:48], sm[:, 16:32], AF.Ln, bias=cst[:, 1:2], scale=1.0)
                P.tt("dve", af[:, :], sm[:, 32:48], negA[:, 0:16], ALU.mult)
                small_scan_terms(af[:, :], 16, typ, PS[4][:, 64:112], ex)
                P.sec("token-major x, xdt, xD, xw, B")
                P.trs([(PT[:, c * 128:(c + 1) * 128], xc[:, c, :]) for c in range(8)], identb[:, :])
                P.copy("act", xtm[:, :], PT[:, :])
                x3 = xtm[:, :].rearrange("p (h d) -> p h d", h=16)
                P.tt("dve", xdt[:, :].rearrange("p (h d) -> p h d", h=16), x3, bc(sm[:, 32:48], 2, 64), ALU.mult)
                P.tt("pool", xDs[:, :].rearrange("p (h d) -> p h d", h=16), x3, bc(smv[:, 32:48], 2, 64), ALU.mult)
                P.tt("pool", xw[:, :].rearrange("p (h d) -> p h d", h=16),
                     xdt[:, :].rearrange("p (h d) -> p h d", h=16), bc(ex[:, 16:32], 2, 64), ALU.mult)
                P.trs([(PT[:, g * 128:(g + 1) * 128], xc[:, 8 + g, :]) for g in range(2)], identb[:, :])
                P.copy("act", Btm[:, :, :], PT[:, 0:256].rearrange("p (g n) -> p g n", g=2))
                P.sec("CB^T masked")
                pc = PS[5]
                for g in range(2):
                    P.mm(pc[:, g * 128:(g + 1) * 128], xc[:, 8 + g, :], xc[:, 10 + g, :])
                P.tt("dve", CBm[:, :, :], pc[:, 0:256].rearrange("p (g l) -> p g l", g=2), bc(mk[:, MINC, :], 1, 2), ALU.mult)
                P.sec("y_off raw")
                if typ == 1:
                    for g in range(2):
                        P.mm(PS[2 + g][:, :], xc[:, 10 + g, :], hTb[:, g * 512:(g + 1) * 512])
                else:
                    for g in range(2):
                        base = CmT[g][:, :]
                        dst = bass.AP(base.tensor, base.offset, [[16 * 128, 128], [136, 16], [1, 8]])
                        P.op("pool", (lambda e, dst=dst, src=xc[:, 10 + g, :].rearrange("p (s t) -> p s t", s=16):
                                      e.tensor_copy(out=dst, in_=src)), ins=[xc[:, 0, :]], outs=[base])
                    for hh in range(2):
                        P.tt("pool", rsel[:, hh, :].rearrange("p (j s) -> p j s", j=8),
                             bc(af[:, hh:16:2], 2, 16), bc(blk16[:, :], 1, 8), ALU.mult)
                        P.mm(PS[5][:, 256 + hh * 128:256 + (hh + 1) * 128], onesf[:, :], rsel[:, hh, :])
                    P.act(decsel[0:64, :], PS[5][0:64, 256:384], AF.Exp)
                    P.act(decsel[64:128, :], PS[5][64:128, 384:512], AF.Exp)
                    for s in range(16):
                        h0_ = h0[s % 2]
                        hn_ = hn[s % 2]
                        P.dma("sp", h0_[:, :, :], st_ssd[s].rearrange("(j p) n -> p j n", p=128))
                        pa, pb = PS[0], PS[1]
                        P.trs([((pa if j < 4 else pb)[:, (j % 4) * 128:(j % 4 + 1) * 128], h0_[:, j, :]) for j in range(8)],
                              identf[:, :])
                        P.copy("act", h0Tb[:, 0:512], pa[:, :])
                        P.copy("dve", h0Tb[:, 512:1024], pb[:, :])
                        for g in range(2):
                            P.op("pe", (lambda e, o=PS[2 + g][:, :], l=CmT[g][:, s * 128:(s + 1) * 128],
                                        r=h0Tb[:, g * 512:(g + 1) * 512], s=s:
                                        e.matmul(o, lhsT=l, rhs=r, start=(s == 0), stop=(s == 15))),
                                 ins=[CmT[g][:, :], h0Tb[:, :]], outs=[PS[2 + g][:, :]])
                        P.act(xwm[:, :], xw[:, :], AF.Identity, scale=blk16[:, s:s + 1])
                        for half in range(2):
                            pn = PS[half]
                            for jj in range(4):
                                j = half * 4 + jj
                                P.mm(pn[:, jj * 128:(jj + 1) * 128], xwm[:, j * 128:(j + 1) * 128], Btm[:, j // 4, :])
                            for jj in range(4):
                                j = half * 4 + jj
                                P.stt("dve", hn_[:, j, :], h0_[:, j, :], decsel[:, j * 16 + s:j * 16 + s + 1],
                                      pn[:, jj * 128:(jj + 1) * 128], ALU.mult, ALU.add)
                        P.dma("sp", o_ssd_s[s].rearrange("(j p) n -> p j n", p=128), hn_[:, :, :])
                P.sec("per-head intra-chunk")
                for q in range(4):
                    g = q // 2
                    L_, E_, M_ = L4[q % 2], E4[q % 2], M4[q % 2]
                    P.tt("dve", L_[:, :, :], bc(mk[:, MSTT, :], 1, 4), bc(af[:, 4 * q:4 * q + 4], 2, 128), ALU.mult)
                    pg = PS[5 + (q % 2)]
                    for j in range(4):
                        P.mm(pg[:, j * 128:(j + 1) * 128], L_[:, j, :], mk[:, MINC, :])
                    P.act(E_[:, :, :], pg[:, :].rearrange("p (j l) -> p j l", j=4), AF.Exp)
                    P.tt("dve", M_[:, :, :], E_[:, :, :], bc(CBm[:, g, :], 1, 4), ALU.mult)
                    py = PS[0 + g]
                    for j in range(4):
                        h = 4 * q + j
                        hh = h % 8
                        P.mmk(py[:, hh * 64:(hh + 1) * 64],
                              [(identb[:, :], xDs[:, h * 64:(h + 1) * 64]), (M_[:, j, :], xdt[:, h * 64:(h + 1) * 64])])
                P.sec("combine y = y_diag + eacs * y_off")
                for g in range(2):
                    P.copy("act", yo[:, g * 512:(g + 1) * 512], PS[2 + g][:, :])
                P.tt("dve", yo[:, :].rearrange("p (h d) -> p h d", h=16), yo[:, :].rearrange("p (h d) -> p h d", h=16),
                     bc(ex[:, 0:16], 2, 64), ALU.mult)
                for g in range(2):
                    P.tt("dve", yy[:, g * 512:(g + 1) * 512], yo[:, g * 512:(g + 1) * 512], PS[0 + g][:, :], ALU.add)
                P.sec("gate with silu(z), group rmsnorm")
                P.tt("dve", yy[:, :], yy[:, :], zs[:, :], ALU.mult)
                for g in range(2):
                    P.sqsum(yo[:, g * 512:(g + 1) * 512], yy[:, g * 512:(g + 1) * 512], ss[:, g:g + 1])
                P.act(ss[:, 2:4], ss[:, 0:2], AF.Sqrt, bias=cst[:, 0:1], scale=1.0 / 512)
                P.recip(ss[:, 4:6], ss[:, 2:4])
                y_ = ysb[ti % 2]
                for g in range(2):
                    P.stt("dve", y_[:, g * 512:(g + 1) * 512], yy[:, g * 512:(g + 1) * 512], ss[:, 4 + g:5 + g],
                          nw[:, g * 512:(g + 1) * 512], ALU.mult, ALU.mult)
                P.dma("sp", yssd_scr[ti], y_[:, :])
                P.sec("state update (prompt chunks)")
                if typ == 1:
                    for g in range(2):
                        P.mm(PS[2 + g][:, :], Btm[:, g, :], xw[:, g * 512:(g + 1) * 512])
                    h3 = hTf[:, :].rearrange("p (h d) -> p h d", h=16)
                    P.tt("dve", h3, h3, bc(ex[:, 32:48], 2, 64), ALU.mult)
                    for g in range(2):
                        P.tt("dve", hTf[:, g * 512:(g + 1) * 512], hTf[:, g * 512:(g + 1) * 512], PS[2 + g][:, :], ALU.add)
                    P.copy("act", hTb[:, :], hTf[:, :])
                P.sec("conv-state outputs (last 3 raw xbc rows)")
                if ti == 0 or ti == NT - 1:
                    M3 = 48 if typ == 0 else 3
                    if typ == 0:
                        P.copy("pool", lhs3[:, :, :].rearrange("p k (s t) -> p k s t", s=16),
                               hT[:, :, :].rearrange("p k (s t) -> p k s t", s=16)[:, :, :, 5:8])
                    else:
                        P.copy("pool", lhs3[:, :, 0:3], hT[:, :, 125:128])
                    for nb in range(3):
                        ps = PS[5 + (nb % 2)]
                        P.mmk(ps[0:M3, :], [(lhs3[:, kc, 0:M3], wI.col(kc, 1024 + nb * 512, 512))
                                            for kc in range(8)])
                        P.copy("act", cv[0:M3, nb * 512:(nb + 1) * 512], ps[0:M3, :])
                    P.dma("sp", (o_ssdc_s if typ == 0 else o_ssdc_p)[:, :], cv[0:M3, :])
            for half in range(2):
                ps = PS[half]
                P.trs([(ps[:, jj * 128:(jj + 1) * 128], hTf[:, (half * 4 + jj) * 128:(half * 4 + jj + 1) * 128])
                       for jj in range(4)], identf[:, :])
                P.copy("act", hn[half][:, 0:4, :], ps[:, :].rearrange("p (j n) -> p j n", j=4))
                P.dma("sp", o_ssd_p[half * 512:(half + 1) * 512, :].rearrange("(j p) n -> p j n", p=128), hn[half][:, 0:4, :])
            P.barrier()
            P.emit()
        ps1.close()
        if STOP_AFTER >= 2:
          with contextlib.ExitStack() as st2:
            PA = [st2.enter_context(nc.psum_tensor("pa%d" % i, [128, 512], F32)) for i in range(2)]
            PM = st2.enter_context(nc.psum_tensor("pm", [128, 512], F32))
            LPB = [st2.enter_context(nc.psum_tensor("lpb%d" % i, [128, 512], F32)) for i in range(4)]
            LPS = [[LPB[ln][:, i * 128:(i + 1) * 128] for i in range(4)] for ln in range(4)]
            wII = WBlocks(st2, "wII", 8, [0, 512, 1024, 1536, 2048, 2560, 3072, 3584, 4096, 4112])
            wII.load(w_in_v, 2576)
            wO = WBlocks(st2, "wO", 16, [0, 512, 1024])
            w_out_v = w_out.rearrange("(kc p) n -> p kc n", p=128)
            wO.load(w_out_v, 0)
            modv = sb(st2, "modv2", [128, 3, D])
            cwg = sb(st2, "cwg", [128, 24, 4])
            gnw = sb(st2, "gnw", [128, 128])
            histP = sb(st2, "histPg", [128, 24, 3])
            P.dma("sp", cwg[:, :, :], gdn_cw[:, :, :])
            P.dma("sp", gnw[:, :], gdn_nw[:, :])
            P.memset("pool", histP[:, :, :], 0.0)
            tmpA = sb(st2, "tmpA2", [128, D])
            hT = sb(st2, "hT2", [128, 8, 128], BF16)
            stt_ = sb(st2, "stt2", [128, 4])
            xinP_l = [sb(st2, "xinP2_%d" % i, [128, 4, 131]) for i in range(2)]
            acc = [sb(st2, "acc2_%d" % i, [128, 128]) for i in range(4)]
            lhs3 = sb(st2, "lhs3g", [128, 8, 48], BF16)
            tmpT = sb(st2, "tmpT2", [128, D])
            sttT = sb(st2, "sttT2", [128, 4])
            ss8 = sb(st2, "ss8", [128, 24])
            otm = sb(st2, "otm", [128, D])
            mixed = sb(st2, "mixed", [128, 2 * D], BF16)
            mixedT = sb(st2, "mixedT", [128, 16, 128], BF16)

            class FB:
                pass

            def make_fb(i, stk):
                fb = FB()
                fb.xt = sb(stk, "f%d_xt" % i, [128, D])
                fb.hb = sb(stk, "f%d_hb" % i, [128, D], BF16)
                fb.qk = sb(stk, "f%d_qk" % i, [128, 16, 128], BF16)
                fb.vfm = sb(stk, "f%d_vfm" % i, [128, 8, 128], BF16)
                fb.ktm = sb(stk, "f%d_ktm" % i, [128, 8, 128], BF16)
                fb.vtm = sb(stk, "f%d_vtm" % i, [128, 8, 128], BF16)
                fb.gs = sb(stk, "f%d_gs" % i, [128, D], BF16)
                fb.sm = sb(stk, "f%d_sm" % i, [128, 64])
                fb.gf = sb(stk, "f%d_gf" % i, [128, 8])
                fb.ex = sb(stk, "f%d_ex" % i, [128, 24])
                return fb

            class Lane:
                pass

            lanes = []

            def make_lane(ln, stk):
                B = Lane()
                B.ps = LPS[ln]
                f = lambda nm, dt=F32: sb(stk, "ln%d_%s" % (ln, nm), [128, 128], dt)
                B.P = [f("P0"), f("P1")]
                B.PT = [f("PT0"), f("PT1")]
                B.X = [f("X0"), f("X1")]
                B.ot = f("ot")
                B.L, B.Dm, B.DmI, B.DmS = B.ot, B.X[1], B.PT[1], B.P[1]
                B.attnT, B.TTb, B.R, B.vn, B.kout = (f("attnT", BF16), f("TTb", BF16), f("R", BF16),
                                                     f("vn", BF16), f("kout", BF16))
                lanes.append(B)

            make_lane(0, st2)
            fbs = [make_fb(0, st2)]

            def front_gen(ti, typ, SB, fb):
                xt, hb, qk, vfm, sm, gf, ex = fb.xt, fb.hb, fb.qk, fb.vfm, fb.sm, fb.gf, fb.ex
                P.sec("F:load+norm")
                if ti <= 1:
                    P.dma("sp", modv[:, :, :], modscr[typ].rearrange("p (s d) -> p s d", s=6)[:, 0:3, :])
                    yield
                P.dma("sp", xt[:, :], xs_all[ti])
                yield
                yield from norm_to_T_g(xt, modv, 1, 0, tmpA, hb, hT, stt_)
                yield
                yield from proj_conv_gen(typ, 6, wII, 0, hT, PA, (SB.xinS_l if typ == 0 else None), xinP_l,
                                         (SB.histS if typ == 0 else None), histP, cwg, False, acc,
                                         lambda c: (qk[:, c, :] if c < 16 else vfm[:, c - 16, :]))
                if ti == 0 or ti == NT - 1:
                    P.sec("F:convstate")
                    M3 = 48 if typ == 0 else 3
                    if typ == 0:
                        P.copy("pool", lhs3[:, :, :].rearrange("p k (s t) -> p k s t", s=16),
                               hT[:, :, :].rearrange("p k (s t) -> p k s t", s=16)[:, :, :, 5:8])
                        yield
                    else:
                        P.copy("pool", lhs3[:, :, 0:3], hT[:, :, 125:128])
                        yield
                    for cc in range(3):
                        for nb in range(2):
                            c0 = cc * 1024 + nb * 512
                            P.mmk(PA[nb][0:M3, :], [(lhs3[:, kc, 0:M3], wII.col(kc, c0, 512)) for kc in range(8)])
                            yield
                            P.copy("act", tmpA[0:M3, nb * 512:(nb + 1) * 512], PA[nb][0:M3, :])
                            yield
                        P.dma("sp", (o_gdnc_s if typ == 0 else o_gdnc_p)[:, cc * 1024:(cc + 1) * 1024], tmpA[0:M3, :])
                        yield
                    yield
                P.sec("F:gate")
                for nb in range(2):
                    P.mmk(PA[nb][:, :], [(hT[:, kc, :], wII.col(kc, 3072 + nb * 512, 512))
                                         for kc in range(8)])
                    yield
                    P.act(fb.gs[:, nb * 512:(nb + 1) * 512], PA[nb][:, :], AF.Silu)
                    yield
                yield
                P.sec("F:beta/g")
                P.mmk(PM[:, 0:16], [(hT[:, kc, :], wII.col(kc, 4096, 16)) for kc in range(8)])
                yield
                P.act(sm[:, 0:8], PM[:, 0:8], AF.Exp, scale=-1.0)
                yield
                P.ts("dve", sm[:, 0:8], sm[:, 0:8], 1.0, None, ALU.add)
                yield
                P.recip(sm[:, 8:16], sm[:, 0:8])
                yield
                P.ts("dve", sm[:, 16:24], sm[:, 8:16], -1.0, None, ALU.mult)
                yield
                P.tt("dve", sm[:, 24:32], PM[:, 8:16], smv[:, 48:56], ALU.add)
                yield
                P.act(sm[:, 32:40], sm[:, 24:32], AF.Exp)
                yield
                P.act(sm[:, 40:48], sm[:, 32:40], AF.Ln, bias=cst[:, 1:2], scale=1.0)
                yield
                P.tt("dve", gf[:, :], sm[:, 40:48], negA[:, 16:24], ALU.mult)
                yield
                yield
                P.sec("F:beta/g")
                yield from small_scan_terms_g(gf[:, :], 8, typ, PM[:, 64:88], ex)
                P.ts("dve", sm[:, 48:56], ex[:, 0:8], -1.0, None, ALU.mult)
                yield
                yield
                for half in range(2):
                    P.sec("F:l2norm")
                    src = qk[:, half * 8:(half + 1) * 8, :]
                    P.tt("pool", hb[:, :].rearrange("p (c t) -> p c t", c=8), src, src, ALU.mult)
                    yield
                    for i in range(2):
                        rq = tmpA[:, i * 512:(i + 1) * 512]
                        P.mm(PA[i][:, :], onesb[:, :], hb[:, i * 512:(i + 1) * 512])
                        yield
                        if half == 0:
                            P.act(rq, PA[i][:, :], AF.Sqrt, bias=cst[:, 2:3], scale=128.0)
                            yield
                        else:
                            P.act(rq, PA[i][:, :], AF.Sqrt, bias=cst[:, 0:1], scale=1.0)
                            yield
                        P.recip(rq, rq)
                        yield
                        dst = qk[:, half * 8 + 4 * i:half * 8 + 4 * i + 4, :]
                        P.tt("dve", dst, dst, rq.rearrange("p (c t) -> p c t", c=4), ALU.mult)
                        yield
                    yield
                P.sec("F:transposes")
                P.copy("pool", hb[:, :].rearrange("p (c t) -> p c t", c=8), qk[:, 8:16, :])
                yield
                P.trs([(PT[:, j * 128:(j + 1) * 128], qk[:, 8 + j, :]) for j in range(8)], identb[:, :])
                yield
                P.copy("act", fb.ktm[:, :, :], PT[:, :].rearrange("p (h d) -> p h d", h=8))
                yield
                yield
                P.sec("F:transposes")
                P.trs([(PT[:, j * 128:(j + 1) * 128], vfm[:, j, :]) for j in range(8)], identb[:, :])
                yield
                P.copy("act", fb.vtm[:, :, :], PT[:, :].rearrange("p (h d) -> p h d", h=8))
                yield
                if typ == 0:
                    P.tt("pool", SB.rselg[:, :].rearrange("p (h s) -> p h s", h=8),
                         bc(gf[:, :], 2, 16), bc(blk16[:, :], 1, 8), ALU.mult)
                    yield
                    P.mm(PM[:, 128:256], onesf[:, :], SB.rselg[:, :])
                    yield
                    P.act(SB.gendsel[:, :], PM[:, 128:256], AF.Exp)
                    yield
                yield

            def head_gen(h, B, typ, SB, fb):
                mk = masks[typ]
                m_lev = 3 if typ == 0 else 7
                qk, hb, sm, gf, ex, ktm, vtm = fb.qk, fb.hb, fb.sm, fb.gf, fb.ex, fb.ktm, fb.vtm
                kT = qk[:, 8 + h, :]
                qT = qk[:, h, :]
                _st = ["H:decay"]
                P.sec(_st[0])
                P.act(B.L[:, :], mk[:, MSTT, :], AF.Identity, scale=gf[:, h:h + 1])
                yield
                P.mm(B.ps[3][:, :], B.L[:, :], mk[:, MINC, :])
                yield
                P.act(B.Dm[:, :], B.ps[3][:, :], AF.Exp)
                yield
                P.tt("dve", B.DmI[:, :], B.Dm[:, :], mk[:, MINC, :], ALU.mult)
                yield
                P.tt("dve", B.DmS[:, :], B.Dm[:, :], mk[:, MSTR, :], ALU.mult)
                yield
                P.mm(B.ps[0][:, :], kT, hb[:, h * 128:(h + 1) * 128])
                yield
                P.mm(B.ps[1][:, :], kT, qT)
                yield
                P.stt("dve", B.P[0][:, :], B.ps[0][:, :], sm[:, 16 + h:17 + h], B.DmS[:, :], ALU.mult, ALU.mult)
                yield
                P.tt("dve", B.attnT[:, :], B.ps[1][:, :], B.DmI[:, :], ALU.mult)
                yield
                yield
                _st[0] = "H:dbl"
                P.sec(_st[0])
                P.tr(B.ps[2][:, :], B.P[0][:, :], identf[:, :])
                yield
                P.copy("act", B.PT[0][:, :], B.ps[2][:, :])
                yield
                P.tt("dve", B.X[0][:, :], B.P[0][:, :], identf[:, :], ALU.add)
                yield
                yield
                P.sec(_st[0])
                if m_lev > 1:
                    P.mm(B.ps[0][:, :], B.P[0][:, :], B.PT[0][:, :])
                    yield
                    if m_lev > 2:
                        P.mm(B.ps[1][:, :], B.PT[0][:, :], B.P[0][:, :])
                        yield
                    P.copy("act", B.PT[1][:, :], B.ps[0][:, :])
                    yield
                    if m_lev > 2:
                        P.copy("act", B.P[1][:, :], B.ps[1][:, :])
                        yield
                    yield
                    P.sec(_st[0])
                xi = 0
                for j in range(1, m_lev):
                    cur, nx = j % 2, (j + 1) % 2
                    if j + 1 < m_lev:
                        P.mm(B.ps[0][:, :], B.P[cur][:, :], B.PT[cur][:, :])
                        yield
                        if j + 2 < m_lev:
                            P.mm(B.ps[1][:, :], B.PT[cur][:, :], B.P[cur][:, :])
                            yield
                    P.mm(B.ps[2][:, :], B.PT[cur][:, :], B.X[xi][:, :])
                    yield
                    if j + 1 < m_lev:
                        P.copy("act", B.PT[nx][:, :], B.ps[0][:, :])
                        yield
                        if j + 2 < m_lev:
                            P.copy("act", B.P[nx][:, :], B.ps[1][:, :])
                            yield
                    P.tt("dve", B.X[1 - xi][:, :], B.X[xi][:, :], B.ps[2][:, :], ALU.add)
                    yield
                    xi = 1 - xi
                    yield
                    P.sec(_st[0])
                _st[0] = "H:state"
                P.sec(_st[0])
                P.copy("act", B.TTb[:, :], B.X[xi][:, :])
                yield
                if typ == 1:
                    Sf_h, Sb_h = SB.Sf[h], SB.Sb[h]
                    P.mm(B.ps[3][:, :], kT, Sb_h[:, :])
                    yield
                else:
                    P.dma("sp", SB.S0f[:, :, :], st_gdn[:, h, :, :].rearrange("s d e -> d s e"))
                    yield
                    P.copy("act", SB.S0b[:, :, :], SB.S0f[:, :, :])
                    yield
                    for (dstT, srcT) in ((SB.kTm, kT), (SB.qTm, qT)):
                        base = dstT[:, :]
                        dap = bass.AP(base.tensor, base.offset, [[16 * 128, 128], [136, 16], [1, 8]])
                        P.op("pool", (lambda e, dap=dap, src=srcT.rearrange("p (s t) -> p s t", s=16):
                                      e.tensor_copy(out=dap, in_=src)), ins=[srcT], outs=[base])
                        yield
                    P.mmk(B.ps[3][:, :], [(SB.kTm[:, s * 128:(s + 1) * 128], SB.S0b[:, s, :]) for s in range(16)])
                    yield
                P.stt("dve", B.R[:, :], B.ps[3][:, :], sm[:, 48 + h:49 + h], vtm[:, h, :], ALU.mult, ALU.add)
                yield
                P.mm(B.ps[0][:, :], B.TTb[:, :], B.R[:, :])
                yield
                P.act(B.vn[:, :], B.ps[0][:, :], AF.Identity, scale=sm[:, 8 + h:9 + h])
                yield
                yield
                P.sec(_st[0])
                if typ == 1:
                    P.mm(B.ps[1][:, :], qT, Sb_h[:, :])
                    yield
                else:
                    P.mmk(B.ps[1][:, :], [(SB.qTm[:, s * 128:(s + 1) * 128], SB.S0b[:, s, :]) for s in range(16)])
                    yield
                P.mm(B.ps[2][:, :], B.attnT[:, :], B.vn[:, :])
                yield
                P.act(B.ot[:, :], B.ps[1][:, :], AF.Identity, scale=ex[:, h:h + 1])
                yield
                P.tt("dve", otm[:, h * 128:(h + 1) * 128], B.ot[:, :], B.ps[2][:, :], ALU.add)
                yield
                P.act(B.kout[:, :], ktm[:, h, :], AF.Identity, scale=ex[:, 8 + h:9 + h])
                yield
                yield
                P.sec(_st[0])
                if typ == 1:
                    P.mm(B.ps[3][:, :], B.kout[:, :], B.vn[:, :])
                    yield
                    P.stt("dve", Sf_h[:, :], Sf_h[:, :], ex[:, 16 + h:17 + h], B.ps[3][:, :], ALU.mult, ALU.add)
                    yield
                    P.copy("act", Sb_h[:, :], Sf_h[:, :])
                    yield
                else:
                    P.tt("dve", SB.koutm_all[:, :, :], bc(B.kout[:, :], 1, 16), bc(blk16[:, :], 2, 128), ALU.mult)
                    yield
                    for s in range(16):
                        P.mm(LPB[s // 4][:, (s % 4) * 128:(s % 4 + 1) * 128], SB.koutm_all[:, s, :], B.vn[:, :])
                    yield
                    for j4 in range(4):
                        S4 = SB.S0f[:, 4 * j4:4 * j4 + 4, :]
                        gsel = SB.gendsel[:, h * 16 + 4 * j4:h * 16 + 4 * j4 + 4]
                        P.tt("dve", S4, S4, bc(gsel, 2, 128), ALU.mult)
                        yield
                        P.tt("dve", S4, S4, LPB[j4][:, :].rearrange("p (s e) -> p s e", s=4), ALU.add)
                        yield
                    P.dma("sp", o_gdn_s[:, h, :, :].rearrange("s d e -> d s e"), SB.S0f[:, :, :])
                    yield
                yield

            def tail_gen(ti, fb):
                xt = fb.xt
                P.sec("T:onorm")
                P.dma("sp", mixed[:, 0:D], yssd_scr[ti])
                yield
                P.tt("pool", tmpT[:, :], otm[:, :], otm[:, :], ALU.mult)
                yield
                P.red("dve", ss8[:, 0:8], tmpT[:, :].rearrange("p (h d) -> p h d", h=8))
                yield
                P.act(ss8[:, 8:16], ss8[:, 0:8], AF.Sqrt, bias=cst[:, 0:1], scale=1.0 / 128)
                yield
                P.recip(ss8[:, 16:24], ss8[:, 8:16])
                yield
                o3 = otm[:, :].rearrange("p (h d) -> p h d", h=8)
                P.tt("dve", o3, o3, bc(ss8[:, 16:24], 2, 128), ALU.mult)
                yield
                P.tt("dve", o3, o3, bc(gnw[:, :], 1, 8), ALU.mult)
                yield
                P.tt("dve", mixed[:, D:2 * D], otm[:, :], fb.gs[:, :], ALU.mult)
                yield
                yield
                P.sec("T:outproj")
                for half in range(2):
                    P.trs([(PT[:, j * 128:(j + 1) * 128], mixed[:, (half * 8 + j) * 128:(half * 8 + j + 1) * 128])
                           for j in range(8)], identb[:, :])
                    yield
                    P.copy("act", mixedT[:, half * 8:(half + 1) * 8, :], PT[:, :].rearrange("p (c t) -> p c t", c=8))
                    yield
                yield
                P.sec("T:outproj")
                for nb in range(2):
                    P.mmk(PA[nb][:, :], [(mixedT[:, kc, :], wO.col(kc, nb * 512, 512)) for kc in range(16)])
                    yield
                    P.copy("act", tmpT[:, nb * 512:(nb + 1) * 512], PA[nb][:, :])
                    yield
                yield
                P.sec("T:resid")
                P.sqsum(otm[:, :], tmpT[:, :], sttT[:, 0:1])
                yield
                P.act(sttT[:, 1:2], sttT[:, 0:1], AF.Sqrt, bias=cst[:, 0:1], scale=1.0 / D)
                yield
                P.recip(sttT[:, 2:3], sttT[:, 1:2])
                yield
                P.stt("dve", tmpT[:, :], tmpT[:, :], sttT[:, 2:3], modv[:, 2, :], ALU.mult, ALU.mult)
                yield
                P.tt("dve", xt[:, :], tmpT[:, :], xt[:, :], ALU.add)
                yield
                P.dma("sp", x1_scr[ti], xt[:, :])
                yield
                yield

            def speed(g, k):
                while True:
                    for _ in range(k):
                        try:
                            next(g)
                        except StopIteration:
                            return
                    yield

            def run_rr(gens):
                alive = list(gens)
                while alive:
                    nxt = []
                    for gn in alive:
                        try:
                            next(gn)
                            nxt.append(gn)
                        except StopIteration:
                            pass
                    alive = nxt

            def back(ti, typ, SB, fb, nl, side, with_tail=True):
                side = [side] if side is not None else []
                for h0_ in range(0, 8, nl):
                    gens = [head_gen(h0_ + i, lanes[i], typ, SB, fb) for i in range(min(nl, 8 - h0_))]
                    alive = gens + side
                    while any(g in alive for g in gens):
                        nxt = []
                        for gn in alive:
                            try:
                                next(gn)
                                nxt.append(gn)
                            except StopIteration:
                                if gn in side:
                                    side = []
                        alive = nxt
                if with_tail:
                    run_rr([tail_gen(ti, fb)] + side)
                else:
                    run_rr(side)

            class SBufs:
                pass

            with contextlib.ExitStack() as st2s:
                SB = SBufs()
                SB.histS = sb(st2s, "histSg", [128, 24, 16, 3])
                SB.xinS_l = [sb(st2s, "xinS2_%d" % i, [128, 4, 16, 11]) for i in range(2)]
                SB.S0f = sb(st2s, "S0f", [128, 16, 128])
                SB.S0b = sb(st2s, "S0b", [128, 16, 128], BF16)
                SB.kTm = sb(st2s, "kTm", [128, 16 * 128], BF16)
                SB.qTm = sb(st2s, "qTm", [128, 16 * 128], BF16)
                SB.koutm_all = sb(st2s, "koutm_all", [128, 16, 128], BF16)
                SB.rselg = sb(st2s, "rselg", [128, 128])
                SB.gendsel = sb(st2s, "gendsel", [128, 128])
                P.dma("sp", SB.histS[:, :, :, :], hist_gdn[:, :, :, :])
                P.memset("pool", SB.kTm[:, :], 0.0)
                P.memset("pool", SB.qTm[:, :], 0.0)
                run_rr([front_gen(0, 0, SB, fbs[0])])
                back(0, 0, SB, fbs[0], 1, None)
                P.barrier()
                P.emit()
            with contextlib.ExitStack() as st2p:
                SB = SBufs()
                SB.Sf = [sb(st2p, "Sf%d" % h, [128, 128]) for h in range(8)]
                SB.Sb = [sb(st2p, "Sb%d" % h, [128, 128], BF16) for h in range(8)]
                for ln in range(1, P2_NL):
                    make_lane(ln, st2p)
                fbs.append(make_fb(1, st2p))
                for h in range(8):
                    P.memset("pool", SB.Sf[h][:, :], 0.0)
                    P.memset("pool", SB.Sb[h][:, :], 0.0)
                run_rr([front_gen(1, 1, SB, fbs[1])])
                pending_tail = None
                for ti in range(1, NT):
                    parts = []
                    if pending_tail is not None:
                        parts.append(pending_tail)
                    if ti + 1 < NT:
                        parts.append(speed(front_gen(ti + 1, 1, SB, fbs[(ti + 1) % 2]), 2))
                    side = itertools.chain(*parts) if parts else None
                    back(ti, 1, SB, fbs[ti % 2], P2_NL, side, with_tail=False)
                    pending_tail = tail_gen(ti, fbs[ti % 2])
                run_rr([pending_tail])
                for h in range(8):
                    P.dma("sp", o_gdn_p[h * 128:(h + 1) * 128, :], SB.Sf[h][:, :])
                P.barrier()
                P.emit()
        if STOP_AFTER >= 3:
          with contextlib.ExitStack() as st3:
            PB = [st3.enter_context(nc.psum_tensor("pb%d" % i, [128, 512], F32)) for i in range(7)]
            gb = [0, 512, 1024, 1536, 2048, 2560, 2816]
            wG = WBlocks(st3, "wG", 8, gb + [DFF + b for b in gb[1:]])
            w_gu_v = w_gu.rearrange("(kc p) n -> p kc n", p=128)
            wG.load(w_gu_v, 0, order=[j for i in range(6) for j in (i, 6 + i)])
            wD = WBlocks(st3, "wD", 22, [0, 512, 1024])
            w_dn_v = w_dn.rearrange("(kc p) n -> p kc n", p=128)
            wD.load(w_dn_v, 0)
            modv = sb(st3, "modv3", [128, 3, D])
            xt = [sb(st3, "xt3_%d" % i, [128, D]) for i in range(2)]
            tmpA = [sb(st3, "tmpA3_%d" % i, [128, D]) for i in range(2)]
            tmpB = sb(st3, "tmpB3", [128, D])
            hb = [sb(st3, "hb3_%d" % i, [128, D], BF16) for i in range(2)]
            hT = [sb(st3, "hT3_%d" % i, [128, 8, 128], BF16) for i in range(2)]
            stt_ = [sb(st3, "stt3_%d" % i, [128, 4]) for i in range(2)]
            sg = [sb(st3, "sg%d" % i, [128, 512]) for i in range(2)]
            hid = sb(st3, "hid", [128, DFF], BF16)
            hidT = sb(st3, "hidT", [128, 22, 128], BF16)

            def ffn_gen(ti):
                typ = 0 if ti == 0 else 1
                pr = ti % 2
                x_, tA, hb_, hT_, st_ = xt[pr], tmpA[pr], hb[pr], hT[pr], stt_[pr]
                if ti <= 1:
                    P.dma("sp", modv[:, :, :], modscr[typ].rearrange("p (s d) -> p s d", s=6)[:, 3:6, :])
                P.dma("sp", x_[:, :], x1_scr[ti])
                yield
                yield from norm_to_T_g(x_, modv, 1, 0, tA, hb_, hT_, st_)
                for i in range(6):
                    w = 512 if i < 5 else 256
                    pg_, pu_ = PB[(2 * i) % 6], PB[(2 * i + 1) % 6]
                    P.mmk(pg_[:, 0:w], [(hT_[:, kc, :], wG.col(kc, i * 512, w)) for kc in range(8)])
                    yield
                    P.mmk(pu_[:, 0:w], [(hT_[:, kc, :], wG.col(kc, DFF + i * 512, w)) for kc in range(8)])
                    yield
                    s_ = sg[i % 2]
                    P.act(s_[:, 0:w], pg_[:, 0:w], AF.Silu)
                    yield
                    P.tt("dve", hid[:, i * 512:i * 512 + w], s_[:, 0:w], pu_[:, 0:w], ALU.mult)
                    yield
                yield "HALF"
                for grp in range(3):
                    n = 8 if grp < 2 else 6
                    P.trs([(PT[:, j * 128:(j + 1) * 128], hid[:, (grp * 8 + j) * 128:(grp * 8 + j + 1) * 128])
                           for j in range(n)], identb[:, :])
                    yield
                    P.copy("act", hidT[:, grp * 8:grp * 8 + n, :],
                           PT[:, 0:n * 128].rearrange("p (c t) -> p c t", c=n))
                    yield
                for nb in range(2):
                    P.mmk(PB[6][:, :], [(hidT[:, kc, :], wD.col(kc, nb * 512, 512)) for kc in range(22)])
                    yield
                    P.copy("act", tA[:, nb * 512:(nb + 1) * 512], PB[6][:, :])
                    yield
                P.sqsum(tmpB[:, :], tA[:, :], st_[:, 0:1])
                yield
                P.act(st_[:, 1:2], st_[:, 0:1], AF.Sqrt, bias=cst[:, 0:1], scale=1.0 / D)
                yield
                P.recip(st_[:, 2:3], st_[:, 1:2])
                yield
                P.stt("dve", tA[:, :], tA[:, :], st_[:, 2:3], modv[:, 2, :], ALU.mult, ALU.mult)
                yield
                P.tt("dve", x_[:, :], tA[:, :], x_[:, :], ALU.add)
                yield
                P.dma("sp", y_all[ti], x_[:, :])
                yield

            run_gen(ffn_gen(0))
            nxt = 2
            cur = ffn_gen(1)
            young = None
            while cur is not None:
                try:
                    v = next(cur)
                    if v == "HALF" and young is None and nxt < NT:
                        young = ffn_gen(nxt)
                        nxt += 1
                except StopIteration:
                    cur, young = young, None
                    if cur is None and nxt < NT:
                        cur = ffn_gen(nxt)
                        nxt += 1
                    continue
                if young is not None:
                    try:
                        v2 = next(young)
                        if v2 == "HALF":
                            pass
                    except StopIteration:
                        young = None
            P.barrier()
            P.emit()
    return nc


def _prep_inputs(inp):
    f = lambda a: np.ascontiguousarray(np.asarray(a, dtype=np.float32))
    xp, xs = f(inp["x_prompt"]), f(inp["x_sample"])
    cp, cs = f(inp["c_prompt"]), f(inp["c_sample"])
    bcast = lambda v: np.ascontiguousarray(np.broadcast_to(np.asarray(v, np.float32).reshape(1, -1), (128, v.size)))
    shared = {}
    shared["w_ada"] = f(inp["w_ada"][0])
    shared["b_ada_b"] = bcast(inp["b_ada"][0])
    shared["normvecs"] = np.ascontiguousarray(np.stack(
        [bcast(inp[k][0]) for k in ("norm_mix_pre", "norm_mix_post", "norm_ffn_pre", "norm_ffn_post")], axis=1))
    shared["w_in"] = f(inp["w_in"][0])
    shared["w_out"] = f(inp["w_out"][0])
    shared["w_gu"] = f(inp["w_gate_up"][0])
    shared["w_dn"] = f(inp["w_down"][0])
    scw = np.concatenate([f(inp["ssd_conv_w"][0]), f(inp["ssd_conv_b"][0])[None, :]], axis=0)
    shared["ssd_cw"] = np.ascontiguousarray(scw.reshape(5, 12, 128).transpose(2, 1, 0))
    shared["gdn_cw"] = np.ascontiguousarray(f(inp["gdn_conv_w"][0]).reshape(4, 24, 128).transpose(2, 1, 0))
    sv = np.concatenate([f(inp["ssd_dt_bias"][0]), f(inp["ssd_A_log"][0]), f(inp["ssd_D"][0]),
                         f(inp["gdn_dt_bias"][0]), f(inp["gdn_A_log"][0])])
    shared["smallv"] = bcast(sv)
    shared["ssd_nw"] = bcast(inp["ssd_norm_w"][0])
    shared["gdn_nw"] = bcast(inp["gdn_norm_w"][0])
    shared["ident"] = np.eye(128, dtype=np.float32)
    idx = np.arange(128)
    m = np.zeros((2, 128, 4, 128), np.float32)
    for typ, bs in ((0, 8), (1, 128)):
        same = (idx[:, None] // bs) == (idx[None, :] // bs)
        m[typ, :, 0, :] = same & (idx[:, None] > idx[None, :])
        m[typ, :, 1, :] = same & (idx[:, None] <= idx[None, :])
        m[typ, :, 2, :] = same & (idx[:, None] < idx[None, :])
        m[typ, :, 3, :] = same
    shared["masks"] = m
    shared["blk16"] = np.ascontiguousarray(((idx[:, None] // 8) == np.arange(16)[None, :]).astype(np.float32))
    maps = []
    for i in range(NCORES):
        d = dict(shared)
        sl = slice(16 * i, 16 * (i + 1))
        d["xs_all"] = np.ascontiguousarray(np.concatenate(
            [xs[sl].reshape(1, 128, D), xp[i].reshape(16, 128, D)], axis=0))
        d["cexp"] = np.ascontiguousarray(np.stack(
            [np.repeat(cs[sl], 8, axis=0), np.broadcast_to(cp[i][None, :], (128, D))], axis=0))
        d["st_ssd"] = np.ascontiguousarray(f(inp["state_ssd"][0, sl]).reshape(16, 1024, 128))
        hs = f(inp["state_ssd_conv"][0, sl])
        d["hist_ssd"] = np.ascontiguousarray(hs.reshape(16, 3, 12, 128).transpose(3, 2, 0, 1))
        d["st_gdn"] = np.ascontiguousarray(f(inp["state_gdn"][0, sl]))
        hg = f(inp["state_gdn_conv"][0, sl])
        d["hist_gdn"] = np.ascontiguousarray(hg.reshape(16, 3, 24, 128).transpose(3, 2, 0, 1))
        maps.append(d)
    return maps


def kernel(**inp):
    maps = _prep_inputs(inp)
    nc = build_program()
    res = run_bass_kernel_spmd(nc, maps, core_ids=list(range(NCORES)))
    R = res.results
    cat = lambda k: np.stack([np.asarray(r[k]) for r in R], axis=0)
    y_all = cat("y_all")
    y_prompt = y_all[:, 1:].reshape(8, 2048, D)
    y_sample = y_all[:, 0].reshape(128, 8, D)
    ssd_p = cat("o_ssd_p").reshape(1, 8, 16, 64, 128)
    ssdc_p = cat("o_ssdc_p").reshape(1, 8, 3, 1536)
    gdn_p = cat("o_gdn_p").reshape(1, 8, 8, 128, 128)
    gdnc_p = cat("o_gdnc_p").reshape(1, 8, 3, 3072)
    ssd_s = cat("o_ssd_s").reshape(1, 128, 16, 64, 128)
    ssdc_s = cat("o_ssdc_s").reshape(1, 128, 3, 1536)
    gdn_s = cat("o_gdn_s").reshape(1, 128, 8, 128, 128)
    gdnc_s = cat("o_gdnc_s").reshape(1, 128, 3, 3072)
    outs = (y_prompt, y_sample, ssd_p, ssdc_p, gdn_p, gdnc_p, ssd_s, ssdc_s, gdn_s, gdnc_s)
    return tuple(np.ascontiguousarray(o, dtype=np.float32) for o in outs)
```

```python
import contextlib
import itertools
import os
import numpy as np
import concourse.bass as bass
import concourse.mybir as mybir
from concourse.bass_utils import run_bass_kernel_spmd

F32 = mybir.dt.float32
BF16 = mybir.dt.bfloat16
AF = mybir.ActivationFunctionType
ALU = mybir.AluOpType
AX = mybir.AxisListType

NCORES = 8
D = 1024
NT = 17
DFF = 2816
P2_MODE = 0
ANNOTATE = bool(int(os.environ.get('K_ANN', '0')))
SKIP1 = False
P2_STAGE = 99
P2_HSTEP = 99
P2_SKIP = ()
P2_NL = 4
P2_TILES = 16
STOP_AFTER = int(os.environ.get('K_STOP', '99'))


class Prog:
    ENGS = ("pe", "act", "dve", "pool", "sp")

    def __init__(self, nc, stack):
        self.nc = nc
        self.sems = {}
        for e in self.ENGS:
            self.sems[e] = stack.enter_context(nc.semaphore("s_" + e))
        self.dsems = {"sp": [], "pool": [], "act": []}
        for q, n in (("sp", 8), ("pool", 4), ("act", 2)):
            for j in range(n):
                nm = "d_%s%d" % (q, j)
                self.sems[nm] = stack.enter_context(nc.semaphore(nm))
                self.dsems[q].append(nm)
        self.cnt = {k: 0 for k in self.sems}
        self.known = {e: {} for e in self.ENGS}
        self.lastw = {}
        self.readers = {}
        self.ops = {e: [] for e in self.ENGS}
        self.rr = {"sp": 0, "pool": 0, "act": 0}
        self.nops = 0
        self.tag = "init"

    def sec(self, name):
        self.tag = name

    fine = {}

    def _keys(self, aps):
        ks = []
        for a in aps:
            if a is None or isinstance(a, (int, float)):
                continue
            if isinstance(a, str):
                ks.append(a)
                continue
            nm = a.name
            if nm in self.fine:
                row, gr = self.fine[nm]
                nm = "%s:%d" % (nm, (int(a.offset) % row) // gr)
            ks.append(nm)
        return ks

    def _deps(self, eng, r, w):
        need = {}

        def add(c):
            if c is None:
                return
            s, v = c
            if s == "pe" and eng == "pe":
                return
            if need.get(s, 0) < v:
                need[s] = v

        for k in r:
            add(self.lastw.get(k))
        for k in w:
            add(self.lastw.get(k))
            for c in self.readers.get(k, {}).items():
                add(c)
        waits = []
        kn = self.known[eng]
        for s, v in need.items():
            if kn.get(s, 0) < v:
                kn[s] = v
                waits.append((s, v))
        return waits

    def _commit(self, c, r, w):
        for k in w:
            self.lastw[k] = c
            self.readers[k] = {}
        for k in r:
            d = self.readers.setdefault(k, {})
            if d.get(c[0], 0) < c[1]:
                d[c[0]] = c[1]

    def op(self, eng, fn, ins=(), outs=()):
        r = self._keys(ins)
        w = self._keys(outs)
        w = w + [k for k in r if k.startswith("ps") or k.startswith("pa") or k.startswith("pm")
                 or k.startswith("lpb") or k.startswith("pb")]
        waits = self._deps(eng, r, w)
        self.cnt[eng] += 1
        self.ops[eng].append((waits, fn, eng, 1, self.tag))
        self._commit((eng, self.cnt[eng]), r, w)
        self.nops += 1

    def dma(self, q, out, in_, extra_ins=(), extra_outs=()):
        r = self._keys([in_] + list(extra_ins))
        w = self._keys([out] + list(extra_outs))
        waits = self._deps(q, r, w)
        sems = self.dsems[q]
        j = self.rr[q]
        self.rr[q] = (j + 1) % len(sems)
        nm = sems[j]
        prev = self.cnt[nm]
        if prev > 0 and self.known[q].get(nm, 0) < prev:
            self.known[q][nm] = prev
            waits.append((nm, prev))
        self.cnt[nm] += 16
        self.ops[q].append((waits, lambda e: e.dma_start(out=out, in_=in_), nm, 16, self.tag))
        self._commit((nm, self.cnt[nm]), r, w)
        self.nops += 1

    def barrier(self):
        for e in self.ENGS:
            waits = []
            for s, v in self.cnt.items():
                if v > 0 and self.known[e].get(s, 0) < v:
                    self.known[e][s] = v
                    waits.append((s, v))
            self.ops[e].append((waits, None, None, 0, self.tag))
        self.lastw = {}
        self.readers = {}

    def emit(self):
        nc = self.nc
        with nc.Block() as block:
            for ename, deco in (("sp", block.sync), ("act", block.scalar), ("dve", block.vector),
                                ("pool", block.gpsimd), ("pe", block.tensor)):
                ops = self.ops[ename]

                def body(e, ops=ops):
                    for waits, fn, sname, inc, tag in ops:
                        for s, v in waits:
                            e.wait_ge(self.sems[s], v)
                        if fn is not None:
                            ins = fn(e)
                            ins.then_inc(self.sems[sname], inc)
                            if ANNOTATE:
                                ins.annotate(tag)

                deco(body)
                self.ops[ename] = []

    def mm(self, out, lhsT, rhs, start=True, stop=True):
        self.op("pe", lambda e: e.matmul(out, lhsT=lhsT, rhs=rhs, start=start, stop=stop),
                ins=[lhsT, rhs], outs=[out])

    def mmk(self, out, pairs):
        n = len(pairs)

        def fn(e):
            ins = None
            for i, (l, r) in enumerate(pairs):
                ins = e.matmul(out, lhsT=l, rhs=r, start=(i == 0), stop=(i == n - 1))
            return ins

        self.op("pe", fn, ins=[x for p in pairs for x in p], outs=[out])

    def tr(self, out, in_, ident):
        self.op("pe", lambda e: e.transpose(out, in_, ident), ins=[in_, ident], outs=[out])

    def trs(self, items, ident):
        def fn(e):
            ins = None
            for o, i in items:
                ins = e.transpose(o, i, ident)
            return ins
        self.op("pe", fn, ins=[i for _, i in items] + [ident], outs=[o for o, _ in items])

    def act(self, out, in_, func, bias=None, scale=None):
        kw = {}
        if bias is not None:
            kw["bias"] = bias
        if scale is not None:
            kw["scale"] = scale
        self.op("act", lambda e: e.activation(out=out, in_=in_, func=func, **kw),
                ins=[in_, bias, scale], outs=[out])

    def sqsum(self, junk, in_, accum):
        self.op("act", lambda e: e.activation(out=junk, in_=in_, func=AF.Square, accum_out=accum),
                ins=[in_], outs=[junk, accum])

    def tt(self, eng, out, in0, in1, op):
        self.op(eng, lambda e: e.tensor_tensor(out=out, in0=in0, in1=in1, op=op), ins=[in0, in1], outs=[out])

    def ts(self, eng, out, in0, s1, s2, op0, op1=None):
        if op1 is None:
            self.op(eng, lambda e: e.tensor_scalar(out=out, in0=in0, scalar1=s1, scalar2=None, op0=op0),
                    ins=[in0, s1], outs=[out])
        else:
            self.op(eng, lambda e: e.tensor_scalar(out=out, in0=in0, scalar1=s1, scalar2=s2, op0=op0, op1=op1),
                    ins=[in0, s1, s2], outs=[out])

    def stt(self, eng, out, in0, scalar, in1, op0, op1):
        self.op(eng, lambda e: e.scalar_tensor_tensor(out=out, in0=in0, scalar=scalar, in1=in1, op0=op0, op1=op1),
                ins=[in0, scalar, in1], outs=[out])

    def copy(self, eng, out, in_):
        if eng == "act":
            self.op(eng, lambda e: e.copy(out=out, in_=in_), ins=[in_], outs=[out])
        else:
            self.op(eng, lambda e: e.tensor_copy(out=out, in_=in_), ins=[in_], outs=[out])

    def red(self, eng, out, in_):
        self.op(eng, lambda e: e.tensor_reduce(out=out, in_=in_, axis=AX.X, op=ALU.add), ins=[in_], outs=[out])

    def recip(self, out, in_):
        self.op("dve", lambda e: e.reciprocal(out=out, in_=in_), ins=[in_], outs=[out])

    def memset(self, eng, ap, val):
        self.op(eng, lambda e: e.memset(ap, val), ins=[], outs=[ap])


def bc(ap, axis, n):
    u = ap.unsqueeze(axis)
    shp = list(u.shape)
    shp[axis] = n
    return u.broadcast_to(shp)


def build_program():
    nc = bass.Bass("TRN2", target_bir_lowering=False)

    def din(name, shape, dt=F32):
        return nc.dram_tensor(name, list(shape), dt, kind="ExternalInput").ap()

    def dout(name, shape, dt=F32):
        return nc.dram_tensor(name, list(shape), dt, kind="ExternalOutput").ap()

    def dscr(name, shape, dt=F32):
        return nc.dram_tensor(name, list(shape), dt).ap()

    xs_all = din("xs_all", [NT, 128, D])
    cexp = din("cexp", [2, 128, D])
    w_ada = din("w_ada", [D, 6 * D])
    b_ada_b = din("b_ada_b", [128, 6 * D])
    normvecs = din("normvecs", [128, 4, D])
    w_in = din("w_in", [D, 6688])
    w_out = din("w_out", [2 * D, D])
    w_gu = din("w_gu", [D, 2 * DFF])
    w_dn = din("w_dn", [DFF, D])
    ssd_cw = din("ssd_cw", [128, 12, 5])
    gdn_cw = din("gdn_cw", [128, 24, 4])
    smallv = din("smallv", [128, 64])
    ssd_nw = din("ssd_nw", [128, D])
    gdn_nw = din("gdn_nw", [128, 128])
    st_ssd = din("st_ssd", [16, 1024, 128])
    hist_ssd = din("hist_ssd", [128, 12, 16, 3])
    st_gdn = din("st_gdn", [16, 8, 128, 128])
    hist_gdn = din("hist_gdn", [128, 24, 16, 3])
    ident_d = din("ident", [128, 128])
    masks_d = din("masks", [2, 128, 4, 128])
    blk16_d = din("blk16", [128, 16])

    y_all = dout("y_all", [NT, 128, D])
    o_ssd_p = dout("o_ssd_p", [1024, 128])
    o_ssdc_p = dout("o_ssdc_p", [3, 1536])
    o_gdn_p = dout("o_gdn_p", [1024, 128])
    o_gdnc_p = dout("o_gdnc_p", [3, 3072])
    o_ssd_s = dout("o_ssd_s", [16, 1024, 128])
    o_ssdc_s = dout("o_ssdc_s", [48, 1536])
    o_gdn_s = dout("o_gdn_s", [16, 8, 128, 128])
    o_gdnc_s = dout("o_gdnc_s", [48, 3072])

    modscr = dscr("modscr", [2, 128, 6 * D])
    yssd_scr = dscr("yssd_scr", [NT, 128, D], BF16)
    x1_scr = dscr("x1_scr", [NT, 128, D])

    with contextlib.ExitStack() as gstack:
        P = Prog(nc, gstack)

        def sb(stack, name, shape, dt=F32):
            return stack.enter_context(nc.sbuf_tensor(name, list(shape), dt))

        class WBlocks:
            def __init__(self, stk, name, nk, bounds):
                self.bounds = bounds
                self.t = [sb(stk, "%s_%d" % (name, j), [128, nk, bounds[j + 1] - bounds[j]], BF16)
                          for j in range(len(bounds) - 1)]

            def load(self, dview, col_off, order=None):
                for j in (order if order is not None else range(len(self.t))):
                    b0, b1 = self.bounds[j], self.bounds[j + 1]
                    P.dma("pool", self.t[j][:, :, :], dview[:, :, col_off + b0:col_off + b1])

            def col(self, kc, c0, w):
                for j in range(len(self.t)):
                    if self.bounds[j] <= c0 and c0 + w <= self.bounds[j + 1]:
                        return self.t[j][:, kc, c0 - self.bounds[j]:c0 - self.bounds[j] + w]
                raise ValueError("column range straddles weight blocks")

        PT = gstack.enter_context(nc.psum_tensor("pst", [128, 1024], BF16))
        ps1 = contextlib.ExitStack()
        PS = [ps1.enter_context(nc.psum_tensor("ps%d" % i, [128, 512], F32)) for i in range(7)]

        identf = sb(gstack, "identf", [128, 128])
        identb = sb(gstack, "identb", [128, 128], BF16)
        onesf = sb(gstack, "onesf", [128, 128])
        onesb = sb(gstack, "onesb", [128, 128], BF16)
        cst = sb(gstack, "cst", [128, 4])
        masks = [sb(gstack, "masks%d" % t, [128, 4, 128]) for t in range(2)]
        blk16 = sb(gstack, "blk16s", [128, 16])
        smv = sb(gstack, "smv", [128, 64])
        negA = sb(gstack, "negA", [128, 24])
        P.dma("sp", identf[:, :], ident_d[:, :])
        P.dma("pool", identb[:, :], ident_d[:, :])
        for t in range(2):
            P.dma("sp", masks[t][:, :, :], masks_d[t])
        P.dma("sp", blk16[:, :], blk16_d[:, :])
        P.dma("sp", smv[:, :], smallv[:, :])
        P.memset("dve", onesf[:, :], 1.0)
        P.memset("dve", onesb[:, :], 1.0)
        P.memset("dve", cst[:, 0:1], 1e-6)
        P.memset("dve", cst[:, 1:2], 1.0)
        P.memset("dve", cst[:, 2:3], 128e-6)
        P.memset("dve", cst[:, 3:4], 0.0)
        P.act(negA[:, 0:16], smv[:, 16:32], AF.Exp)
        P.act(negA[:, 16:24], smv[:, 56:64], AF.Exp)
        P.ts("dve", negA[:, :], negA[:, :], -1.0, None, ALU.mult)

        MSTT, MINC, MSTR, MBLK = 0, 1, 2, 3

        with contextlib.ExitStack() as st0:
            ct = sb(st0, "ct", [128, D])
            cb = sb(st0, "cb", [128, D], BF16)
            cT = [sb(st0, "cT%d" % t, [128, 8, 128], BF16) for t in range(2)]
            modt = [sb(st0, "modt%d" % t, [128, 6 * D]) for t in range(2)]
            nv = sb(st0, "nv", [128, 4, D])
            wa = [sb(st0, "wa%d" % i, [128, 8, 512], BF16) for i in range(2)]
            bb = [sb(st0, "bb%d" % i, [128, 512]) for i in range(2)]
            P.dma("sp", nv[:, :, :], normvecs[:, :, :])
            for t in range(2):
                P.dma("sp", ct[:, :], cexp[t])
                P.act(cb[:, :], ct[:, :], AF.Silu)
                P.trs([(PT[:, c * 128:(c + 1) * 128], cb[:, c * 128:(c + 1) * 128]) for c in range(8)], identb[:, :])
                P.copy("dve", cT[t][:, :, :], PT[:, :].rearrange("p (c t) -> p c t", c=8))
            wv = w_ada.rearrange("(kc p) n -> p kc n", p=128)
            for j in range(12):
                P.dma("pool", wa[j % 2][:, :, :], wv[:, :, j * 512:(j + 1) * 512])
                P.dma("sp", bb[j % 2][:, :], b_ada_b[:, j * 512:(j + 1) * 512])
                for t in range(2):
                    ps = PS[(2 * j + t) % 4]
                    P.mmk(ps[:, :], [(cT[t][:, kc, :], wa[j % 2][:, kc, :]) for kc in range(8)])
                    P.tt("dve", modt[t][:, j * 512:(j + 1) * 512], ps[:, :], bb[j % 2][:, :], ALU.add)
            for t in range(2):
                m = modt[t]
                P.stt("dve", m[:, D:2 * D], m[:, D:2 * D], 1.0, nv[:, 0, :], ALU.add, ALU.mult)
                P.tt("dve", m[:, 2 * D:3 * D], m[:, 2 * D:3 * D], nv[:, 1, :], ALU.mult)
                P.stt("dve", m[:, 4 * D:5 * D], m[:, 4 * D:5 * D], 1.0, nv[:, 2, :], ALU.add, ALU.mult)
                P.tt("dve", m[:, 5 * D:6 * D], m[:, 5 * D:6 * D], nv[:, 3, :], ALU.mult)
                P.dma("sp", modscr[t], m[:, :])
            P.barrier()
            P.emit()

        def norm_to_T(xt, modv, sidx, hidx, tmpA, hb, hT, stt_):
            P.sec("norm_to_T")
            P.sqsum(tmpA[:, :], xt[:, :], stt_[:, 0:1])
            P.act(stt_[:, 1:2], stt_[:, 0:1], AF.Sqrt, bias=cst[:, 0:1], scale=1.0 / D)
            P.recip(stt_[:, 2:3], stt_[:, 1:2])
            P.stt("dve", tmpA[:, :], xt[:, :], stt_[:, 2:3], modv[:, sidx, :], ALU.mult, ALU.mult)
            P.tt("dve", hb[:, :], tmpA[:, :], modv[:, hidx, :], ALU.add)
            P.trs([(PT[:, c * 128:(c + 1) * 128], hb[:, c * 128:(c + 1) * 128]) for c in range(8)], identb[:, :])
            P.copy("act", hT[:, :, :], PT[:, :].rearrange("p (c t) -> p c t", c=8))

        def norm_to_T_g(xt, modv, sidx, hidx, tmpA, hb, hT, stt_):
            P.sec("norm_to_T")
            P.sqsum(tmpA[:, :], xt[:, :], stt_[:, 0:1])
            yield
            P.act(stt_[:, 1:2], stt_[:, 0:1], AF.Sqrt, bias=cst[:, 0:1], scale=1.0 / D)
            yield
            P.recip(stt_[:, 2:3], stt_[:, 1:2])
            yield
            P.stt("dve", tmpA[:, :], xt[:, :], stt_[:, 2:3], modv[:, sidx, :], ALU.mult, ALU.mult)
            yield
            P.tt("dve", hb[:, :], tmpA[:, :], modv[:, hidx, :], ALU.add)
            yield
            P.trs([(PT[:, c * 128:(c + 1) * 128], hb[:, c * 128:(c + 1) * 128]) for c in range(8)], identb[:, :])
            yield
            P.copy("act", hT[:, :, :], PT[:, :].rearrange("p (c t) -> p c t", c=8))
            yield

        def small_scan_terms(af, nh, typ, pss, ex):
            mk = masks[typ]
            P.mm(pss[:, 0:nh], mk[:, MINC, :], af)
            P.mm(pss[:, nh:2 * nh], mk[:, MSTT, :], af)
            P.mm(pss[:, 2 * nh:3 * nh], mk[:, MBLK, :], af)
            P.act(ex[:, 0:3 * nh], pss[:, 0:3 * nh], AF.Exp)

        def small_scan_terms_g(af, nh, typ, pss, ex):
            mk = masks[typ]
            P.mm(pss[:, 0:nh], mk[:, MINC, :], af)
            yield
            P.mm(pss[:, nh:2 * nh], mk[:, MSTT, :], af)
            yield
            P.mm(pss[:, 2 * nh:3 * nh], mk[:, MBLK, :], af)
            yield
            P.act(ex[:, 0:3 * nh], pss[:, 0:3 * nh], AF.Exp)
            yield

        def proj_conv_gen(typ, ngroups, wt, wcol0, hT, psb, xinS_l, xinP_l, histS, histP, cwt, has_bias, accs, dst_fn):
            items = []

            def slot(i):
                n = len(items)
                if 0 <= i < n:
                    it = items[i]
                    if has_bias:
                        P.act(it[0], it[2][0], AF.Identity, bias=cwt[:, it[4], 4:5], scale=cwt[:, it[4], 0:1])
                    else:
                        P.act(it[0], it[2][0], AF.Identity, scale=cwt[:, it[4], 0:1])
                    yield
                if 0 <= i - 1 < n:
                    it = items[i - 1]
                    for k in range(1, 4):
                        P.stt("dve", it[0], it[2][k], cwt[:, it[4], k:k + 1], it[0], ALU.mult, ALU.add)
                        yield
                if 0 <= i - 2 < n:
                    it = items[i - 2]
                    P.act(it[3], it[1][:, :], AF.Silu)
                    yield

            for g in range(ngroups):
                ps = psb[g % 2]
                for j in range(4):
                    c = 4 * g + j
                    P.mmk(ps[:, j * 128:(j + 1) * 128],
                          [((wt.col(kc, wcol0 + c * 128, 128) if hasattr(wt, "col")
                             else wt[:, kc, wcol0 + c * 128:wcol0 + (c + 1) * 128]), hT[:, kc, :]) for kc in range(8)])
                    yield
                if typ == 0:
                    xin = xinS_l[g % 2]
                    P.copy("pool", xin[:, :, :, 0:3], histS[:, 4 * g:4 * g + 4, :, :])
                    yield
                    P.copy("act", xin[:, :, :, 3:11], ps[:, :].rearrange("p (j s t) -> p j s t", j=4, s=16))
                    yield
                else:
                    xin = xinP_l[g % 2]
                    P.copy("pool", xin[:, :, 0:3], histP[:, 4 * g:4 * g + 4, :])
                    yield
                    P.copy("act", xin[:, :, 3:131], ps[:, :].rearrange("p (j t) -> p j t", j=4))
                    yield
                    P.copy("pool", histP[:, 4 * g:4 * g + 4, :], xin[:, :, 128:131])
                    yield
                for j in range(4):
                    c = 4 * g + j
                    a_ = accs[c % 4]
                    if typ == 0:
                        av = a_[:, :].rearrange("p (s t) -> p s t", s=16)
                        sh = [xin[:, j, :, k:k + 8] for k in range(4)]
                    else:
                        av = a_[:, :]
                        sh = [xin[:, j, k:k + 128] for k in range(4)]
                    items.append((av, a_, sh, dst_fn(c), c))
                    yield from slot(c)
            yield from slot(4 * ngroups)
            yield from slot(4 * ngroups + 1)

        def run_gen(g):
            for _ in g:
                pass

        w_in_v = w_in.rearrange("(kc p) n -> p kc n", p=128)
        if STOP_AFTER >= 1 and not SKIP1:
          with contextlib.ExitStack() as st1:
            wI = WBlocks(st1, "wI", 8, [0, 512, 1024, 1536, 2048, 2560, 2576])
            wI.load(w_in_v, 0, order=[2, 3, 4, 0, 1, 5])
            modv = sb(st1, "modv", [128, 3, D])
            cw = sb(st1, "cw", [128, 12, 5])
            nw = sb(st1, "ssdnw", [128, D])
            histS = sb(st1, "histS", [128, 12, 16, 3])
            histP = sb(st1, "histP", [128, 12, 3])
            P.dma("sp", cw[:, :, :], ssd_cw[:, :, :])
            P.dma("sp", nw[:, :], ssd_nw[:, :])
            P.dma("sp", histS[:, :, :, :], hist_ssd[:, :, :, :])
            P.memset("pool", histP[:, :, :], 0.0)
            xt = [sb(st1, "xt%d" % i, [128, D]) for i in range(2)]
            tmpA = sb(st1, "tmpA", [128, D])
            hb = sb(st1, "hb", [128, D], BF16)
            hT = sb(st1, "hT", [128, 8, 128], BF16)
            stt_ = sb(st1, "stt", [128, 4])
            xinS_l = [sb(st1, "xinS%d" % i, [128, 4, 16, 11]) for i in range(2)]
            xinP_l = [sb(st1, "xinP%d" % i, [128, 4, 131]) for i in range(2)]
            acc = [sb(st1, "acc%d" % i, [128, 128]) for i in range(4)]
            xc_2 = [sb(st1, "xc_%d" % i, [128, 12, 128], BF16) for i in range(2)]
            zs_2 = [sb(st1, "zs_%d" % i, [128, D], BF16) for i in range(2)]
            sm = sb(st1, "sm", [128, 48])
            af_2 = [sb(st1, "af_%d" % i, [128, 16]) for i in range(2)]
            ex_2 = [sb(st1, "ex_%d" % i, [128, 48]) for i in range(2)]
            xtm = sb(st1, "xtm", [128, D], BF16)
            xdt_2 = [sb(st1, "xdt_%d" % i, [128, D], BF16) for i in range(2)]
            xDs_2 = [sb(st1, "xDs_%d" % i, [128, D], BF16) for i in range(2)]
            xw_2 = [sb(st1, "xw_%d" % i, [128, D], BF16) for i in range(2)]
            Btm_2 = [sb(st1, "Btm_%d" % i, [128, 2, 128], BF16) for i in range(2)]
            CBm_2 = [sb(st1, "CBm_%d" % i, [128, 2, 128]) for i in range(2)]
            L4 = [sb(st1, "L4_%d" % i, [128, 4, 128]) for i in range(2)]
            E4 = [sb(st1, "E4_%d" % i, [128, 4, 128]) for i in range(2)]
            M4 = [sb(st1, "M4_%d" % i, [128, 4, 128], BF16) for i in range(2)]
            yo = sb(st1, "yo", [128, D])
            yy = sb(st1, "yy", [128, D])
            ss = sb(st1, "ss", [128, 8])
            ysb = [sb(st1, "ysb%d" % i, [128, D], BF16) for i in range(2)]
            hTf = sb(st1, "hTf", [128, D])
            hTb = sb(st1, "hTb", [128, D], BF16)
            lhs3 = sb(st1, "lhs3", [128, 8, 48], BF16)
            cv = sb(st1, "cv", [48, 1536])
            CmT = [sb(st1, "CmT%d" % g, [128, 16 * 128], BF16) for g in range(2)]
            h0 = [sb(st1, "h0_%d" % i, [128, 8, 128]) for i in range(2)]
            h0Tb = sb(st1, "h0Tb", [128, D], BF16)
            xwm = sb(st1, "xwm", [128, D], BF16)
            hn = [sb(st1, "hn%d" % i, [128, 8, 128]) for i in range(2)]
            rsel = sb(st1, "rsel", [128, 2, 128])
            decsel = sb(st1, "decsel", [128, 128])
            D2N = ("xc", "zs", "af", "ex", "xdt", "xDs", "xw", "Btm", "CBm")
            D2 = dict(xc=xc_2, zs=zs_2, af=af_2, ex=ex_2, xdt=xdt_2, xDs=xDs_2, xw=xw_2, Btm=Btm_2, CBm=CBm_2)
            P.memset("pool", hTf[:, :], 0.0)
            P.memset("pool", hTb[:, :], 0.0)
            for g in range(2):
                P.memset("pool", CmT[g][:, :], 0.0)

            def ssd_gen(ti):
                typ = 0 if ti == 0 else 1
                xc, zs, af, ex, xdt, xDs, xw, Btm, CBm = (D2[n][ti % 2] for n in D2N)
                PY = [PS[2], PS[3]] if typ == 1 else [PS[0], PS[1]]
                PYO = [PS[5], PS[6]] if typ == 1 else [PS[2], PS[3]]
                mk = masks[typ]
                if ti <= 1:
                    P.dma("sp", modv[:, :, :], modscr[typ].rearrange("p (s d) -> p s d", s=6)[:, 0:3, :])
                    yield
                x_ = xt[ti % 2]
                P.dma("sp", x_[:, :], xs_all[ti])
                yield
                yield from norm_to_T_g(x_, modv, 1, 0, tmpA, hb, hT, stt_)
                yield from proj_conv_gen(typ, 3, wI, 1024, hT, PS[0:2], xinS_l, xinP_l, histS, histP, cw, True, acc,
                                         lambda c: xc[:, c, :])
                P.sec("z (token-major) -> silu")
                for nb in range(2):
                    ps = PS[nb]
                    P.mmk(ps[:, :], [(hT[:, kc, :], wI.col(kc, nb * 512, 512)) for kc in range(8)])
                    yield
                    P.act(zs[:, nb * 512:(nb + 1) * 512], ps[:, :], AF.Silu)
                    yield
                P.sec("dt")
                pd = PS[4]
                P.mmk(pd[:, 0:16], [(hT[:, kc, :], wI.col(kc, 2560, 16)) for kc in range(8)])
                yield
                P.tt("dve", sm[:, 0:16], pd[:, 0:16], smv[:, 0:16], ALU.add)
                yield
                P.act(sm[:, 16:32], sm[:, 0:16], AF.Exp)
                yield
                P.act(sm[:, 32:48], sm[:, 16:32], AF.Ln, bias=cst[:, 1:2], scale=1.0)
                yield
                P.tt("dve", af[:, :], sm[:, 32:48], negA[:, 0:16], ALU.mult)
                yield
                yield from small_scan_terms_g(af[:, :], 16, typ, PS[4][:, 64:112], ex)
                P.sec("token-major x, xdt, xD, xw, B")
                P.trs([(PT[:, c * 128:(c + 1) * 128], xc[:, c, :]) for c in range(8)], identb[:, :])
                yield
                P.copy("act", xtm[:, :], PT[:, :])
                yield
                x3 = xtm[:, :].rearrange("p (h d) -> p h d", h=16)
                P.tt("dve", xdt[:, :].rearrange("p (h d) -> p h d", h=16), x3, bc(sm[:, 32:48], 2, 64), ALU.mult)
                yield
                P.tt("pool", xDs[:, :].rearrange("p (h d) -> p h d", h=16), x3, bc(smv[:, 32:48], 2, 64), ALU.mult)
                yield
                P.tt("pool", xw[:, :].rearrange("p (h d) -> p h d", h=16),
                     xdt[:, :].rearrange("p (h d) -> p h d", h=16), bc(ex[:, 16:32], 2, 64), ALU.mult)
                yield
                P.trs([(PT[:, g * 128:(g + 1) * 128], xc[:, 8 + g, :]) for g in range(2)], identb[:, :])
                yield
                P.copy("act", Btm[:, :, :], PT[:, 0:256].rearrange("p (g n) -> p g n", g=2))
                yield
                P.sec("CB^T masked")
                pc = PS[4][:, 256:512]
                for g in range(2):
                    P.mm(pc[:, g * 128:(g + 1) * 128], xc[:, 8 + g, :], xc[:, 10 + g, :])
                    yield
                P.tt("dve", CBm[:, :, :], pc[:, 0:256].rearrange("p (g l) -> p g l", g=2), bc(mk[:, MINC, :], 1, 2), ALU.mult)
                yield
                P.sec("y_off raw")
                if typ == 1:
                    yield "HALF"
                else:
                    for g in range(2):
                        base = CmT[g][:, :]
                        dst = bass.AP(base.tensor, base.offset, [[16 * 128, 128], [136, 16], [1, 8]])
                        P.op("pool", (lambda e, dst=dst, src=xc[:, 10 + g, :].rearrange("p (s t) -> p s t", s=16):
                                      e.tensor_copy(out=dst, in_=src)), ins=[xc[:, 0, :]], outs=[base])
                        yield
                    for hh in range(2):
                        P.tt("pool", rsel[:, hh, :].rearrange("p (j s) -> p j s", j=8),
                             bc(af[:, hh:16:2], 2, 16), bc(blk16[:, :], 1, 8), ALU.mult)
                        yield
                        P.mm(PS[5][:, 256 + hh * 128:256 + (hh + 1) * 128], onesf[:, :], rsel[:, hh, :])
                        yield
                    P.act(decsel[0:64, :], PS[5][0:64, 256:384], AF.Exp)
                    yield
                    P.act(decsel[64:128, :], PS[5][64:128, 384:512], AF.Exp)
                    yield
                    for s in range(16):
                        h0_ = h0[s % 2]
                        hn_ = hn[s % 2]
                        P.dma("sp", h0_[:, :, :], st_ssd[s].rearrange("(j p) n -> p j n", p=128))
                        yield
                        pa, pb = PS[0], PS[1]
                        P.trs([((pa if j < 4 else pb)[:, (j % 4) * 128:(j % 4 + 1) * 128], h0_[:, j, :]) for j in range(8)],
                              identf[:, :])
                        yield
                        P.copy("act", h0Tb[:, 0:512], pa[:, :])
                        yield
                        P.copy("dve", h0Tb[:, 512:1024], pb[:, :])
                        yield
                        for g in range(2):
                            P.op("pe", (lambda e, o=PS[2 + g][:, :], l=CmT[g][:, s * 128:(s + 1) * 128],
                                        r=h0Tb[:, g * 512:(g + 1) * 512], s=s:
                                        e.matmul(o, lhsT=l, rhs=r, start=(s == 0), stop=(s == 15))),
                                 ins=[CmT[g][:, :], h0Tb[:, :]], outs=[PS[2 + g][:, :]])
                            yield
                        P.act(xwm[:, :], xw[:, :], AF.Identity, scale=blk16[:, s:s + 1])
                        yield
                        for half in range(2):
                            pn = PS[half]
                            for jj in range(4):
                                j = half * 4 + jj
                                P.mm(pn[:, jj * 128:(jj + 1) * 128], xwm[:, j * 128:(j + 1) * 128], Btm[:, j // 4, :])
                                yield
                            for jj in range(4):
                                j = half * 4 + jj
                                P.stt("dve", hn_[:, j, :], h0_[:, j, :], decsel[:, j * 16 + s:j * 16 + s + 1],
                                      pn[:, jj * 128:(jj + 1) * 128], ALU.mult, ALU.add)
                                yield
                        P.dma("sp", o_ssd_s[s].rearrange("(j p) n -> p j n", p=128), hn_[:, :, :])
                        yield
                P.sec("per-head intra-chunk")
                for q in range(4):
                    g = q // 2
                    L_, E_, M_ = L4[q % 2], E4[q % 2], M4[q % 2]
                    P.tt("dve", L_[:, :, :], bc(mk[:, MSTT, :], 1, 4), bc(af[:, 4 * q:4 * q + 4], 2, 128), ALU.mult)
                    yield
                    pg = PS[5 + (q % 2)]
                    for j in range(4):
                        P.mm(pg[:, j * 128:(j + 1) * 128], L_[:, j, :], mk[:, MINC, :])
                        yield
                    P.act(E_[:, :, :], pg[:, :].rearrange("p (j l) -> p j l", j=4), AF.Exp)
                    yield
                    P.tt("dve", M_[:, :, :], E_[:, :, :], bc(CBm[:, g, :], 1, 4), ALU.mult)
                    yield
                    py = PY[g]
                    for j in range(4):
                        h = 4 * q + j
                        hh = h % 8
                        P.mmk(py[:, hh * 64:(hh + 1) * 64],
                              [(identb[:, :], xDs[:, h * 64:(h + 1) * 64]), (M_[:, j, :], xdt[:, h * 64:(h + 1) * 64])])
                        yield
                P.sec("combine y = y_diag + eacs * y_off")
                if typ == 1:
                    for g in range(2):
                        P.mm(PYO[g][:, :], xc[:, 10 + g, :], hTb[:, g * 512:(g + 1) * 512])
                        yield
                for g in range(2):
                    P.copy("act", yo[:, g * 512:(g + 1) * 512], PYO[g][:, :])
                    yield
                P.tt("dve", yo[:, :].rearrange("p (h d) -> p h d", h=16), yo[:, :].rearrange("p (h d) -> p h d", h=16),
                     bc(ex[:, 0:16], 2, 64), ALU.mult)
                yield
                for g in range(2):
                    P.tt("dve", yy[:, g * 512:(g + 1) * 512], yo[:, g * 512:(g + 1) * 512], PY[g][:, :], ALU.add)
                    yield
                P.sec("gate with silu(z), group rmsnorm")
                P.tt("dve", yy[:, :], yy[:, :], zs[:, :], ALU.mult)
                yield
                for g in range(2):
                    P.sqsum(yo[:, g * 512:(g + 1) * 512], yy[:, g * 512:(g + 1) * 512], ss[:, g:g + 1])
                    yield
                P.act(ss[:, 2:4], ss[:, 0:2], AF.Sqrt, bias=cst[:, 0:1], scale=1.0 / 512)
                yield
                P.recip(ss[:, 4:6], ss[:, 2:4])
                yield
                y_ = ysb[ti % 2]
                for g in range(2):
                    P.stt("dve", y_[:, g * 512:(g + 1) * 512], yy[:, g * 512:(g + 1) * 512], ss[:, 4 + g:5 + g],
                          nw[:, g * 512:(g + 1) * 512], ALU.mult, ALU.mult)
                    yield
                P.dma("sp", yssd_scr[ti], y_[:, :])
                yield
                P.sec("state update (prompt chunks)")
                if typ == 1:
                    for g in range(2):
                        P.mm(PYO[g][:, :], Btm[:, g, :], xw[:, g * 512:(g + 1) * 512])
                        yield
                    h3 = hTf[:, :].rearrange("p (h d) -> p h d", h=16)
                    P.tt("dve", h3, h3, bc(ex[:, 32:48], 2, 64), ALU.mult)
                    yield
                    for g in range(2):
                        P.tt("dve", hTf[:, g * 512:(g + 1) * 512], hTf[:, g * 512:(g + 1) * 512], PYO[g][:, :], ALU.add)
                        yield
                    P.copy("act", hTb[:, :], hTf[:, :])
                    yield
                P.sec("conv-state outputs (last 3 raw xbc rows)")
                if ti == 0 or ti == NT - 1:
                    M3 = 48 if typ == 0 else 3
                    if typ == 0:
                        P.copy("pool", lhs3[:, :, :].rearrange("p k (s t) -> p k s t", s=16),
                               hT[:, :, :].rearrange("p k (s t) -> p k s t", s=16)[:, :, :, 5:8])
                        yield
                    else:
                        P.copy("pool", lhs3[:, :, 0:3], hT[:, :, 125:128])
                        yield
                    for nb in range(3):
                        ps = PS[5 + (nb % 2)]
                        P.mmk(ps[0:M3, :], [(lhs3[:, kc, 0:M3], wI.col(kc, 1024 + nb * 512, 512))
                                            for kc in range(8)])
                        yield
                        P.copy("act", cv[0:M3, nb * 512:(nb + 1) * 512], ps[0:M3, :])
                        yield
                    P.dma("sp", (o_ssdc_s if typ == 0 else o_ssdc_p)[:, :], cv[0:M3, :])
                    yield

            def rolling(genf, first, last):
                nxt_ = first + 1
                cur = genf(first)
                young = None
                hold = [False]
                while cur is not None:
                    try:
                        v = next(cur)
                        if v == "HALF" and young is None and nxt_ < last:
                            young = genf(nxt_)
                            nxt_ += 1
                    except StopIteration:
                        cur, young = young, None
                        if hold[0] and cur is not None and nxt_ < last:
                            young = genf(nxt_)
                            nxt_ += 1
                        hold[0] = False
                        if cur is None and nxt_ < last:
                            cur = genf(nxt_)
                            nxt_ += 1
                        continue
                    if young is not None and not hold[0]:
                        try:
                            if next(young) == "HALF":
                                hold[0] = True
                        except StopIteration:
                            young = None

            run_gen(ssd_gen(0))
            rolling(ssd_gen, 1, NT)
            for half in range(2):
                ps = PS[half]
                P.trs([(ps[:, jj * 128:(jj + 1) * 128], hTf[:, (half * 4 + jj) * 128:(half * 4 + jj + 1) * 128])
                       for jj in range(4)], identf[:, :])
                P.copy("act", hn[half][:, 0:4, :], ps[:, :].rearrange("p (j n) -> p j n", j=4))
                P.dma("sp", o_ssd_p[half * 512:(half + 1) * 512, :].rearrange("(j p) n -> p j n", p=128), hn[half][:, 0:4, :])
            P.barrier()
            P.emit()
        ps1.close()
        if STOP_AFTER >= 2:
          with contextlib.ExitStack() as st2:
            PA = [st2.enter_context(nc.psum_tensor("pa%d" % i, [128, 512], F32)) for i in range(2)]
            PM = st2.enter_context(nc.psum_tensor("pm", [128, 512], F32))
            LPB = [st2.enter_context(nc.psum_tensor("lpb%d" % i, [128, 512], F32)) for i in range(4)]
            LPS = [[LPB[ln][:, i * 128:(i + 1) * 128] for i in range(4)] for ln in range(4)]
            wII = WBlocks(st2, "wII", 8, [0, 512, 1024, 1536, 2048, 2560, 3072, 3584, 4096, 4112])
            wII.load(w_in_v, 2576)
            wO = WBlocks(st2, "wO", 16, [0, 512, 1024])
            w_out_v = w_out.rearrange("(kc p) n -> p kc n", p=128)
            wO.load(w_out_v, 0)
            modv = sb(st2, "modv2", [128, 3, D])
            cwg = sb(st2, "cwg", [128, 24, 4])
            gnw = sb(st2, "gnw", [128, 128])
            histP = sb(st2, "histPg", [128, 24, 3])
            P.dma("sp", cwg[:, :, :], gdn_cw[:, :, :])
            P.dma("sp", gnw[:, :], gdn_nw[:, :])
            P.memset("pool", histP[:, :, :], 0.0)
            tmpA = sb(st2, "tmpA2", [128, D])
            hT = sb(st2, "hT2", [128, 8, 128], BF16)
            stt_ = sb(st2, "stt2", [128, 4])
            xinP_l = [sb(st2, "xinP2_%d" % i, [128, 4, 131]) for i in range(2)]
            acc = [sb(st2, "acc2_%d" % i, [128, 128]) for i in range(4)]
            lhs3 = sb(st2, "lhs3g", [128, 8, 48], BF16)
            tmpT = sb(st2, "tmpT2", [128, D])
            sttT = sb(st2, "sttT2", [128, 4])
            ss8 = sb(st2, "ss8", [128, 24])
            otm = sb(st2, "otm", [128, D])
            mixed = sb(st2, "mixed", [128, 2 * D], BF16)
            mixedT = sb(st2, "mixedT", [128, 16, 128], BF16)

            class FB:
                pass

            def make_fb(i, stk):
                fb = FB()
                fb.xt = sb(stk, "f%d_xt" % i, [128, D])
                fb.hb = sb(stk, "f%d_hb" % i, [128, D], BF16)
                fb.qk = sb(stk, "f%d_qk" % i, [128, 16, 128], BF16)
                fb.vfm = sb(stk, "f%d_vfm" % i, [128, 8, 128], BF16)
                fb.ktm = sb(stk, "f%d_ktm" % i, [128, 8, 128], BF16)
                fb.vtm = sb(stk, "f%d_vtm" % i, [128, 8, 128], BF16)
                fb.gs = sb(stk, "f%d_gs" % i, [128, D], BF16)
                fb.sm = sb(stk, "f%d_sm" % i, [128, 64])
                fb.gf = sb(stk, "f%d_gf" % i, [128, 8])
                fb.ex = sb(stk, "f%d_ex" % i, [128, 24])
                return fb

            class Lane:
                pass

            lanes = []

            def make_lane(ln, stk):
                B = Lane()
                B.ps = LPS[ln]
                f = lambda nm, dt=F32: sb(stk, "ln%d_%s" % (ln, nm), [128, 128], dt)
                B.P = [f("P0"), f("P1")]
                B.PT = [f("PT0"), f("PT1")]
                B.X = [f("X0"), f("X1")]
                B.ot = f("ot")
                B.L, B.Dm, B.DmI, B.DmS = B.ot, B.X[1], B.PT[1], B.P[1]
                B.attnT, B.TTb, B.R, B.vn, B.kout = (f("attnT", BF16), f("TTb", BF16), f("R", BF16),
                                                     f("vn", BF16), f("kout", BF16))
                lanes.append(B)

            make_lane(0, st2)
            fbs = [make_fb(0, st2)]

            def front_gen(ti, typ, SB, fb):
                xt, hb, qk, vfm, sm, gf, ex = fb.xt, fb.hb, fb.qk, fb.vfm, fb.sm, fb.gf, fb.ex
                P.sec("F:load+norm")
                if ti <= 1:
                    P.dma("sp", modv[:, :, :], modscr[typ].rearrange("p (s d) -> p s d", s=6)[:, 0:3, :])
                    yield
                P.dma("sp", xt[:, :], xs_all[ti])
                yield
                yield from norm_to_T_g(xt, modv, 1, 0, tmpA, hb, hT, stt_)
                yield
                yield from proj_conv_gen(typ, 6, wII, 0, hT, PA, (SB.xinS_l if typ == 0 else None), xinP_l,
                                         (SB.histS if typ == 0 else None), histP, cwg, False, acc,
                                         lambda c: (qk[:, c, :] if c < 16 else vfm[:, c - 16, :]))
                if ti == 0 or ti == NT - 1:
                    P.sec("F:convstate")
                    M3 = 48 if typ == 0 else 3
                    if typ == 0:
                        P.copy("pool", lhs3[:, :, :].rearrange("p k (s t) -> p k s t", s=16),
                               hT[:, :, :].rearrange("p k (s t) -> p k s t", s=16)[:, :, :, 5:8])
                        yield
                    else:
                        P.copy("pool", lhs3[:, :, 0:3], hT[:, :, 125:128])
                        yield
                    for cc in range(3):
                        for nb in range(2):
                            c0 = cc * 1024 + nb * 512
                            P.mmk(PA[nb][0:M3, :], [(lhs3[:, kc, 0:M3], wII.col(kc, c0, 512)) for kc in range(8)])
                            yield
                            P.copy("act", tmpA[0:M3, nb * 512:(nb + 1) * 512], PA[nb][0:M3, :])
                            yield
                        P.dma("sp", (o_gdnc_s if typ == 0 else o_gdnc_p)[:, cc * 1024:(cc + 1) * 1024], tmpA[0:M3, :])
                        yield
                    yield
                P.sec("F:gate")
                for nb in range(2):
                    P.mmk(PA[nb][:, :], [(hT[:, kc, :], wII.col(kc, 3072 + nb * 512, 512))
                                         for kc in range(8)])
                    yield
                    P.act(fb.gs[:, nb * 512:(nb + 1) * 512], PA[nb][:, :], AF.Silu)
                    yield
                yield
                P.sec("F:beta/g")
                P.mmk(PM[:, 0:16], [(hT[:, kc, :], wII.col(kc, 4096, 16)) for kc in range(8)])
                yield
                P.act(sm[:, 0:8], PM[:, 0:8], AF.Exp, scale=-1.0)
                yield
                P.ts("dve", sm[:, 0:8], sm[:, 0:8], 1.0, None, ALU.add)
                yield
                P.recip(sm[:, 8:16], sm[:, 0:8])
                yield
                P.ts("dve", sm[:, 16:24], sm[:, 8:16], -1.0, None, ALU.mult)
                yield
                P.tt("dve", sm[:, 24:32], PM[:, 8:16], smv[:, 48:56], ALU.add)
                yield
                P.act(sm[:, 32:40], sm[:, 24:32], AF.Exp)
                yield
                P.act(sm[:, 40:48], sm[:, 32:40], AF.Ln, bias=cst[:, 1:2], scale=1.0)
                yield
                P.tt("dve", gf[:, :], sm[:, 40:48], negA[:, 16:24], ALU.mult)
                yield
                yield
                P.sec("F:beta/g")
                yield from small_scan_terms_g(gf[:, :], 8, typ, PM[:, 64:88], ex)
                P.ts("dve", sm[:, 48:56], ex[:, 0:8], -1.0, None, ALU.mult)
                yield
                yield
                for half in range(2):
                    P.sec("F:l2norm")
                    src = qk[:, half * 8:(half + 1) * 8, :]
                    P.tt("pool", hb[:, :].rearrange("p (c t) -> p c t", c=8), src, src, ALU.mult)
                    yield
                    for i in range(2):
                        rq = tmpA[:, i * 512:(i + 1) * 512]
                        P.mm(PA[i][:, :], onesb[:, :], hb[:, i * 512:(i + 1) * 512])
                        yield
                        if half == 0:
                            P.act(rq, PA[i][:, :], AF.Sqrt, bias=cst[:, 2:3], scale=128.0)
                            yield
                        else:
                            P.act(rq, PA[i][:, :], AF.Sqrt, bias=cst[:, 0:1], scale=1.0)
                            yield
                        P.recip(rq, rq)
                        yield
                        dst = qk[:, half * 8 + 4 * i:half * 8 + 4 * i + 4, :]
                        P.tt("dve", dst, dst, rq.rearrange("p (c t) -> p c t", c=4), ALU.mult)
                        yield
                    yield
                P.sec("F:transposes")
                P.copy("pool", hb[:, :].rearrange("p (c t) -> p c t", c=8), qk[:, 8:16, :])
                yield
                P.trs([(PT[:, j * 128:(j + 1) * 128], qk[:, 8 + j, :]) for j in range(8)], identb[:, :])
                yield
                P.copy("act", fb.ktm[:, :, :], PT[:, :].rearrange("p (h d) -> p h d", h=8))
                yield
                yield
                P.sec("F:transposes")
                P.trs([(PT[:, j * 128:(j + 1) * 128], vfm[:, j, :]) for j in range(8)], identb[:, :])
                yield
                P.copy("act", fb.vtm[:, :, :], PT[:, :].rearrange("p (h d) -> p h d", h=8))
                yield
                if typ == 0:
                    P.tt("pool", SB.rselg[:, :].rearrange("p (h s) -> p h s", h=8),
                         bc(gf[:, :], 2, 16), bc(blk16[:, :], 1, 8), ALU.mult)
                    yield
                    P.mm(PM[:, 128:256], onesf[:, :], SB.rselg[:, :])
                    yield
                    P.act(SB.gendsel[:, :], PM[:, 128:256], AF.Exp)
                    yield
                yield

            def head_gen(h, B, typ, SB, fb):
                mk = masks[typ]
                m_lev = 3 if typ == 0 else 7
                qk, hb, sm, gf, ex, ktm, vtm = fb.qk, fb.hb, fb.sm, fb.gf, fb.ex, fb.ktm, fb.vtm
                kT = qk[:, 8 + h, :]
                qT = qk[:, h, :]
                _st = ["H:decay"]
                P.sec(_st[0])
                P.act(B.L[:, :], mk[:, MSTT, :], AF.Identity, scale=gf[:, h:h + 1])
                yield
                P.mm(B.ps[3][:, :], B.L[:, :], mk[:, MINC, :])
                yield
                P.act(B.Dm[:, :], B.ps[3][:, :], AF.Exp)
                yield
                P.tt("dve", B.DmI[:, :], B.Dm[:, :], mk[:, MINC, :], ALU.mult)
                yield
                P.tt("dve", B.DmS[:, :], B.Dm[:, :], mk[:, MSTR, :], ALU.mult)
                yield
                P.mm(B.ps[0][:, :], kT, hb[:, h * 128:(h + 1) * 128])
                yield
                P.mm(B.ps[1][:, :], kT, qT)
                yield
                P.stt("dve", B.P[0][:, :], B.ps[0][:, :], sm[:, 16 + h:17 + h], B.DmS[:, :], ALU.mult, ALU.mult)
                yield
                P.tt("dve", B.attnT[:, :], B.ps[1][:, :], B.DmI[:, :], ALU.mult)
                yield
                yield
                _st[0] = "H:dbl"
                P.sec(_st[0])
                P.tr(B.ps[2][:, :], B.P[0][:, :], identf[:, :])
                yield
                P.copy("act", B.PT[0][:, :], B.ps[2][:, :])
                yield
                P.tt("dve", B.X[0][:, :], B.P[0][:, :], identf[:, :], ALU.add)
                yield
                yield
                P.sec(_st[0])
                if m_lev > 1:
                    P.mm(B.ps[0][:, :], B.P[0][:, :], B.PT[0][:, :])
                    yield
                    if m_lev > 2:
                        P.mm(B.ps[1][:, :], B.PT[0][:, :], B.P[0][:, :])
                        yield
                    P.copy("act", B.PT[1][:, :], B.ps[0][:, :])
                    yield
                    if m_lev > 2:
                        P.copy("act", B.P[1][:, :], B.ps[1][:, :])
                        yield
                    yield
                    P.sec(_st[0])
                xi = 0
                for j in range(1, m_lev):
                    cur, nx = j % 2, (j + 1) % 2
                    if j + 1 < m_lev:
                        P.mm(B.ps[0][:, :], B.P[cur][:, :], B.PT[cur][:, :])
                        yield
                        if j + 2 < m_lev:
                            P.mm(B.ps[1][:, :], B.PT[cur][:, :], B.P[cur][:, :])
                            yield
                    P.mm(B.ps[2][:, :], B.PT[cur][:, :], B.X[xi][:, :])
                    yield
                    if j + 1 < m_lev:
                        P.copy("act", B.PT[nx][:, :], B.ps[0][:, :])
                        yield
                        if j + 2 < m_lev:
                            P.copy("act", B.P[nx][:, :], B.ps[1][:, :])
                            yield
                    P.tt("dve", B.X[1 - xi][:, :], B.X[xi][:, :], B.ps[2][:, :], ALU.add)
                    yield
                    xi = 1 - xi
                    yield
                    P.sec(_st[0])
                _st[0] = "H:state"
                P.sec(_st[0])
                P.copy("act", B.TTb[:, :], B.X[xi][:, :])
                yield
                if typ == 1:
                    Sf_h, Sb_h = SB.Sf[h], SB.Sb[h]
                    P.mm(B.ps[3][:, :], kT, Sb_h[:, :])
                    yield
                else:
                    P.dma("sp", SB.S0f[:, :, :], st_gdn[:, h, :, :].rearrange("s d e -> d s e"))
                    yield
                    P.copy("act", SB.S0b[:, :, :], SB.S0f[:, :, :])
                    yield
                    for (dstT, srcT) in ((SB.kTm, kT), (SB.qTm, qT)):
                        base = dstT[:, :]
                        dap = bass.AP(base.tensor, base.offset, [[16 * 128, 128], [136, 16], [1, 8]])
                        P.op("pool", (lambda e, dap=dap, src=srcT.rearrange("p (s t) -> p s t", s=16):
                                      e.tensor_copy(out=dap, in_=src)), ins=[srcT], outs=[base])
                        yield
                    P.mmk(B.ps[3][:, :], [(SB.kTm[:, s * 128:(s + 1) * 128], SB.S0b[:, s, :]) for s in range(16)])
                    yield
                P.stt("dve", B.R[:, :], B.ps[3][:, :], sm[:, 48 + h:49 + h], vtm[:, h, :], ALU.mult, ALU.add)
                yield
                P.mm(B.ps[0][:, :], B.TTb[:, :], B.R[:, :])
                yield
                P.act(B.vn[:, :], B.ps[0][:, :], AF.Identity, scale=sm[:, 8 + h:9 + h])
                yield
                yield
                P.sec(_st[0])
                if typ == 1:
                    P.mm(B.ps[1][:, :], qT, Sb_h[:, :])
                    yield
                else:
                    P.mmk(B.ps[1][:, :], [(SB.qTm[:, s * 128:(s + 1) * 128], SB.S0b[:, s, :]) for s in range(16)])
                    yield
                P.mm(B.ps[2][:, :], B.attnT[:, :], B.vn[:, :])
                yield
                P.act(B.ot[:, :], B.ps[1][:, :], AF.Identity, scale=ex[:, h:h + 1])
                yield
                P.tt("dve", otm[:, h * 128:(h + 1) * 128], B.ot[:, :], B.ps[2][:, :], ALU.add)
                yield
                P.act(B.kout[:, :], ktm[:, h, :], AF.Identity, scale=ex[:, 8 + h:9 + h])
                yield
                yield
                P.sec(_st[0])
                if typ == 1:
                    P.mm(B.ps[3][:, :], B.kout[:, :], B.vn[:, :])
                    yield
                    P.stt("dve", Sf_h[:, :], Sf_h[:, :], ex[:, 16 + h:17 + h], B.ps[3][:, :], ALU.mult, ALU.add)
                    yield
                    P.copy("act", Sb_h[:, :], Sf_h[:, :])
                    yield
                else:
                    P.tt("dve", SB.koutm_all[:, :, :], bc(B.kout[:, :], 1, 16), bc(blk16[:, :], 2, 128), ALU.mult)
                    yield
                    for s in range(16):
                        P.mm(LPB[s // 4][:, (s % 4) * 128:(s % 4 + 1) * 128], SB.koutm_all[:, s, :], B.vn[:, :])
                    yield
                    for j4 in range(4):
                        S4 = SB.S0f[:, 4 * j4:4 * j4 + 4, :]
                        gsel = SB.gendsel[:, h * 16 + 4 * j4:h * 16 + 4 * j4 + 4]
                        P.tt("dve", S4, S4, bc(gsel, 2, 128), ALU.mult)
                        yield
                        P.tt("dve", S4, S4, LPB[j4][:, :].rearrange("p (s e) -> p s e", s=4), ALU.add)
                        yield
                    P.dma("sp", o_gdn_s[:, h, :, :].rearrange("s d e -> d s e"), SB.S0f[:, :, :])
                    yield
                yield

            def tail_gen(ti, fb):
                xt = fb.xt
                P.sec("T:onorm")
                P.dma("sp", mixed[:, 0:D], yssd_scr[ti])
                yield
                P.tt("pool", tmpT[:, :], otm[:, :], otm[:, :], ALU.mult)
                yield
                P.red("dve", ss8[:, 0:8], tmpT[:, :].rearrange("p (h d) -> p h d", h=8))
                yield
                P.act(ss8[:, 8:16], ss8[:, 0:8], AF.Sqrt, bias=cst[:, 0:1], scale=1.0 / 128)
                yield
                P.recip(ss8[:, 16:24], ss8[:, 8:16])
                yield
                o3 = otm[:, :].rearrange("p (h d) -> p h d", h=8)
                P.tt("dve", o3, o3, bc(ss8[:, 16:24], 2, 128), ALU.mult)
                yield
                P.tt("dve", o3, o3, bc(gnw[:, :], 1, 8), ALU.mult)
                yield
                P.tt("dve", mixed[:, D:2 * D], otm[:, :], fb.gs[:, :], ALU.mult)
                yield
                yield
                P.sec("T:outproj")
                for half in range(2):
                    P.trs([(PT[:, j * 128:(j + 1) * 128], mixed[:, (half * 8 + j) * 128:(half * 8 + j + 1) * 128])
                           for j in range(8)], identb[:, :])
                    yield
                    P.copy("act", mixedT[:, half * 8:(half + 1) * 8, :], PT[:, :].rearrange("p (c t) -> p c t", c=8))
                    yield
                yield
                P.sec("T:outproj")
                for nb in range(2):
                    P.mmk(PA[nb][:, :], [(mixedT[:, kc, :], wO.col(kc, nb * 512, 512)) for kc in range(16)])
                    yield
                    P.copy("act", tmpT[:, nb * 512:(nb + 1) * 512], PA[nb][:, :])
                    yield
                yield
                P.sec("T:resid")
                P.sqsum(otm[:, :], tmpT[:, :], sttT[:, 0:1])
                yield
                P.act(sttT[:, 1:2], sttT[:, 0:1], AF.Sqrt, bias=cst[:, 0:1], scale=1.0 / D)
                yield
                P.recip(sttT[:, 2:3], sttT[:, 1:2])
                yield
                P.stt("dve", tmpT[:, :], tmpT[:, :], sttT[:, 2:3], modv[:, 2, :], ALU.mult, ALU.mult)
                yield
                P.tt("dve", xt[:, :], tmpT[:, :], xt[:, :], ALU.add)
                yield
                P.dma("sp", x1_scr[ti], xt[:, :])
                yield
                yield

            def speed(g, k):
                while True:
                    for _ in range(k):
                        try:
                            next(g)
                        except StopIteration:
                            return
                    yield

            def run_rr(gens):
                alive = list(gens)
                while alive:
                    nxt = []
                    for gn in alive:
                        try:
                            next(gn)
                            nxt.append(gn)
                        except StopIteration:
                            pass
                    alive = nxt

            def back(ti, typ, SB, fb, nl, side, with_tail=True):
                side = [side] if side is not None else []
                for h0_ in range(0, 8, nl):
                    gens = [head_gen(h0_ + i, lanes[i], typ, SB, fb) for i in range(min(nl, 8 - h0_))]
                    alive = gens + side
                    while any(g in alive for g in gens):
                        nxt = []
                        for gn in alive:
                            try:
                                next(gn)
                                nxt.append(gn)
                            except StopIteration:
                                if gn in side:
                                    side = []
                        alive = nxt
                if with_tail:
                    run_rr([tail_gen(ti, fb)] + side)
                else:
                    run_rr(side)

            class SBufs:
                pass

            with contextlib.ExitStack() as st2s:
                SB = SBufs()
                SB.histS = sb(st2s, "histSg", [128, 24, 16, 3])
                SB.xinS_l = [sb(st2s, "xinS2_%d" % i, [128, 4, 16, 11]) for i in range(2)]
                SB.S0f = sb(st2s, "S0f", [128, 16, 128])
                SB.S0b = sb(st2s, "S0b", [128, 16, 128], BF16)
                SB.kTm = sb(st2s, "kTm", [128, 16 * 128], BF16)
                SB.qTm = sb(st2s, "qTm", [128, 16 * 128], BF16)
                SB.koutm_all = sb(st2s, "koutm_all", [128, 16, 128], BF16)
                SB.rselg = sb(st2s, "rselg", [128, 128])
                SB.gendsel = sb(st2s, "gendsel", [128, 128])
                P.dma("sp", SB.histS[:, :, :, :], hist_gdn[:, :, :, :])
                P.memset("pool", SB.kTm[:, :], 0.0)
                P.memset("pool", SB.qTm[:, :], 0.0)
                run_rr([front_gen(0, 0, SB, fbs[0])])
                back(0, 0, SB, fbs[0], 1, None)
                P.barrier()
                P.emit()
            with contextlib.ExitStack() as st2p:
                SB = SBufs()
                SB.Sf = [sb(st2p, "Sf%d" % h, [128, 128]) for h in range(8)]
                SB.Sb = [sb(st2p, "Sb%d" % h, [128, 128], BF16) for h in range(8)]
                for ln in range(1, P2_NL):
                    make_lane(ln, st2p)
                fbs.append(make_fb(1, st2p))
                for h in range(8):
                    P.memset("pool", SB.Sf[h][:, :], 0.0)
                    P.memset("pool", SB.Sb[h][:, :], 0.0)
                run_rr([front_gen(1, 1, SB, fbs[1])])
                pending_tail = None
                for ti in range(1, NT):
                    parts = []
                    if pending_tail is not None:
                        parts.append(pending_tail)
                    if ti + 1 < NT:
                        parts.append(speed(front_gen(ti + 1, 1, SB, fbs[(ti + 1) % 2]), 2))
                    side = itertools.chain(*parts) if parts else None
                    back(ti, 1, SB, fbs[ti % 2], P2_NL, side, with_tail=False)
                    pending_tail = tail_gen(ti, fbs[ti % 2])
                run_rr([pending_tail])
                for h in range(8):
                    P.dma("sp", o_gdn_p[h * 128:(h + 1) * 128, :], SB.Sf[h][:, :])
                P.barrier()
                P.emit()
        if STOP_AFTER >= 3:
          with contextlib.ExitStack() as st3:
            PB = [st3.enter_context(nc.psum_tensor("pb%d" % i, [128, 512], F32)) for i in range(7)]
            gb = [0, 512, 1024, 1536, 2048, 2560, 2816]
            wG = WBlocks(st3, "wG", 8, gb + [DFF + b for b in gb[1:]])
            w_gu_v = w_gu.rearrange("(kc p) n -> p kc n", p=128)
            wG.load(w_gu_v, 0, order=[j for i in range(6) for j in (i, 6 + i)])
            wD = WBlocks(st3, "wD", 22, [0, 512, 1024])
            w_dn_v = w_dn.rearrange("(kc p) n -> p kc n", p=128)
            wD.load(w_dn_v, 0)
            modv = sb(st3, "modv3", [128, 3, D])
            xt = [sb(st3, "xt3_%d" % i, [128, D]) for i in range(2)]
            tmpA = [sb(st3, "tmpA3_%d" % i, [128, D]) for i in range(2)]
            tmpB = sb(st3, "tmpB3", [128, D])
            hb = [sb(st3, "hb3_%d" % i, [128, D], BF16) for i in range(2)]
            hT = [sb(st3, "hT3_%d" % i, [128, 8, 128], BF16) for i in range(2)]
            stt_ = [sb(st3, "stt3_%d" % i, [128, 4]) for i in range(2)]
            sg = [sb(st3, "sg%d" % i, [128, 512]) for i in range(2)]
            hid = sb(st3, "hid", [128, DFF], BF16)
            hidT = sb(st3, "hidT", [128, 22, 128], BF16)

            def ffn_gen(ti):
                typ = 0 if ti == 0 else 1
                pr = ti % 2
                x_, tA, hb_, hT_, st_ = xt[pr], tmpA[pr], hb[pr], hT[pr], stt_[pr]
                if ti <= 1:
                    P.dma("sp", modv[:, :, :], modscr[typ].rearrange("p (s d) -> p s d", s=6)[:, 3:6, :])
                P.dma("sp", x_[:, :], x1_scr[ti])
                yield
                yield from norm_to_T_g(x_, modv, 1, 0, tA, hb_, hT_, st_)
                for i in range(6):
                    w = 512 if i < 5 else 256
                    pg_, pu_ = PB[(2 * i) % 6], PB[(2 * i + 1) % 6]
                    P.mmk(pg_[:, 0:w], [(hT_[:, kc, :], wG.col(kc, i * 512, w)) for kc in range(8)])
                    yield
                    P.mmk(pu_[:, 0:w], [(hT_[:, kc, :], wG.col(kc, DFF + i * 512, w)) for kc in range(8)])
                    yield
                    s_ = sg[i % 2]
                    P.act(s_[:, 0:w], pg_[:, 0:w], AF.Silu)
                    yield
                    P.tt("dve", hid[:, i * 512:i * 512 + w], s_[:, 0:w], pu_[:, 0:w], ALU.mult)
                    yield
                yield "HALF"
                for grp in range(3):
                    n = 8 if grp < 2 else 6
                    P.trs([(PT[:, j * 128:(j + 1) * 128], hid[:, (grp * 8 + j) * 128:(grp * 8 + j + 1) * 128])
                           for j in range(n)], identb[:, :])
                    yield
                    P.copy("act", hidT[:, grp * 8:grp * 8 + n, :],
                           PT[:, 0:n * 128].rearrange("p (c t) -> p c t", c=n))
                    yield
                for nb in range(2):
                    P.mmk(PB[6][:, :], [(hidT[:, kc, :], wD.col(kc, nb * 512, 512)) for kc in range(22)])
                    yield
                    P.copy("act", tA[:, nb * 512:(nb + 1) * 512], PB[6][:, :])
                    yield
                P.sqsum(tmpB[:, :], tA[:, :], st_[:, 0:1])
                yield
                P.act(st_[:, 1:2], st_[:, 0:1], AF.Sqrt, bias=cst[:, 0:1], scale=1.0 / D)
                yield
                P.recip(st_[:, 2:3], st_[:, 1:2])
                yield
                P.stt("dve", tA[:, :], tA[:, :], st_[:, 2:3], modv[:, 2, :], ALU.mult, ALU.mult)
                yield
                P.tt("dve", x_[:, :], tA[:, :], x_[:, :], ALU.add)
                yield
                P.dma("sp", y_all[ti], x_[:, :])
                yield

            run_gen(ffn_gen(0))
            nxt = 2
            cur = ffn_gen(1)
            young = None
            while cur is not None:
                try:
                    v = next(cur)
                    if v == "HALF" and young is None and nxt < NT:
                        young = ffn_gen(nxt)
                        nxt += 1
                except StopIteration:
                    cur, young = young, None
                    if cur is None and nxt < NT:
                        cur = ffn_gen(nxt)
                        nxt += 1
                    continue
                if young is not None:
                    try:
                        v2 = next(young)
                        if v2 == "HALF":
                            pass
                    except StopIteration:
                        young = None
            P.barrier()
            P.emit()
    return nc


def _prep_inputs(inp):
    f = lambda a: np.ascontiguousarray(np.asarray(a, dtype=np.float32))
    xp, xs = f(inp["x_prompt"]), f(inp["x_sample"])
    cp, cs = f(inp["c_prompt"]), f(inp["c_sample"])
    bcast = lambda v: np.ascontiguousarray(np.broadcast_to(np.asarray(v, np.float32).reshape(1, -1), (128, v.size)))
    shared = {}
    shared["w_ada"] = f(inp["w_ada"][0])
    shared["b_ada_b"] = bcast(inp["b_ada"][0])
    shared["normvecs"] = np.ascontiguousarray(np.stack(
        [bcast(inp[k][0]) for k in ("norm_mix_pre", "norm_mix_post", "norm_ffn_pre", "norm_ffn_post")], axis=1))
    shared["w_in"] = f(inp["w_in"][0])
    shared["w_out"] = f(inp["w_out"][0])
    shared["w_gu"] = f(inp["w_gate_up"][0])
    shared["w_dn"] = f(inp["w_down"][0])
    scw = np.concatenate([f(inp["ssd_conv_w"][0]), f(inp["ssd_conv_b"][0])[None, :]], axis=0)
    shared["ssd_cw"] = np.ascontiguousarray(scw.reshape(5, 12, 128).transpose(2, 1, 0))
    shared["gdn_cw"] = np.ascontiguousarray(f(inp["gdn_conv_w"][0]).reshape(4, 24, 128).transpose(2, 1, 0))
    sv = np.concatenate([f(inp["ssd_dt_bias"][0]), f(inp["ssd_A_log"][0]), f(inp["ssd_D"][0]),
                         f(inp["gdn_dt_bias"][0]), f(inp["gdn_A_log"][0])])
    shared["smallv"] = bcast(sv)
    shared["ssd_nw"] = bcast(inp["ssd_norm_w"][0])
    shared["gdn_nw"] = bcast(inp["gdn_norm_w"][0])
    shared["ident"] = np.eye(128, dtype=np.float32)
    idx = np.arange(128)
    m = np.zeros((2, 128, 4, 128), np.float32)
    for typ, bs in ((0, 8), (1, 128)):
        same = (idx[:, None] // bs) == (idx[None, :] // bs)
        m[typ, :, 0, :] = same & (idx[:, None] > idx[None, :])
        m[typ, :, 1, :] = same & (idx[:, None] <= idx[None, :])
        m[typ, :, 2, :] = same & (idx[:, None] < idx[None, :])
        m[typ, :, 3, :] = same
    shared["masks"] = m
    shared["blk16"] = np.ascontiguousarray(((idx[:, None] // 8) == np.arange(16)[None, :]).astype(np.float32))
    maps = []
    for i in range(NCORES):
        d = dict(shared)
        sl = slice(16 * i, 16 * (i + 1))
        d["xs_all"] = np.ascontiguousarray(np.concatenate(
            [xs[sl].reshape(1, 128, D), xp[i].reshape(16, 128, D)], axis=0))
        d["cexp"] = np.ascontiguousarray(np.stack(
            [np.repeat(cs[sl], 8, axis=0), np.broadcast_to(cp[i][None, :], (128, D))], axis=0))
        d["st_ssd"] = np.ascontiguousarray(f(inp["state_ssd"][0, sl]).reshape(16, 1024, 128))
        hs = f(inp["state_ssd_conv"][0, sl])
        d["hist_ssd"] = np.ascontiguousarray(hs.reshape(16, 3, 12, 128).transpose(3, 2, 0, 1))
        d["st_gdn"] = np.ascontiguousarray(f(inp["state_gdn"][0, sl]))
        hg = f(inp["state_gdn_conv"][0, sl])
        d["hist_gdn"] = np.ascontiguousarray(hg.reshape(16, 3, 24, 128).transpose(3, 2, 0, 1))
        maps.append(d)
    return maps


def kernel(**inp):
    maps = _prep_inputs(inp)
    nc = build_program()
    res = run_bass_kernel_spmd(nc, maps, core_ids=list(range(NCORES)))
    R = res.results
    cat = lambda k: np.stack([np.asarray(r[k]) for r in R], axis=0)
    y_all = cat("y_all")
    y_prompt = y_all[:, 1:].reshape(8, 2048, D)
    y_sample = y_all[:, 0].reshape(128, 8, D)
    ssd_p = cat("o_ssd_p").reshape(1, 8, 16, 64, 128)
    ssdc_p = cat("o_ssdc_p").reshape(1, 8, 3, 1536)
    gdn_p = cat("o_gdn_p").reshape(1, 8, 8, 128, 128)
    gdnc_p = cat("o_gdnc_p").reshape(1, 8, 3, 3072)
    ssd_s = cat("o_ssd_s").reshape(1, 128, 16, 64, 128)
    ssdc_s = cat("o_ssdc_s").reshape(1, 128, 3, 1536)
    gdn_s = cat("o_gdn_s").reshape(1, 128, 8, 128, 128)
    gdnc_s = cat("o_gdnc_s").reshape(1, 128, 3, 3072)
    outs = (y_prompt, y_sample, ssd_p, ssdc_p, gdn_p, gdnc_p, ssd_s, ssdc_s, gdn_s, gdnc_s)
    return tuple(np.ascontiguousarray(o, dtype=np.float32) for o in outs)
```

```python
import contextlib
import itertools
import os
import numpy as np
import concourse.bass as bass
import concourse.mybir as mybir
from concourse.bass_utils import run_bass_kernel_spmd

F32 = mybir.dt.float32
BF16 = mybir.dt.bfloat16
AF = mybir.ActivationFunctionType
ALU = mybir.AluOpType
AX = mybir.AxisListType

NCORES = 8
D = 1024
NT = 17
DFF = 2816
P2_MODE = 0
ANNOTATE = bool(int(os.environ.get('K_ANN', '0')))
SKIP1 = False
P2_STAGE = 99
P2_HSTEP = 99
P2_SKIP = ()
P2_NL = 4
P2_TILES = 16
STOP_AFTER = int(os.environ.get('K_STOP', '99'))


class Prog:
    ENGS = ("pe", "act", "dve", "pool", "sp")

    def __init__(self, nc, stack):
        self.nc = nc
        self.sems = {}
        for e in self.ENGS:
            self.sems[e] = stack.enter_context(nc.semaphore("s_" + e))
        self.dsems = {"sp": [], "pool": [], "act": []}
        for q, n in (("sp", 8), ("pool", 4), ("act", 2)):
            for j in range(n):
                nm = "d_%s%d" % (q, j)
                self.sems[nm] = stack.enter_context(nc.semaphore(nm))
                self.dsems[q].append(nm)
        self.cnt = {k: 0 for k in self.sems}
        self.known = {e: {} for e in self.ENGS}
        self.lastw = {}
        self.readers = {}
        self.ops = {e: [] for e in self.ENGS}
        self.rr = {"sp": 0, "pool": 0, "act": 0}
        self.nops = 0
        self.tag = "init"

    def sec(self, name):
        self.tag = name

    fine = {}

    def _keys(self, aps):
        ks = []
        for a in aps:
            if a is None or isinstance(a, (int, float)):
                continue
            if isinstance(a, str):
                ks.append(a)
                continue
            nm = a.name
            if nm in self.fine:
                row, gr = self.fine[nm]
                nm = "%s:%d" % (nm, (int(a.offset) % row) // gr)
            ks.append(nm)
        return ks

    def _deps(self, eng, r, w):
        need = {}

        def add(c):
            if c is None:
                return
            s, v = c
            if s == "pe" and eng == "pe":
                return
            if need.get(s, 0) < v:
                need[s] = v

        for k in r:
            add(self.lastw.get(k))
        for k in w:
            add(self.lastw.get(k))
            for c in self.readers.get(k, {}).items():
                add(c)
        waits = []
        kn = self.known[eng]
        for s, v in need.items():
            if kn.get(s, 0) < v:
                kn[s] = v
                waits.append((s, v))
        return waits

    def _commit(self, c, r, w):
        for k in w:
            self.lastw[k] = c
            self.readers[k] = {}
        for k in r:
            d = self.readers.setdefault(k, {})
            if d.get(c[0], 0) < c[1]:
                d[c[0]] = c[1]

    def op(self, eng, fn, ins=(), outs=()):
        r = self._keys(ins)
        w = self._keys(outs)
        w = w + [k for k in r if k.startswith("ps") or k.startswith("pa") or k.startswith("pm")
                 or k.startswith("lpb") or k.startswith("pb")]
        waits = self._deps(eng, r, w)
        self.cnt[eng] += 1
        self.ops[eng].append((waits, fn, eng, 1, self.tag))
        self._commit((eng, self.cnt[eng]), r, w)
        self.nops += 1

    def dma(self, q, out, in_, extra_ins=(), extra_outs=()):
        r = self._keys([in_] + list(extra_ins))
        w = self._keys([out] + list(extra_outs))
        waits = self._deps(q, r, w)
        sems = self.dsems[q]
        j = self.rr[q]
        self.rr[q] = (j + 1) % len(sems)
        nm = sems[j]
        prev = self.cnt[nm]
        if prev > 0 and self.known[q].get(nm, 0) < prev:
            self.known[q][nm] = prev
            waits.append((nm, prev))
        self.cnt[nm] += 16
        self.ops[q].append((waits, lambda e: e.dma_start(out=out, in_=in_), nm, 16, self.tag))
        self._commit((nm, self.cnt[nm]), r, w)
        self.nops += 1

    def barrier(self):
        for e in self.ENGS:
            waits = []
            for s, v in self.cnt.items():
                if v > 0 and self.known[e].get(s, 0) < v:
                    self.known[e][s] = v
                    waits.append((s, v))
            self.ops[e].append((waits, None, None, 0, self.tag))
        self.lastw = {}
        self.readers = {}

    def emit(self):
        nc = self.nc
        with nc.Block() as block:
            for ename, deco in (("sp", block.sync), ("act", block.scalar), ("dve", block.vector),
                                ("pool", block.gpsimd), ("pe", block.tensor)):
                ops = self.ops[ename]

                def body(e, ops=ops):
                    for waits, fn, sname, inc, tag in ops:
                        for s, v in waits:
                            e.wait_ge(self.sems[s], v)
                        if fn is not None:
                            ins = fn(e)
                            ins.then_inc(self.sems[sname], inc)
                            if ANNOTATE:
                                ins.annotate(tag)

                deco(body)
                self.ops[ename] = []

    def mm(self, out, lhsT, rhs, start=True, stop=True):
        self.op("pe", lambda e: e.matmul(out, lhsT=lhsT, rhs=rhs, start=start, stop=stop),
                ins=[lhsT, rhs], outs=[out])

    def mmk(self, out, pairs):
        n = len(pairs)

        def fn(e):
            ins = None
            for i, (l, r) in enumerate(pairs):
                ins = e.matmul(out, lhsT=l, rhs=r, start=(i == 0), stop=(i == n - 1))
            return ins

        self.op("pe", fn, ins=[x for p in pairs for x in p], outs=[out])

    def tr(self, out, in_, ident):
        self.op("pe", lambda e: e.transpose(out, in_, ident), ins=[in_, ident], outs=[out])

    def trs(self, items, ident):
        def fn(e):
            ins = None
            for o, i in items:
                ins = e.transpose(o, i, ident)
            return ins
        self.op("pe", fn, ins=[i for _, i in items] + [ident], outs=[o for o, _ in items])

    def act(self, out, in_, func, bias=None, scale=None):
        kw = {}
        if bias is not None:
            kw["bias"] = bias
        if scale is not None:
            kw["scale"] = scale
        self.op("act", lambda e: e.activation(out=out, in_=in_, func=func, **kw),
                ins=[in_, bias, scale], outs=[out])

    def sqsum(self, junk, in_, accum):
        self.op("act", lambda e: e.activation(out=junk, in_=in_, func=AF.Square, accum_out=accum),
                ins=[in_], outs=[junk, accum])

    def tt(self, eng, out, in0, in1, op):
        self.op(eng, lambda e: e.tensor_tensor(out=out, in0=in0, in1=in1, op=op), ins=[in0, in1], outs=[out])

    def ts(self, eng, out, in0, s1, s2, op0, op1=None):
        if op1 is None:
            self.op(eng, lambda e: e.tensor_scalar(out=out, in0=in0, scalar1=s1, scalar2=None, op0=op0),
                    ins=[in0, s1], outs=[out])
        else:
            self.op(eng, lambda e: e.tensor_scalar(out=out, in0=in0, scalar1=s1, scalar2=s2, op0=op0, op1=op1),
                    ins=[in0, s1, s2], outs=[out])

    def stt(self, eng, out, in0, scalar, in1, op0, op1):
        self.op(eng, lambda e: e.scalar_tensor_tensor(out=out, in0=in0, scalar=scalar, in1=in1, op0=op0, op1=op1),
                ins=[in0, scalar, in1], outs=[out])

    def copy(self, eng, out, in_):
        if eng == "act":
            self.op(eng, lambda e: e.copy(out=out, in_=in_), ins=[in_], outs=[out])
        else:
            self.op(eng, lambda e: e.tensor_copy(out=out, in_=in_), ins=[in_], outs=[out])

    def red(self, eng, out, in_):
        self.op(eng, lambda e: e.tensor_reduce(out=out, in_=in_, axis=AX.X, op=ALU.add), ins=[in_], outs=[out])

    def recip(self, out, in_):
        self.op("dve", lambda e: e.reciprocal(out=out, in_=in_), ins=[in_], outs=[out])

    def memset(self, eng, ap, val):
        self.op(eng, lambda e: e.memset(ap, val), ins=[], outs=[ap])


def bc(ap, axis, n):
    u = ap.unsqueeze(axis)
    shp = list(u.shape)
    shp[axis] = n
    return u.broadcast_to(shp)


def build_program():
    nc = bass.Bass("TRN2", target_bir_lowering=False)

    def din(name, shape, dt=F32):
        return nc.dram_tensor(name, list(shape), dt, kind="ExternalInput").ap()

    def dout(name, shape, dt=F32):
        return nc.dram_tensor(name, list(shape), dt, kind="ExternalOutput").ap()

    def dscr(name, shape, dt=F32):
        return nc.dram_tensor(name, list(shape), dt).ap()

    xs_all = din("xs_all", [NT, 128, D])
    cexp = din("cexp", [2, 128, D])
    w_ada = din("w_ada", [D, 6 * D])
    b_ada_b = din("b_ada_b", [128, 6 * D])
    normvecs = din("normvecs", [128, 4, D])
    w_in = din("w_in", [D, 6688])
    w_out = din("w_out", [2 * D, D])
    w_gu = din("w_gu", [D, 2 * DFF])
    w_dn = din("w_dn", [DFF, D])
    ssd_cw = din("ssd_cw", [128, 12, 5])
    gdn_cw = din("gdn_cw", [128, 24, 4])
    smallv = din("smallv", [128, 64])
    ssd_nw = din("ssd_nw", [128, D])
    gdn_nw = din("gdn_nw", [128, 128])
    st_ssd = din("st_ssd", [16, 1024, 128])
    hist_ssd = din("hist_ssd", [128, 12, 16, 3])
    st_gdn = din("st_gdn", [16, 8, 128, 128])
    hist_gdn = din("hist_gdn", [128, 24, 16, 3])
    ident_d = din("ident", [128, 128])
    masks_d = din("masks", [2, 128, 4, 128])
    blk16_d = din("blk16", [128, 16])

    y_all = dout("y_all", [NT, 128, D])
    o_ssd_p = dout("o_ssd_p", [1024, 128])
    o_ssdc_p = dout("o_ssdc_p", [3, 1536])
    o_gdn_p = dout("o_gdn_p", [1024, 128])
    o_gdnc_p = dout("o_gdnc_p", [3, 3072])
    o_ssd_s = dout("o_ssd_s", [16, 1024, 128])
    o_ssdc_s = dout("o_ssdc_s", [48, 1536])
    o_gdn_s = dout("o_gdn_s", [16, 8, 128, 128])
    o_gdnc_s = dout("o_gdnc_s", [48, 3072])

    modscr = dscr("modscr", [2, 128, 6 * D])
    yssd_scr = dscr("yssd_scr", [NT, 128, D], BF16)
    x1_scr = dscr("x1_scr", [NT, 128, D])

    with contextlib.ExitStack() as gstack:
        P = Prog(nc, gstack)

        def sb(stack, name, shape, dt=F32):
            return stack.enter_context(nc.sbuf_tensor(name, list(shape), dt))

        class WBlocks:
            def __init__(self, stk, name, nk, bounds):
                self.bounds = bounds
                self.t = [sb(stk, "%s_%d" % (name, j), [128, nk, bounds[j + 1] - bounds[j]], BF16)
                          for j in range(len(bounds) - 1)]

            def load(self, dview, col_off, order=None):
                for j in (order if order is not None else range(len(self.t))):
                    b0, b1 = self.bounds[j], self.bounds[j + 1]
                    P.dma("pool", self.t[j][:, :, :], dview[:, :, col_off + b0:col_off + b1])

            def col(self, kc, c0, w):
                for j in range(len(self.t)):
                    if self.bounds[j] <= c0 and c0 + w <= self.bounds[j + 1]:
                        return self.t[j][:, kc, c0 - self.bounds[j]:c0 - self.bounds[j] + w]
                raise ValueError("column range straddles weight blocks")

        PT = gstack.enter_context(nc.psum_tensor("pst", [128, 1024], BF16))
        ps1 = contextlib.ExitStack()
        PS = [ps1.enter_context(nc.psum_tensor("ps%d" % i, [128, 512], F32)) for i in range(7)]

        identf = sb(gstack, "identf", [128, 128])
        identb = sb(gstack, "identb", [128, 128], BF16)
        onesf = sb(gstack, "onesf", [128, 128])
        onesb = sb(gstack, "onesb", [128, 128], BF16)
        cst = sb(gstack, "cst", [128, 4])
        masks = [sb(gstack, "masks%d" % t, [128, 4, 128]) for t in range(2)]
        blk16 = sb(gstack, "blk16s", [128, 16])
        smv = sb(gstack, "smv", [128, 64])
        negA = sb(gstack, "negA", [128, 24])
        P.dma("sp", identf[:, :], ident_d[:, :])
        P.dma("pool", identb[:, :], ident_d[:, :])
        for t in range(2):
            P.dma("sp", masks[t][:, :, :], masks_d[t])
        P.dma("sp", blk16[:, :], blk16_d[:, :])
        P.dma("sp", smv[:, :], smallv[:, :])
        P.memset("dve", onesf[:, :], 1.0)
        P.memset("dve", onesb[:, :], 1.0)
        P.memset("dve", cst[:, 0:1], 1e-6)
        P.memset("dve", cst[:, 1:2], 1.0)
        P.memset("dve", cst[:, 2:3], 128e-6)
        P.memset("dve", cst[:, 3:4], 0.0)
        P.act(negA[:, 0:16], smv[:, 16:32], AF.Exp)
        P.act(negA[:, 16:24], smv[:, 56:64], AF.Exp)
        P.ts("dve", negA[:, :], negA[:, :], -1.0, None, ALU.mult)

        MSTT, MINC, MSTR, MBLK = 0, 1, 2, 3

        with contextlib.ExitStack() as st0:
            ct = sb(st0, "ct", [128, D])
            cb = sb(st0, "cb", [128, D], BF16)
            cT = [sb(st0, "cT%d" % t, [128, 8, 128], BF16) for t in range(2)]
            modt = [sb(st0, "modt%d" % t, [128, 6 * D]) for t in range(2)]
            nv = sb(st0, "nv", [128, 4, D])
            wa = [sb(st0, "wa%d" % i, [128, 8, 512], BF16) for i in range(2)]
            bb = [sb(st0, "bb%d" % i, [128, 512]) for i in range(2)]
            P.dma("sp", nv[:, :, :], normvecs[:, :, :])
            for t in range(2):
                P.dma("sp", ct[:, :], cexp[t])
                P.act(cb[:, :], ct[:, :], AF.Silu)
                P.trs([(PT[:, c * 128:(c + 1) * 128], cb[:, c * 128:(c + 1) * 128]) for c in range(8)], identb[:, :])
                P.copy("dve", cT[t][:, :, :], PT[:, :].rearrange("p (c t) -> p c t", c=8))
            wv = w_ada.rearrange("(kc p) n -> p kc n", p=128)
            for j in range(12):
                P.dma("pool", wa[j % 2][:, :, :], wv[:, :, j * 512:(j + 1) * 512])
                P.dma("sp", bb[j % 2][:, :], b_ada_b[:, j * 512:(j + 1) * 512])
                for t in range(2):
                    ps = PS[(2 * j + t) % 4]
                    P.mmk(ps[:, :], [(cT[t][:, kc, :], wa[j % 2][:, kc, :]) for kc in range(8)])
                    P.tt("dve", modt[t][:, j * 512:(j + 1) * 512], ps[:, :], bb[j % 2][:, :], ALU.add)
            for t in range(2):
                m = modt[t]
                P.stt("dve", m[:, D:2 * D], m[:, D:2 * D], 1.0, nv[:, 0, :], ALU.add, ALU.mult)
                P.tt("dve", m[:, 2 * D:3 * D], m[:, 2 * D:3 * D], nv[:, 1, :], ALU.mult)
                P.stt("dve", m[:, 4 * D:5 * D], m[:, 4 * D:5 * D], 1.0, nv[:, 2, :], ALU.add, ALU.mult)
                P.tt("dve", m[:, 5 * D:6 * D], m[:, 5 * D:6 * D], nv[:, 3, :], ALU.mult)
                P.dma("sp", modscr[t], m[:, :])
            P.barrier()
            P.emit()

        def norm_to_T(xt, modv, sidx, hidx, tmpA, hb, hT, stt_):
            P.sec("norm_to_T")
            P.sqsum(tmpA[:, :], xt[:, :], stt_[:, 0:1])
            P.act(stt_[:, 1:2], stt_[:, 0:1], AF.Sqrt, bias=cst[:, 0:1], scale=1.0 / D)
            P.recip(stt_[:, 2:3], stt_[:, 1:2])
            P.stt("dve", tmpA[:, :], xt[:, :], stt_[:, 2:3], modv[:, sidx, :], ALU.mult, ALU.mult)
            P.tt("dve", hb[:, :], tmpA[:, :], modv[:, hidx, :], ALU.add)
            P.trs([(PT[:, c * 128:(c + 1) * 128], hb[:, c * 128:(c + 1) * 128]) for c in range(8)], identb[:, :])
            P.copy("act", hT[:, :, :], PT[:, :].rearrange("p (c t) -> p c t", c=8))

        def norm_to_T_g(xt, modv, sidx, hidx, tmpA, hb, hT, stt_):
            P.sec("norm_to_T")
            P.sqsum(tmpA[:, :], xt[:, :], stt_[:, 0:1])
            yield
            P.act(stt_[:, 1:2], stt_[:, 0:1], AF.Sqrt, bias=cst[:, 0:1], scale=1.0 / D)
            yield
            P.recip(stt_[:, 2:3], stt_[:, 1:2])
            yield
            P.stt("dve", tmpA[:, :], xt[:, :], stt_[:, 2:3], modv[:, sidx, :], ALU.mult, ALU.mult)
            yield
            P.tt("dve", hb[:, :], tmpA[:, :], modv[:, hidx, :], ALU.add)
            yield
            P.trs([(PT[:, c * 128:(c + 1) * 128], hb[:, c * 128:(c + 1) * 128]) for c in range(8)], identb[:, :])
            yield
            P.copy("act", hT[:, :, :], PT[:, :].rearrange("p (c t) -> p c t", c=8))
            yield

        def small_scan_terms(af, nh, typ, pss, ex):
            mk = masks[typ]
            P.mm(pss[:, 0:nh], mk[:, MINC, :], af)
            P.mm(pss[:, nh:2 * nh], mk[:, MSTT, :], af)
            P.mm(pss[:, 2 * nh:3 * nh], mk[:, MBLK, :], af)
            P.act(ex[:, 0:3 * nh], pss[:, 0:3 * nh], AF.Exp)

        def small_scan_terms_g(af, nh, typ, pss, ex):
            mk = masks[typ]
            P.mm(pss[:, 0:nh], mk[:, MINC, :], af)
            yield
            P.mm(pss[:, nh:2 * nh], mk[:, MSTT, :], af)
            yield
            P.mm(pss[:, 2 * nh:3 * nh], mk[:, MBLK, :], af)
            yield
            P.act(ex[:, 0:3 * nh], pss[:, 0:3 * nh], AF.Exp)
            yield

        def proj_conv_gen(typ, ngroups, wt, wcol0, hT, psb, xinS_l, xinP_l, histS, histP, cwt, has_bias, accs, dst_fn):
            items = []

            def slot(i):
                n = len(items)
                if 0 <= i < n:
                    it = items[i]
                    if has_bias:
                        P.act(it[0], it[2][0], AF.Identity, bias=cwt[:, it[4], 4:5], scale=cwt[:, it[4], 0:1])
                    else:
                        P.act(it[0], it[2][0], AF.Identity, scale=cwt[:, it[4], 0:1])
                    yield
                if 0 <= i - 1 < n:
                    it = items[i - 1]
                    for k in range(1, 4):
                        P.stt("dve", it[0], it[2][k], cwt[:, it[4], k:k + 1], it[0], ALU.mult, ALU.add)
                        yield
                if 0 <= i - 2 < n:
                    it = items[i - 2]
                    P.act(it[3], it[1][:, :], AF.Silu)
                    yield

            for g in range(ngroups):
                ps = psb[g % 2]
                for j in range(4):
                    c = 4 * g + j
                    P.mmk(ps[:, j * 128:(j + 1) * 128],
                          [((wt.col(kc, wcol0 + c * 128, 128) if hasattr(wt, "col")
                             else wt[:, kc, wcol0 + c * 128:wcol0 + (c + 1) * 128]), hT[:, kc, :]) for kc in range(8)])
                    yield
                if typ == 0:
                    xin = xinS_l[g % 2]
                    P.copy("pool", xin[:, :, :, 0:3], histS[:, 4 * g:4 * g + 4, :, :])
                    yield
                    P.copy("act", xin[:, :, :, 3:11], ps[:, :].rearrange("p (j s t) -> p j s t", j=4, s=16))
                    yield
                else:
                    xin = xinP_l[g % 2]
                    P.copy("pool", xin[:, :, 0:3], histP[:, 4 * g:4 * g + 4, :])
                    yield
                    P.copy("act", xin[:, :, 3:131], ps[:, :].rearrange("p (j t) -> p j t", j=4))
                    yield
                    P.copy("pool", histP[:, 4 * g:4 * g + 4, :], xin[:, :, 128:131])
                    yield
                for j in range(4):
                    c = 4 * g + j
                    a_ = accs[c % 4]
                    if typ == 0:
                        av = a_[:, :].rearrange("p (s t) -> p s t", s=16)
                        sh = [xin[:, j, :, k:k + 8] for k in range(4)]
                    else:
                        av = a_[:, :]
                        sh = [xin[:, j, k:k + 128] for k in range(4)]
                    items.append((av, a_, sh, dst_fn(c), c))
                    yield from slot(c)
            yield from slot(4 * ngroups)
            yield from slot(4 * ngroups + 1)

        def run_gen(g):
            for _ in g:
                pass

        w_in_v = w_in.rearrange("(kc p) n -> p kc n", p=128)
        if STOP_AFTER >= 1 and not SKIP1:
          with contextlib.ExitStack() as st1:
            wI = WBlocks(st1, "wI", 8, [0, 512, 1024, 1536, 2048, 2560, 2576])
            wI.load(w_in_v, 0, order=[2, 3, 4, 0, 1, 5])
            modv = sb(st1, "modv", [128, 3, D])
            cw = sb(st1, "cw", [128, 12, 5])
            nw = sb(st1, "ssdnw", [128, D])
            histS = sb(st1, "histS", [128, 12, 16, 3])
            histP = sb(st1, "histP", [128, 12, 3])
            P.dma("sp", cw[:, :, :], ssd_cw[:, :, :])
            P.dma("sp", nw[:, :], ssd_nw[:, :])
            P.dma("sp", histS[:, :, :, :], hist_ssd[:, :, :, :])
            P.memset("pool", histP[:, :, :], 0.0)
            xt = [sb(st1, "xt%d" % i, [128, D]) for i in range(2)]
            tmpA = sb(st1, "tmpA", [128, D])
            hb = sb(st1, "hb", [128, D], BF16)
            hT = sb(st1, "hT", [128, 8, 128], BF16)
            stt_ = sb(st1, "stt", [128, 4])
            xinS_l = [sb(st1, "xinS%d" % i, [128, 4, 16, 11]) for i in range(2)]
            xinP_l = [sb(st1, "xinP%d" % i, [128, 4, 131]) for i in range(2)]
            acc = [sb(st1, "acc%d" % i, [128, 128]) for i in range(4)]
            xc_2 = [sb(st1, "xc_%d" % i, [128, 12, 128], BF16) for i in range(2)]
            zs_2 = [sb(st1, "zs_%d" % i, [128, D], BF16) for i in range(2)]
            sm = sb(st1, "sm", [128, 48])
            af_2 = [sb(st1, "af_%d" % i, [128, 16]) for i in range(2)]
            ex_2 = [sb(st1, "ex_%d" % i, [128, 48]) for i in range(2)]
            xtm = sb(st1, "xtm", [128, D], BF16)
            xdt_2 = [sb(st1, "xdt_%d" % i, [128, D], BF16) for i in range(2)]
            xDs_2 = [sb(st1, "xDs_%d" % i, [128, D], BF16) for i in range(2)]
            xw_2 = [sb(st1, "xw_%d" % i, [128, D], BF16) for i in range(2)]
            Btm_2 = [sb(st1, "Btm_%d" % i, [128, 2, 128], BF16) for i in range(2)]
            CBm_2 = [sb(st1, "CBm_%d" % i, [128, 2, 128]) for i in range(2)]
            L4 = [sb(st1, "L4_%d" % i, [128, 4, 128]) for i in range(2)]
            E4 = [sb(st1, "E4_%d" % i, [128, 4, 128]) for i in range(2)]
            M4 = [sb(st1, "M4_%d" % i, [128, 4, 128], BF16) for i in range(2)]
            yo = sb(st1, "yo", [128, D])
            yy = sb(st1, "yy", [128, D])
            ss = sb(st1, "ss", [128, 8])
            ysb = [sb(st1, "ysb%d" % i, [128, D], BF16) for i in range(2)]
            hTf = sb(st1, "hTf", [128, D])
            hTb = sb(st1, "hTb", [128, D], BF16)
            lhs3 = sb(st1, "lhs3", [128, 8, 48], BF16)
            cv = sb(st1, "cv", [48, 1536])
            CmT = [sb(st1, "CmT%d" % g, [128, 16 * 128], BF16) for g in range(2)]
            h0 = [sb(st1, "h0_%d" % i, [128, 8, 128]) for i in range(2)]
            h0Tb = sb(st1, "h0Tb", [128, D], BF16)
            xwm = sb(st1, "xwm", [128, D], BF16)
            hn = [sb(st1, "hn%d" % i, [128, 8, 128]) for i in range(2)]
            rsel = sb(st1, "rsel", [128, 2, 128])
            decsel = sb(st1, "decsel", [128, 128])
            D2N = ("xc", "zs", "af", "ex", "xdt", "xDs", "xw", "Btm", "CBm")
            D2 = dict(xc=xc_2, zs=zs_2, af=af_2, ex=ex_2, xdt=xdt_2, xDs=xDs_2, xw=xw_2, Btm=Btm_2, CBm=CBm_2)
            P.memset("pool", hTf[:, :], 0.0)
            P.memset("pool", hTb[:, :], 0.0)
            for g in range(2):
                P.memset("pool", CmT[g][:, :], 0.0)

            def ssd_gen(ti):
                typ = 0 if ti == 0 else 1
                xc, zs, af, ex, xdt, xDs, xw, Btm, CBm = (D2[n][ti % 2] for n in D2N)
                PY = [PS[2], PS[3]] if typ == 1 else [PS[0], PS[1]]
                PYO = [PS[5], PS[6]] if typ == 1 else [PS[2], PS[3]]
                mk = masks[typ]
                if ti <= 1:
                    P.dma("sp", modv[:, :, :], modscr[typ].rearrange("p (s d) -> p s d", s=6)[:, 0:3, :])
                    yield
                x_ = xt[ti % 2]
                P.dma("sp", x_[:, :], xs_all[ti])
                yield
                yield from norm_to_T_g(x_, modv, 1, 0, tmpA, hb, hT, stt_)
                yield from proj_conv_gen(typ, 3, wI, 1024, hT, PS[0:2], xinS_l, xinP_l, histS, histP, cw, True, acc,
                                         lambda c: xc[:, c, :])
                P.sec("z (token-major) -> silu")
                for nb in range(2):
                    ps = PS[nb]
                    P.mmk(ps[:, :], [(hT[:, kc, :], wI.col(kc, nb * 512, 512)) for kc in range(8)])
                    yield
                    P.act(zs[:, nb * 512:(nb + 1) * 512], ps[:, :], AF.Silu)
                    yield
                P.sec("dt")
                pd = PS[4]
                P.mmk(pd[:, 0:16], [(hT[:, kc, :], wI.col(kc, 2560, 16)) for kc in range(8)])
                yield
                P.tt("dve", sm[:, 0:16], pd[:, 0:16], smv[:, 0:16], ALU.add)
                yield
                P.act(sm[:, 16:32], sm[:, 0:16], AF.Exp)
                yield
                P.act(sm[:, 32:48], sm[:, 16:32], AF.Ln, bias=cst[:, 1:2], scale=1.0)
                yield
                P.tt("dve", af[:, :], sm[:, 32:48], negA[:, 0:16], ALU.mult)
                yield
                yield from small_scan_terms_g(af[:, :], 16, typ, PS[4][:, 64:112], ex)
                P.sec("token-major x, xdt, xD, xw, B")
                P.trs([(PT[:, c * 128:(c + 1) * 128], xc[:, c, :]) for c in range(8)], identb[:, :])
                yield
                P.copy("act", xtm[:, :], PT[:, :])
                yield
                x3 = xtm[:, :].rearrange("p (h d) -> p h d", h=16)
                P.tt("dve", xdt[:, :].rearrange("p (h d) -> p h d", h=16), x3, bc(sm[:, 32:48], 2, 64), ALU.mult)
                yield
                P.tt("pool", xDs[:, :].rearrange("p (h d) -> p h d", h=16), x3, bc(smv[:, 32:48], 2, 64), ALU.mult)
                yield
                P.tt("pool", xw[:, :].rearrange("p (h d) -> p h d", h=16),
                     xdt[:, :].rearrange("p (h d) -> p h d", h=16), bc(ex[:, 16:32], 2, 64), ALU.mult)
                yield
                P.trs([(PT[:, g * 128:(g + 1) * 128], xc[:, 8 + g, :]) for g in range(2)], identb[:, :])
                yield
                P.copy("act", Btm[:, :, :], PT[:, 0:256].rearrange("p (g n) -> p g n", g=2))
                yield
                P.sec("CB^T masked")
                pc = PS[4][:, 256:512]
                for g in range(2):
                    P.mm(pc[:, g * 128:(g + 1) * 128], xc[:, 8 + g, :], xc[:, 10 + g, :])
                    yield
                P.tt("dve", CBm[:, :, :], pc[:, 0:256].rearrange("p (g l) -> p g l", g=2), bc(mk[:, MINC, :], 1, 2), ALU.mult)
                yield
                P.sec("y_off raw")
                if typ == 1:
                    yield "HALF"
                else:
                    for g in range(2):
                        base = CmT[g][:, :]
                        dst = bass.AP(base.tensor, base.offset, [[16 * 128, 128], [136, 16], [1, 8]])
                        P.op("pool", (lambda e, dst=dst, src=xc[:, 10 + g, :].rearrange("p (s t) -> p s t", s=16):
                                      e.tensor_copy(out=dst, in_=src)), ins=[xc[:, 0, :]], outs=[base])
                        yield
                    for hh in range(2):
                        P.tt("pool", rsel[:, hh, :].rearrange("p (j s) -> p j s", j=8),
                             bc(af[:, hh:16:2], 2, 16), bc(blk16[:, :], 1, 8), ALU.mult)
                        yield
                        P.mm(PS[5][:, 256 + hh * 128:256 + (hh + 1) * 128], onesf[:, :], rsel[:, hh, :])
                        yield
                    P.act(decsel[0:64, :], PS[5][0:64, 256:384], AF.Exp)
                    yield
                    P.act(decsel[64:128, :], PS[5][64:128, 384:512], AF.Exp)
                    yield
                    for s in range(16):
                        h0_ = h0[s % 2]
                        hn_ = hn[s % 2]
                        P.dma("sp", h0_[:, :, :], st_ssd[s].rearrange("(j p) n -> p j n", p=128))
                        yield
                        pa, pb = PS[0], PS[1]
                        P.trs([((pa if j < 4 else pb)[:, (j % 4) * 128:(j % 4 + 1) * 128], h0_[:, j, :]) for j in range(8)],
                              identf[:, :])
                        yield
                        P.copy("act", h0Tb[:, 0:512], pa[:, :])
                        yield
                        P.copy("dve", h0Tb[:, 512:1024], pb[:, :])
                        yield
                        for g in range(2):
                            P.op("pe", (lambda e, o=PS[2 + g][:, :], l=CmT[g][:, s * 128:(s + 1) * 128],
                                        r=h0Tb[:, g * 512:(g + 1) * 512], s=s:
                                        e.matmul(o, lhsT=l, rhs=r, start=(s == 0), stop=(s == 15))),
                                 ins=[CmT[g][:, :], h0Tb[:, :]], outs=[PS[2 + g][:, :]])
                            yield
                        P.act(xwm[:, :], xw[:, :], AF.Identity, scale=blk16[:, s:s + 1])
                        yield
                        for half in range(2):
                            pn = PS[half]
                            for jj in range(4):
                                j = half * 4 + jj
                                P.mm(pn[:, jj * 128:(jj + 1) * 128], xwm[:, j * 128:(j + 1) * 128], Btm[:, j // 4, :])
                                yield
                            for jj in range(4):
                                j = half * 4 + jj
                                P.stt("dve", hn_[:, j, :], h0_[:, j, :], decsel[:, j * 16 + s:j * 16 + s + 1],
                                      pn[:, jj * 128:(jj + 1) * 128], ALU.mult, ALU.add)
                                yield
                        P.dma("sp", o_ssd_s[s].rearrange("(j p) n -> p j n", p=128), hn_[:, :, :])
                        yield
                P.sec("per-head intra-chunk")
                for q in range(4):
                    g = q // 2
                    L_, E_, M_ = L4[q % 2], E4[q % 2], M4[q % 2]
                    P.tt("dve", L_[:, :, :], bc(mk[:, MSTT, :], 1, 4), bc(af[:, 4 * q:4 * q + 4], 2, 128), ALU.mult)
                    yield
                    pg = PS[5 + (q % 2)]
                    for j in range(4):
                        P.mm(pg[:, j * 128:(j + 1) * 128], L_[:, j, :], mk[:, MINC, :])
                        yield
                    P.act(E_[:, :, :], pg[:, :].rearrange("p (j l) -> p j l", j=4), AF.Exp)
                    yield
                    P.tt("dve", M_[:, :, :], E_[:, :, :], bc(CBm[:, g, :], 1, 4), ALU.mult)
                    yield
                    py = PY[g]
                    for j in range(4):
                        h = 4 * q + j
                        hh = h % 8
                        P.mmk(py[:, hh * 64:(hh + 1) * 64],
                              [(identb[:, :], xDs[:, h * 64:(h + 1) * 64]), (M_[:, j, :], xdt[:, h * 64:(h + 1) * 64])])
                        yield
                P.sec("combine y = y_diag + eacs * y_off")
                if typ == 1:
                    for g in range(2):
                        P.mm(PYO[g][:, :], xc[:, 10 + g, :], hTb[:, g * 512:(g + 1) * 512])
                        yield
                for g in range(2):
                    P.copy("act", yo[:, g * 512:(g + 1) * 512], PYO[g][:, :])
                    yield
                P.tt("dve", yo[:, :].rearrange("p (h d) -> p h d", h=16), yo[:, :].rearrange("p (h d) -> p h d", h=16),
                     bc(ex[:, 0:16], 2, 64), ALU.mult)
                yield
                for g in range(2):
                    P.tt("dve", yy[:, g * 512:(g + 1) * 512], yo[:, g * 512:(g + 1) * 512], PY[g][:, :], ALU.add)
                    yield
                P.sec("gate with silu(z), group rmsnorm")
                P.tt("dve", yy[:, :], yy[:, :], zs[:, :], ALU.mult)
                yield
                for g in range(2):
                    P.sqsum(yo[:, g * 512:(g + 1) * 512], yy[:, g * 512:(g + 1) * 512], ss[:, g:g + 1])
                    yield
                P.act(ss[:, 2:4], ss[:, 0:2], AF.Sqrt, bias=cst[:, 0:1], scale=1.0 / 512)
                yield
                P.recip(ss[:, 4:6], ss[:, 2:4])
                yield
                y_ = ysb[ti % 2]
                for g in range(2):
                    P.stt("dve", y_[:, g * 512:(g + 1) * 512], yy[:, g * 512:(g + 1) * 512], ss[:, 4 + g:5 + g],
                          nw[:, g * 512:(g + 1) * 512], ALU.mult, ALU.mult)
                    yield
                P.dma("sp", yssd_scr[ti], y_[:, :])
                yield
                P.sec("state update (prompt chunks)")
                if typ == 1:
                    for g in range(2):
                        P.mm(PYO[g][:, :], Btm[:, g, :], xw[:, g * 512:(g + 1) * 512])
                        yield
                    h3 = hTf[:, :].rearrange("p (h d) -> p h d", h=16)
                    P.tt("dve", h3, h3, bc(ex[:, 32:48], 2, 64), ALU.mult)
                    yield
                    for g in range(2):
                        P.tt("dve", hTf[:, g * 512:(g + 1) * 512], hTf[:, g * 512:(g + 1) * 512], PYO[g][:, :], ALU.add)
                        yield
                    P.copy("act", hTb[:, :], hTf[:, :])
                    yield
                P.sec("conv-state outputs (last 3 raw xbc rows)")
                if ti == 0 or ti == NT - 1:
                    M3 = 48 if typ == 0 else 3
                    if typ == 0:
                        P.copy("pool", lhs3[:, :, :].rearrange("p k (s t) -> p k s t", s=16),
                               hT[:, :, :].rearrange("p k (s t) -> p k s t", s=16)[:, :, :, 5:8])
                        yield
                    else:
                        P.copy("pool", lhs3[:, :, 0:3], hT[:, :, 125:128])
                        yield
                    for nb in range(3):
                        ps = PS[5 + (nb % 2)]
                        P.mmk(ps[0:M3, :], [(lhs3[:, kc, 0:M3], wI.col(kc, 1024 + nb * 512, 512))
                                            for kc in range(8)])
                        yield
                        P.copy("act", cv[0:M3, nb * 512:(nb + 1) * 512], ps[0:M3, :])
                        yield
                    P.dma("sp", (o_ssdc_s if typ == 0 else o_ssdc_p)[:, :], cv[0:M3, :])
                    yield

            def rolling(genf, first, last, yspeed=1):
                nxt_ = first + 1
                cur = genf(first)
                young = None
                hold = [False]
                while cur is not None:
                    try:
                        v = next(cur)
                        if v == "HALF" and young is None and nxt_ < last:
                            young = genf(nxt_)
                            nxt_ += 1
                    except StopIteration:
                        cur, young = young, None
                        if hold[0] and cur is not None and nxt_ < last:
                            young = genf(nxt_)
                            nxt_ += 1
                        hold[0] = False
                        if cur is None and nxt_ < last:
                            cur = genf(nxt_)
                            nxt_ += 1
                        continue
                    for _k in range(yspeed):
                        if young is not None and not hold[0]:
                            try:
                                if next(young) == "HALF":
                                    hold[0] = True
                            except StopIteration:
                                young = None

            run_gen(ssd_gen(0))
            rolling(ssd_gen, 1, NT, yspeed=2)
            for half in range(2):
                ps = PS[half]
                P.trs([(ps[:, jj * 128:(jj + 1) * 128], hTf[:, (half * 4 + jj) * 128:(half * 4 + jj + 1) * 128])
                       for jj in range(4)], identf[:, :])
                P.copy("act", hn[half][:, 0:4, :], ps[:, :].rearrange("p (j n) -> p j n", j=4))
                P.dma("sp", o_ssd_p[half * 512:(half + 1) * 512, :].rearrange("(j p) n -> p j n", p=128), hn[half][:, 0:4, :])
            P.barrier()
            P.emit()
        ps1.close()
        if STOP_AFTER >= 2:
          with contextlib.ExitStack() as st2:
            PA = [st2.enter_context(nc.psum_tensor("pa%d" % i, [128, 512], F32)) for i in range(2)]
            PM = st2.enter_context(nc.psum_tensor("pm", [128, 512], F32))
            LPB = [st2.enter_context(nc.psum_tensor("lpb%d" % i, [128, 512], F32)) for i in range(4)]
            LPS = [[LPB[ln][:, i * 128:(i + 1) * 128] for i in range(4)] for ln in range(4)]
            wII = WBlocks(st2, "wII", 8, [0, 512, 1024, 1536, 2048, 2560, 3072, 3584, 4096, 4112])
            wII.load(w_in_v, 2576)
            wO = WBlocks(st2, "wO", 16, [0, 512, 1024])
            w_out_v = w_out.rearrange("(kc p) n -> p kc n", p=128)
            wO.load(w_out_v, 0)
            modv = sb(st2, "modv2", [128, 3, D])
            cwg = sb(st2, "cwg", [128, 24, 4])
            gnw = sb(st2, "gnw", [128, 128])
            histP = sb(st2, "histPg", [128, 24, 3])
            P.dma("sp", cwg[:, :, :], gdn_cw[:, :, :])
            P.dma("sp", gnw[:, :], gdn_nw[:, :])
            P.memset("pool", histP[:, :, :], 0.0)
            tmpA = sb(st2, "tmpA2", [128, D])
            hT = sb(st2, "hT2", [128, 8, 128], BF16)
            stt_ = sb(st2, "stt2", [128, 4])
            xinP_l = [sb(st2, "xinP2_%d" % i, [128, 4, 131]) for i in range(2)]
            acc = [sb(st2, "acc2_%d" % i, [128, 128]) for i in range(4)]
            lhs3 = sb(st2, "lhs3g", [128, 8, 48], BF16)
            tmpT = sb(st2, "tmpT2", [128, D])
            sttT = sb(st2, "sttT2", [128, 4])
            ss8 = sb(st2, "ss8", [128, 24])
            otm = sb(st2, "otm", [128, D])
            mixed = sb(st2, "mixed", [128, 2 * D], BF16)
            mixedT = sb(st2, "mixedT", [128, 16, 128], BF16)

            class FB:
                pass

            def make_fb(i, stk):
                fb = FB()
                fb.xt = sb(stk, "f%d_xt" % i, [128, D])
                fb.hb = sb(stk, "f%d_hb" % i, [128, D], BF16)
                fb.qk = sb(stk, "f%d_qk" % i, [128, 16, 128], BF16)
                fb.vfm = sb(stk, "f%d_vfm" % i, [128, 8, 128], BF16)
                fb.ktm = sb(stk, "f%d_ktm" % i, [128, 8, 128], BF16)
                fb.vtm = sb(stk, "f%d_vtm" % i, [128, 8, 128], BF16)
                fb.gs = sb(stk, "f%d_gs" % i, [128, D], BF16)
                fb.sm = sb(stk, "f%d_sm" % i, [128, 64])
                fb.gf = sb(stk, "f%d_gf" % i, [128, 8])
                fb.ex = sb(stk, "f%d_ex" % i, [128, 24])
                return fb

            class Lane:
                pass

            lanes = []

            def make_lane(ln, stk):
                B = Lane()
                B.ps = LPS[ln]
                f = lambda nm, dt=F32: sb(stk, "ln%d_%s" % (ln, nm), [128, 128], dt)
                B.P = [f("P0"), f("P1")]
                B.PT = [f("PT0"), f("PT1")]
                B.X = [f("X0"), f("X1")]
                B.ot = f("ot")
                B.L, B.Dm, B.DmI, B.DmS = B.ot, B.X[1], B.PT[1], B.P[1]
                B.attnT, B.TTb, B.R, B.vn, B.kout = (f("attnT", BF16), f("TTb", BF16), f("R", BF16),
                                                     f("vn", BF16), f("kout", BF16))
                lanes.append(B)

            make_lane(0, st2)
            fbs = [make_fb(0, st2)]

            def front_gen(ti, typ, SB, fb):
                xt, hb, qk, vfm, sm, gf, ex = fb.xt, fb.hb, fb.qk, fb.vfm, fb.sm, fb.gf, fb.ex
                P.sec("F:load+norm")
                if ti <= 1:
                    P.dma("sp", modv[:, :, :], modscr[typ].rearrange("p (s d) -> p s d", s=6)[:, 0:3, :])
                    yield
                P.dma("sp", xt[:, :], xs_all[ti])
                yield
                yield from norm_to_T_g(xt, modv, 1, 0, tmpA, hb, hT, stt_)
                yield
                yield from proj_conv_gen(typ, 6, wII, 0, hT, PA, (SB.xinS_l if typ == 0 else None), xinP_l,
                                         (SB.histS if typ == 0 else None), histP, cwg, False, acc,
                                         lambda c: (qk[:, c, :] if c < 16 else vfm[:, c - 16, :]))
                if ti == 0 or ti == NT - 1:
                    P.sec("F:convstate")
                    M3 = 48 if typ == 0 else 3
                    if typ == 0:
                        P.copy("pool", lhs3[:, :, :].rearrange("p k (s t) -> p k s t", s=16),
                               hT[:, :, :].rearrange("p k (s t) -> p k s t", s=16)[:, :, :, 5:8])
                        yield
                    else:
                        P.copy("pool", lhs3[:, :, 0:3], hT[:, :, 125:128])
                        yield
                    for cc in range(3):
                        for nb in range(2):
                            c0 = cc * 1024 + nb * 512
                            P.mmk(PA[nb][0:M3, :], [(lhs3[:, kc, 0:M3], wII.col(kc, c0, 512)) for kc in range(8)])
                            yield
                            P.copy("act", tmpA[0:M3, nb * 512:(nb + 1) * 512], PA[nb][0:M3, :])
                            yield
                        P.dma("sp", (o_gdnc_s if typ == 0 else o_gdnc_p)[:, cc * 1024:(cc + 1) * 1024], tmpA[0:M3, :])
                        yield
                    yield
                P.sec("F:gate")
                for nb in range(2):
                    P.mmk(PA[nb][:, :], [(hT[:, kc, :], wII.col(kc, 3072 + nb * 512, 512))
                                         for kc in range(8)])
                    yield
                    P.act(fb.gs[:, nb * 512:(nb + 1) * 512], PA[nb][:, :], AF.Silu)
                    yield
                yield
                P.sec("F:beta/g")
                P.mmk(PM[:, 0:16], [(hT[:, kc, :], wII.col(kc, 4096, 16)) for kc in range(8)])
                yield
                P.act(sm[:, 0:8], PM[:, 0:8], AF.Exp, scale=-1.0)
                yield
                P.ts("dve", sm[:, 0:8], sm[:, 0:8], 1.0, None, ALU.add)
                yield
                P.recip(sm[:, 8:16], sm[:, 0:8])
                yield
                P.ts("dve", sm[:, 16:24], sm[:, 8:16], -1.0, None, ALU.mult)
                yield
                P.tt("dve", sm[:, 24:32], PM[:, 8:16], smv[:, 48:56], ALU.add)
                yield
                P.act(sm[:, 32:40], sm[:, 24:32], AF.Exp)
                yield
                P.act(sm[:, 40:48], sm[:, 32:40], AF.Ln, bias=cst[:, 1:2], scale=1.0)
                yield
                P.tt("dve", gf[:, :], sm[:, 40:48], negA[:, 16:24], ALU.mult)
                yield
                yield
                P.sec("F:beta/g")
                yield from small_scan_terms_g(gf[:, :], 8, typ, PM[:, 64:88], ex)
                P.ts("dve", sm[:, 48:56], ex[:, 0:8], -1.0, None, ALU.mult)
                yield
                yield
                for half in range(2):
                    P.sec("F:l2norm")
                    src = qk[:, half * 8:(half + 1) * 8, :]
                    P.tt("pool", hb[:, :].rearrange("p (c t) -> p c t", c=8), src, src, ALU.mult)
                    yield
                    for i in range(2):
                        rq = tmpA[:, i * 512:(i + 1) * 512]
                        P.mm(PA[i][:, :], onesb[:, :], hb[:, i * 512:(i + 1) * 512])
                        yield
                        if half == 0:
                            P.act(rq, PA[i][:, :], AF.Sqrt, bias=cst[:, 2:3], scale=128.0)
                            yield
                        else:
                            P.act(rq, PA[i][:, :], AF.Sqrt, bias=cst[:, 0:1], scale=1.0)
                            yield
                        P.recip(rq, rq)
                        yield
                        dst = qk[:, half * 8 + 4 * i:half * 8 + 4 * i + 4, :]
                        P.tt("dve", dst, dst, rq.rearrange("p (c t) -> p c t", c=4), ALU.mult)
                        yield
                    yield
                P.sec("F:transposes")
                P.copy("pool", hb[:, :].rearrange("p (c t) -> p c t", c=8), qk[:, 8:16, :])
                yield
                P.trs([(PT[:, j * 128:(j + 1) * 128], qk[:, 8 + j, :]) for j in range(8)], identb[:, :])
                yield
                P.copy("act", fb.ktm[:, :, :], PT[:, :].rearrange("p (h d) -> p h d", h=8))
                yield
                yield
                P.sec("F:transposes")
                P.trs([(PT[:, j * 128:(j + 1) * 128], vfm[:, j, :]) for j in range(8)], identb[:, :])
                yield
                P.copy("act", fb.vtm[:, :, :], PT[:, :].rearrange("p (h d) -> p h d", h=8))
                yield
                if typ == 0:
                    P.tt("pool", SB.rselg[:, :].rearrange("p (h s) -> p h s", h=8),
                         bc(gf[:, :], 2, 16), bc(blk16[:, :], 1, 8), ALU.mult)
                    yield
                    P.mm(PM[:, 128:256], onesf[:, :], SB.rselg[:, :])
                    yield
                    P.act(SB.gendsel[:, :], PM[:, 128:256], AF.Exp)
                    yield
                yield

            def head_gen(h, B, typ, SB, fb):
                mk = masks[typ]
                m_lev = 3 if typ == 0 else 7
                qk, hb, sm, gf, ex, ktm, vtm = fb.qk, fb.hb, fb.sm, fb.gf, fb.ex, fb.ktm, fb.vtm
                kT = qk[:, 8 + h, :]
                qT = qk[:, h, :]
                _st = ["H:decay"]
                P.sec(_st[0])
                P.act(B.L[:, :], mk[:, MSTT, :], AF.Identity, scale=gf[:, h:h + 1])
                yield
                P.mm(B.ps[3][:, :], B.L[:, :], mk[:, MINC, :])
                yield
                P.act(B.Dm[:, :], B.ps[3][:, :], AF.Exp)
                yield
                P.tt("dve", B.DmI[:, :], B.Dm[:, :], mk[:, MINC, :], ALU.mult)
                yield
                P.tt("dve", B.DmS[:, :], B.Dm[:, :], mk[:, MSTR, :], ALU.mult)
                yield
                P.mm(B.ps[0][:, :], kT, hb[:, h * 128:(h + 1) * 128])
                yield
                P.mm(B.ps[1][:, :], kT, qT)
                yield
                P.stt("dve", B.P[0][:, :], B.ps[0][:, :], sm[:, 16 + h:17 + h], B.DmS[:, :], ALU.mult, ALU.mult)
                yield
                P.tt("dve", B.attnT[:, :], B.ps[1][:, :], B.DmI[:, :], ALU.mult)
                yield
                yield
                _st[0] = "H:dbl"
                P.sec(_st[0])
                P.tr(B.ps[2][:, :], B.P[0][:, :], identf[:, :])
                yield
                P.copy("act", B.PT[0][:, :], B.ps[2][:, :])
                yield
                P.tt("dve", B.X[0][:, :], B.P[0][:, :], identf[:, :], ALU.add)
                yield
                yield
                P.sec(_st[0])
                if m_lev > 1:
                    P.mm(B.ps[0][:, :], B.P[0][:, :], B.PT[0][:, :])
                    yield
                    if m_lev > 2:
                        P.mm(B.ps[1][:, :], B.PT[0][:, :], B.P[0][:, :])
                        yield
                    P.copy("act", B.PT[1][:, :], B.ps[0][:, :])
                    yield
                    if m_lev > 2:
                        P.copy("act", B.P[1][:, :], B.ps[1][:, :])
                        yield
                    yield
                    P.sec(_st[0])
                xi = 0
                for j in range(1, m_lev):
                    cur, nx = j % 2, (j + 1) % 2
                    if j + 1 < m_lev:
                        P.mm(B.ps[0][:, :], B.P[cur][:, :], B.PT[cur][:, :])
                        yield
                        if j + 2 < m_lev:
                            P.mm(B.ps[1][:, :], B.PT[cur][:, :], B.P[cur][:, :])
                            yield
                    P.mm(B.ps[2][:, :], B.PT[cur][:, :], B.X[xi][:, :])
                    yield
                    if j + 1 < m_lev:
                        P.copy("act", B.PT[nx][:, :], B.ps[0][:, :])
                        yield
                        if j + 2 < m_lev:
                            P.copy("act", B.P[nx][:, :], B.ps[1][:, :])
                            yield
                    P.tt("dve", B.X[1 - xi][:, :], B.X[xi][:, :], B.ps[2][:, :], ALU.add)
                    yield
                    xi = 1 - xi
                    yield
                    P.sec(_st[0])
                _st[0] = "H:state"
                P.sec(_st[0])
                P.copy("act", B.TTb[:, :], B.X[xi][:, :])
                yield
                if typ == 1:
                    Sf_h, Sb_h = SB.Sf[h], SB.Sb[h]
                    P.mm(B.ps[3][:, :], kT, Sb_h[:, :])
                    yield
                else:
                    P.dma("sp", SB.S0f[:, :, :], st_gdn[:, h, :, :].rearrange("s d e -> d s e"))
                    yield
                    P.copy("act", SB.S0b[:, :, :], SB.S0f[:, :, :])
                    yield
                    for (dstT, srcT) in ((SB.kTm, kT), (SB.qTm, qT)):
                        base = dstT[:, :]
                        dap = bass.AP(base.tensor, base.offset, [[16 * 128, 128], [136, 16], [1, 8]])
                        P.op("pool", (lambda e, dap=dap, src=srcT.rearrange("p (s t) -> p s t", s=16):
                                      e.tensor_copy(out=dap, in_=src)), ins=[srcT], outs=[base])
                        yield
                    P.mmk(B.ps[3][:, :], [(SB.kTm[:, s * 128:(s + 1) * 128], SB.S0b[:, s, :]) for s in range(16)])
                    yield
                P.stt("dve", B.R[:, :], B.ps[3][:, :], sm[:, 48 + h:49 + h], vtm[:, h, :], ALU.mult, ALU.add)
                yield
                P.mm(B.ps[0][:, :], B.TTb[:, :], B.R[:, :])
                yield
                P.act(B.vn[:, :], B.ps[0][:, :], AF.Identity, scale=sm[:, 8 + h:9 + h])
                yield
                yield
                P.sec(_st[0])
                if typ == 1:
                    P.mm(B.ps[1][:, :], qT, Sb_h[:, :])
                    yield
                else:
                    P.mmk(B.ps[1][:, :], [(SB.qTm[:, s * 128:(s + 1) * 128], SB.S0b[:, s, :]) for s in range(16)])
                    yield
                P.mm(B.ps[2][:, :], B.attnT[:, :], B.vn[:, :])
                yield
                P.act(B.ot[:, :], B.ps[1][:, :], AF.Identity, scale=ex[:, h:h + 1])
                yield
                P.tt("dve", otm[:, h * 128:(h + 1) * 128], B.ot[:, :], B.ps[2][:, :], ALU.add)
                yield
                P.act(B.kout[:, :], ktm[:, h, :], AF.Identity, scale=ex[:, 8 + h:9 + h])
                yield
                yield
                P.sec(_st[0])
                if typ == 1:
                    P.mm(B.ps[3][:, :], B.kout[:, :], B.vn[:, :])
                    yield
                    P.stt("dve", Sf_h[:, :], Sf_h[:, :], ex[:, 16 + h:17 + h], B.ps[3][:, :], ALU.mult, ALU.add)
                    yield
                    P.copy("act", Sb_h[:, :], Sf_h[:, :])
                    yield
                else:
                    P.tt("dve", SB.koutm_all[:, :, :], bc(B.kout[:, :], 1, 16), bc(blk16[:, :], 2, 128), ALU.mult)
                    yield
                    for s in range(16):
                        P.mm(LPB[s // 4][:, (s % 4) * 128:(s % 4 + 1) * 128], SB.koutm_all[:, s, :], B.vn[:, :])
                    yield
                    for j4 in range(4):
                        S4 = SB.S0f[:, 4 * j4:4 * j4 + 4, :]
                        gsel = SB.gendsel[:, h * 16 + 4 * j4:h * 16 + 4 * j4 + 4]
                        P.tt("dve", S4, S4, bc(gsel, 2, 128), ALU.mult)
                        yield
                        P.tt("dve", S4, S4, LPB[j4][:, :].rearrange("p (s e) -> p s e", s=4), ALU.add)
                        yield
                    P.dma("sp", o_gdn_s[:, h, :, :].rearrange("s d e -> d s e"), SB.S0f[:, :, :])
                    yield
                yield

            def tail_gen(ti, fb):
                xt = fb.xt
                P.sec("T:onorm")
                P.dma("sp", mixed[:, 0:D], yssd_scr[ti])
                yield
                P.tt("pool", tmpT[:, :], otm[:, :], otm[:, :], ALU.mult)
                yield
                P.red("dve", ss8[:, 0:8], tmpT[:, :].rearrange("p (h d) -> p h d", h=8))
                yield
                P.act(ss8[:, 8:16], ss8[:, 0:8], AF.Sqrt, bias=cst[:, 0:1], scale=1.0 / 128)
                yield
                P.recip(ss8[:, 16:24], ss8[:, 8:16])
                yield
                o3 = otm[:, :].rearrange("p (h d) -> p h d", h=8)
                P.tt("dve", o3, o3, bc(ss8[:, 16:24], 2, 128), ALU.mult)
                yield
                P.tt("dve", o3, o3, bc(gnw[:, :], 1, 8), ALU.mult)
                yield
                P.tt("dve", mixed[:, D:2 * D], otm[:, :], fb.gs[:, :], ALU.mult)
                yield
                yield
                P.sec("T:outproj")
                for half in range(2):
                    P.trs([(PT[:, j * 128:(j + 1) * 128], mixed[:, (half * 8 + j) * 128:(half * 8 + j + 1) * 128])
                           for j in range(8)], identb[:, :])
                    yield
                    P.copy("act", mixedT[:, half * 8:(half + 1) * 8, :], PT[:, :].rearrange("p (c t) -> p c t", c=8))
                    yield
                yield
                P.sec("T:outproj")
                for nb in range(2):
                    P.mmk(PA[nb][:, :], [(mixedT[:, kc, :], wO.col(kc, nb * 512, 512)) for kc in range(16)])
                    yield
                    P.copy("act", tmpT[:, nb * 512:(nb + 1) * 512], PA[nb][:, :])
                    yield
                yield
                P.sec("T:resid")
                P.sqsum(otm[:, :], tmpT[:, :], sttT[:, 0:1])
                yield
                P.act(sttT[:, 1:2], sttT[:, 0:1], AF.Sqrt, bias=cst[:, 0:1], scale=1.0 / D)
                yield
                P.recip(sttT[:, 2:3], sttT[:, 1:2])
                yield
                P.stt("dve", tmpT[:, :], tmpT[:, :], sttT[:, 2:3], modv[:, 2, :], ALU.mult, ALU.mult)
                yield
                P.tt("dve", xt[:, :], tmpT[:, :], xt[:, :], ALU.add)
                yield
                P.dma("sp", x1_scr[ti], xt[:, :])
                yield
                yield

            def speed(g, k):
                while True:
                    for _ in range(k):
                        try:
                            next(g)
                        except StopIteration:
                            return
                    yield

            def run_rr(gens):
                alive = list(gens)
                while alive:
                    nxt = []
                    for gn in alive:
                        try:
                            next(gn)
                            nxt.append(gn)
                        except StopIteration:
                            pass
                    alive = nxt

            def back(ti, typ, SB, fb, nl, side, with_tail=True):
                side = [side] if side is not None else []
                for h0_ in range(0, 8, nl):
                    gens = [head_gen(h0_ + i, lanes[i], typ, SB, fb) for i in range(min(nl, 8 - h0_))]
                    alive = gens + side
                    while any(g in alive for g in gens):
                        nxt = []
                        for gn in alive:
                            try:
                                next(gn)
                                nxt.append(gn)
                            except StopIteration:
                                if gn in side:
                                    side = []
                        alive = nxt
                if with_tail:
                    run_rr([tail_gen(ti, fb)] + side)
                else:
                    run_rr(side)

            class SBufs:
                pass

            with contextlib.ExitStack() as st2s:
                SB = SBufs()
                SB.histS = sb(st2s, "histSg", [128, 24, 16, 3])
                SB.xinS_l = [sb(st2s, "xinS2_%d" % i, [128, 4, 16, 11]) for i in range(2)]
                SB.S0f = sb(st2s, "S0f", [128, 16, 128])
                SB.S0b = sb(st2s, "S0b", [128, 16, 128], BF16)
                SB.kTm = sb(st2s, "kTm", [128, 16 * 128], BF16)
                SB.qTm = sb(st2s, "qTm", [128, 16 * 128], BF16)
                SB.koutm_all = sb(st2s, "koutm_all", [128, 16, 128], BF16)
                SB.rselg = sb(st2s, "rselg", [128, 128])
                SB.gendsel = sb(st2s, "gendsel", [128, 128])
                P.dma("sp", SB.histS[:, :, :, :], hist_gdn[:, :, :, :])
                P.memset("pool", SB.kTm[:, :], 0.0)
                P.memset("pool", SB.qTm[:, :], 0.0)
                run_rr([front_gen(0, 0, SB, fbs[0])])
                back(0, 0, SB, fbs[0], 1, None)
                P.barrier()
                P.emit()
            with contextlib.ExitStack() as st2p:
                SB = SBufs()
                SB.Sf = [sb(st2p, "Sf%d" % h, [128, 128]) for h in range(8)]
                SB.Sb = [sb(st2p, "Sb%d" % h, [128, 128], BF16) for h in range(8)]
                for ln in range(1, P2_NL):
                    make_lane(ln, st2p)
                fbs.append(make_fb(1, st2p))
                for h in range(8):
                    P.memset("pool", SB.Sf[h][:, :], 0.0)
                    P.memset("pool", SB.Sb[h][:, :], 0.0)
                run_rr([front_gen(1, 1, SB, fbs[1])])
                pending_tail = None
                for ti in range(1, NT):
                    parts = []
                    if pending_tail is not None:
                        parts.append(pending_tail)
                    if ti + 1 < NT:
                        parts.append(speed(front_gen(ti + 1, 1, SB, fbs[(ti + 1) % 2]), 2))
                    side = itertools.chain(*parts) if parts else None
                    back(ti, 1, SB, fbs[ti % 2], P2_NL, side, with_tail=False)
                    pending_tail = tail_gen(ti, fbs[ti % 2])
                run_rr([pending_tail])
                for h in range(8):
                    P.dma("sp", o_gdn_p[h * 128:(h + 1) * 128, :], SB.Sf[h][:, :])
                P.barrier()
                P.emit()
        if STOP_AFTER >= 3:
          with contextlib.ExitStack() as st3:
            PB = [st3.enter_context(nc.psum_tensor("pb%d" % i, [128, 512], F32)) for i in range(7)]
            gb = [0, 512, 1024, 1536, 2048, 2560, 2816]
            wG = WBlocks(st3, "wG", 8, gb + [DFF + b for b in gb[1:]])
            w_gu_v = w_gu.rearrange("(kc p) n -> p kc n", p=128)
            wG.load(w_gu_v, 0, order=[j for i in range(6) for j in (i, 6 + i)])
            wD = WBlocks(st3, "wD", 22, [0, 512, 1024])
            w_dn_v = w_dn.rearrange("(kc p) n -> p kc n", p=128)
            wD.load(w_dn_v, 0)
            modv = sb(st3, "modv3", [128, 3, D])
            xt = [sb(st3, "xt3_%d" % i, [128, D]) for i in range(2)]
            tmpA = [sb(st3, "tmpA3_%d" % i, [128, D]) for i in range(2)]
            tmpB = sb(st3, "tmpB3", [128, D])
            hb = [sb(st3, "hb3_%d" % i, [128, D], BF16) for i in range(2)]
            hT = [sb(st3, "hT3_%d" % i, [128, 8, 128], BF16) for i in range(2)]
            stt_ = [sb(st3, "stt3_%d" % i, [128, 4]) for i in range(2)]
            sg = [sb(st3, "sg%d" % i, [128, 512]) for i in range(2)]
            hid = sb(st3, "hid", [128, DFF], BF16)
            hidT = sb(st3, "hidT", [128, 22, 128], BF16)

            def ffn_gen(ti):
                typ = 0 if ti == 0 else 1
                pr = ti % 2
                x_, tA, hb_, hT_, st_ = xt[pr], tmpA[pr], hb[pr], hT[pr], stt_[pr]
                if ti <= 1:
                    P.dma("sp", modv[:, :, :], modscr[typ].rearrange("p (s d) -> p s d", s=6)[:, 3:6, :])
                P.dma("sp", x_[:, :], x1_scr[ti])
                yield
                yield from norm_to_T_g(x_, modv, 1, 0, tA, hb_, hT_, st_)
                for i in range(6):
                    w = 512 if i < 5 else 256
                    pg_, pu_ = PB[(2 * i) % 6], PB[(2 * i + 1) % 6]
                    P.mmk(pg_[:, 0:w], [(hT_[:, kc, :], wG.col(kc, i * 512, w)) for kc in range(8)])
                    yield
                    P.mmk(pu_[:, 0:w], [(hT_[:, kc, :], wG.col(kc, DFF + i * 512, w)) for kc in range(8)])
                    yield
                    s_ = sg[i % 2]
                    P.act(s_[:, 0:w], pg_[:, 0:w], AF.Silu)
                    yield
                    P.tt("dve", hid[:, i * 512:i * 512 + w], s_[:, 0:w], pu_[:, 0:w], ALU.mult)
                    yield
                yield "HALF"
                for grp in range(3):
                    n = 8 if grp < 2 else 6
                    P.trs([(PT[:, j * 128:(j + 1) * 128], hid[:, (grp * 8 + j) * 128:(grp * 8 + j + 1) * 128])
                           for j in range(n)], identb[:, :])
                    yield
                    P.copy("act", hidT[:, grp * 8:grp * 8 + n, :],
                           PT[:, 0:n * 128].rearrange("p (c t) -> p c t", c=n))
                    yield
                for nb in range(2):
                    P.mmk(PB[6][:, :], [(hidT[:, kc, :], wD.col(kc, nb * 512, 512)) for kc in range(22)])
                    yield
                    P.copy("act", tA[:, nb * 512:(nb + 1) * 512], PB[6][:, :])
                    yield
                P.sqsum(tmpB[:, :], tA[:, :], st_[:, 0:1])
                yield
                P.act(st_[:, 1:2], st_[:, 0:1], AF.Sqrt, bias=cst[:, 0:1], scale=1.0 / D)
                yield
                P.recip(st_[:, 2:3], st_[:, 1:2])
                yield
                P.stt("dve", tA[:, :], tA[:, :], st_[:, 2:3], modv[:, 2, :], ALU.mult, ALU.mult)
                yield
                P.tt("dve", x_[:, :], tA[:, :], x_[:, :], ALU.add)
                yield
                P.dma("sp", y_all[ti], x_[:, :])
                yield

            run_gen(ffn_gen(0))
            nxt = 2
            cur = ffn_gen(1)
            young = None
            while cur is not None:
                try:
                    v = next(cur)
                    if v == "HALF" and young is None and nxt < NT:
                        young = ffn_gen(nxt)
                        nxt += 1
                except StopIteration:
                    cur, young = young, None
                    if cur is None and nxt < NT:
                        cur = ffn_gen(nxt)
                        nxt += 1
                    continue
                if young is not None:
                    try:
                        v2 = next(young)
                        if v2 == "HALF":
                            pass
                    except StopIteration:
                        young = None
            P.barrier()
            P.emit()
    return nc


def _prep_inputs(inp):
    f = lambda a: np.ascontiguousarray(np.asarray(a, dtype=np.float32))
    xp, xs = f(inp["x_prompt"]), f(inp["x_sample"])
    cp, cs = f(inp["c_prompt"]), f(inp["c_sample"])
    bcast = lambda v: np.ascontiguousarray(np.broadcast_to(np.asarray(v, np.float32).reshape(1, -1), (128, v.size)))
    shared = {}
    shared["w_ada"] = f(inp["w_ada"][0])
    shared["b_ada_b"] = bcast(inp["b_ada"][0])
    shared["normvecs"] = np.ascontiguousarray(np.stack(
        [bcast(inp[k][0]) for k in ("norm_mix_pre", "norm_mix_post", "norm_ffn_pre", "norm_ffn_post")], axis=1))
    shared["w_in"] = f(inp["w_in"][0])
    shared["w_out"] = f(inp["w_out"][0])
    shared["w_gu"] = f(inp["w_gate_up"][0])
    shared["w_dn"] = f(inp["w_down"][0])
    scw = np.concatenate([f(inp["ssd_conv_w"][0]), f(inp["ssd_conv_b"][0])[None, :]], axis=0)
    shared["ssd_cw"] = np.ascontiguousarray(scw.reshape(5, 12, 128).transpose(2, 1, 0))
    shared["gdn_cw"] = np.ascontiguousarray(f(inp["gdn_conv_w"][0]).reshape(4, 24, 128).transpose(2, 1, 0))
    sv = np.concatenate([f(inp["ssd_dt_bias"][0]), f(inp["ssd_A_log"][0]), f(inp["ssd_D"][0]),
                         f(inp["gdn_dt_bias"][0]), f(inp["gdn_A_log"][0])])
    shared["smallv"] = bcast(sv)
    shared["ssd_nw"] = bcast(inp["ssd_norm_w"][0])
    shared["gdn_nw"] = bcast(inp["gdn_norm_w"][0])
    shared["ident"] = np.eye(128, dtype=np.float32)
    idx = np.arange(128)
    m = np.zeros((2, 128, 4, 128), np.float32)
    for typ, bs in ((0, 8), (1, 128)):
        same = (idx[:, None] // bs) == (idx[None, :] // bs)
        m[typ, :, 0, :] = same & (idx[:, None] > idx[None, :])
        m[typ, :, 1, :] = same & (idx[:, None] <= idx[None, :])
        m[typ, :, 2, :] = same & (idx[:, None] < idx[None, :])
        m[typ, :, 3, :] = same
    shared["masks"] = m
    shared["blk16"] = np.ascontiguousarray(((idx[:, None] // 8) == np.arange(16)[None, :]).astype(np.float32))
    maps = []
    for i in range(NCORES):
        d = dict(shared)
        sl = slice(16 * i, 16 * (i + 1))
        d["xs_all"] = np.ascontiguousarray(np.concatenate(
            [xs[sl].reshape(1, 128, D), xp[i].reshape(16, 128, D)], axis=0))
        d["cexp"] = np.ascontiguousarray(np.stack(
            [np.repeat(cs[sl], 8, axis=0), np.broadcast_to(cp[i][None, :], (128, D))], axis=0))
        d["st_ssd"] = np.ascontiguousarray(f(inp["state_ssd"][0, sl]).reshape(16, 1024, 128))
        hs = f(inp["state_ssd_conv"][0, sl])
        d["hist_ssd"] = np.ascontiguousarray(hs.reshape(16, 3, 12, 128).transpose(3, 2, 0, 1))
        d["st_gdn"] = np.ascontiguousarray(f(inp["state_gdn"][0, sl]))
        hg = f(inp["state_gdn_conv"][0, sl])
        d["hist_gdn"] = np.ascontiguousarray(hg.reshape(16, 3, 24, 128).transpose(3, 2, 0, 1))
        maps.append(d)
    return maps


def kernel(**inp):
    maps = _prep_inputs(inp)
    nc = build_program()
    res = run_bass_kernel_spmd(nc, maps, core_ids=list(range(NCORES)))
    R = res.results
    cat = lambda k: np.stack([np.asarray(r[k]) for r in R], axis=0)
    y_all = cat("y_all")
    y_prompt = y_all[:, 1:].reshape(8, 2048, D)
    y_sample = y_all[:, 0].reshape(128, 8, D)
    ssd_p = cat("o_ssd_p").reshape(1, 8, 16, 64, 128)
    ssdc_p = cat("o_ssdc_p").reshape(1, 8, 3, 1536)
    gdn_p = cat("o_gdn_p").reshape(1, 8, 8, 128, 128)
    gdnc_p = cat("o_gdnc_p").reshape(1, 8, 3, 3072)
    ssd_s = cat("o_ssd_s").reshape(1, 128, 16, 64, 128)
    ssdc_s = cat("o_ssdc_s").reshape(1, 128, 3, 1536)
    gdn_s = cat("o_gdn_s").reshape(1, 128, 8, 128, 128)
    gdnc_s = cat("o_gdnc_s").reshape(1, 128, 3, 3072)
    outs = (y_prompt, y_sample, ssd_p, ssdc_p, gdn_p, gdnc_p, ssd_s, ssdc_s, gdn_s, gdnc_s)
    return tuple(np.ascontiguousarray(o, dtype=np.float32) for o in outs)
```

```python
import contextlib
import itertools
import os
import numpy as np
import concourse.bass as bass
import concourse.mybir as mybir
from concourse.bass_utils import run_bass_kernel_spmd

F32 = mybir.dt.float32
BF16 = mybir.dt.bfloat16
AF = mybir.ActivationFunctionType
ALU = mybir.AluOpType
AX = mybir.AxisListType

NCORES = 8
D = 1024
NT = 17
DFF = 2816
P2_MODE = 0
ANNOTATE = bool(int(os.environ.get('K_ANN', '0')))
SKIP1 = False
P2_STAGE = 99
P2_HSTEP = 99
P2_SKIP = ()
P2_NL = 4
P2_TILES = 16
STOP_AFTER = int(os.environ.get('K_STOP', '99'))


class Prog:
    ENGS = ("pe", "act", "dve", "pool", "sp")

    def __init__(self, nc, stack):
        self.nc = nc
        self.sems = {}
        for e in self.ENGS:
            self.sems[e] = stack.enter_context(nc.semaphore("s_" + e))
        self.dsems = {"sp": [], "pool": [], "act": []}
        for q, n in (("sp", 8), ("pool", 4), ("act", 2)):
            for j in range(n):
                nm = "d_%s%d" % (q, j)
                self.sems[nm] = stack.enter_context(nc.semaphore(nm))
                self.dsems[q].append(nm)
        self.cnt = {k: 0 for k in self.sems}
        self.known = {e: {} for e in self.ENGS}
        self.lastw = {}
        self.readers = {}
        self.ops = {e: [] for e in self.ENGS}
        self.rr = {"sp": 0, "pool": 0, "act": 0}
        self.nops = 0
        self.tag = "init"

    def sec(self, name):
        self.tag = name

    fine = {}

    def _keys(self, aps):
        ks = []
        for a in aps:
            if a is None or isinstance(a, (int, float)):
                continue
            if isinstance(a, str):
                ks.append(a)
                continue
            nm = a.name
            if nm in self.fine:
                row, gr = self.fine[nm]
                nm = "%s:%d" % (nm, (int(a.offset) % row) // gr)
            ks.append(nm)
        return ks

    def _deps(self, eng, r, w):
        need = {}

        def add(c):
            if c is None:
                return
            s, v = c
            if s == "pe" and eng == "pe":
                return
            if need.get(s, 0) < v:
                need[s] = v

        for k in r:
            add(self.lastw.get(k))
        for k in w:
            add(self.lastw.get(k))
            for c in self.readers.get(k, {}).items():
                add(c)
        waits = []
        kn = self.known[eng]
        for s, v in need.items():
            if kn.get(s, 0) < v:
                kn[s] = v
                waits.append((s, v))
        return waits

    def _commit(self, c, r, w):
        for k in w:
            self.lastw[k] = c
            self.readers[k] = {}
        for k in r:
            d = self.readers.setdefault(k, {})
            if d.get(c[0], 0) < c[1]:
                d[c[0]] = c[1]

    def op(self, eng, fn, ins=(), outs=()):
        r = self._keys(ins)
        w = self._keys(outs)
        w = w + [k for k in r if k.startswith("ps") or k.startswith("pa") or k.startswith("pm")
                 or k.startswith("lpb") or k.startswith("pb")]
        waits = self._deps(eng, r, w)
        self.cnt[eng] += 1
        self.ops[eng].append((waits, fn, eng, 1, self.tag))
        self._commit((eng, self.cnt[eng]), r, w)
        self.nops += 1

    def dma(self, q, out, in_, extra_ins=(), extra_outs=()):
        r = self._keys([in_] + list(extra_ins))
        w = self._keys([out] + list(extra_outs))
        waits = self._deps(q, r, w)
        sems = self.dsems[q]
        j = self.rr[q]
        self.rr[q] = (j + 1) % len(sems)
        nm = sems[j]
        prev = self.cnt[nm]
        if prev > 0 and self.known[q].get(nm, 0) < prev:
            self.known[q][nm] = prev
            waits.append((nm, prev))
        self.cnt[nm] += 16
        self.ops[q].append((waits, lambda e: e.dma_start(out=out, in_=in_), nm, 16, self.tag))
        self._commit((nm, self.cnt[nm]), r, w)
        self.nops += 1

    def barrier(self):
        for e in self.ENGS:
            waits = []
            for s, v in self.cnt.items():
                if v > 0 and self.known[e].get(s, 0) < v:
                    self.known[e][s] = v
                    waits.append((s, v))
            self.ops[e].append((waits, None, None, 0, self.tag))
        self.lastw = {}
        self.readers = {}

    def emit(self):
        nc = self.nc
        with nc.Block() as block:
            for ename, deco in (("sp", block.sync), ("act", block.scalar), ("dve", block.vector),
                                ("pool", block.gpsimd), ("pe", block.tensor)):
                ops = self.ops[ename]

                def body(e, ops=ops):
                    for waits, fn, sname, inc, tag in ops:
                        for s, v in waits:
                            e.wait_ge(self.sems[s], v)
                        if fn is not None:
                            ins = fn(e)
                            ins.then_inc(self.sems[sname], inc)
                            if ANNOTATE:
                                ins.annotate(tag)

                deco(body)
                self.ops[ename] = []

    def mm(self, out, lhsT, rhs, start=True, stop=True):
        self.op("pe", lambda e: e.matmul(out, lhsT=lhsT, rhs=rhs, start=start, stop=stop),
                ins=[lhsT, rhs], outs=[out])

    def mmk(self, out, pairs):
        n = len(pairs)

        def fn(e):
            ins = None
            for i, (l, r) in enumerate(pairs):
                ins = e.matmul(out, lhsT=l, rhs=r, start=(i == 0), stop=(i == n - 1))
            return ins

        self.op("pe", fn, ins=[x for p in pairs for x in p], outs=[out])

    def tr(self, out, in_, ident):
        self.op("pe", lambda e: e.transpose(out, in_, ident), ins=[in_, ident], outs=[out])

    def trs(self, items, ident):
        def fn(e):
            ins = None
            for o, i in items:
                ins = e.transpose(o, i, ident)
            return ins
        self.op("pe", fn, ins=[i for _, i in items] + [ident], outs=[o for o, _ in items])

    def act(self, out, in_, func, bias=None, scale=None):
        kw = {}
        if bias is not None:
            kw["bias"] = bias
        if scale is not None:
            kw["scale"] = scale
        self.op("act", lambda e: e.activation(out=out, in_=in_, func=func, **kw),
                ins=[in_, bias, scale], outs=[out])

    def sqsum(self, junk, in_, accum):
        self.op("act", lambda e: e.activation(out=junk, in_=in_, func=AF.Square, accum_out=accum),
                ins=[in_], outs=[junk, accum])

    def tt(self, eng, out, in0, in1, op):
        self.op(eng, lambda e: e.tensor_tensor(out=out, in0=in0, in1=in1, op=op), ins=[in0, in1], outs=[out])

    def ts(self, eng, out, in0, s1, s2, op0, op1=None):
        if op1 is None:
            self.op(eng, lambda e: e.tensor_scalar(out=out, in0=in0, scalar1=s1, scalar2=None, op0=op0),
                    ins=[in0, s1], outs=[out])
        else:
            self.op(eng, lambda e: e.tensor_scalar(out=out, in0=in0, scalar1=s1, scalar2=s2, op0=op0, op1=op1),
                    ins=[in0, s1, s2], outs=[out])

    def stt(self, eng, out, in0, scalar, in1, op0, op1):
        self.op(eng, lambda e: e.scalar_tensor_tensor(out=out, in0=in0, scalar=scalar, in1=in1, op0=op0, op1=op1),
                ins=[in0, scalar, in1], outs=[out])

    def copy(self, eng, out, in_):
        if eng == "act":
            self.op(eng, lambda e: e.copy(out=out, in_=in_), ins=[in_], outs=[out])
        else:
            self.op(eng, lambda e: e.tensor_copy(out=out, in_=in_), ins=[in_], outs=[out])

    def red(self, eng, out, in_):
        self.op(eng, lambda e: e.tensor_reduce(out=out, in_=in_, axis=AX.X, op=ALU.add), ins=[in_], outs=[out])

    def recip(self, out, in_):
        self.op("dve", lambda e: e.reciprocal(out=out, in_=in_), ins=[in_], outs=[out])

    def memset(self, eng, ap, val):
        self.op(eng, lambda e: e.memset(ap, val), ins=[], outs=[ap])


def bc(ap, axis, n):
    u = ap.unsqueeze(axis)
    shp = list(u.shape)
    shp[axis] = n
    return u.broadcast_to(shp)


def build_program():
    nc = bass.Bass("TRN2", target_bir_lowering=False)

    def din(name, shape, dt=F32):
        return nc.dram_tensor(name, list(shape), dt, kind="ExternalInput").ap()

    def dout(name, shape, dt=F32):
        return nc.dram_tensor(name, list(shape), dt, kind="ExternalOutput").ap()

    def dscr(name, shape, dt=F32):
        return nc.dram_tensor(name, list(shape), dt).ap()

    xs_all = din("xs_all", [NT, 128, D])
    cexp = din("cexp", [2, 128, D])
    w_ada = din("w_ada", [D, 6 * D])
    b_ada_b = din("b_ada_b", [128, 6 * D])
    normvecs = din("normvecs", [128, 4, D])
    w_in = din("w_in", [D, 6688])
    w_out = din("w_out", [2 * D, D])
    w_gu = din("w_gu", [D, 2 * DFF])
    w_dn = din("w_dn", [DFF, D])
    ssd_cw = din("ssd_cw", [128, 12, 5])
    gdn_cw = din("gdn_cw", [128, 24, 4])
    smallv = din("smallv", [128, 64])
    ssd_nw = din("ssd_nw", [128, D])
    gdn_nw = din("gdn_nw", [128, 128])
    st_ssd = din("st_ssd", [16, 1024, 128])
    hist_ssd = din("hist_ssd", [128, 12, 16, 3])
    st_gdn = din("st_gdn", [16, 8, 128, 128])
    hist_gdn = din("hist_gdn", [128, 24, 16, 3])
    ident_d = din("ident", [128, 128])
    masks_d = din("masks", [2, 128, 4, 128])
    blk16_d = din("blk16", [128, 16])

    y_all = dout("y_all", [NT, 128, D])
    o_ssd_p = dout("o_ssd_p", [1024, 128])
    o_ssdc_p = dout("o_ssdc_p", [3, 1536])
    o_gdn_p = dout("o_gdn_p", [1024, 128])
    o_gdnc_p = dout("o_gdnc_p", [3, 3072])
    o_ssd_s = dout("o_ssd_s", [16, 1024, 128])
    o_ssdc_s = dout("o_ssdc_s", [48, 1536])
    o_gdn_s = dout("o_gdn_s", [16, 8, 128, 128])
    o_gdnc_s = dout("o_gdnc_s", [48, 3072])

    modscr = dscr("modscr", [2, 128, 6 * D])
    yssd_scr = dscr("yssd_scr", [NT, 128, D], BF16)
    x1_scr = dscr("x1_scr", [NT, 128, D])

    with contextlib.ExitStack() as gstack:
        P = Prog(nc, gstack)

        def sb(stack, name, shape, dt=F32):
            return stack.enter_context(nc.sbuf_tensor(name, list(shape), dt))

        class WBlocks:
            def __init__(self, stk, name, nk, bounds):
                self.bounds = bounds
                self.t = [sb(stk, "%s_%d" % (name, j), [128, nk, bounds[j + 1] - bounds[j]], BF16)
                          for j in range(len(bounds) - 1)]

            def load(self, dview, col_off, order=None):
                for j in (order if order is not None else range(len(self.t))):
                    b0, b1 = self.bounds[j], self.bounds[j + 1]
                    P.dma("pool", self.t[j][:, :, :], dview[:, :, col_off + b0:col_off + b1])

            def col(self, kc, c0, w):
                for j in range(len(self.t)):
                    if self.bounds[j] <= c0 and c0 + w <= self.bounds[j + 1]:
                        return self.t[j][:, kc, c0 - self.bounds[j]:c0 - self.bounds[j] + w]
                raise ValueError("column range straddles weight blocks")

        PT = gstack.enter_context(nc.psum_tensor("pst", [128, 1024], BF16))
        ps1 = contextlib.ExitStack()
        PS = [ps1.enter_context(nc.psum_tensor("ps%d" % i, [128, 512], F32)) for i in range(7)]

        identf = sb(gstack, "identf", [128, 128])
        identb = sb(gstack, "identb", [128, 128], BF16)
        onesf = sb(gstack, "onesf", [128, 128])
        onesb = sb(gstack, "onesb", [128, 128], BF16)
        cst = sb(gstack, "cst", [128, 4])
        masks = [sb(gstack, "masks%d" % t, [128, 4, 128]) for t in range(2)]
        blk16 = sb(gstack, "blk16s", [128, 16])
        smv = sb(gstack, "smv", [128, 64])
        negA = sb(gstack, "negA", [128, 24])
        P.dma("sp", identf[:, :], ident_d[:, :])
        P.dma("pool", identb[:, :], ident_d[:, :])
        for t in range(2):
            P.dma("sp", masks[t][:, :, :], masks_d[t])
        P.dma("sp", blk16[:, :], blk16_d[:, :])
        P.dma("sp", smv[:, :], smallv[:, :])
        P.memset("dve", onesf[:, :], 1.0)
        P.memset("dve", onesb[:, :], 1.0)
        P.memset("dve", cst[:, 0:1], 1e-6)
        P.memset("dve", cst[:, 1:2], 1.0)
        P.memset("dve", cst[:, 2:3], 128e-6)
        P.memset("dve", cst[:, 3:4], 0.0)
        P.act(negA[:, 0:16], smv[:, 16:32], AF.Exp)
        P.act(negA[:, 16:24], smv[:, 56:64], AF.Exp)
        P.ts("dve", negA[:, :], negA[:, :], -1.0, None, ALU.mult)

        MSTT, MINC, MSTR, MBLK = 0, 1, 2, 3

        with contextlib.ExitStack() as st0:
            ct = sb(st0, "ct", [128, D])
            cb = sb(st0, "cb", [128, D], BF16)
            cT = [sb(st0, "cT%d" % t, [128, 8, 128], BF16) for t in range(2)]
            modt = [sb(st0, "modt%d" % t, [128, 6 * D]) for t in range(2)]
            nv = sb(st0, "nv", [128, 4, D])
            wa = [sb(st0, "wa%d" % i, [128, 8, 512], BF16) for i in range(2)]
            bb = [sb(st0, "bb%d" % i, [128, 512]) for i in range(2)]
            P.dma("sp", nv[:, :, :], normvecs[:, :, :])
            for t in range(2):
                P.dma("sp", ct[:, :], cexp[t])
                P.act(cb[:, :], ct[:, :], AF.Silu)
                P.trs([(PT[:, c * 128:(c + 1) * 128], cb[:, c * 128:(c + 1) * 128]) for c in range(8)], identb[:, :])
                P.copy("dve", cT[t][:, :, :], PT[:, :].rearrange("p (c t) -> p c t", c=8))
            wv = w_ada.rearrange("(kc p) n -> p kc n", p=128)
            for j in range(12):
                P.dma("pool", wa[j % 2][:, :, :], wv[:, :, j * 512:(j + 1) * 512])
                P.dma("sp", bb[j % 2][:, :], b_ada_b[:, j * 512:(j + 1) * 512])
                for t in range(2):
                    ps = PS[(2 * j + t) % 4]
                    P.mmk(ps[:, :], [(cT[t][:, kc, :], wa[j % 2][:, kc, :]) for kc in range(8)])
                    P.tt("dve", modt[t][:, j * 512:(j + 1) * 512], ps[:, :], bb[j % 2][:, :], ALU.add)
            for t in range(2):
                m = modt[t]
                P.stt("dve", m[:, D:2 * D], m[:, D:2 * D], 1.0, nv[:, 0, :], ALU.add, ALU.mult)
                P.tt("dve", m[:, 2 * D:3 * D], m[:, 2 * D:3 * D], nv[:, 1, :], ALU.mult)
                P.stt("dve", m[:, 4 * D:5 * D], m[:, 4 * D:5 * D], 1.0, nv[:, 2, :], ALU.add, ALU.mult)
                P.tt("dve", m[:, 5 * D:6 * D], m[:, 5 * D:6 * D], nv[:, 3, :], ALU.mult)
                P.dma("sp", modscr[t], m[:, :])
            P.barrier()
            P.emit()

        def norm_to_T(xt, modv, sidx, hidx, tmpA, hb, hT, stt_):
            P.sec("norm_to_T")
            P.sqsum(tmpA[:, :], xt[:, :], stt_[:, 0:1])
            P.act(stt_[:, 1:2], stt_[:, 0:1], AF.Sqrt, bias=cst[:, 0:1], scale=1.0 / D)
            P.recip(stt_[:, 2:3], stt_[:, 1:2])
            P.stt("dve", tmpA[:, :], xt[:, :], stt_[:, 2:3], modv[:, sidx, :], ALU.mult, ALU.mult)
            P.tt("dve", hb[:, :], tmpA[:, :], modv[:, hidx, :], ALU.add)
            P.trs([(PT[:, c * 128:(c + 1) * 128], hb[:, c * 128:(c + 1) * 128]) for c in range(8)], identb[:, :])
            P.copy("act", hT[:, :, :], PT[:, :].rearrange("p (c t) -> p c t", c=8))

        def norm_to_T_g(xt, modv, sidx, hidx, tmpA, hb, hT, stt_):
            P.sec("norm_to_T")
            P.sqsum(tmpA[:, :], xt[:, :], stt_[:, 0:1])
            yield
            P.act(stt_[:, 1:2], stt_[:, 0:1], AF.Sqrt, bias=cst[:, 0:1], scale=1.0 / D)
            yield
            P.recip(stt_[:, 2:3], stt_[:, 1:2])
            yield
            P.stt("dve", tmpA[:, :], xt[:, :], stt_[:, 2:3], modv[:, sidx, :], ALU.mult, ALU.mult)
            yield
            P.tt("dve", hb[:, :], tmpA[:, :], modv[:, hidx, :], ALU.add)
            yield
            P.trs([(PT[:, c * 128:(c + 1) * 128], hb[:, c * 128:(c + 1) * 128]) for c in range(8)], identb[:, :])
            yield
            P.copy("act", hT[:, :, :], PT[:, :].rearrange("p (c t) -> p c t", c=8))
            yield

        def small_scan_terms(af, nh, typ, pss, ex):
            mk = masks[typ]
            P.mm(pss[:, 0:nh], mk[:, MINC, :], af)
            P.mm(pss[:, nh:2 * nh], mk[:, MSTT, :], af)
            P.mm(pss[:, 2 * nh:3 * nh], mk[:, MBLK, :], af)
            P.act(ex[:, 0:3 * nh], pss[:, 0:3 * nh], AF.Exp)

        def small_scan_terms_g(af, nh, typ, pss, ex):
            mk = masks[typ]
            P.mm(pss[:, 0:nh], mk[:, MINC, :], af)
            yield
            P.mm(pss[:, nh:2 * nh], mk[:, MSTT, :], af)
            yield
            P.mm(pss[:, 2 * nh:3 * nh], mk[:, MBLK, :], af)
            yield
            P.act(ex[:, 0:3 * nh], pss[:, 0:3 * nh], AF.Exp)
            yield

        def proj_conv_gen(typ, ngroups, wt, wcol0, hT, psb, xinS_l, xinP_l, histS, histP, cwt, has_bias, accs, dst_fn):
            items = []

            def slot(i):
                n = len(items)
                if 0 <= i < n:
                    it = items[i]
                    if has_bias:
                        P.act(it[0], it[2][0], AF.Identity, bias=cwt[:, it[4], 4:5], scale=cwt[:, it[4], 0:1])
                    else:
                        P.act(it[0], it[2][0], AF.Identity, scale=cwt[:, it[4], 0:1])
                    yield
                if 0 <= i - 1 < n:
                    it = items[i - 1]
                    for k in range(1, 4):
                        P.stt("dve", it[0], it[2][k], cwt[:, it[4], k:k + 1], it[0], ALU.mult, ALU.add)
                        yield
                if 0 <= i - 2 < n:
                    it = items[i - 2]
                    P.act(it[3], it[1][:, :], AF.Silu)
                    yield

            for g in range(ngroups):
                ps = psb[g % 2]
                for j in range(4):
                    c = 4 * g + j
                    P.mmk(ps[:, j * 128:(j + 1) * 128],
                          [((wt.col(kc, wcol0 + c * 128, 128) if hasattr(wt, "col")
                             else wt[:, kc, wcol0 + c * 128:wcol0 + (c + 1) * 128]), hT[:, kc, :]) for kc in range(8)])
                    yield
                if typ == 0:
                    xin = xinS_l[g % 2]
                    P.copy("pool", xin[:, :, :, 0:3], histS[:, 4 * g:4 * g + 4, :, :])
                    yield
                    P.copy("act", xin[:, :, :, 3:11], ps[:, :].rearrange("p (j s t) -> p j s t", j=4, s=16))
                    yield
                else:
                    xin = xinP_l[g % 2]
                    P.copy("pool", xin[:, :, 0:3], histP[:, 4 * g:4 * g + 4, :])
                    yield
                    P.copy("act", xin[:, :, 3:131], ps[:, :].rearrange("p (j t) -> p j t", j=4))
                    yield
                    P.copy("pool", histP[:, 4 * g:4 * g + 4, :], xin[:, :, 128:131])
                    yield
                for j in range(4):
                    c = 4 * g + j
                    a_ = accs[c % 4]
                    if typ == 0:
                        av = a_[:, :].rearrange("p (s t) -> p s t", s=16)
                        sh = [xin[:, j, :, k:k + 8] for k in range(4)]
                    else:
                        av = a_[:, :]
                        sh = [xin[:, j, k:k + 128] for k in range(4)]
                    items.append((av, a_, sh, dst_fn(c), c))
                    yield from slot(c)
            yield from slot(4 * ngroups)
            yield from slot(4 * ngroups + 1)

        def run_gen(g):
            for _ in g:
                pass

        w_in_v = w_in.rearrange("(kc p) n -> p kc n", p=128)
        if STOP_AFTER >= 1 and not SKIP1:
          with contextlib.ExitStack() as st1:
            wI = WBlocks(st1, "wI", 8, [0, 512, 1024, 1536, 2048, 2560, 2576])
            wI.load(w_in_v, 0, order=[2, 3, 4, 0, 1, 5])
            modv = sb(st1, "modv", [128, 3, D])
            cw = sb(st1, "cw", [128, 12, 5])
            nw = sb(st1, "ssdnw", [128, D])
            histS = sb(st1, "histS", [128, 12, 16, 3])
            histP = sb(st1, "histP", [128, 12, 3])
            P.dma("sp", cw[:, :, :], ssd_cw[:, :, :])
            P.dma("sp", nw[:, :], ssd_nw[:, :])
            P.dma("sp", histS[:, :, :, :], hist_ssd[:, :, :, :])
            P.memset("pool", histP[:, :, :], 0.0)
            xt = [sb(st1, "xt%d" % i, [128, D]) for i in range(2)]
            tmpA = sb(st1, "tmpA", [128, D])
            hb = sb(st1, "hb", [128, D], BF16)
            hT = sb(st1, "hT", [128, 8, 128], BF16)
            stt_ = sb(st1, "stt", [128, 4])
            xinS_l = [sb(st1, "xinS%d" % i, [128, 4, 16, 11]) for i in range(2)]
            xinP_l = [sb(st1, "xinP%d" % i, [128, 4, 131]) for i in range(2)]
            acc = [sb(st1, "acc%d" % i, [128, 128]) for i in range(4)]
            xc_2 = [sb(st1, "xc_%d" % i, [128, 12, 128], BF16) for i in range(2)]
            zs_2 = [sb(st1, "zs_%d" % i, [128, D], BF16) for i in range(2)]
            sm = sb(st1, "sm", [128, 48])
            af_2 = [sb(st1, "af_%d" % i, [128, 16]) for i in range(2)]
            ex_2 = [sb(st1, "ex_%d" % i, [128, 48]) for i in range(2)]
            xtm = sb(st1, "xtm", [128, D], BF16)
            xdt_2 = [sb(st1, "xdt_%d" % i, [128, D], BF16) for i in range(2)]
            xDs_2 = [sb(st1, "xDs_%d" % i, [128, D], BF16) for i in range(2)]
            xw_2 = [sb(st1, "xw_%d" % i, [128, D], BF16) for i in range(2)]
            Btm_2 = [sb(st1, "Btm_%d" % i, [128, 2, 128], BF16) for i in range(2)]
            CBm_2 = [sb(st1, "CBm_%d" % i, [128, 2, 128]) for i in range(2)]
            L4 = [sb(st1, "L4_%d" % i, [128, 4, 128]) for i in range(2)]
            E4 = [sb(st1, "E4_%d" % i, [128, 4, 128]) for i in range(2)]
            M4 = [sb(st1, "M4_%d" % i, [128, 4, 128], BF16) for i in range(2)]
            yo = sb(st1, "yo", [128, D])
            yy = sb(st1, "yy", [128, D])
            ss = sb(st1, "ss", [128, 8])
            ysb = [sb(st1, "ysb%d" % i, [128, D], BF16) for i in range(2)]
            hTf = sb(st1, "hTf", [128, D])
            hTb = sb(st1, "hTb", [128, D], BF16)
            lhs3 = sb(st1, "lhs3", [128, 8, 48], BF16)
            cv = sb(st1, "cv", [48, 1536])
            CmT = [sb(st1, "CmT%d" % g, [128, 16 * 128], BF16) for g in range(2)]
            h0 = [sb(st1, "h0_%d" % i, [128, 8, 128]) for i in range(2)]
            h0Tb = sb(st1, "h0Tb", [128, D], BF16)
            xwm = sb(st1, "xwm", [128, D], BF16)
            hn = [sb(st1, "hn%d" % i, [128, 8, 128]) for i in range(2)]
            rsel = sb(st1, "rsel", [128, 2, 128])
            decsel = sb(st1, "decsel", [128, 128])
            D2N = ("xc", "zs", "af", "ex", "xdt", "xDs", "xw", "Btm", "CBm")
            D2 = dict(xc=xc_2, zs=zs_2, af=af_2, ex=ex_2, xdt=xdt_2, xDs=xDs_2, xw=xw_2, Btm=Btm_2, CBm=CBm_2)
            P.memset("pool", hTf[:, :], 0.0)
            P.memset("pool", hTb[:, :], 0.0)
            for g in range(2):
                P.memset("pool", CmT[g][:, :], 0.0)

            def ssd_gen(ti):
                typ = 0 if ti == 0 else 1
                xc, zs, af, ex, xdt, xDs, xw, Btm, CBm = (D2[n][ti % 2] for n in D2N)
                PY = [PS[2], PS[3]] if typ == 1 else [PS[0], PS[1]]
                PYO = [PS[5], PS[6]] if typ == 1 else [PS[2], PS[3]]
                mk = masks[typ]
                if ti <= 1:
                    P.dma("sp", modv[:, :, :], modscr[typ].rearrange("p (s d) -> p s d", s=6)[:, 0:3, :])
                    yield
                x_ = xt[ti % 2]
                P.dma("sp", x_[:, :], xs_all[ti])
                yield
                yield from norm_to_T_g(x_, modv, 1, 0, tmpA, hb, hT, stt_)
                yield from proj_conv_gen(typ, 3, wI, 1024, hT, PS[0:2], xinS_l, xinP_l, histS, histP, cw, True, acc,
                                         lambda c: xc[:, c, :])
                P.sec("z (token-major) -> silu")
                for nb in range(2):
                    ps = PS[nb]
                    P.mmk(ps[:, :], [(hT[:, kc, :], wI.col(kc, nb * 512, 512)) for kc in range(8)])
                    yield
                    P.act(zs[:, nb * 512:(nb + 1) * 512], ps[:, :], AF.Silu)
                    yield
                P.sec("dt")
                pd = PS[4]
                P.mmk(pd[:, 0:16], [(hT[:, kc, :], wI.col(kc, 2560, 16)) for kc in range(8)])
                yield
                P.tt("dve", sm[:, 0:16], pd[:, 0:16], smv[:, 0:16], ALU.add)
                yield
                P.act(sm[:, 16:32], sm[:, 0:16], AF.Exp)
                yield
                P.act(sm[:, 32:48], sm[:, 16:32], AF.Ln, bias=cst[:, 1:2], scale=1.0)
                yield
                P.tt("dve", af[:, :], sm[:, 32:48], negA[:, 0:16], ALU.mult)
                yield
                yield from small_scan_terms_g(af[:, :], 16, typ, PS[4][:, 64:112], ex)
                P.sec("token-major x, xdt, xD, xw, B")
                P.trs([(PT[:, c * 128:(c + 1) * 128], xc[:, c, :]) for c in range(8)], identb[:, :])
                yield
                P.copy("act", xtm[:, :], PT[:, :])
                yield
                x3 = xtm[:, :].rearrange("p (h d) -> p h d", h=16)
                P.tt("dve", xdt[:, :].rearrange("p (h d) -> p h d", h=16), x3, bc(sm[:, 32:48], 2, 64), ALU.mult)
                yield
                P.tt("pool", xDs[:, :].rearrange("p (h d) -> p h d", h=16), x3, bc(smv[:, 32:48], 2, 64), ALU.mult)
                yield
                P.tt("pool", xw[:, :].rearrange("p (h d) -> p h d", h=16),
                     xdt[:, :].rearrange("p (h d) -> p h d", h=16), bc(ex[:, 16:32], 2, 64), ALU.mult)
                yield
                P.trs([(PT[:, g * 128:(g + 1) * 128], xc[:, 8 + g, :]) for g in range(2)], identb[:, :])
                yield
                P.copy("act", Btm[:, :, :], PT[:, 0:256].rearrange("p (g n) -> p g n", g=2))
                yield
                P.sec("CB^T masked")
                pc = PS[4][:, 256:512]
                for g in range(2):
                    P.mm(pc[:, g * 128:(g + 1) * 128], xc[:, 8 + g, :], xc[:, 10 + g, :])
                    yield
                P.tt("dve", CBm[:, :, :], pc[:, 0:256].rearrange("p (g l) -> p g l", g=2), bc(mk[:, MINC, :], 1, 2), ALU.mult)
                yield
                P.sec("y_off raw")
                if typ == 1:
                    yield "HALF"
                else:
                    for g in range(2):
                        base = CmT[g][:, :]
                        dst = bass.AP(base.tensor, base.offset, [[16 * 128, 128], [136, 16], [1, 8]])
                        P.op("pool", (lambda e, dst=dst, src=xc[:, 10 + g, :].rearrange("p (s t) -> p s t", s=16):
                                      e.tensor_copy(out=dst, in_=src)), ins=[xc[:, 0, :]], outs=[base])
                        yield
                    for hh in range(2):
                        P.tt("pool", rsel[:, hh, :].rearrange("p (j s) -> p j s", j=8),
                             bc(af[:, hh:16:2], 2, 16), bc(blk16[:, :], 1, 8), ALU.mult)
                        yield
                        P.mm(PS[5][:, 256 + hh * 128:256 + (hh + 1) * 128], onesf[:, :], rsel[:, hh, :])
                        yield
                    P.act(decsel[0:64, :], PS[5][0:64, 256:384], AF.Exp)
                    yield
                    P.act(decsel[64:128, :], PS[5][64:128, 384:512], AF.Exp)
                    yield
                    for s in range(16):
                        h0_ = h0[s % 2]
                        hn_ = hn[s % 2]
                        P.dma("sp", h0_[:, :, :], st_ssd[s].rearrange("(j p) n -> p j n", p=128))
                        yield
                        pa, pb = PS[0], PS[1]
                        P.trs([((pa if j < 4 else pb)[:, (j % 4) * 128:(j % 4 + 1) * 128], h0_[:, j, :]) for j in range(8)],
                              identf[:, :])
                        yield
                        P.copy("act", h0Tb[:, 0:512], pa[:, :])
                        yield
                        P.copy("dve", h0Tb[:, 512:1024], pb[:, :])
                        yield
                        for g in range(2):
                            P.op("pe", (lambda e, o=PS[2 + g][:, :], l=CmT[g][:, s * 128:(s + 1) * 128],
                                        r=h0Tb[:, g * 512:(g + 1) * 512], s=s:
                                        e.matmul(o, lhsT=l, rhs=r, start=(s == 0), stop=(s == 15))),
                                 ins=[CmT[g][:, :], h0Tb[:, :]], outs=[PS[2 + g][:, :]])
                            yield
                        P.act(xwm[:, :], xw[:, :], AF.Identity, scale=blk16[:, s:s + 1])
                        yield
                        for half in range(2):
                            pn = PS[half]
                            for jj in range(4):
                                j = half * 4 + jj
                                P.mm(pn[:, jj * 128:(jj + 1) * 128], xwm[:, j * 128:(j + 1) * 128], Btm[:, j // 4, :])
                                yield
                            for jj in range(4):
                                j = half * 4 + jj
                                P.stt("dve", hn_[:, j, :], h0_[:, j, :], decsel[:, j * 16 + s:j * 16 + s + 1],
                                      pn[:, jj * 128:(jj + 1) * 128], ALU.mult, ALU.add)
                                yield
                        P.dma("sp", o_ssd_s[s].rearrange("(j p) n -> p j n", p=128), hn_[:, :, :])
                        yield
                P.sec("per-head intra-chunk")
                for q in range(4):
                    g = q // 2
                    L_, E_, M_ = L4[q % 2], E4[q % 2], M4[q % 2]
                    P.tt("dve", L_[:, :, :], bc(mk[:, MSTT, :], 1, 4), bc(af[:, 4 * q:4 * q + 4], 2, 128), ALU.mult)
                    yield
                    pg = PS[5 + (q % 2)]
                    for j in range(4):
                        P.mm(pg[:, j * 128:(j + 1) * 128], L_[:, j, :], mk[:, MINC, :])
                        yield
                    P.act(E_[:, :, :], pg[:, :].rearrange("p (j l) -> p j l", j=4), AF.Exp)
                    yield
                    P.tt("dve", M_[:, :, :], E_[:, :, :], bc(CBm[:, g, :], 1, 4), ALU.mult)
                    yield
                    py = PY[g]
                    for j in range(4):
                        h = 4 * q + j
                        hh = h % 8
                        P.mmk(py[:, hh * 64:(hh + 1) * 64],
                              [(identb[:, :], xDs[:, h * 64:(h + 1) * 64]), (M_[:, j, :], xdt[:, h * 64:(h + 1) * 64])])
                        yield
                P.sec("combine y = y_diag + eacs * y_off")
                if typ == 1:
                    for g in range(2):
                        P.mm(PYO[g][:, :], xc[:, 10 + g, :], hTb[:, g * 512:(g + 1) * 512])
                        yield
                for g in range(2):
                    P.copy("act", yo[:, g * 512:(g + 1) * 512], PYO[g][:, :])
                    yield
                P.tt("dve", yo[:, :].rearrange("p (h d) -> p h d", h=16), yo[:, :].rearrange("p (h d) -> p h d", h=16),
                     bc(ex[:, 0:16], 2, 64), ALU.mult)
                yield
                for g in range(2):
                    P.tt("dve", yy[:, g * 512:(g + 1) * 512], yo[:, g * 512:(g + 1) * 512], PY[g][:, :], ALU.add)
                    yield
                P.sec("gate with silu(z), group rmsnorm")
                P.tt("dve", yy[:, :], yy[:, :], zs[:, :], ALU.mult)
                yield
                for g in range(2):
                    P.sqsum(yo[:, g * 512:(g + 1) * 512], yy[:, g * 512:(g + 1) * 512], ss[:, g:g + 1])
                    yield
                P.act(ss[:, 2:4], ss[:, 0:2], AF.Sqrt, bias=cst[:, 0:1], scale=1.0 / 512)
                yield
                P.recip(ss[:, 4:6], ss[:, 2:4])
                yield
                y_ = ysb[ti % 2]
                for g in range(2):
                    P.stt("dve", y_[:, g * 512:(g + 1) * 512], yy[:, g * 512:(g + 1) * 512], ss[:, 4 + g:5 + g],
                          nw[:, g * 512:(g + 1) * 512], ALU.mult, ALU.mult)
                    yield
                P.dma("sp", yssd_scr[ti], y_[:, :])
                yield
                P.sec("state update (prompt chunks)")
                if typ == 1:
                    for g in range(2):
                        P.mm(PYO[g][:, :], Btm[:, g, :], xw[:, g * 512:(g + 1) * 512])
                        yield
                    h3 = hTf[:, :].rearrange("p (h d) -> p h d", h=16)
                    P.tt("dve", h3, h3, bc(ex[:, 32:48], 2, 64), ALU.mult)
                    yield
                    for g in range(2):
                        P.tt("dve", hTf[:, g * 512:(g + 1) * 512], hTf[:, g * 512:(g + 1) * 512], PYO[g][:, :], ALU.add)
                        yield
                    P.copy("act", hTb[:, :], hTf[:, :])
                    yield
                P.sec("conv-state outputs (last 3 raw xbc rows)")
                if ti == 0 or ti == NT - 1:
                    M3 = 48 if typ == 0 else 3
                    if typ == 0:
                        P.copy("pool", lhs3[:, :, :].rearrange("p k (s t) -> p k s t", s=16),
                               hT[:, :, :].rearrange("p k (s t) -> p k s t", s=16)[:, :, :, 5:8])
                        yield
                    else:
                        P.copy("pool", lhs3[:, :, 0:3], hT[:, :, 125:128])
                        yield
                    for nb in range(3):
                        ps = PS[5 + (nb % 2)]
                        P.mmk(ps[0:M3, :], [(lhs3[:, kc, 0:M3], wI.col(kc, 1024 + nb * 512, 512))
                                            for kc in range(8)])
                        yield
                        P.copy("act", cv[0:M3, nb * 512:(nb + 1) * 512], ps[0:M3, :])
                        yield
                    P.dma("sp", (o_ssdc_s if typ == 0 else o_ssdc_p)[:, :], cv[0:M3, :])
                    yield

            def rolling(genf, first, last, yspeed=1):
                nxt_ = first + 1
                cur = genf(first)
                young = None
                hold = [False]
                while cur is not None:
                    try:
                        v = next(cur)
                        if v == "HALF" and young is None and nxt_ < last:
                            young = genf(nxt_)
                            nxt_ += 1
                    except StopIteration:
                        cur, young = young, None
                        if hold[0] and cur is not None and nxt_ < last:
                            young = genf(nxt_)
                            nxt_ += 1
                        hold[0] = False
                        if cur is None and nxt_ < last:
                            cur = genf(nxt_)
                            nxt_ += 1
                        continue
                    for _k in range(yspeed):
                        if young is not None and not hold[0]:
                            try:
                                if next(young) == "HALF":
                                    hold[0] = True
                            except StopIteration:
                                young = None

            run_gen(ssd_gen(0))
            rolling(ssd_gen, 1, NT, yspeed=2)
            for half in range(2):
                ps = PS[half]
                P.trs([(ps[:, jj * 128:(jj + 1) * 128], hTf[:, (half * 4 + jj) * 128:(half * 4 + jj + 1) * 128])
                       for jj in range(4)], identf[:, :])
                P.copy("act", hn[half][:, 0:4, :], ps[:, :].rearrange("p (j n) -> p j n", j=4))
                P.dma("sp", o_ssd_p[half * 512:(half + 1) * 512, :].rearrange("(j p) n -> p j n", p=128), hn[half][:, 0:4, :])
            P.barrier()
            P.emit()
        ps1.close()
        if STOP_AFTER >= 2:
          with contextlib.ExitStack() as st2:
            PA = [st2.enter_context(nc.psum_tensor("pa%d" % i, [128, 512], F32)) for i in range(2)]
            PM = st2.enter_context(nc.psum_tensor("pm", [128, 512], F32))
            LPB = [st2.enter_context(nc.psum_tensor("lpb%d" % i, [128, 512], F32)) for i in range(4)]
            LPS = [[LPB[ln][:, i * 128:(i + 1) * 128] for i in range(4)] for ln in range(4)]
            wII = WBlocks(st2, "wII", 8, [0, 512, 1024, 1536, 2048, 2560, 3072, 3584, 4096, 4112])
            wII.load(w_in_v, 2576)
            wO = WBlocks(st2, "wO", 16, [0, 512, 1024])
            w_out_v = w_out.rearrange("(kc p) n -> p kc n", p=128)
            wO.load(w_out_v, 0)
            modv = sb(st2, "modv2", [128, 3, D])
            cwg = sb(st2, "cwg", [128, 24, 4])
            gnw = sb(st2, "gnw", [128, 128])
            histP = sb(st2, "histPg", [128, 24, 3])
            P.dma("sp", cwg[:, :, :], gdn_cw[:, :, :])
            P.dma("sp", gnw[:, :], gdn_nw[:, :])
            P.memset("pool", histP[:, :, :], 0.0)
            tmpA = sb(st2, "tmpA2", [128, D])
            hT = sb(st2, "hT2", [128, 8, 128], BF16)
            stt_ = sb(st2, "stt2", [128, 4])
            xinP_l = [sb(st2, "xinP2_%d" % i, [128, 4, 131]) for i in range(2)]
            acc = [sb(st2, "acc2_%d" % i, [128, 128]) for i in range(4)]
            lhs3 = sb(st2, "lhs3g", [128, 8, 48], BF16)
            tmpT = sb(st2, "tmpT2", [128, D])
            sttT = sb(st2, "sttT2", [128, 4])
            ss8 = sb(st2, "ss8", [128, 24])
            otm = sb(st2, "otm", [128, D])
            mixed = sb(st2, "mixed", [128, 2 * D], BF16)
            mixedT = sb(st2, "mixedT", [128, 16, 128], BF16)

            class FB:
                pass

            def make_fb(i, stk):
                fb = FB()
                fb.xt = sb(stk, "f%d_xt" % i, [128, D])
                fb.hb = sb(stk, "f%d_hb" % i, [128, D], BF16)
                fb.qk = sb(stk, "f%d_qk" % i, [128, 16, 128], BF16)
                fb.vfm = sb(stk, "f%d_vfm" % i, [128, 8, 128], BF16)
                fb.ktm = sb(stk, "f%d_ktm" % i, [128, 8, 128], BF16)
                fb.vtm = sb(stk, "f%d_vtm" % i, [128, 8, 128], BF16)
                fb.gs = sb(stk, "f%d_gs" % i, [128, D], BF16)
                fb.sm = sb(stk, "f%d_sm" % i, [128, 64])
                fb.gf = sb(stk, "f%d_gf" % i, [128, 8])
                fb.ex = sb(stk, "f%d_ex" % i, [128, 24])
                return fb

            class Lane:
                pass

            lanes = []

            def make_lane(ln, stk):
                B = Lane()
                B.ps = LPS[ln]
                f = lambda nm, dt=F32: sb(stk, "ln%d_%s" % (ln, nm), [128, 128], dt)
                B.P = [f("P0"), f("P1")]
                B.PT = [f("PT0"), f("PT1")]
                B.X = [f("X0"), f("X1")]
                B.ot = f("ot")
                B.L, B.Dm, B.DmI, B.DmS = B.ot, B.X[1], B.PT[1], B.P[1]
                B.attnT, B.TTb, B.R, B.vn, B.kout = (f("attnT", BF16), f("TTb", BF16), f("R", BF16),
                                                     f("vn", BF16), f("kout", BF16))
                lanes.append(B)

            make_lane(0, st2)
            fbs = [make_fb(0, st2)]

            def front_gen(ti, typ, SB, fb):
                xt, hb, qk, vfm, sm, gf, ex = fb.xt, fb.hb, fb.qk, fb.vfm, fb.sm, fb.gf, fb.ex
                P.sec("F:load+norm")
                if ti <= 1:
                    P.dma("sp", modv[:, :, :], modscr[typ].rearrange("p (s d) -> p s d", s=6)[:, 0:3, :])
                    yield
                P.dma("sp", xt[:, :], xs_all[ti])
                yield
                yield from norm_to_T_g(xt, modv, 1, 0, tmpA, hb, hT, stt_)
                yield
                yield from proj_conv_gen(typ, 6, wII, 0, hT, PA, (SB.xinS_l if typ == 0 else None), xinP_l,
                                         (SB.histS if typ == 0 else None), histP, cwg, False, acc,
                                         lambda c: (qk[:, c, :] if c < 16 else vfm[:, c - 16, :]))
                if ti == 0 or ti == NT - 1:
                    P.sec("F:convstate")
                    M3 = 48 if typ == 0 else 3
                    if typ == 0:
                        P.copy("pool", lhs3[:, :, :].rearrange("p k (s t) -> p k s t", s=16),
                               hT[:, :, :].rearrange("p k (s t) -> p k s t", s=16)[:, :, :, 5:8])
                        yield
                    else:
                        P.copy("pool", lhs3[:, :, 0:3], hT[:, :, 125:128])
                        yield
                    for cc in range(3):
                        for nb in range(2):
                            c0 = cc * 1024 + nb * 512
                            P.mmk(PA[nb][0:M3, :], [(lhs3[:, kc, 0:M3], wII.col(kc, c0, 512)) for kc in range(8)])
                            yield
                            P.copy("act", tmpA[0:M3, nb * 512:(nb + 1) * 512], PA[nb][0:M3, :])
                            yield
                        P.dma("sp", (o_gdnc_s if typ == 0 else o_gdnc_p)[:, cc * 1024:(cc + 1) * 1024], tmpA[0:M3, :])
                        yield
                    yield
                P.sec("F:gate")
                for nb in range(2):
                    P.mmk(PA[nb][:, :], [(hT[:, kc, :], wII.col(kc, 3072 + nb * 512, 512))
                                         for kc in range(8)])
                    yield
                    P.act(fb.gs[:, nb * 512:(nb + 1) * 512], PA[nb][:, :], AF.Silu)
                    yield
                yield
                P.sec("F:beta/g")
                P.mmk(PM[:, 0:16], [(hT[:, kc, :], wII.col(kc, 4096, 16)) for kc in range(8)])
                yield
                P.act(sm[:, 0:8], PM[:, 0:8], AF.Exp, scale=-1.0)
                yield
                P.ts("dve", sm[:, 0:8], sm[:, 0:8], 1.0, None, ALU.add)
                yield
                P.recip(sm[:, 8:16], sm[:, 0:8])
                yield
                P.ts("dve", sm[:, 16:24], sm[:, 8:16], -1.0, None, ALU.mult)
                yield
                P.tt("dve", sm[:, 24:32], PM[:, 8:16], smv[:, 48:56], ALU.add)
                yield
                P.act(sm[:, 32:40], sm[:, 24:32], AF.Exp)
                yield
                P.act(sm[:, 40:48], sm[:, 32:40], AF.Ln, bias=cst[:, 1:2], scale=1.0)
                yield
                P.tt("dve", gf[:, :], sm[:, 40:48], negA[:, 16:24], ALU.mult)
                yield
                yield
                P.sec("F:beta/g")
                yield from small_scan_terms_g(gf[:, :], 8, typ, PM[:, 64:88], ex)
                P.ts("dve", sm[:, 48:56], ex[:, 0:8], -1.0, None, ALU.mult)
                yield
                yield
                for half in range(2):
                    P.sec("F:l2norm")
                    src = qk[:, half * 8:(half + 1) * 8, :]
                    P.tt("pool", hb[:, :].rearrange("p (c t) -> p c t", c=8), src, src, ALU.mult)
                    yield
                    for i in range(2):
                        rq = tmpA[:, i * 512:(i + 1) * 512]
                        P.mm(PA[i][:, :], onesb[:, :], hb[:, i * 512:(i + 1) * 512])
                        yield
                        if half == 0:
                            P.act(rq, PA[i][:, :], AF.Sqrt, bias=cst[:, 2:3], scale=128.0)
                            yield
                        else:
                            P.act(rq, PA[i][:, :], AF.Sqrt, bias=cst[:, 0:1], scale=1.0)
                            yield
                        P.recip(rq, rq)
                        yield
                        dst = qk[:, half * 8 + 4 * i:half * 8 + 4 * i + 4, :]
                        P.tt("dve", dst, dst, rq.rearrange("p (c t) -> p c t", c=4), ALU.mult)
                        yield
                    yield
                P.sec("F:transposes")
                P.copy("pool", hb[:, :].rearrange("p (c t) -> p c t", c=8), qk[:, 8:16, :])
                yield
                P.trs([(PT[:, j * 128:(j + 1) * 128], qk[:, 8 + j, :]) for j in range(8)], identb[:, :])
                yield
                P.copy("act", fb.ktm[:, :, :], PT[:, :].rearrange("p (h d) -> p h d", h=8))
                yield
                yield
                P.sec("F:transposes")
                P.trs([(PT[:, j * 128:(j + 1) * 128], vfm[:, j, :]) for j in range(8)], identb[:, :])
                yield
                P.copy("act", fb.vtm[:, :, :], PT[:, :].rearrange("p (h d) -> p h d", h=8))
                yield
                if typ == 0:
                    P.tt("pool", SB.rselg[:, :].rearrange("p (h s) -> p h s", h=8),
                         bc(gf[:, :], 2, 16), bc(blk16[:, :], 1, 8), ALU.mult)
                    yield
                    P.mm(PM[:, 128:256], onesf[:, :], SB.rselg[:, :])
                    yield
                    P.act(SB.gendsel[:, :], PM[:, 128:256], AF.Exp)
                    yield
                yield

            def head_gen(h, B, typ, SB, fb):
                mk = masks[typ]
                m_lev = 3 if typ == 0 else 7
                qk, hb, sm, gf, ex, ktm, vtm = fb.qk, fb.hb, fb.sm, fb.gf, fb.ex, fb.ktm, fb.vtm
                kT = qk[:, 8 + h, :]
                qT = qk[:, h, :]
                _st = ["H:decay"]
                P.sec(_st[0])
                P.act(B.L[:, :], mk[:, MSTT, :], AF.Identity, scale=gf[:, h:h + 1])
                yield
                P.mm(B.ps[3][:, :], B.L[:, :], mk[:, MINC, :])
                yield
                P.act(B.Dm[:, :], B.ps[3][:, :], AF.Exp)
                yield
                P.tt("dve", B.DmI[:, :], B.Dm[:, :], mk[:, MINC, :], ALU.mult)
                yield
                P.tt("dve", B.DmS[:, :], B.Dm[:, :], mk[:, MSTR, :], ALU.mult)
                yield
                P.mm(B.ps[0][:, :], kT, hb[:, h * 128:(h + 1) * 128])
                yield
                P.mm(B.ps[1][:, :], kT, qT)
                yield
                P.stt("dve", B.P[0][:, :], B.ps[0][:, :], sm[:, 16 + h:17 + h], B.DmS[:, :], ALU.mult, ALU.mult)
                yield
                P.tt("dve", B.attnT[:, :], B.ps[1][:, :], B.DmI[:, :], ALU.mult)
                yield
                yield
                _st[0] = "H:dbl"
                P.sec(_st[0])
                P.tr(B.ps[2][:, :], B.P[0][:, :], identf[:, :])
                yield
                P.copy("act", B.PT[0][:, :], B.ps[2][:, :])
                yield
                P.tt("dve", B.X[0][:, :], B.P[0][:, :], identf[:, :], ALU.add)
                yield
                yield
                P.sec(_st[0])
                if m_lev > 1:
                    P.mm(B.ps[0][:, :], B.P[0][:, :], B.PT[0][:, :])
                    yield
                    if m_lev > 2:
                        P.mm(B.ps[1][:, :], B.PT[0][:, :], B.P[0][:, :])
                        yield
                    P.copy("act", B.PT[1][:, :], B.ps[0][:, :])
                    yield
                    if m_lev > 2:
                        P.copy("act", B.P[1][:, :], B.ps[1][:, :])
                        yield
                    yield
                    P.sec(_st[0])
                xi = 0
                for j in range(1, m_lev):
                    cur, nx = j % 2, (j + 1) % 2
                    if j + 1 < m_lev:
                        P.mm(B.ps[0][:, :], B.P[cur][:, :], B.PT[cur][:, :])
                        yield
                        if j + 2 < m_lev:
                            P.mm(B.ps[1][:, :], B.PT[cur][:, :], B.P[cur][:, :])
                            yield
                    P.mm(B.ps[2][:, :], B.PT[cur][:, :], B.X[xi][:, :])
                    yield
                    if j + 1 < m_lev:
                        P.copy("act", B.PT[nx][:, :], B.ps[0][:, :])
                        yield
                        if j + 2 < m_lev:
                            P.copy("act", B.P[nx][:, :], B.ps[1][:, :])
                            yield
                    P.tt("dve", B.X[1 - xi][:, :], B.X[xi][:, :], B.ps[2][:, :], ALU.add)
                    yield
                    xi = 1 - xi
                    yield
                    P.sec(_st[0])
                _st[0] = "H:state"
                P.sec(_st[0])
                P.copy("act", B.TTb[:, :], B.X[xi][:, :])
                yield
                if typ == 1:
                    Sf_h, Sb_h = SB.Sf[h], SB.Sb[h]
                    P.mm(B.ps[3][:, :], kT, Sb_h[:, :])
                    yield
                else:
                    P.dma("sp", SB.S0f[:, :, :], st_gdn[:, h, :, :].rearrange("s d e -> d s e"))
                    yield
                    P.copy("act", SB.S0b[:, :, :], SB.S0f[:, :, :])
                    yield
                    for (dstT, srcT) in ((SB.kTm, kT), (SB.qTm, qT)):
                        base = dstT[:, :]
                        dap = bass.AP(base.tensor, base.offset, [[16 * 128, 128], [136, 16], [1, 8]])
                        P.op("pool", (lambda e, dap=dap, src=srcT.rearrange("p (s t) -> p s t", s=16):
                                      e.tensor_copy(out=dap, in_=src)), ins=[srcT], outs=[base])
                        yield
                    P.mmk(B.ps[3][:, :], [(SB.kTm[:, s * 128:(s + 1) * 128], SB.S0b[:, s, :]) for s in range(16)])
                    yield
                P.stt("dve", B.R[:, :], B.ps[3][:, :], sm[:, 48 + h:49 + h], vtm[:, h, :], ALU.mult, ALU.add)
                yield
                P.mm(B.ps[0][:, :], B.TTb[:, :], B.R[:, :])
                yield
                P.act(B.vn[:, :], B.ps[0][:, :], AF.Identity, scale=sm[:, 8 + h:9 + h])
                yield
                yield
                P.sec(_st[0])
                if typ == 1:
                    P.mm(B.ps[1][:, :], qT, Sb_h[:, :])
                    yield
                else:
                    P.mmk(B.ps[1][:, :], [(SB.qTm[:, s * 128:(s + 1) * 128], SB.S0b[:, s, :]) for s in range(16)])
                    yield
                P.mm(B.ps[2][:, :], B.attnT[:, :], B.vn[:, :])
                yield
                P.act(B.ot[:, :], B.ps[1][:, :], AF.Identity, scale=ex[:, h:h + 1])
                yield
                P.tt("dve", otm[:, h * 128:(h + 1) * 128], B.ot[:, :], B.ps[2][:, :], ALU.add)
                yield
                P.act(B.kout[:, :], ktm[:, h, :], AF.Identity, scale=ex[:, 8 + h:9 + h])
                yield
                yield
                P.sec(_st[0])
                if typ == 1:
                    P.mm(B.ps[3][:, :], B.kout[:, :], B.vn[:, :])
                    yield
                    P.stt("dve", Sf_h[:, :], Sf_h[:, :], ex[:, 16 + h:17 + h], B.ps[3][:, :], ALU.mult, ALU.add)
                    yield
                    P.copy("act", Sb_h[:, :], Sf_h[:, :])
                    yield
                else:
                    P.tt("dve", SB.koutm_all[:, :, :], bc(B.kout[:, :], 1, 16), bc(blk16[:, :], 2, 128), ALU.mult)
                    yield
                    for s in range(16):
                        P.mm(LPB[s // 4][:, (s % 4) * 128:(s % 4 + 1) * 128], SB.koutm_all[:, s, :], B.vn[:, :])
                    yield
                    for j4 in range(4):
                        S4 = SB.S0f[:, 4 * j4:4 * j4 + 4, :]
                        gsel = SB.gendsel[:, h * 16 + 4 * j4:h * 16 + 4 * j4 + 4]
                        P.tt("dve", S4, S4, bc(gsel, 2, 128), ALU.mult)
                        yield
                        P.tt("dve", S4, S4, LPB[j4][:, :].rearrange("p (s e) -> p s e", s=4), ALU.add)
                        yield
                    P.dma("sp", o_gdn_s[:, h, :, :].rearrange("s d e -> d s e"), SB.S0f[:, :, :])
                    yield
                yield

            def tail_gen(ti, fb):
                xt = fb.xt
                P.sec("T:onorm")
                P.dma("sp", mixed[:, 0:D], yssd_scr[ti])
                yield
                P.tt("pool", tmpT[:, :], otm[:, :], otm[:, :], ALU.mult)
                yield
                P.red("dve", ss8[:, 0:8], tmpT[:, :].rearrange("p (h d) -> p h d", h=8))
                yield
                P.act(ss8[:, 8:16], ss8[:, 0:8], AF.Sqrt, bias=cst[:, 0:1], scale=1.0 / 128)
                yield
                P.recip(ss8[:, 16:24], ss8[:, 8:16])
                yield
                o3 = otm[:, :].rearrange("p (h d) -> p h d", h=8)
                P.tt("dve", o3, o3, bc(ss8[:, 16:24], 2, 128), ALU.mult)
                yield
                P.tt("dve", o3, o3, bc(gnw[:, :], 1, 8), ALU.mult)
                yield
                P.tt("dve", mixed[:, D:2 * D], otm[:, :], fb.gs[:, :], ALU.mult)
                yield
                yield
                P.sec("T:outproj")
                for half in range(2):
                    P.trs([(PT[:, j * 128:(j + 1) * 128], mixed[:, (half * 8 + j) * 128:(half * 8 + j + 1) * 128])
                           for j in range(8)], identb[:, :])
                    yield
                    P.copy("act", mixedT[:, half * 8:(half + 1) * 8, :], PT[:, :].rearrange("p (c t) -> p c t", c=8))
                    yield
                yield
                P.sec("T:outproj")
                for nb in range(2):
                    P.mmk(PA[nb][:, :], [(mixedT[:, kc, :], wO.col(kc, nb * 512, 512)) for kc in range(16)])
                    yield
                    P.copy("act", tmpT[:, nb * 512:(nb + 1) * 512], PA[nb][:, :])
                    yield
                yield
                P.sec("T:resid")
                P.sqsum(otm[:, :], tmpT[:, :], sttT[:, 0:1])
                yield
                P.act(sttT[:, 1:2], sttT[:, 0:1], AF.Sqrt, bias=cst[:, 0:1], scale=1.0 / D)
                yield
                P.recip(sttT[:, 2:3], sttT[:, 1:2])
                yield
                P.stt("dve", tmpT[:, :], tmpT[:, :], sttT[:, 2:3], modv[:, 2, :], ALU.mult, ALU.mult)
                yield
                P.tt("dve", xt[:, :], tmpT[:, :], xt[:, :], ALU.add)
                yield
                P.dma("sp", x1_scr[ti], xt[:, :])
                yield
                yield

            def speed(g, k):
                while True:
                    for _ in range(k):
                        try:
                            next(g)
                        except StopIteration:
                            return
                    yield

            def run_rr(gens):
                alive = list(gens)
                while alive:
                    nxt = []
                    for gn in alive:
                        try:
                            next(gn)
                            nxt.append(gn)
                        except StopIteration:
                            pass
                    alive = nxt

            def back(ti, typ, SB, fb, nl, side, with_tail=True):
                side = [side] if side is not None else []
                for h0_ in range(0, 8, nl):
                    gens = [head_gen(h0_ + i, lanes[i], typ, SB, fb) for i in range(min(nl, 8 - h0_))]
                    alive = gens + side
                    while any(g in alive for g in gens):
                        nxt = []
                        for gn in alive:
                            try:
                                next(gn)
                                nxt.append(gn)
                            except StopIteration:
                                if gn in side:
                                    side = []
                        alive = nxt
                if with_tail:
                    run_rr([tail_gen(ti, fb)] + side)
                else:
                    run_rr(side)

            class SBufs:
                pass

            with contextlib.ExitStack() as st2s:
                SB = SBufs()
                SB.histS = sb(st2s, "histSg", [128, 24, 16, 3])
                SB.xinS_l = [sb(st2s, "xinS2_%d" % i, [128, 4, 16, 11]) for i in range(2)]
                SB.S0f = sb(st2s, "S0f", [128, 16, 128])
                SB.S0b = sb(st2s, "S0b", [128, 16, 128], BF16)
                SB.kTm = sb(st2s, "kTm", [128, 16 * 128], BF16)
                SB.qTm = sb(st2s, "qTm", [128, 16 * 128], BF16)
                SB.koutm_all = sb(st2s, "koutm_all", [128, 16, 128], BF16)
                SB.rselg = sb(st2s, "rselg", [128, 128])
                SB.gendsel = sb(st2s, "gendsel", [128, 128])
                P.dma("sp", SB.histS[:, :, :, :], hist_gdn[:, :, :, :])
                P.memset("pool", SB.kTm[:, :], 0.0)
                P.memset("pool", SB.qTm[:, :], 0.0)
                run_rr([front_gen(0, 0, SB, fbs[0])])
                back(0, 0, SB, fbs[0], 1, None)
                P.barrier()
                P.emit()
            with contextlib.ExitStack() as st2p:
                SB = SBufs()
                SB.Sf = [sb(st2p, "Sf%d" % h, [128, 128]) for h in range(8)]
                SB.Sb = [sb(st2p, "Sb%d" % h, [128, 128], BF16) for h in range(8)]
                for ln in range(1, P2_NL):
                    make_lane(ln, st2p)
                fbs.append(make_fb(1, st2p))
                for h in range(8):
                    P.memset("pool", SB.Sf[h][:, :], 0.0)
                    P.memset("pool", SB.Sb[h][:, :], 0.0)
                run_rr([front_gen(1, 1, SB, fbs[1])])
                pending_tail = None
                for ti in range(1, NT):
                    parts = []
                    if pending_tail is not None:
                        parts.append(pending_tail)
                    if ti + 1 < NT:
                        parts.append(speed(front_gen(ti + 1, 1, SB, fbs[(ti + 1) % 2]), 3))
                    side = itertools.chain(*parts) if parts else None
                    back(ti, 1, SB, fbs[ti % 2], P2_NL, side, with_tail=False)
                    pending_tail = tail_gen(ti, fbs[ti % 2])
                run_rr([pending_tail])
                for h in range(8):
                    P.dma("sp", o_gdn_p[h * 128:(h + 1) * 128, :], SB.Sf[h][:, :])
                P.barrier()
                P.emit()
        if STOP_AFTER >= 3:
          with contextlib.ExitStack() as st3:
            PB = [st3.enter_context(nc.psum_tensor("pb%d" % i, [128, 512], F32)) for i in range(7)]
            gb = [0, 512, 1024, 1536, 2048, 2560, 2816]
            wG = WBlocks(st3, "wG", 8, gb + [DFF + b for b in gb[1:]])
            w_gu_v = w_gu.rearrange("(kc p) n -> p kc n", p=128)
            wG.load(w_gu_v, 0, order=[j for i in range(6) for j in (i, 6 + i)])
            wD = WBlocks(st3, "wD", 22, [0, 512, 1024])
            w_dn_v = w_dn.rearrange("(kc p) n -> p kc n", p=128)
            wD.load(w_dn_v, 0)
            modv = sb(st3, "modv3", [128, 3, D])
            xt = [sb(st3, "xt3_%d" % i, [128, D]) for i in range(2)]
            tmpA = [sb(st3, "tmpA3_%d" % i, [128, D]) for i in range(2)]
            tmpB = sb(st3, "tmpB3", [128, D])
            hb = [sb(st3, "hb3_%d" % i, [128, D], BF16) for i in range(2)]
            hT = [sb(st3, "hT3_%d" % i, [128, 8, 128], BF16) for i in range(2)]
            stt_ = [sb(st3, "stt3_%d" % i, [128, 4]) for i in range(2)]
            sg = [sb(st3, "sg%d" % i, [128, 512]) for i in range(2)]
            hid = sb(st3, "hid", [128, DFF], BF16)
            hidT = sb(st3, "hidT", [128, 22, 128], BF16)

            def ffn_gen(ti):
                typ = 0 if ti == 0 else 1
                pr = ti % 2
                x_, tA, hb_, hT_, st_ = xt[pr], tmpA[pr], hb[pr], hT[pr], stt_[pr]
                if ti <= 1:
                    P.dma("sp", modv[:, :, :], modscr[typ].rearrange("p (s d) -> p s d", s=6)[:, 3:6, :])
                P.dma("sp", x_[:, :], x1_scr[ti])
                yield
                yield from norm_to_T_g(x_, modv, 1, 0, tA, hb_, hT_, st_)
                for i in range(6):
                    w = 512 if i < 5 else 256
                    pg_, pu_ = PB[(2 * i) % 6], PB[(2 * i + 1) % 6]
                    P.mmk(pg_[:, 0:w], [(hT_[:, kc, :], wG.col(kc, i * 512, w)) for kc in range(8)])
                    yield
                    P.mmk(pu_[:, 0:w], [(hT_[:, kc, :], wG.col(kc, DFF + i * 512, w)) for kc in range(8)])
                    yield
                    s_ = sg[i % 2]
                    P.act(s_[:, 0:w], pg_[:, 0:w], AF.Silu)
                    yield
                    P.tt("dve", hid[:, i * 512:i * 512 + w], s_[:, 0:w], pu_[:, 0:w], ALU.mult)
                    yield
                yield "HALF"
                for grp in range(3):
                    n = 8 if grp < 2 else 6
                    P.trs([(PT[:, j * 128:(j + 1) * 128], hid[:, (grp * 8 + j) * 128:(grp * 8 + j + 1) * 128])
                           for j in range(n)], identb[:, :])
                    yield
                    P.copy("act", hidT[:, grp * 8:grp * 8 + n, :],
                           PT[:, 0:n * 128].rearrange("p (c t) -> p c t", c=n))
                    yield
                for nb in range(2):
                    P.mmk(PB[6][:, :], [(hidT[:, kc, :], wD.col(kc, nb * 512, 512)) for kc in range(22)])
                    yield
                    P.copy("act", tA[:, nb * 512:(nb + 1) * 512], PB[6][:, :])
                    yield
                P.sqsum(tmpB[:, :], tA[:, :], st_[:, 0:1])
                yield
                P.act(st_[:, 1:2], st_[:, 0:1], AF.Sqrt, bias=cst[:, 0:1], scale=1.0 / D)
                yield
                P.recip(st_[:, 2:3], st_[:, 1:2])
                yield
                P.stt("dve", tA[:, :], tA[:, :], st_[:, 2:3], modv[:, 2, :], ALU.mult, ALU.mult)
                yield
                P.tt("dve", x_[:, :], tA[:, :], x_[:, :], ALU.add)
                yield
                P.dma("sp", y_all[ti], x_[:, :])
                yield

            run_gen(ffn_gen(0))
            nxt = 2
            cur = ffn_gen(1)
            young = None
            while cur is not None:
                try:
                    v = next(cur)
                    if v == "HALF" and young is None and nxt < NT:
                        young = ffn_gen(nxt)
                        nxt += 1
                except StopIteration:
                    cur, young = young, None
                    if cur is None and nxt < NT:
                        cur = ffn_gen(nxt)
                        nxt += 1
                    continue
                if young is not None:
                    try:
                        v2 = next(young)
                        if v2 == "HALF":
                            pass
                    except StopIteration:
                        young = None
            P.barrier()
            P.emit()
    return nc


def _prep_inputs(inp):
    f = lambda a: np.ascontiguousarray(np.asarray(a, dtype=np.float32))
    xp, xs = f(inp["x_prompt"]), f(inp["x_sample"])
    cp, cs = f(inp["c_prompt"]), f(inp["c_sample"])
    bcast = lambda v: np.ascontiguousarray(np.broadcast_to(np.asarray(v, np.float32).reshape(1, -1), (128, v.size)))
    shared = {}
    shared["w_ada"] = f(inp["w_ada"][0])
    shared["b_ada_b"] = bcast(inp["b_ada"][0])
    shared["normvecs"] = np.ascontiguousarray(np.stack(
        [bcast(inp[k][0]) for k in ("norm_mix_pre", "norm_mix_post", "norm_ffn_pre", "norm_ffn_post")], axis=1))
    shared["w_in"] = f(inp["w_in"][0])
    shared["w_out"] = f(inp["w_out"][0])
    shared["w_gu"] = f(inp["w_gate_up"][0])
    shared["w_dn"] = f(inp["w_down"][0])
    scw = np.concatenate([f(inp["ssd_conv_w"][0]), f(inp["ssd_conv_b"][0])[None, :]], axis=0)
    shared["ssd_cw"] = np.ascontiguousarray(scw.reshape(5, 12, 128).transpose(2, 1, 0))
    shared["gdn_cw"] = np.ascontiguousarray(f(inp["gdn_conv_w"][0]).reshape(4, 24, 128).transpose(2, 1, 0))
    sv = np.concatenate([f(inp["ssd_dt_bias"][0]), f(inp["ssd_A_log"][0]), f(inp["ssd_D"][0]),
                         f(inp["gdn_dt_bias"][0]), f(inp["gdn_A_log"][0])])
    shared["smallv"] = bcast(sv)
    shared["ssd_nw"] = bcast(inp["ssd_norm_w"][0])
    shared["gdn_nw"] = bcast(inp["gdn_norm_w"][0])
    shared["ident"] = np.eye(128, dtype=np.float32)
    idx = np.arange(128)
    m = np.zeros((2, 128, 4, 128), np.float32)
    for typ, bs in ((0, 8), (1, 128)):
        same = (idx[:, None] // bs) == (idx[None, :] // bs)
        m[typ, :, 0, :] = same & (idx[:, None] > idx[None, :])
        m[typ, :, 1, :] = same & (idx[:, None] <= idx[None, :])
        m[typ, :, 2, :] = same & (idx[:, None] < idx[None, :])
        m[typ, :, 3, :] = same
    shared["masks"] = m
    shared["blk16"] = np.ascontiguousarray(((idx[:, None] // 8) == np.arange(16)[None, :]).astype(np.float32))
    maps = []
    for i in range(NCORES):
        d = dict(shared)
        sl = slice(16 * i, 16 * (i + 1))
        d["xs_all"] = np.ascontiguousarray(np.concatenate(
            [xs[sl].reshape(1, 128, D), xp[i].reshape(16, 128, D)], axis=0))
        d["cexp"] = np.ascontiguousarray(np.stack(
            [np.repeat(cs[sl], 8, axis=0), np.broadcast_to(cp[i][None, :], (128, D))], axis=0))
        d["st_ssd"] = np.ascontiguousarray(f(inp["state_ssd"][0, sl]).reshape(16, 1024, 128))
        hs = f(inp["state_ssd_conv"][0, sl])
        d["hist_ssd"] = np.ascontiguousarray(hs.reshape(16, 3, 12, 128).transpose(3, 2, 0, 1))
        d["st_gdn"] = np.ascontiguousarray(f(inp["state_gdn"][0, sl]))
        hg = f(inp["state_gdn_conv"][0, sl])
        d["hist_gdn"] = np.ascontiguousarray(hg.reshape(16, 3, 24, 128).transpose(3, 2, 0, 1))
        maps.append(d)
    return maps


def kernel(**inp):
    maps = _prep_inputs(inp)
    nc = build_program()
    res = run_bass_kernel_spmd(nc, maps, core_ids=list(range(NCORES)))
    R = res.results
    cat = lambda k: np.stack([np.asarray(r[k]) for r in R], axis=0)
    y_all = cat("y_all")
    y_prompt = y_all[:, 1:].reshape(8, 2048, D)
    y_sample = y_all[:, 0].reshape(128, 8, D)
    ssd_p = cat("o_ssd_p").reshape(1, 8, 16, 64, 128)
    ssdc_p = cat("o_ssdc_p").reshape(1, 8, 3, 1536)
    gdn_p = cat("o_gdn_p").reshape(1, 8, 8, 128, 128)
    gdnc_p = cat("o_gdnc_p").reshape(1, 8, 3, 3072)
    ssd_s = cat("o_ssd_s").reshape(1, 128, 16, 64, 128)
    ssdc_s = cat("o_ssdc_s").reshape(1, 128, 3, 1536)
    gdn_s = cat("o_gdn_s").reshape(1, 128, 8, 128, 128)
    gdnc_s = cat("o_gdnc_s").reshape(1, 128, 3, 3072)
    outs = (y_prompt, y_sample, ssd_p, ssdc_p, gdn_p, gdnc_p, ssd_s, ssdc_s, gdn_s, gdnc_s)
    return tuple(np.ascontiguousarray(o, dtype=np.float32) for o in outs)
```

```python
import contextlib
import itertools
import os
import numpy as np
import concourse.bass as bass
import concourse.mybir as mybir
from concourse.bass_utils import run_bass_kernel_spmd

F32 = mybir.dt.float32
BF16 = mybir.dt.bfloat16
AF = mybir.ActivationFunctionType
ALU = mybir.AluOpType
AX = mybir.AxisListType

NCORES = 8
D = 1024
NT = 17
DFF = 2816
P2_MODE = 0
ANNOTATE = bool(int(os.environ.get('K_ANN', '0')))
SKIP1 = False
P2_STAGE = 99
P2_HSTEP = 99
P2_SKIP = ()
P2_NL = 4
P2_TILES = 16
STOP_AFTER = int(os.environ.get('K_STOP', '99'))


class Prog:
    ENGS = ("pe", "act", "dve", "pool", "sp")

    def __init__(self, nc, stack):
        self.nc = nc
        self.sems = {}
        for e in self.ENGS:
            self.sems[e] = stack.enter_context(nc.semaphore("s_" + e))
        self.dsems = {"sp": [], "pool": [], "act": []}
        for q, n in (("sp", 8), ("pool", 8), ("act", 2)):
            for j in range(n):
                nm = "d_%s%d" % (q, j)
                self.sems[nm] = stack.enter_context(nc.semaphore(nm))
                self.dsems[q].append(nm)
        self.cnt = {k: 0 for k in self.sems}
        self.known = {e: {} for e in self.ENGS}
        self.lastw = {}
        self.readers = {}
        self.ops = {e: [] for e in self.ENGS}
        self.rr = {"sp": 0, "pool": 0, "act": 0}
        self.nops = 0
        self.tag = "init"

    def sec(self, name):
        self.tag = name

    fine = {}

    def _keys(self, aps):
        ks = []
        for a in aps:
            if a is None or isinstance(a, (int, float)):
                continue
            if isinstance(a, str):
                ks.append(a)
                continue
            nm = a.name
            if nm in self.fine:
                row, gr = self.fine[nm]
                nm = "%s:%d" % (nm, (int(a.offset) % row) // gr)
            ks.append(nm)
        return ks

    def _deps(self, eng, r, w):
        need = {}

        def add(c):
            if c is None:
                return
            s, v = c
            if s == "pe" and eng == "pe":
                return
            if need.get(s, 0) < v:
                need[s] = v

        for k in r:
            add(self.lastw.get(k))
        for k in w:
            add(self.lastw.get(k))
            for c in self.readers.get(k, {}).items():
                add(c)
        waits = []
        kn = self.known[eng]
        for s, v in need.items():
            if kn.get(s, 0) < v:
                kn[s] = v
                waits.append((s, v))
        return waits

    def _commit(self, c, r, w):
        for k in w:
            self.lastw[k] = c
            self.readers[k] = {}
        for k in r:
            d = self.readers.setdefault(k, {})
            if d.get(c[0], 0) < c[1]:
                d[c[0]] = c[1]

    def op(self, eng, fn, ins=(), outs=()):
        r = self._keys(ins)
        w = self._keys(outs)
        w = w + [k for k in r if k.startswith("ps") or k.startswith("pa") or k.startswith("pm")
                 or k.startswith("lpb") or k.startswith("pb")]
        waits = self._deps(eng, r, w)
        self.cnt[eng] += 1
        self.ops[eng].append((waits, fn, eng, 1, self.tag))
        self._commit((eng, self.cnt[eng]), r, w)
        self.nops += 1

    def dma(self, q, out, in_, extra_ins=(), extra_outs=()):
        r = self._keys([in_] + list(extra_ins))
        w = self._keys([out] + list(extra_outs))
        waits = self._deps(q, r, w)
        sems = self.dsems[q]
        j = self.rr[q]
        self.rr[q] = (j + 1) % len(sems)
        nm = sems[j]
        prev = self.cnt[nm]
        if prev > 0 and self.known[q].get(nm, 0) < prev:
            self.known[q][nm] = prev
            waits.append((nm, prev))
        self.cnt[nm] += 16
        self.ops[q].append((waits, lambda e: e.dma_start(out=out, in_=in_), nm, 16, self.tag))
        self._commit((nm, self.cnt[nm]), r, w)
        self.nops += 1

    def barrier(self):
        for e in self.ENGS:
            waits = []
            for s, v in self.cnt.items():
                if v > 0 and self.known[e].get(s, 0) < v:
                    self.known[e][s] = v
                    waits.append((s, v))
            self.ops[e].append((waits, None, None, 0, self.tag))
        self.lastw = {}
        self.readers = {}

    def emit(self):
        nc = self.nc
        with nc.Block() as block:
            for ename, deco in (("sp", block.sync), ("act", block.scalar), ("dve", block.vector),
                                ("pool", block.gpsimd), ("pe", block.tensor)):
                ops = self.ops[ename]

                def body(e, ops=ops):
                    for waits, fn, sname, inc, tag in ops:
                        for s, v in waits:
                            e.wait_ge(self.sems[s], v)
                        if fn is not None:
                            ins = fn(e)
                            ins.then_inc(self.sems[sname], inc)
                            if ANNOTATE:
                                ins.annotate(tag)

                deco(body)
                self.ops[ename] = []

    def mm(self, out, lhsT, rhs, start=True, stop=True):
        self.op("pe", lambda e: e.matmul(out, lhsT=lhsT, rhs=rhs, start=start, stop=stop),
                ins=[lhsT, rhs], outs=[out])

    def mmk(self, out, pairs):
        n = len(pairs)

        def fn(e):
            ins = None
            for i, (l, r) in enumerate(pairs):
                ins = e.matmul(out, lhsT=l, rhs=r, start=(i == 0), stop=(i == n - 1))
            return ins

        self.op("pe", fn, ins=[x for p in pairs for x in p], outs=[out])

    def tr(self, out, in_, ident):
        self.op("pe", lambda e: e.transpose(out, in_, ident), ins=[in_, ident], outs=[out])

    def trs(self, items, ident):
        def fn(e):
            ins = None
            for o, i in items:
                ins = e.transpose(o, i, ident)
            return ins
        self.op("pe", fn, ins=[i for _, i in items] + [ident], outs=[o for o, _ in items])

    def act(self, out, in_, func, bias=None, scale=None):
        kw = {}
        if bias is not None:
            kw["bias"] = bias
        if scale is not None:
            kw["scale"] = scale
        self.op("act", lambda e: e.activation(out=out, in_=in_, func=func, **kw),
                ins=[in_, bias, scale], outs=[out])

    def sqsum(self, junk, in_, accum):
        self.op("act", lambda e: e.activation(out=junk, in_=in_, func=AF.Square, accum_out=accum),
                ins=[in_], outs=[junk, accum])

    def tt(self, eng, out, in0, in1, op):
        self.op(eng, lambda e: e.tensor_tensor(out=out, in0=in0, in1=in1, op=op), ins=[in0, in1], outs=[out])

    def ts(self, eng, out, in0, s1, s2, op0, op1=None):
        if op1 is None:
            self.op(eng, lambda e: e.tensor_scalar(out=out, in0=in0, scalar1=s1, scalar2=None, op0=op0),
                    ins=[in0, s1], outs=[out])
        else:
            self.op(eng, lambda e: e.tensor_scalar(out=out, in0=in0, scalar1=s1, scalar2=s2, op0=op0, op1=op1),
                    ins=[in0, s1, s2], outs=[out])

    def stt(self, eng, out, in0, scalar, in1, op0, op1):
        self.op(eng, lambda e: e.scalar_tensor_tensor(out=out, in0=in0, scalar=scalar, in1=in1, op0=op0, op1=op1),
                ins=[in0, scalar, in1], outs=[out])

    def copy(self, eng, out, in_):
        if eng == "act":
            self.op(eng, lambda e: e.copy(out=out, in_=in_), ins=[in_], outs=[out])
        else:
            self.op(eng, lambda e: e.tensor_copy(out=out, in_=in_), ins=[in_], outs=[out])

    def red(self, eng, out, in_):
        self.op(eng, lambda e: e.tensor_reduce(out=out, in_=in_, axis=AX.X, op=ALU.add), ins=[in_], outs=[out])

    def recip(self, out, in_):
        self.op("dve", lambda e: e.reciprocal(out=out, in_=in_), ins=[in_], outs=[out])

    def memset(self, eng, ap, val):
        self.op(eng, lambda e: e.memset(ap, val), ins=[], outs=[ap])


def bc(ap, axis, n):
    u = ap.unsqueeze(axis)
    shp = list(u.shape)
    shp[axis] = n
    return u.broadcast_to(shp)


def build_program():
    nc = bass.Bass("TRN2", target_bir_lowering=False)

    def din(name, shape, dt=F32):
        return nc.dram_tensor(name, list(shape), dt, kind="ExternalInput").ap()

    def dout(name, shape, dt=F32):
        return nc.dram_tensor(name, list(shape), dt, kind="ExternalOutput").ap()

    def dscr(name, shape, dt=F32):
        return nc.dram_tensor(name, list(shape), dt).ap()

    xs_all = din("xs_all", [NT, 128, D])
    cexp = din("cexp", [2, 128, D])
    w_ada = din("w_ada", [D, 6 * D])
    b_ada_b = din("b_ada_b", [128, 6 * D])
    normvecs = din("normvecs", [128, 4, D])
    w_in = din("w_in", [D, 6688])
    w_out = din("w_out", [2 * D, D])
    w_gu = din("w_gu", [D, 2 * DFF])
    w_dn = din("w_dn", [DFF, D])
    ssd_cw = din("ssd_cw", [128, 12, 5])
    gdn_cw = din("gdn_cw", [128, 24, 4])
    smallv = din("smallv", [128, 64])
    ssd_nw = din("ssd_nw", [128, D])
    gdn_nw = din("gdn_nw", [128, 128])
    st_ssd = din("st_ssd", [16, 1024, 128])
    hist_ssd = din("hist_ssd", [128, 12, 16, 3])
    st_gdn = din("st_gdn", [16, 8, 128, 128])
    hist_gdn = din("hist_gdn", [128, 24, 16, 3])
    ident_d = din("ident", [128, 128])
    masks_d = din("masks", [2, 128, 4, 128])
    blk16_d = din("blk16", [128, 16])

    y_all = dout("y_all", [NT, 128, D])
    o_ssd_p = dout("o_ssd_p", [1024, 128])
    o_ssdc_p = dout("o_ssdc_p", [3, 1536])
    o_gdn_p = dout("o_gdn_p", [1024, 128])
    o_gdnc_p = dout("o_gdnc_p", [3, 3072])
    o_ssd_s = dout("o_ssd_s", [16, 1024, 128])
    o_ssdc_s = dout("o_ssdc_s", [48, 1536])
    o_gdn_s = dout("o_gdn_s", [16, 8, 128, 128])
    o_gdnc_s = dout("o_gdnc_s", [48, 3072])

    modscr = dscr("modscr", [2, 128, 6 * D])
    yssd_scr = dscr("yssd_scr", [NT, 128, D], BF16)
    x1_scr = dscr("x1_scr", [NT, 128, D])

    with contextlib.ExitStack() as gstack:
        P = Prog(nc, gstack)

        def sb(stack, name, shape, dt=F32):
            return stack.enter_context(nc.sbuf_tensor(name, list(shape), dt))

        class WBlocks:
            def __init__(self, stk, name, nk, bounds):
                self.bounds = bounds
                self.t = [sb(stk, "%s_%d" % (name, j), [128, nk, bounds[j + 1] - bounds[j]], BF16)
                          for j in range(len(bounds) - 1)]

            def load(self, dview, col_off, order=None):
                for j in (order if order is not None else range(len(self.t))):
                    b0, b1 = self.bounds[j], self.bounds[j + 1]
                    P.dma("pool", self.t[j][:, :, :], dview[:, :, col_off + b0:col_off + b1])

            def col(self, kc, c0, w):
                for j in range(len(self.t)):
                    if self.bounds[j] <= c0 and c0 + w <= self.bounds[j + 1]:
                        return self.t[j][:, kc, c0 - self.bounds[j]:c0 - self.bounds[j] + w]
                raise ValueError("column range straddles weight blocks")

        PT = gstack.enter_context(nc.psum_tensor("pst", [128, 1024], BF16))
        ps1 = contextlib.ExitStack()
        PS = [ps1.enter_context(nc.psum_tensor("ps%d" % i, [128, 512], F32)) for i in range(7)]

        identf = sb(gstack, "identf", [128, 128])
        identb = sb(gstack, "identb", [128, 128], BF16)
        onesf = sb(gstack, "onesf", [128, 128])
        onesb = sb(gstack, "onesb", [128, 128], BF16)
        cst = sb(gstack, "cst", [128, 4])
        masks = [sb(gstack, "masks%d" % t, [128, 4, 128]) for t in range(2)]
        blk16 = sb(gstack, "blk16s", [128, 16])
        smv = sb(gstack, "smv", [128, 64])
        negA = sb(gstack, "negA", [128, 24])
        P.dma("sp", identf[:, :], ident_d[:, :])
        P.dma("pool", identb[:, :], ident_d[:, :])
        for t in range(2):
            P.dma("sp", masks[t][:, :, :], masks_d[t])
        P.dma("sp", blk16[:, :], blk16_d[:, :])
        P.dma("sp", smv[:, :], smallv[:, :])
        P.memset("dve", onesf[:, :], 1.0)
        P.memset("dve", onesb[:, :], 1.0)
        P.memset("dve", cst[:, 0:1], 1e-6)
        P.memset("dve", cst[:, 1:2], 1.0)
        P.memset("dve", cst[:, 2:3], 128e-6)
        P.memset("dve", cst[:, 3:4], 0.0)
        P.act(negA[:, 0:16], smv[:, 16:32], AF.Exp)
        P.act(negA[:, 16:24], smv[:, 56:64], AF.Exp)
        P.ts("dve", negA[:, :], negA[:, :], -1.0, None, ALU.mult)

        MSTT, MINC, MSTR, MBLK = 0, 1, 2, 3

        with contextlib.ExitStack() as st0:
            ct = sb(st0, "ct", [128, D])
            cb = sb(st0, "cb", [128, D], BF16)
            cT = [sb(st0, "cT%d" % t, [128, 8, 128], BF16) for t in range(2)]
            modt = [sb(st0, "modt%d" % t, [128, 6 * D]) for t in range(2)]
            nv = sb(st0, "nv", [128, 4, D])
            wa = [sb(st0, "wa%d" % i, [128, 8, 512], BF16) for i in range(2)]
            bb = [sb(st0, "bb%d" % i, [128, 512]) for i in range(2)]
            P.dma("sp", nv[:, :, :], normvecs[:, :, :])
            for t in range(2):
                P.dma("sp", ct[:, :], cexp[t])
                P.act(cb[:, :], ct[:, :], AF.Silu)
                P.trs([(PT[:, c * 128:(c + 1) * 128], cb[:, c * 128:(c + 1) * 128]) for c in range(8)], identb[:, :])
                P.copy("dve", cT[t][:, :, :], PT[:, :].rearrange("p (c t) -> p c t", c=8))
            wv = w_ada.rearrange("(kc p) n -> p kc n", p=128)
            for j in range(12):
                P.dma("pool", wa[j % 2][:, :, :], wv[:, :, j * 512:(j + 1) * 512])
                P.dma("sp", bb[j % 2][:, :], b_ada_b[:, j * 512:(j + 1) * 512])
                for t in range(2):
                    ps = PS[(2 * j + t) % 4]
                    P.mmk(ps[:, :], [(cT[t][:, kc, :], wa[j % 2][:, kc, :]) for kc in range(8)])
                    P.tt("dve", modt[t][:, j * 512:(j + 1) * 512], ps[:, :], bb[j % 2][:, :], ALU.add)
            for t in range(2):
                m = modt[t]
                P.stt("dve", m[:, D:2 * D], m[:, D:2 * D], 1.0, nv[:, 0, :], ALU.add, ALU.mult)
                P.tt("dve", m[:, 2 * D:3 * D], m[:, 2 * D:3 * D], nv[:, 1, :], ALU.mult)
                P.stt("dve", m[:, 4 * D:5 * D], m[:, 4 * D:5 * D], 1.0, nv[:, 2, :], ALU.add, ALU.mult)
                P.tt("dve", m[:, 5 * D:6 * D], m[:, 5 * D:6 * D], nv[:, 3, :], ALU.mult)
                P.dma("sp", modscr[t], m[:, :])
            P.barrier()
            P.emit()

        def norm_to_T(xt, modv, sidx, hidx, tmpA, hb, hT, stt_):
            P.sec("norm_to_T")
            P.sqsum(tmpA[:, :], xt[:, :], stt_[:, 0:1])
            P.act(stt_[:, 1:2], stt_[:, 0:1], AF.Sqrt, bias=cst[:, 0:1], scale=1.0 / D)
            P.recip(stt_[:, 2:3], stt_[:, 1:2])
            P.stt("dve", tmpA[:, :], xt[:, :], stt_[:, 2:3], modv[:, sidx, :], ALU.mult, ALU.mult)
            P.tt("dve", hb[:, :], tmpA[:, :], modv[:, hidx, :], ALU.add)
            P.trs([(PT[:, c * 128:(c + 1) * 128], hb[:, c * 128:(c + 1) * 128]) for c in range(8)], identb[:, :])
            P.copy("act", hT[:, :, :], PT[:, :].rearrange("p (c t) -> p c t", c=8))

        def norm_to_T_g(xt, modv, sidx, hidx, tmpA, hb, hT, stt_):
            P.sec("norm_to_T")
            P.sqsum(tmpA[:, :], xt[:, :], stt_[:, 0:1])
            yield
            P.act(stt_[:, 1:2], stt_[:, 0:1], AF.Sqrt, bias=cst[:, 0:1], scale=1.0 / D)
            yield
            P.recip(stt_[:, 2:3], stt_[:, 1:2])
            yield
            P.stt("dve", tmpA[:, :], xt[:, :], stt_[:, 2:3], modv[:, sidx, :], ALU.mult, ALU.mult)
            yield
            P.tt("dve", hb[:, :], tmpA[:, :], modv[:, hidx, :], ALU.add)
            yield
            P.trs([(PT[:, c * 128:(c + 1) * 128], hb[:, c * 128:(c + 1) * 128]) for c in range(8)], identb[:, :])
            yield
            P.copy("act", hT[:, :, :], PT[:, :].rearrange("p (c t) -> p c t", c=8))
            yield

        def small_scan_terms(af, nh, typ, pss, ex):
            mk = masks[typ]
            P.mm(pss[:, 0:nh], mk[:, MINC, :], af)
            P.mm(pss[:, nh:2 * nh], mk[:, MSTT, :], af)
            P.mm(pss[:, 2 * nh:3 * nh], mk[:, MBLK, :], af)
            P.act(ex[:, 0:3 * nh], pss[:, 0:3 * nh], AF.Exp)

        def small_scan_terms_g(af, nh, typ, pss, ex):
            mk = masks[typ]
            P.mm(pss[:, 0:nh], mk[:, MINC, :], af)
            yield
            P.mm(pss[:, nh:2 * nh], mk[:, MSTT, :], af)
            yield
            P.mm(pss[:, 2 * nh:3 * nh], mk[:, MBLK, :], af)
            yield
            P.act(ex[:, 0:3 * nh], pss[:, 0:3 * nh], AF.Exp)
            yield

        def proj_conv_gen(typ, ngroups, wt, wcol0, hT, psb, xinS_l, xinP_l, histS, histP, cwt, has_bias, accs, dst_fn):
            items = []

            def slot(i):
                n = len(items)
                if 0 <= i < n:
                    it = items[i]
                    if has_bias:
                        P.act(it[0], it[2][0], AF.Identity, bias=cwt[:, it[4], 4:5], scale=cwt[:, it[4], 0:1])
                    else:
                        P.act(it[0], it[2][0], AF.Identity, scale=cwt[:, it[4], 0:1])
                    yield
                if 0 <= i - 1 < n:
                    it = items[i - 1]
                    for k in range(1, 4):
                        P.stt("dve", it[0], it[2][k], cwt[:, it[4], k:k + 1], it[0], ALU.mult, ALU.add)
                        yield
                if 0 <= i - 2 < n:
                    it = items[i - 2]
                    P.act(it[3], it[1][:, :], AF.Silu)
                    yield

            for g in range(ngroups):
                ps = psb[g % 2]
                for j in range(4):
                    c = 4 * g + j
                    P.mmk(ps[:, j * 128:(j + 1) * 128],
                          [((wt.col(kc, wcol0 + c * 128, 128) if hasattr(wt, "col")
                             else wt[:, kc, wcol0 + c * 128:wcol0 + (c + 1) * 128]), hT[:, kc, :]) for kc in range(8)])
                    yield
                if typ == 0:
                    xin = xinS_l[g % 2]
                    P.copy("pool", xin[:, :, :, 0:3], histS[:, 4 * g:4 * g + 4, :, :])
                    yield
                    P.copy("act", xin[:, :, :, 3:11], ps[:, :].rearrange("p (j s t) -> p j s t", j=4, s=16))
                    yield
                else:
                    xin = xinP_l[g % 2]
                    P.copy("pool", xin[:, :, 0:3], histP[:, 4 * g:4 * g + 4, :])
                    yield
                    P.copy("act", xin[:, :, 3:131], ps[:, :].rearrange("p (j t) -> p j t", j=4))
                    yield
                    P.copy("pool", histP[:, 4 * g:4 * g + 4, :], xin[:, :, 128:131])
                    yield
                for j in range(4):
                    c = 4 * g + j
                    a_ = accs[c % 4]
                    if typ == 0:
                        av = a_[:, :].rearrange("p (s t) -> p s t", s=16)
                        sh = [xin[:, j, :, k:k + 8] for k in range(4)]
                    else:
                        av = a_[:, :]
                        sh = [xin[:, j, k:k + 128] for k in range(4)]
                    items.append((av, a_, sh, dst_fn(c), c))
                    yield from slot(c)
            yield from slot(4 * ngroups)
            yield from slot(4 * ngroups + 1)

        def run_gen(g):
            for _ in g:
                pass

        w_in_v = w_in.rearrange("(kc p) n -> p kc n", p=128)
        if STOP_AFTER >= 1 and not SKIP1:
          with contextlib.ExitStack() as st1:
            wI = WBlocks(st1, "wI", 8, [0, 512, 1024, 1536, 2048, 2560, 2576])
            wI.load(w_in_v, 0, order=[2, 3, 4, 0, 1, 5])
            modv = sb(st1, "modv", [128, 3, D])
            cw = sb(st1, "cw", [128, 12, 5])
            nw = sb(st1, "ssdnw", [128, D])
            histS = sb(st1, "histS", [128, 12, 16, 3])
            histP = sb(st1, "histP", [128, 12, 3])
            P.dma("sp", cw[:, :, :], ssd_cw[:, :, :])
            P.dma("sp", nw[:, :], ssd_nw[:, :])
            P.dma("sp", histS[:, :, :, :], hist_ssd[:, :, :, :])
            P.memset("pool", histP[:, :, :], 0.0)
            xt = [sb(st1, "xt%d" % i, [128, D]) for i in range(2)]
            tmpA = sb(st1, "tmpA", [128, D])
            hb = sb(st1, "hb", [128, D], BF16)
            hT = sb(st1, "hT", [128, 8, 128], BF16)
            stt_ = sb(st1, "stt", [128, 4])
            xinS_l = [sb(st1, "xinS%d" % i, [128, 4, 16, 11]) for i in range(2)]
            xinP_l = [sb(st1, "xinP%d" % i, [128, 4, 131]) for i in range(2)]
            acc = [sb(st1, "acc%d" % i, [128, 128]) for i in range(4)]
            xc_2 = [sb(st1, "xc_%d" % i, [128, 12, 128], BF16) for i in range(2)]
            zs_2 = [sb(st1, "zs_%d" % i, [128, D], BF16) for i in range(2)]
            sm = sb(st1, "sm", [128, 48])
            af_2 = [sb(st1, "af_%d" % i, [128, 16]) for i in range(2)]
            ex_2 = [sb(st1, "ex_%d" % i, [128, 48]) for i in range(2)]
            xtm = sb(st1, "xtm", [128, D], BF16)
            xdt_2 = [sb(st1, "xdt_%d" % i, [128, D], BF16) for i in range(2)]
            xDs_2 = [sb(st1, "xDs_%d" % i, [128, D], BF16) for i in range(2)]
            xw_2 = [sb(st1, "xw_%d" % i, [128, D], BF16) for i in range(2)]
            Btm_2 = [sb(st1, "Btm_%d" % i, [128, 2, 128], BF16) for i in range(2)]
            CBm_2 = [sb(st1, "CBm_%d" % i, [128, 2, 128]) for i in range(2)]
            L4 = [sb(st1, "L4_%d" % i, [128, 4, 128]) for i in range(2)]
            E4 = [sb(st1, "E4_%d" % i, [128, 4, 128]) for i in range(2)]
            M4 = [sb(st1, "M4_%d" % i, [128, 4, 128], BF16) for i in range(2)]
            yo = sb(st1, "yo", [128, D])
            yy = sb(st1, "yy", [128, D])
            ss = sb(st1, "ss", [128, 8])
            ysb = [sb(st1, "ysb%d" % i, [128, D], BF16) for i in range(2)]
            hTf = sb(st1, "hTf", [128, D])
            hTb = sb(st1, "hTb", [128, D], BF16)
            lhs3 = sb(st1, "lhs3", [128, 8, 48], BF16)
            cv = sb(st1, "cv", [48, 1536])
            CmT = [sb(st1, "CmT%d" % g, [128, 16 * 128], BF16) for g in range(2)]
            h0 = [sb(st1, "h0_%d" % i, [128, 8, 128]) for i in range(2)]
            h0Tb = sb(st1, "h0Tb", [128, D], BF16)
            xwm = sb(st1, "xwm", [128, D], BF16)
            hn = [sb(st1, "hn%d" % i, [128, 8, 128]) for i in range(2)]
            rsel = sb(st1, "rsel", [128, 2, 128])
            decsel = sb(st1, "decsel", [128, 128])
            D2N = ("xc", "zs", "af", "ex", "xdt", "xDs", "xw", "Btm", "CBm")
            D2 = dict(xc=xc_2, zs=zs_2, af=af_2, ex=ex_2, xdt=xdt_2, xDs=xDs_2, xw=xw_2, Btm=Btm_2, CBm=CBm_2)
            P.memset("pool", hTf[:, :], 0.0)
            P.memset("pool", hTb[:, :], 0.0)
            for g in range(2):
                P.memset("pool", CmT[g][:, :], 0.0)

            def ssd_gen(ti):
                typ = 0 if ti == 0 else 1
                xc, zs, af, ex, xdt, xDs, xw, Btm, CBm = (D2[n][ti % 2] for n in D2N)
                PY = [PS[2], PS[3]] if typ == 1 else [PS[0], PS[1]]
                PYO = [PS[5], PS[6]] if typ == 1 else [PS[2], PS[3]]
                mk = masks[typ]
                if ti <= 1:
                    P.dma("sp", modv[:, :, :], modscr[typ].rearrange("p (s d) -> p s d", s=6)[:, 0:3, :])
                    yield
                x_ = xt[ti % 2]
                P.dma("sp", x_[:, :], xs_all[ti])
                yield
                yield from norm_to_T_g(x_, modv, 1, 0, tmpA, hb, hT, stt_)
                yield from proj_conv_gen(typ, 3, wI, 1024, hT, PS[0:2], xinS_l, xinP_l, histS, histP, cw, True, acc,
                                         lambda c: xc[:, c, :])
                P.sec("z (token-major) -> silu")
                for nb in range(2):
                    ps = PS[nb]
                    P.mmk(ps[:, :], [(hT[:, kc, :], wI.col(kc, nb * 512, 512)) for kc in range(8)])
                    yield
                    P.act(zs[:, nb * 512:(nb + 1) * 512], ps[:, :], AF.Silu)
                    yield
                P.sec("dt")
                pd = PS[4]
                P.mmk(pd[:, 0:16], [(hT[:, kc, :], wI.col(kc, 2560, 16)) for kc in range(8)])
                yield
                P.tt("dve", sm[:, 0:16], pd[:, 0:16], smv[:, 0:16], ALU.add)
                yield
                P.act(sm[:, 16:32], sm[:, 0:16], AF.Exp)
                yield
                P.act(sm[:, 32:48], sm[:, 16:32], AF.Ln, bias=cst[:, 1:2], scale=1.0)
                yield
                P.tt("dve", af[:, :], sm[:, 32:48], negA[:, 0:16], ALU.mult)
                yield
                yield from small_scan_terms_g(af[:, :], 16, typ, PS[4][:, 64:112], ex)
                P.sec("token-major x, xdt, xD, xw, B")
                P.trs([(PT[:, c * 128:(c + 1) * 128], xc[:, c, :]) for c in range(8)], identb[:, :])
                yield
                P.copy("act", xtm[:, :], PT[:, :])
                yield
                x3 = xtm[:, :].rearrange("p (h d) -> p h d", h=16)
                P.tt("dve", xdt[:, :].rearrange("p (h d) -> p h d", h=16), x3, bc(sm[:, 32:48], 2, 64), ALU.mult)
                yield
                P.tt("pool", xDs[:, :].rearrange("p (h d) -> p h d", h=16), x3, bc(smv[:, 32:48], 2, 64), ALU.mult)
                yield
                P.tt("pool", xw[:, :].rearrange("p (h d) -> p h d", h=16),
                     xdt[:, :].rearrange("p (h d) -> p h d", h=16), bc(ex[:, 16:32], 2, 64), ALU.mult)
                yield
                P.trs([(PT[:, g * 128:(g + 1) * 128], xc[:, 8 + g, :]) for g in range(2)], identb[:, :])
                yield
                P.copy("act", Btm[:, :, :], PT[:, 0:256].rearrange("p (g n) -> p g n", g=2))
                yield
                P.sec("CB^T masked")
                pc = PS[4][:, 256:512]
                for g in range(2):
                    P.mm(pc[:, g * 128:(g + 1) * 128], xc[:, 8 + g, :], xc[:, 10 + g, :])
                    yield
                P.tt("dve", CBm[:, :, :], pc[:, 0:256].rearrange("p (g l) -> p g l", g=2), bc(mk[:, MINC, :], 1, 2), ALU.mult)
                yield
                P.sec("y_off raw")
                if typ == 1:
                    yield "HALF"
                else:
                    for g in range(2):
                        base = CmT[g][:, :]
                        dst = bass.AP(base.tensor, base.offset, [[16 * 128, 128], [136, 16], [1, 8]])
                        P.op("pool", (lambda e, dst=dst, src=xc[:, 10 + g, :].rearrange("p (s t) -> p s t", s=16):
                                      e.tensor_copy(out=dst, in_=src)), ins=[xc[:, 0, :]], outs=[base])
                        yield
                    for hh in range(2):
                        P.tt("pool", rsel[:, hh, :].rearrange("p (j s) -> p j s", j=8),
                             bc(af[:, hh:16:2], 2, 16), bc(blk16[:, :], 1, 8), ALU.mult)
                        yield
                        P.mm(PS[5][:, 256 + hh * 128:256 + (hh + 1) * 128], onesf[:, :], rsel[:, hh, :])
                        yield
                    P.act(decsel[0:64, :], PS[5][0:64, 256:384], AF.Exp)
                    yield
                    P.act(decsel[64:128, :], PS[5][64:128, 384:512], AF.Exp)
                    yield
                    for s in range(16):
                        h0_ = h0[s % 2]
                        hn_ = hn[s % 2]
                        P.dma("sp", h0_[:, :, :], st_ssd[s].rearrange("(j p) n -> p j n", p=128))
                        yield
                        pa, pb = PS[0], PS[1]
                        P.trs([((pa if j < 4 else pb)[:, (j % 4) * 128:(j % 4 + 1) * 128], h0_[:, j, :]) for j in range(8)],
                              identf[:, :])
                        yield
                        P.copy("act", h0Tb[:, 0:512], pa[:, :])
                        yield
                        P.copy("dve", h0Tb[:, 512:1024], pb[:, :])
                        yield
                        for g in range(2):
                            P.op("pe", (lambda e, o=PS[2 + g][:, :], l=CmT[g][:, s * 128:(s + 1) * 128],
                                        r=h0Tb[:, g * 512:(g + 1) * 512], s=s:
                                        e.matmul(o, lhsT=l, rhs=r, start=(s == 0), stop=(s == 15))),
                                 ins=[CmT[g][:, :], h0Tb[:, :]], outs=[PS[2 + g][:, :]])
                            yield
                        P.act(xwm[:, :], xw[:, :], AF.Identity, scale=blk16[:, s:s + 1])
                        yield
                        for half in range(2):
                            pn = PS[half]
                            for jj in range(4):
                                j = half * 4 + jj
                                P.mm(pn[:, jj * 128:(jj + 1) * 128], xwm[:, j * 128:(j + 1) * 128], Btm[:, j // 4, :])
                                yield
                            for jj in range(4):
                                j = half * 4 + jj
                                P.stt("dve", hn_[:, j, :], h0_[:, j, :], decsel[:, j * 16 + s:j * 16 + s + 1],
                                      pn[:, jj * 128:(jj + 1) * 128], ALU.mult, ALU.add)
                                yield
                        P.dma("sp", o_ssd_s[s].rearrange("(j p) n -> p j n", p=128), hn_[:, :, :])
                        yield
                P.sec("per-head intra-chunk")
                for q in range(4):
                    g = q // 2
                    L_, E_, M_ = L4[q % 2], E4[q % 2], M4[q % 2]
                    P.tt("dve", L_[:, :, :], bc(mk[:, MSTT, :], 1, 4), bc(af[:, 4 * q:4 * q + 4], 2, 128), ALU.mult)
                    yield
                    pg = PS[5 + (q % 2)]
                    for j in range(4):
                        P.mm(pg[:, j * 128:(j + 1) * 128], L_[:, j, :], mk[:, MINC, :])
                        yield
                    P.act(E_[:, :, :], pg[:, :].rearrange("p (j l) -> p j l", j=4), AF.Exp)
                    yield
                    P.tt("dve", M_[:, :, :], E_[:, :, :], bc(CBm[:, g, :], 1, 4), ALU.mult)
                    yield
                    py = PY[g]
                    for j in range(4):
                        h = 4 * q + j
                        hh = h % 8
                        P.mmk(py[:, hh * 64:(hh + 1) * 64],
                              [(identb[:, :], xDs[:, h * 64:(h + 1) * 64]), (M_[:, j, :], xdt[:, h * 64:(h + 1) * 64])])
                        yield
                P.sec("combine y = y_diag + eacs * y_off")
                if typ == 1:
                    for g in range(2):
                        P.mm(PYO[g][:, :], xc[:, 10 + g, :], hTb[:, g * 512:(g + 1) * 512])
                        yield
                for g in range(2):
                    P.copy("act", yo[:, g * 512:(g + 1) * 512], PYO[g][:, :])
                    yield
                P.tt("dve", yo[:, :].rearrange("p (h d) -> p h d", h=16), yo[:, :].rearrange("p (h d) -> p h d", h=16),
                     bc(ex[:, 0:16], 2, 64), ALU.mult)
                yield
                for g in range(2):
                    P.tt("dve", yy[:, g * 512:(g + 1) * 512], yo[:, g * 512:(g + 1) * 512], PY[g][:, :], ALU.add)
                    yield
                P.sec("gate with silu(z), group rmsnorm")
                P.tt("dve", yy[:, :], yy[:, :], zs[:, :], ALU.mult)
                yield
                for g in range(2):
                    P.sqsum(yo[:, g * 512:(g + 1) * 512], yy[:, g * 512:(g + 1) * 512], ss[:, g:g + 1])
                    yield
                P.act(ss[:, 2:4], ss[:, 0:2], AF.Sqrt, bias=cst[:, 0:1], scale=1.0 / 512)
                yield
                P.recip(ss[:, 4:6], ss[:, 2:4])
                yield
                y_ = ysb[ti % 2]
                for g in range(2):
                    P.stt("dve", y_[:, g * 512:(g + 1) * 512], yy[:, g * 512:(g + 1) * 512], ss[:, 4 + g:5 + g],
                          nw[:, g * 512:(g + 1) * 512], ALU.mult, ALU.mult)
                    yield
                P.dma("sp", yssd_scr[ti], y_[:, :])
                yield
                P.sec("state update (prompt chunks)")
                if typ == 1:
                    for g in range(2):
                        P.mm(PYO[g][:, :], Btm[:, g, :], xw[:, g * 512:(g + 1) * 512])
                        yield
                    h3 = hTf[:, :].rearrange("p (h d) -> p h d", h=16)
                    P.tt("dve", h3, h3, bc(ex[:, 32:48], 2, 64), ALU.mult)
                    yield
                    for g in range(2):
                        P.tt("dve", hTf[:, g * 512:(g + 1) * 512], hTf[:, g * 512:(g + 1) * 512], PYO[g][:, :], ALU.add)
                        yield
                    P.copy("act", hTb[:, :], hTf[:, :])
                    yield
                P.sec("conv-state outputs (last 3 raw xbc rows)")
                if ti == 0 or ti == NT - 1:
                    M3 = 48 if typ == 0 else 3
                    if typ == 0:
                        P.copy("pool", lhs3[:, :, :].rearrange("p k (s t) -> p k s t", s=16),
                               hT[:, :, :].rearrange("p k (s t) -> p k s t", s=16)[:, :, :, 5:8])
                        yield
                    else:
                        P.copy("pool", lhs3[:, :, 0:3], hT[:, :, 125:128])
                        yield
                    for nb in range(3):
                        ps = PS[5 + (nb % 2)]
                        P.mmk(ps[0:M3, :], [(lhs3[:, kc, 0:M3], wI.col(kc, 1024 + nb * 512, 512))
                                            for kc in range(8)])
                        yield
                        P.copy("act", cv[0:M3, nb * 512:(nb + 1) * 512], ps[0:M3, :])
                        yield
                    P.dma("sp", (o_ssdc_s if typ == 0 else o_ssdc_p)[:, :], cv[0:M3, :])
                    yield

            def rolling(genf, first, last, yspeed=1):
                nxt_ = first + 1
                cur = genf(first)
                young = None
                hold = [False]
                while cur is not None:
                    try:
                        v = next(cur)
                        if v == "HALF" and young is None and nxt_ < last:
                            young = genf(nxt_)
                            nxt_ += 1
                    except StopIteration:
                        cur, young = young, None
                        if hold[0] and cur is not None and nxt_ < last:
                            young = genf(nxt_)
                            nxt_ += 1
                        hold[0] = False
                        if cur is None and nxt_ < last:
                            cur = genf(nxt_)
                            nxt_ += 1
                        continue
                    for _k in range(yspeed):
                        if young is not None and not hold[0]:
                            try:
                                if next(young) == "HALF":
                                    hold[0] = True
                            except StopIteration:
                                young = None

            run_gen(ssd_gen(0))
            rolling(ssd_gen, 1, NT, yspeed=2)
            for half in range(2):
                ps = PS[half]
                P.trs([(ps[:, jj * 128:(jj + 1) * 128], hTf[:, (half * 4 + jj) * 128:(half * 4 + jj + 1) * 128])
                       for jj in range(4)], identf[:, :])
                P.copy("act", hn[half][:, 0:4, :], ps[:, :].rearrange("p (j n) -> p j n", j=4))
                P.dma("sp", o_ssd_p[half * 512:(half + 1) * 512, :].rearrange("(j p) n -> p j n", p=128), hn[half][:, 0:4, :])
            P.barrier()
            P.emit()
        ps1.close()
        if STOP_AFTER >= 2:
          with contextlib.ExitStack() as st2:
            PA = [st2.enter_context(nc.psum_tensor("pa%d" % i, [128, 512], F32)) for i in range(2)]
            PM = st2.enter_context(nc.psum_tensor("pm", [128, 512], F32))
            LPB = [st2.enter_context(nc.psum_tensor("lpb%d" % i, [128, 512], F32)) for i in range(4)]
            LPS = [[LPB[ln][:, i * 128:(i + 1) * 128] for i in range(4)] for ln in range(4)]
            wII = WBlocks(st2, "wII", 8, [0, 512, 1024, 1536, 2048, 2560, 3072, 3584, 4096, 4112])
            wII.load(w_in_v, 2576)
            wO = WBlocks(st2, "wO", 16, [0, 512, 1024])
            w_out_v = w_out.rearrange("(kc p) n -> p kc n", p=128)
            wO.load(w_out_v, 0)
            modv = sb(st2, "modv2", [128, 3, D])
            cwg = sb(st2, "cwg", [128, 24, 4])
            gnw = sb(st2, "gnw", [128, 128])
            histP = sb(st2, "histPg", [128, 24, 3])
            P.dma("sp", cwg[:, :, :], gdn_cw[:, :, :])
            P.dma("sp", gnw[:, :], gdn_nw[:, :])
            P.memset("pool", histP[:, :, :], 0.0)
            tmpA = sb(st2, "tmpA2", [128, D])
            hT = sb(st2, "hT2", [128, 8, 128], BF16)
            stt_ = sb(st2, "stt2", [128, 4])
            xinP_l = [sb(st2, "xinP2_%d" % i, [128, 4, 131]) for i in range(2)]
            acc = [sb(st2, "acc2_%d" % i, [128, 128]) for i in range(4)]
            lhs3 = sb(st2, "lhs3g", [128, 8, 48], BF16)
            tmpT = sb(st2, "tmpT2", [128, D])
            sttT = sb(st2, "sttT2", [128, 4])
            ss8 = sb(st2, "ss8", [128, 24])
            otm = sb(st2, "otm", [128, D])
            mixed = sb(st2, "mixed", [128, 2 * D], BF16)
            mixedT = sb(st2, "mixedT", [128, 16, 128], BF16)

            class FB:
                pass

            def make_fb(i, stk):
                fb = FB()
                fb.xt = sb(stk, "f%d_xt" % i, [128, D])
                fb.hb = sb(stk, "f%d_hb" % i, [128, D], BF16)
                fb.qk = sb(stk, "f%d_qk" % i, [128, 16, 128], BF16)
                fb.vfm = sb(stk, "f%d_vfm" % i, [128, 8, 128], BF16)
                fb.ktm = sb(stk, "f%d_ktm" % i, [128, 8, 128], BF16)
                fb.vtm = sb(stk, "f%d_vtm" % i, [128, 8, 128], BF16)
                fb.gs = sb(stk, "f%d_gs" % i, [128, D], BF16)
                fb.sm = sb(stk, "f%d_sm" % i, [128, 64])
                fb.gf = sb(stk, "f%d_gf" % i, [128, 8])
                fb.ex = sb(stk, "f%d_ex" % i, [128, 24])
                return fb

            class Lane:
                pass

            lanes = []

            def make_lane(ln, stk):
                B = Lane()
                B.ps = LPS[ln]
                f = lambda nm, dt=F32: sb(stk, "ln%d_%s" % (ln, nm), [128, 128], dt)
                B.P = [f("P0"), f("P1")]
                B.PT = [f("PT0"), f("PT1")]
                B.X = [f("X0"), f("X1")]
                B.ot = f("ot")
                B.L, B.Dm, B.DmI, B.DmS = B.ot, B.X[1], B.PT[1], B.P[1]
                B.attnT, B.TTb, B.R, B.vn, B.kout = (f("attnT", BF16), f("TTb", BF16), f("R", BF16),
                                                     f("vn", BF16), f("kout", BF16))
                lanes.append(B)

            make_lane(0, st2)
            fbs = [make_fb(0, st2)]

            def front_gen(ti, typ, SB, fb):
                xt, hb, qk, vfm, sm, gf, ex = fb.xt, fb.hb, fb.qk, fb.vfm, fb.sm, fb.gf, fb.ex
                P.sec("F:load+norm")
                if ti <= 1:
                    P.dma("sp", modv[:, :, :], modscr[typ].rearrange("p (s d) -> p s d", s=6)[:, 0:3, :])
                    yield
                P.dma("sp", xt[:, :], xs_all[ti])
                yield
                yield from norm_to_T_g(xt, modv, 1, 0, tmpA, hb, hT, stt_)
                yield
                yield from proj_conv_gen(typ, 6, wII, 0, hT, PA, (SB.xinS_l if typ == 0 else None), xinP_l,
                                         (SB.histS if typ == 0 else None), histP, cwg, False, acc,
                                         lambda c: (qk[:, c, :] if c < 16 else vfm[:, c - 16, :]))
                if ti == 0 or ti == NT - 1:
                    P.sec("F:convstate")
                    M3 = 48 if typ == 0 else 3
                    if typ == 0:
                        P.copy("pool", lhs3[:, :, :].rearrange("p k (s t) -> p k s t", s=16),
                               hT[:, :, :].rearrange("p k (s t) -> p k s t", s=16)[:, :, :, 5:8])
                        yield
                    else:
                        P.copy("pool", lhs3[:, :, 0:3], hT[:, :, 125:128])
                        yield
                    for cc in range(3):
                        for nb in range(2):
                            c0 = cc * 1024 + nb * 512
                            P.mmk(PA[nb][0:M3, :], [(lhs3[:, kc, 0:M3], wII.col(kc, c0, 512)) for kc in range(8)])
                            yield
                            P.copy("act", tmpA[0:M3, nb * 512:(nb + 1) * 512], PA[nb][0:M3, :])
                            yield
                        P.dma("sp", (o_gdnc_s if typ == 0 else o_gdnc_p)[:, cc * 1024:(cc + 1) * 1024], tmpA[0:M3, :])
                        yield
                    yield
                P.sec("F:gate")
                for nb in range(2):
                    P.mmk(PA[nb][:, :], [(hT[:, kc, :], wII.col(kc, 3072 + nb * 512, 512))
                                         for kc in range(8)])
                    yield
                    P.act(fb.gs[:, nb * 512:(nb + 1) * 512], PA[nb][:, :], AF.Silu)
                    yield
                yield
                P.sec("F:beta/g")
                P.mmk(PM[:, 0:16], [(hT[:, kc, :], wII.col(kc, 4096, 16)) for kc in range(8)])
                yield
                P.act(sm[:, 0:8], PM[:, 0:8], AF.Exp, scale=-1.0)
                yield
                P.ts("dve", sm[:, 0:8], sm[:, 0:8], 1.0, None, ALU.add)
                yield
                P.recip(sm[:, 8:16], sm[:, 0:8])
                yield
                P.ts("dve", sm[:, 16:24], sm[:, 8:16], -1.0, None, ALU.mult)
                yield
                P.tt("dve", sm[:, 24:32], PM[:, 8:16], smv[:, 48:56], ALU.add)
                yield
                P.act(sm[:, 32:40], sm[:, 24:32], AF.Exp)
                yield
                P.act(sm[:, 40:48], sm[:, 32:40], AF.Ln, bias=cst[:, 1:2], scale=1.0)
                yield
                P.tt("dve", gf[:, :], sm[:, 40:48], negA[:, 16:24], ALU.mult)
                yield
                yield
                P.sec("F:beta/g")
                yield from small_scan_terms_g(gf[:, :], 8, typ, PM[:, 64:88], ex)
                P.ts("dve", sm[:, 48:56], ex[:, 0:8], -1.0, None, ALU.mult)
                yield
                yield
                for half in range(2):
                    P.sec("F:l2norm")
                    src = qk[:, half * 8:(half + 1) * 8, :]
                    P.tt("pool", hb[:, :].rearrange("p (c t) -> p c t", c=8), src, src, ALU.mult)
                    yield
                    for i in range(2):
                        rq = tmpA[:, i * 512:(i + 1) * 512]
                        P.mm(PA[i][:, :], onesb[:, :], hb[:, i * 512:(i + 1) * 512])
                        yield
                        if half == 0:
                            P.act(rq, PA[i][:, :], AF.Sqrt, bias=cst[:, 2:3], scale=128.0)
                            yield
                        else:
                            P.act(rq, PA[i][:, :], AF.Sqrt, bias=cst[:, 0:1], scale=1.0)
                            yield
                        P.recip(rq, rq)
                        yield
                        dst = qk[:, half * 8 + 4 * i:half * 8 + 4 * i + 4, :]
                        P.tt("dve", dst, dst, rq.rearrange("p (c t) -> p c t", c=4), ALU.mult)
                        yield
                    yield
                P.sec("F:transposes")
                P.copy("pool", hb[:, :].rearrange("p (c t) -> p c t", c=8), qk[:, 8:16, :])
                yield
                P.trs([(PT[:, j * 128:(j + 1) * 128], qk[:, 8 + j, :]) for j in range(8)], identb[:, :])
                yield
                P.copy("act", fb.ktm[:, :, :], PT[:, :].rearrange("p (h d) -> p h d", h=8))
                yield
                yield
                P.sec("F:transposes")
                P.trs([(PT[:, j * 128:(j + 1) * 128], vfm[:, j, :]) for j in range(8)], identb[:, :])
                yield
                P.copy("act", fb.vtm[:, :, :], PT[:, :].rearrange("p (h d) -> p h d", h=8))
                yield
                if typ == 0:
                    P.tt("pool", SB.rselg[:, :].rearrange("p (h s) -> p h s", h=8),
                         bc(gf[:, :], 2, 16), bc(blk16[:, :], 1, 8), ALU.mult)
                    yield
                    P.mm(PM[:, 128:256], onesf[:, :], SB.rselg[:, :])
                    yield
                    P.act(SB.gendsel[:, :], PM[:, 128:256], AF.Exp)
                    yield
                yield

            def head_gen(h, B, typ, SB, fb):
                mk = masks[typ]
                m_lev = 3 if typ == 0 else 7
                qk, hb, sm, gf, ex, ktm, vtm = fb.qk, fb.hb, fb.sm, fb.gf, fb.ex, fb.ktm, fb.vtm
                kT = qk[:, 8 + h, :]
                qT = qk[:, h, :]
                _st = ["H:decay"]
                P.sec(_st[0])
                P.act(B.L[:, :], mk[:, MSTT, :], AF.Identity, scale=gf[:, h:h + 1])
                yield
                P.mm(B.ps[3][:, :], B.L[:, :], mk[:, MINC, :])
                yield
                P.act(B.Dm[:, :], B.ps[3][:, :], AF.Exp)
                yield
                P.tt("dve", B.DmI[:, :], B.Dm[:, :], mk[:, MINC, :], ALU.mult)
                yield
                P.tt("dve", B.DmS[:, :], B.Dm[:, :], mk[:, MSTR, :], ALU.mult)
                yield
                P.mm(B.ps[0][:, :], kT, hb[:, h * 128:(h + 1) * 128])
                yield
                P.mm(B.ps[1][:, :], kT, qT)
                yield
                P.stt("dve", B.P[0][:, :], B.ps[0][:, :], sm[:, 16 + h:17 + h], B.DmS[:, :], ALU.mult, ALU.mult)
                yield
                P.tt("dve", B.attnT[:, :], B.ps[1][:, :], B.DmI[:, :], ALU.mult)
                yield
                yield
                _st[0] = "H:dbl"
                P.sec(_st[0])
                P.tr(B.ps[2][:, :], B.P[0][:, :], identf[:, :])
                yield
                P.copy("act", B.PT[0][:, :], B.ps[2][:, :])
                yield
                P.tt("dve", B.X[0][:, :], B.P[0][:, :], identf[:, :], ALU.add)
                yield
                yield
                P.sec(_st[0])
                if m_lev > 1:
                    P.mm(B.ps[0][:, :], B.P[0][:, :], B.PT[0][:, :])
                    yield
                    if m_lev > 2:
                        P.mm(B.ps[1][:, :], B.PT[0][:, :], B.P[0][:, :])
                        yield
                    P.copy("act", B.PT[1][:, :], B.ps[0][:, :])
                    yield
                    if m_lev > 2:
                        P.copy("act", B.P[1][:, :], B.ps[1][:, :])
                        yield
                    yield
                    P.sec(_st[0])
                xi = 0
                for j in range(1, m_lev):
                    cur, nx = j % 2, (j + 1) % 2
                    if j + 1 < m_lev:
                        P.mm(B.ps[0][:, :], B.P[cur][:, :], B.PT[cur][:, :])
                        yield
                        if j + 2 < m_lev:
                            P.mm(B.ps[1][:, :], B.PT[cur][:, :], B.P[cur][:, :])
                            yield
                    P.mm(B.ps[2][:, :], B.PT[cur][:, :], B.X[xi][:, :])
                    yield
                    if j + 1 < m_lev:
                        P.copy("act", B.PT[nx][:, :], B.ps[0][:, :])
                        yield
                        if j + 2 < m_lev:
                            P.copy("act", B.P[nx][:, :], B.ps[1][:, :])
                            yield
                    P.tt("dve", B.X[1 - xi][:, :], B.X[xi][:, :], B.ps[2][:, :], ALU.add)
                    yield
                    xi = 1 - xi
                    yield
                    P.sec(_st[0])
                _st[0] = "H:state"
                P.sec(_st[0])
                P.copy("act", B.TTb[:, :], B.X[xi][:, :])
                yield
                if typ == 1:
                    Sf_h, Sb_h = SB.Sf[h], SB.Sb[h]
                    P.mm(B.ps[3][:, :], kT, Sb_h[:, :])
                    yield
                else:
                    P.dma("sp", SB.S0f[:, :, :], st_gdn[:, h, :, :].rearrange("s d e -> d s e"))
                    yield
                    P.copy("act", SB.S0b[:, :, :], SB.S0f[:, :, :])
                    yield
                    for (dstT, srcT) in ((SB.kTm, kT), (SB.qTm, qT)):
                        base = dstT[:, :]
                        dap = bass.AP(base.tensor, base.offset, [[16 * 128, 128], [136, 16], [1, 8]])
                        P.op("pool", (lambda e, dap=dap, src=srcT.rearrange("p (s t) -> p s t", s=16):
                                      e.tensor_copy(out=dap, in_=src)), ins=[srcT], outs=[base])
                        yield
                    P.mmk(B.ps[3][:, :], [(SB.kTm[:, s * 128:(s + 1) * 128], SB.S0b[:, s, :]) for s in range(16)])
                    yield
                P.stt("dve", B.R[:, :], B.ps[3][:, :], sm[:, 48 + h:49 + h], vtm[:, h, :], ALU.mult, ALU.add)
                yield
                P.mm(B.ps[0][:, :], B.TTb[:, :], B.R[:, :])
                yield
                P.act(B.vn[:, :], B.ps[0][:, :], AF.Identity, scale=sm[:, 8 + h:9 + h])
                yield
                yield
                P.sec(_st[0])
                if typ == 1:
                    P.mm(B.ps[1][:, :], qT, Sb_h[:, :])
                    yield
                else:
                    P.mmk(B.ps[1][:, :], [(SB.qTm[:, s * 128:(s + 1) * 128], SB.S0b[:, s, :]) for s in range(16)])
                    yield
                P.mm(B.ps[2][:, :], B.attnT[:, :], B.vn[:, :])
                yield
                P.act(B.ot[:, :], B.ps[1][:, :], AF.Identity, scale=ex[:, h:h + 1])
                yield
                P.tt("dve", otm[:, h * 128:(h + 1) * 128], B.ot[:, :], B.ps[2][:, :], ALU.add)
                yield
                P.act(B.kout[:, :], ktm[:, h, :], AF.Identity, scale=ex[:, 8 + h:9 + h])
                yield
                yield
                P.sec(_st[0])
                if typ == 1:
                    P.mm(B.ps[3][:, :], B.kout[:, :], B.vn[:, :])
                    yield
                    P.stt("dve", Sf_h[:, :], Sf_h[:, :], ex[:, 16 + h:17 + h], B.ps[3][:, :], ALU.mult, ALU.add)
                    yield
                    P.copy("act", Sb_h[:, :], Sf_h[:, :])
                    yield
                else:
                    P.tt("dve", SB.koutm_all[:, :, :], bc(B.kout[:, :], 1, 16), bc(blk16[:, :], 2, 128), ALU.mult)
                    yield
                    for s in range(16):
                        P.mm(LPB[s // 4][:, (s % 4) * 128:(s % 4 + 1) * 128], SB.koutm_all[:, s, :], B.vn[:, :])
                    yield
                    for j4 in range(4):
                        S4 = SB.S0f[:, 4 * j4:4 * j4 + 4, :]
                        gsel = SB.gendsel[:, h * 16 + 4 * j4:h * 16 + 4 * j4 + 4]
                        P.tt("dve", S4, S4, bc(gsel, 2, 128), ALU.mult)
                        yield
                        P.tt("dve", S4, S4, LPB[j4][:, :].rearrange("p (s e) -> p s e", s=4), ALU.add)
                        yield
                    P.dma("sp", o_gdn_s[:, h, :, :].rearrange("s d e -> d s e"), SB.S0f[:, :, :])
                    yield
                yield

            def tail_gen(ti, fb):
                xt = fb.xt
                P.sec("T:onorm")
                P.dma("sp", mixed[:, 0:D], yssd_scr[ti])
                yield
                P.tt("pool", tmpT[:, :], otm[:, :], otm[:, :], ALU.mult)
                yield
                P.red("dve", ss8[:, 0:8], tmpT[:, :].rearrange("p (h d) -> p h d", h=8))
                yield
                P.act(ss8[:, 8:16], ss8[:, 0:8], AF.Sqrt, bias=cst[:, 0:1], scale=1.0 / 128)
                yield
                P.recip(ss8[:, 16:24], ss8[:, 8:16])
                yield
                o3 = otm[:, :].rearrange("p (h d) -> p h d", h=8)
                P.tt("dve", o3, o3, bc(ss8[:, 16:24], 2, 128), ALU.mult)
                yield
                P.tt("dve", o3, o3, bc(gnw[:, :], 1, 8), ALU.mult)
                yield
                P.tt("dve", mixed[:, D:2 * D], otm[:, :], fb.gs[:, :], ALU.mult)
                yield
                yield
                P.sec("T:outproj")
                for half in range(2):
                    P.trs([(PT[:, j * 128:(j + 1) * 128], mixed[:, (half * 8 + j) * 128:(half * 8 + j + 1) * 128])
                           for j in range(8)], identb[:, :])
                    yield
                    P.copy("act", mixedT[:, half * 8:(half + 1) * 8, :], PT[:, :].rearrange("p (c t) -> p c t", c=8))
                    yield
                yield
                P.sec("T:outproj")
                for nb in range(2):
                    P.mmk(PA[nb][:, :], [(mixedT[:, kc, :], wO.col(kc, nb * 512, 512)) for kc in range(16)])
                    yield
                    P.copy("act", tmpT[:, nb * 512:(nb + 1) * 512], PA[nb][:, :])
                    yield
                yield
                P.sec("T:resid")
                P.sqsum(otm[:, :], tmpT[:, :], sttT[:, 0:1])
                yield
                P.act(sttT[:, 1:2], sttT[:, 0:1], AF.Sqrt, bias=cst[:, 0:1], scale=1.0 / D)
                yield
                P.recip(sttT[:, 2:3], sttT[:, 1:2])
                yield
                P.stt("dve", tmpT[:, :], tmpT[:, :], sttT[:, 2:3], modv[:, 2, :], ALU.mult, ALU.mult)
                yield
                P.tt("dve", xt[:, :], tmpT[:, :], xt[:, :], ALU.add)
                yield
                P.dma("sp", x1_scr[ti], xt[:, :])
                yield
                yield

            def speed(g, k):
                while True:
                    for _ in range(k):
                        try:
                            next(g)
                        except StopIteration:
                            return
                    yield

            def run_rr(gens):
                alive = list(gens)
                while alive:
                    nxt = []
                    for gn in alive:
                        try:
                            next(gn)
                            nxt.append(gn)
                        except StopIteration:
                            pass
                    alive = nxt

            def back(ti, typ, SB, fb, nl, side, with_tail=True):
                side = [side] if side is not None else []
                for h0_ in range(0, 8, nl):
                    gens = [head_gen(h0_ + i, lanes[i], typ, SB, fb) for i in range(min(nl, 8 - h0_))]
                    alive = gens + side
                    while any(g in alive for g in gens):
                        nxt = []
                        for gn in alive:
                            try:
                                next(gn)
                                nxt.append(gn)
                            except StopIteration:
                                if gn in side:
                                    side = []
                        alive = nxt
                if with_tail:
                    run_rr([tail_gen(ti, fb)] + side)
                else:
                    run_rr(side)

            class SBufs:
                pass

            with contextlib.ExitStack() as st2s:
                SB = SBufs()
                SB.histS = sb(st2s, "histSg", [128, 24, 16, 3])
                SB.xinS_l = [sb(st2s, "xinS2_%d" % i, [128, 4, 16, 11]) for i in range(2)]
                SB.S0f = sb(st2s, "S0f", [128, 16, 128])
                SB.S0b = sb(st2s, "S0b", [128, 16, 128], BF16)
                SB.kTm = sb(st2s, "kTm", [128, 16 * 128], BF16)
                SB.qTm = sb(st2s, "qTm", [128, 16 * 128], BF16)
                SB.koutm_all = sb(st2s, "koutm_all", [128, 16, 128], BF16)
                SB.rselg = sb(st2s, "rselg", [128, 128])
                SB.gendsel = sb(st2s, "gendsel", [128, 128])
                P.dma("sp", SB.histS[:, :, :, :], hist_gdn[:, :, :, :])
                P.memset("pool", SB.kTm[:, :], 0.0)
                P.memset("pool", SB.qTm[:, :], 0.0)
                run_rr([front_gen(0, 0, SB, fbs[0])])
                back(0, 0, SB, fbs[0], 1, None)
                P.barrier()
                P.emit()
            with contextlib.ExitStack() as st2p:
                SB = SBufs()
                SB.Sf = [sb(st2p, "Sf%d" % h, [128, 128]) for h in range(8)]
                SB.Sb = [sb(st2p, "Sb%d" % h, [128, 128], BF16) for h in range(8)]
                for ln in range(1, P2_NL):
                    make_lane(ln, st2p)
                fbs.append(make_fb(1, st2p))
                for h in range(8):
                    P.memset("pool", SB.Sf[h][:, :], 0.0)
                    P.memset("pool", SB.Sb[h][:, :], 0.0)
                run_rr([front_gen(1, 1, SB, fbs[1])])
                pending_tail = None
                for ti in range(1, NT):
                    parts = []
                    if pending_tail is not None:
                        parts.append(pending_tail)
                    if ti + 1 < NT:
                        parts.append(speed(front_gen(ti + 1, 1, SB, fbs[(ti + 1) % 2]), 2))
                    side = itertools.chain(*parts) if parts else None
                    back(ti, 1, SB, fbs[ti % 2], P2_NL, side, with_tail=False)
                    pending_tail = tail_gen(ti, fbs[ti % 2])
                run_rr([pending_tail])
                for h in range(8):
                    P.dma("sp", o_gdn_p[h * 128:(h + 1) * 128, :], SB.Sf[h][:, :])
                P.barrier()
                P.emit()
        if STOP_AFTER >= 3:
          with contextlib.ExitStack() as st3:
            PB = [st3.enter_context(nc.psum_tensor("pb%d" % i, [128, 512], F32)) for i in range(7)]
            gb = [0, 512, 1024, 1536, 2048, 2560, 2816]
            wG = WBlocks(st3, "wG", 8, gb + [DFF + b for b in gb[1:]])
            w_gu_v = w_gu.rearrange("(kc p) n -> p kc n", p=128)
            wG.load(w_gu_v, 0, order=[j for i in range(6) for j in (i, 6 + i)])
            wD = WBlocks(st3, "wD", 22, [0, 512, 1024])
            w_dn_v = w_dn.rearrange("(kc p) n -> p kc n", p=128)
            wD.load(w_dn_v, 0)
            modv = sb(st3, "modv3", [128, 3, D])
            xt = [sb(st3, "xt3_%d" % i, [128, D]) for i in range(2)]
            tmpA = [sb(st3, "tmpA3_%d" % i, [128, D]) for i in range(2)]
            tmpB = sb(st3, "tmpB3", [128, D])
            hb = [sb(st3, "hb3_%d" % i, [128, D], BF16) for i in range(2)]
            hT = [sb(st3, "hT3_%d" % i, [128, 8, 128], BF16) for i in range(2)]
            stt_ = [sb(st3, "stt3_%d" % i, [128, 4]) for i in range(2)]
            sg = [sb(st3, "sg%d" % i, [128, 512]) for i in range(2)]
            hid = sb(st3, "hid", [128, DFF], BF16)
            hidT = sb(st3, "hidT", [128, 22, 128], BF16)

            def ffn_gen(ti):
                typ = 0 if ti == 0 else 1
                pr = ti % 2
                x_, tA, hb_, hT_, st_ = xt[pr], tmpA[pr], hb[pr], hT[pr], stt_[pr]
                if ti <= 1:
                    P.dma("sp", modv[:, :, :], modscr[typ].rearrange("p (s d) -> p s d", s=6)[:, 3:6, :])
                P.dma("sp", x_[:, :], x1_scr[ti])
                yield
                yield from norm_to_T_g(x_, modv, 1, 0, tA, hb_, hT_, st_)
                for i in range(6):
                    w = 512 if i < 5 else 256
                    pg_, pu_ = PB[(2 * i) % 6], PB[(2 * i + 1) % 6]
                    P.mmk(pg_[:, 0:w], [(hT_[:, kc, :], wG.col(kc, i * 512, w)) for kc in range(8)])
                    yield
                    P.mmk(pu_[:, 0:w], [(hT_[:, kc, :], wG.col(kc, DFF + i * 512, w)) for kc in range(8)])
                    yield
                    s_ = sg[i % 2]
                    P.act(s_[:, 0:w], pg_[:, 0:w], AF.Silu)
                    yield
                    P.tt("dve", hid[:, i * 512:i * 512 + w], s_[:, 0:w], pu_[:, 0:w], ALU.mult)
                    yield
                yield "HALF"
                for grp in range(3):
                    n = 8 if grp < 2 else 6
                    P.trs([(PT[:, j * 128:(j + 1) * 128], hid[:, (grp * 8 + j) * 128:(grp * 8 + j + 1) * 128])
                           for j in range(n)], identb[:, :])
                    yield
                    P.copy("act", hidT[:, grp * 8:grp * 8 + n, :],
                           PT[:, 0:n * 128].rearrange("p (c t) -> p c t", c=n))
                    yield
                for nb in range(2):
                    P.mmk(PB[6][:, :], [(hidT[:, kc, :], wD.col(kc, nb * 512, 512)) for kc in range(22)])
                    yield
                    P.copy("act", tA[:, nb * 512:(nb + 1) * 512], PB[6][:, :])
                    yield
                P.sqsum(tmpB[:, :], tA[:, :], st_[:, 0:1])
                yield
                P.act(st_[:, 1:2], st_[:, 0:1], AF.Sqrt, bias=cst[:, 0:1], scale=1.0 / D)
                yield
                P.recip(st_[:, 2:3], st_[:, 1:2])
                yield
                P.stt("dve", tA[:, :], tA[:, :], st_[:, 2:3], modv[:, 2, :], ALU.mult, ALU.mult)
                yield
                P.tt("dve", x_[:, :], tA[:, :], x_[:, :], ALU.add)
                yield
                P.dma("sp", y_all[ti], x_[:, :])
                yield

            run_gen(ffn_gen(0))
            nxt = 2
            cur = ffn_gen(1)
            young = None
            while cur is not None:
                try:
                    v = next(cur)
                    if v == "HALF" and young is None and nxt < NT:
                        young = ffn_gen(nxt)
                        nxt += 1
                except StopIteration:
                    cur, young = young, None
                    if cur is None and nxt < NT:
                        cur = ffn_gen(nxt)
                        nxt += 1
                    continue
                if young is not None:
                    try:
                        v2 = next(young)
                        if v2 == "HALF":
                            pass
                    except StopIteration:
                        young = None
            P.barrier()
            P.emit()
    return nc


def _prep_inputs(inp):
    f = lambda a: np.ascontiguousarray(np.asarray(a, dtype=np.float32))
    xp, xs = f(inp["x_prompt"]), f(inp["x_sample"])
    cp, cs = f(inp["c_prompt"]), f(inp["c_sample"])
    bcast = lambda v: np.ascontiguousarray(np.broadcast_to(np.asarray(v, np.float32).reshape(1, -1), (128, v.size)))
    shared = {}
    shared["w_ada"] = f(inp["w_ada"][0])
    shared["b_ada_b"] = bcast(inp["b_ada"][0])
    shared["normvecs"] = np.ascontiguousarray(np.stack(
        [bcast(inp[k][0]) for k in ("norm_mix_pre", "norm_mix_post", "norm_ffn_pre", "norm_ffn_post")], axis=1))
    shared["w_in"] = f(inp["w_in"][0])
    shared["w_out"] = f(inp["w_out"][0])
    shared["w_gu"] = f(inp["w_gate_up"][0])
    shared["w_dn"] = f(inp["w_down"][0])
    scw = np.concatenate([f(inp["ssd_conv_w"][0]), f(inp["ssd_conv_b"][0])[None, :]], axis=0)
    shared["ssd_cw"] = np.ascontiguousarray(scw.reshape(5, 12, 128).transpose(2, 1, 0))
    shared["gdn_cw"] = np.ascontiguousarray(f(inp["gdn_conv_w"][0]).reshape(4, 24, 128).transpose(2, 1, 0))
    sv = np.concatenate([f(inp["ssd_dt_bias"][0]), f(inp["ssd_A_log"][0]), f(inp["ssd_D"][0]),
                         f(inp["gdn_dt_bias"][0]), f(inp["gdn_A_log"][0])])
    shared["smallv"] = bcast(sv)
    shared["ssd_nw"] = bcast(inp["ssd_norm_w"][0])
    shared["gdn_nw"] = bcast(inp["gdn_norm_w"][0])
    shared["ident"] = np.eye(128, dtype=np.float32)
    idx = np.arange(128)
    m = np.zeros((2, 128, 4, 128), np.float32)
    for typ, bs in ((0, 8), (1, 128)):
        same = (idx[:, None] // bs) == (idx[None, :] // bs)
        m[typ, :, 0, :] = same & (idx[:, None] > idx[None, :])
        m[typ, :, 1, :] = same & (idx[:, None] <= idx[None, :])
        m[typ, :, 2, :] = same & (idx[:, None] < idx[None, :])
        m[typ, :, 3, :] = same
    shared["masks"] = m
    shared["blk16"] = np.ascontiguousarray(((idx[:, None] // 8) == np.arange(16)[None, :]).astype(np.float32))
    maps = []
    for i in range(NCORES):
        d = dict(shared)
        sl = slice(16 * i, 16 * (i + 1))
        d["xs_all"] = np.ascontiguousarray(np.concatenate(
            [xs[sl].reshape(1, 128, D), xp[i].reshape(16, 128, D)], axis=0))
        d["cexp"] = np.ascontiguousarray(np.stack(
            [np.repeat(cs[sl], 8, axis=0), np.broadcast_to(cp[i][None, :], (128, D))], axis=0))
        d["st_ssd"] = np.ascontiguousarray(f(inp["state_ssd"][0, sl]).reshape(16, 1024, 128))
        hs = f(inp["state_ssd_conv"][0, sl])
        d["hist_ssd"] = np.ascontiguousarray(hs.reshape(16, 3, 12, 128).transpose(3, 2, 0, 1))
        d["st_gdn"] = np.ascontiguousarray(f(inp["state_gdn"][0, sl]))
        hg = f(inp["state_gdn_conv"][0, sl])
        d["hist_gdn"] = np.ascontiguousarray(hg.reshape(16, 3, 24, 128).transpose(3, 2, 0, 1))
        maps.append(d)
    return maps


def kernel(**inp):
    maps = _prep_inputs(inp)
    nc = build_program()
    res = run_bass_kernel_spmd(nc, maps, core_ids=list(range(NCORES)))
    R = res.results
    cat = lambda k: np.stack([np.asarray(r[k]) for r in R], axis=0)
    y_all = cat("y_all")
    y_prompt = y_all[:, 1:].reshape(8, 2048, D)
    y_sample = y_all[:, 0].reshape(128, 8, D)
    ssd_p = cat("o_ssd_p").reshape(1, 8, 16, 64, 128)
    ssdc_p = cat("o_ssdc_p").reshape(1, 8, 3, 1536)
    gdn_p = cat("o_gdn_p").reshape(1, 8, 8, 128, 128)
    gdnc_p = cat("o_gdnc_p").reshape(1, 8, 3, 3072)
    ssd_s = cat("o_ssd_s").reshape(1, 128, 16, 64, 128)
    ssdc_s = cat("o_ssdc_s").reshape(1, 128, 3, 1536)
    gdn_s = cat("o_gdn_s").reshape(1, 128, 8, 128, 128)
    gdnc_s = cat("o_gdnc_s").reshape(1, 128, 3, 3072)
    outs = (y_prompt, y_sample, ssd_p, ssdc_p, gdn_p, gdnc_p, ssd_s, ssdc_s, gdn_s, gdnc_s)
    return tuple(np.ascontiguousarray(o, dtype=np.float32) for o in outs)
```

```python
import contextlib
import itertools
import os
import numpy as np
import concourse.bass as bass
import concourse.mybir as mybir
from concourse.bass_utils import run_bass_kernel_spmd

F32 = mybir.dt.float32
BF16 = mybir.dt.bfloat16
AF = mybir.ActivationFunctionType
ALU = mybir.AluOpType
AX = mybir.AxisListType

NCORES = 8
D = 1024
NT = 17
DFF = 2816
P2_MODE = 0
ANNOTATE = bool(int(os.environ.get('K_ANN', '0')))
SKIP1 = False
P2_STAGE = 99
P2_HSTEP = 99
P2_SKIP = ()
P2_NL = 4
P2_TILES = 16
STOP_AFTER = int(os.environ.get('K_STOP', '99'))


class Prog:
    ENGS = ("pe", "act", "dve", "pool", "sp")

    def __init__(self, nc, stack):
        self.nc = nc
        self.sems = {}
        for e in self.ENGS:
            self.sems[e] = stack.enter_context(nc.semaphore("s_" + e))
        self.dsems = {"sp": [], "pool": [], "act": []}
        for q, n in (("sp", 8), ("pool", 4), ("act", 2)):
            for j in range(n):
                nm = "d_%s%d" % (q, j)
                self.sems[nm] = stack.enter_context(nc.semaphore(nm))
                self.dsems[q].append(nm)
        self.cnt = {k: 0 for k in self.sems}
        self.known = {e: {} for e in self.ENGS}
        self.lastw = {}
        self.readers = {}
        self.ops = {e: [] for e in self.ENGS}
        self.rr = {"sp": 0, "pool": 0, "act": 0}
        self.nops = 0
        self.tag = "init"

    def sec(self, name):
        self.tag = name

    fine = {}

    def _keys(self, aps):
        ks = []
        for a in aps:
            if a is None or isinstance(a, (int, float)):
                continue
            if isinstance(a, str):
                ks.append(a)
                continue
            nm = a.name
            if nm in self.fine:
                row, gr = self.fine[nm]
                nm = "%s:%d" % (nm, (int(a.offset) % row) // gr)
            ks.append(nm)
        return ks

    def _deps(self, eng, r, w):
        need = {}

        def add(c):
            if c is None:
                return
            s, v = c
            if s == "pe" and eng == "pe":
                return
            if need.get(s, 0) < v:
                need[s] = v

        for k in r:
            add(self.lastw.get(k))
        for k in w:
            add(self.lastw.get(k))
            for c in self.readers.get(k, {}).items():
                add(c)
        waits = []
        kn = self.known[eng]
        for s, v in need.items():
            if kn.get(s, 0) < v:
                kn[s] = v
                waits.append((s, v))
        return waits

    def _commit(self, c, r, w):
        for k in w:
            self.lastw[k] = c
            self.readers[k] = {}
        for k in r:
            d = self.readers.setdefault(k, {})
            if d.get(c[0], 0) < c[1]:
                d[c[0]] = c[1]

    def op(self, eng, fn, ins=(), outs=()):
        r = self._keys(ins)
        w = self._keys(outs)
        w = w + [k for k in r if k.startswith("ps") or k.startswith("pa") or k.startswith("pm")
                 or k.startswith("lpb") or k.startswith("pb")]
        waits = self._deps(eng, r, w)
        self.cnt[eng] += 1
        self.ops[eng].append((waits, fn, eng, 1, self.tag))
        self._commit((eng, self.cnt[eng]), r, w)
        self.nops += 1

    def dma(self, q, out, in_, extra_ins=(), extra_outs=()):
        r = self._keys([in_] + list(extra_ins))
        w = self._keys([out] + list(extra_outs))
        waits = self._deps(q, r, w)
        sems = self.dsems[q]
        j = self.rr[q]
        self.rr[q] = (j + 1) % len(sems)
        nm = sems[j]
        prev = self.cnt[nm]
        if prev > 0 and self.known[q].get(nm, 0) < prev:
            self.known[q][nm] = prev
            waits.append((nm, prev))
        self.cnt[nm] += 16
        self.ops[q].append((waits, lambda e: e.dma_start(out=out, in_=in_), nm, 16, self.tag))
        self._commit((nm, self.cnt[nm]), r, w)
        self.nops += 1

    def barrier(self):
        for e in self.ENGS:
            waits = []
            for s, v in self.cnt.items():
                if v > 0 and self.known[e].get(s, 0) < v:
                    self.known[e][s] = v
                    waits.append((s, v))
            self.ops[e].append((waits, None, None, 0, self.tag))
        self.lastw = {}
        self.readers = {}

    def emit(self):
        nc = self.nc
        with nc.Block() as block:
            for ename, deco in (("sp", block.sync), ("act", block.scalar), ("dve", block.vector),
                                ("pool", block.gpsimd), ("pe", block.tensor)):
                ops = self.ops[ename]

                def body(e, ops=ops):
                    for waits, fn, sname, inc, tag in ops:
                        for s, v in waits:
                            e.wait_ge(self.sems[s], v)
                        if fn is not None:
                            ins = fn(e)
                            ins.then_inc(self.sems[sname], inc)
                            if ANNOTATE:
                                ins.annotate(tag)

                deco(body)
                self.ops[ename] = []

    def mm(self, out, lhsT, rhs, start=True, stop=True):
        self.op("pe", lambda e: e.matmul(out, lhsT=lhsT, rhs=rhs, start=start, stop=stop),
                ins=[lhsT, rhs], outs=[out])

    def mmk(self, out, pairs):
        n = len(pairs)

        def fn(e):
            ins = None
            for i, (l, r) in enumerate(pairs):
                ins = e.matmul(out, lhsT=l, rhs=r, start=(i == 0), stop=(i == n - 1))
            return ins

        self.op("pe", fn, ins=[x for p in pairs for x in p], outs=[out])

    def tr(self, out, in_, ident):
        self.op("pe", lambda e: e.transpose(out, in_, ident), ins=[in_, ident], outs=[out])

    def trs(self, items, ident):
        def fn(e):
            ins = None
            for o, i in items:
                ins = e.transpose(o, i, ident)
            return ins
        self.op("pe", fn, ins=[i for _, i in items] + [ident], outs=[o for o, _ in items])

    def act(self, out, in_, func, bias=None, scale=None):
        kw = {}
        if bias is not None:
            kw["bias"] = bias
        if scale is not None:
            kw["scale"] = scale
        self.op("act", lambda e: e.activation(out=out, in_=in_, func=func, **kw),
                ins=[in_, bias, scale], outs=[out])

    def sqsum(self, junk, in_, accum):
        self.op("act", lambda e: e.activation(out=junk, in_=in_, func=AF.Square, accum_out=accum),
                ins=[in_], outs=[junk, accum])

    def tt(self, eng, out, in0, in1, op):
        self.op(eng, lambda e: e.tensor_tensor(out=out, in0=in0, in1=in1, op=op), ins=[in0, in1], outs=[out])

    def ts(self, eng, out, in0, s1, s2, op0, op1=None):
        if op1 is None:
            self.op(eng, lambda e: e.tensor_scalar(out=out, in0=in0, scalar1=s1, scalar2=None, op0=op0),
                    ins=[in0, s1], outs=[out])
        else:
            self.op(eng, lambda e: e.tensor_scalar(out=out, in0=in0, scalar1=s1, scalar2=s2, op0=op0, op1=op1),
                    ins=[in0, s1, s2], outs=[out])

    def stt(self, eng, out, in0, scalar, in1, op0, op1):
        self.op(eng, lambda e: e.scalar_tensor_tensor(out=out, in0=in0, scalar=scalar, in1=in1, op0=op0, op1=op1),
                ins=[in0, scalar, in1], outs=[out])

    def copy(self, eng, out, in_):
        if eng == "act":
            self.op(eng, lambda e: e.copy(out=out, in_=in_), ins=[in_], outs=[out])
        else:
            self.op(eng, lambda e: e.tensor_copy(out=out, in_=in_), ins=[in_], outs=[out])

    def red(self, eng, out, in_):
        self.op(eng, lambda e: e.tensor_reduce(out=out, in_=in_, axis=AX.X, op=ALU.add), ins=[in_], outs=[out])

    def recip(self, out, in_):
        self.op("dve", lambda e: e.reciprocal(out=out, in_=in_), ins=[in_], outs=[out])

    def memset(self, eng, ap, val):
        self.op(eng, lambda e: e.memset(ap, val), ins=[], outs=[ap])


def bc(ap, axis, n):
    u = ap.unsqueeze(axis)
    shp = list(u.shape)
    shp[axis] = n
    return u.broadcast_to(shp)


def build_program():
    nc = bass.Bass("TRN2", target_bir_lowering=False)

    def din(name, shape, dt=F32):
        return nc.dram_tensor(name, list(shape), dt, kind="ExternalInput").ap()

    def dout(name, shape, dt=F32):
        return nc.dram_tensor(name, list(shape), dt, kind="ExternalOutput").ap()

    def dscr(name, shape, dt=F32):
        return nc.dram_tensor(name, list(shape), dt).ap()

    xs_all = din("xs_all", [NT, 128, D])
    cexp = din("cexp", [2, 128, D])
    w_ada = din("w_ada", [D, 6 * D])
    b_ada_b = din("b_ada_b", [128, 6 * D])
    normvecs = din("normvecs", [128, 4, D])
    w_in = din("w_in", [D, 6688])
    w_out = din("w_out", [2 * D, D])
    w_gu = din("w_gu", [D, 2 * DFF])
    w_dn = din("w_dn", [DFF, D])
    ssd_cw = din("ssd_cw", [128, 12, 5])
    gdn_cw = din("gdn_cw", [128, 24, 4])
    smallv = din("smallv", [128, 64])
    ssd_nw = din("ssd_nw", [128, D])
    gdn_nw = din("gdn_nw", [128, 128])
    st_ssd = din("st_ssd", [16, 1024, 128])
    hist_ssd = din("hist_ssd", [128, 12, 16, 3])
    st_gdn = din("st_gdn", [16, 8, 128, 128])
    hist_gdn = din("hist_gdn", [128, 24, 16, 3])
    ident_d = din("ident", [128, 128])
    masks_d = din("masks", [2, 128, 4, 128])
    blk16_d = din("blk16", [128, 16])

    y_all = dout("y_all", [NT, 128, D])
    o_ssd_p = dout("o_ssd_p", [1024, 128])
    o_ssdc_p = dout("o_ssdc_p", [3, 1536])
    o_gdn_p = dout("o_gdn_p", [1024, 128])
    o_gdnc_p = dout("o_gdnc_p", [3, 3072])
    o_ssd_s = dout("o_ssd_s", [16, 1024, 128])
    o_ssdc_s = dout("o_ssdc_s", [48, 1536])
    o_gdn_s = dout("o_gdn_s", [16, 8, 128, 128])
    o_gdnc_s = dout("o_gdnc_s", [48, 3072])

    modscr = dscr("modscr", [2, 128, 6 * D])
    yssd_scr = dscr("yssd_scr", [NT, 128, D], BF16)
    x1_scr = dscr("x1_scr", [NT, 128, D])

    with contextlib.ExitStack() as gstack:
        P = Prog(nc, gstack)

        def sb(stack, name, shape, dt=F32):
            return stack.enter_context(nc.sbuf_tensor(name, list(shape), dt))

        class WBlocks:
            def __init__(self, stk, name, nk, bounds):
                self.bounds = bounds
                self.t = [sb(stk, "%s_%d" % (name, j), [128, nk, bounds[j + 1] - bounds[j]], BF16)
                          for j in range(len(bounds) - 1)]

            def load(self, dview, col_off, order=None):
                for j in (order if order is not None else range(len(self.t))):
                    b0, b1 = self.bounds[j], self.bounds[j + 1]
                    P.dma("pool", self.t[j][:, :, :], dview[:, :, col_off + b0:col_off + b1])

            def col(self, kc, c0, w):
                for j in range(len(self.t)):
                    if self.bounds[j] <= c0 and c0 + w <= self.bounds[j + 1]:
                        return self.t[j][:, kc, c0 - self.bounds[j]:c0 - self.bounds[j] + w]
                raise ValueError("column range straddles weight blocks")

        PT = gstack.enter_context(nc.psum_tensor("pst", [128, 1024], BF16))
        ps1 = contextlib.ExitStack()
        PS = [ps1.enter_context(nc.psum_tensor("ps%d" % i, [128, 512], F32)) for i in range(7)]

        identf = sb(gstack, "identf", [128, 128])
        identb = sb(gstack, "identb", [128, 128], BF16)
        onesf = sb(gstack, "onesf", [128, 128])
        onesb = sb(gstack, "onesb", [128, 128], BF16)
        cst = sb(gstack, "cst", [128, 4])
        masks = [sb(gstack, "masks%d" % t, [128, 4, 128]) for t in range(2)]
        blk16 = sb(gstack, "blk16s", [128, 16])
        smv = sb(gstack, "smv", [128, 64])
        negA = sb(gstack, "negA", [128, 24])
        P.dma("sp", identf[:, :], ident_d[:, :])
        P.dma("pool", identb[:, :], ident_d[:, :])
        for t in range(2):
            P.dma("sp", masks[t][:, :, :], masks_d[t])
        P.dma("sp", blk16[:, :], blk16_d[:, :])
        P.dma("sp", smv[:, :], smallv[:, :])
        P.memset("dve", onesf[:, :], 1.0)
        P.memset("dve", onesb[:, :], 1.0)
        P.memset("dve", cst[:, 0:1], 1e-6)
        P.memset("dve", cst[:, 1:2], 1.0)
        P.memset("dve", cst[:, 2:3], 128e-6)
        P.memset("dve", cst[:, 3:4], 0.0)
        P.act(negA[:, 0:16], smv[:, 16:32], AF.Exp)
        P.act(negA[:, 16:24], smv[:, 56:64], AF.Exp)
        P.ts("dve", negA[:, :], negA[:, :], -1.0, None, ALU.mult)

        MSTT, MINC, MSTR, MBLK = 0, 1, 2, 3

        with contextlib.ExitStack() as st0:
            ct = sb(st0, "ct", [128, D])
            cb = sb(st0, "cb", [128, D], BF16)
            cT = [sb(st0, "cT%d" % t, [128, 8, 128], BF16) for t in range(2)]
            modt = [sb(st0, "modt%d" % t, [128, 6 * D]) for t in range(2)]
            nv = sb(st0, "nv", [128, 4, D])
            wa = [sb(st0, "wa%d" % i, [128, 8, 512], BF16) for i in range(2)]
            bb = [sb(st0, "bb%d" % i, [128, 512]) for i in range(2)]
            P.dma("sp", nv[:, :, :], normvecs[:, :, :])
            for t in range(2):
                P.dma("sp", ct[:, :], cexp[t])
                P.act(cb[:, :], ct[:, :], AF.Silu)
                P.trs([(PT[:, c * 128:(c + 1) * 128], cb[:, c * 128:(c + 1) * 128]) for c in range(8)], identb[:, :])
                P.copy("dve", cT[t][:, :, :], PT[:, :].rearrange("p (c t) -> p c t", c=8))
            wv = w_ada.rearrange("(kc p) n -> p kc n", p=128)
            for j in range(12):
                P.dma("pool", wa[j % 2][:, :, :], wv[:, :, j * 512:(j + 1) * 512])
                P.dma("sp", bb[j % 2][:, :], b_ada_b[:, j * 512:(j + 1) * 512])
                for t in range(2):
                    ps = PS[(2 * j + t) % 4]
                    P.mmk(ps[:, :], [(cT[t][:, kc, :], wa[j % 2][:, kc, :]) for kc in range(8)])
                    P.tt("dve", modt[t][:, j * 512:(j + 1) * 512], ps[:, :], bb[j % 2][:, :], ALU.add)
            for t in range(2):
                m = modt[t]
                P.stt("dve", m[:, D:2 * D], m[:, D:2 * D], 1.0, nv[:, 0, :], ALU.add, ALU.mult)
                P.tt("dve", m[:, 2 * D:3 * D], m[:, 2 * D:3 * D], nv[:, 1, :], ALU.mult)
                P.stt("dve", m[:, 4 * D:5 * D], m[:, 4 * D:5 * D], 1.0, nv[:, 2, :], ALU.add, ALU.mult)
                P.tt("dve", m[:, 5 * D:6 * D], m[:, 5 * D:6 * D], nv[:, 3, :], ALU.mult)
                P.dma("sp", modscr[t], m[:, :])
            P.barrier()
            P.emit()

        def norm_to_T(xt, modv, sidx, hidx, tmpA, hb, hT, stt_):
            P.sec("norm_to_T")
            P.sqsum(tmpA[:, :], xt[:, :], stt_[:, 0:1])
            P.act(stt_[:, 1:2], stt_[:, 0:1], AF.Sqrt, bias=cst[:, 0:1], scale=1.0 / D)
            P.recip(stt_[:, 2:3], stt_[:, 1:2])
            P.stt("dve", tmpA[:, :], xt[:, :], stt_[:, 2:3], modv[:, sidx, :], ALU.mult, ALU.mult)
            P.tt("dve", hb[:, :], tmpA[:, :], modv[:, hidx, :], ALU.add)
            P.trs([(PT[:, c * 128:(c + 1) * 128], hb[:, c * 128:(c + 1) * 128]) for c in range(8)], identb[:, :])
            P.copy("act", hT[:, :, :], PT[:, :].rearrange("p (c t) -> p c t", c=8))

        def norm_to_T_g(xt, modv, sidx, hidx, tmpA, hb, hT, stt_):
            P.sec("norm_to_T")
            P.sqsum(tmpA[:, :], xt[:, :], stt_[:, 0:1])
            yield
            P.act(stt_[:, 1:2], stt_[:, 0:1], AF.Sqrt, bias=cst[:, 0:1], scale=1.0 / D)
            yield
            P.recip(stt_[:, 2:3], stt_[:, 1:2])
            yield
            P.stt("dve", tmpA[:, :], xt[:, :], stt_[:, 2:3], modv[:, sidx, :], ALU.mult, ALU.mult)
            yield
            P.tt("dve", hb[:, :], tmpA[:, :], modv[:, hidx, :], ALU.add)
            yield
            P.trs([(PT[:, c * 128:(c + 1) * 128], hb[:, c * 128:(c + 1) * 128]) for c in range(8)], identb[:, :])
            yield
            P.copy("act", hT[:, :, :], PT[:, :].rearrange("p (c t) -> p c t", c=8))
            yield

        def small_scan_terms(af, nh, typ, pss, ex):
            mk = masks[typ]
            P.mm(pss[:, 0:nh], mk[:, MINC, :], af)
            P.mm(pss[:, nh:2 * nh], mk[:, MSTT, :], af)
            P.mm(pss[:, 2 * nh:3 * nh], mk[:, MBLK, :], af)
            P.act(ex[:, 0:3 * nh], pss[:, 0:3 * nh], AF.Exp)

        def small_scan_terms_g(af, nh, typ, pss, ex):
            mk = masks[typ]
            P.mm(pss[:, 0:nh], mk[:, MINC, :], af)
            yield
            P.mm(pss[:, nh:2 * nh], mk[:, MSTT, :], af)
            yield
            P.mm(pss[:, 2 * nh:3 * nh], mk[:, MBLK, :], af)
            yield
            P.act(ex[:, 0:3 * nh], pss[:, 0:3 * nh], AF.Exp)
            yield

        def proj_conv_gen(typ, ngroups, wt, wcol0, hT, psb, xinS_l, xinP_l, histS, histP, cwt, has_bias, accs, dst_fn):
            items = []

            def slot(i):
                n = len(items)
                if 0 <= i < n:
                    it = items[i]
                    if has_bias:
                        P.act(it[0], it[2][0], AF.Identity, bias=cwt[:, it[4], 4:5], scale=cwt[:, it[4], 0:1])
                    else:
                        P.act(it[0], it[2][0], AF.Identity, scale=cwt[:, it[4], 0:1])
                    yield
                if 0 <= i - 1 < n:
                    it = items[i - 1]
                    for k in range(1, 4):
                        P.stt("dve", it[0], it[2][k], cwt[:, it[4], k:k + 1], it[0], ALU.mult, ALU.add)
                        yield
                if 0 <= i - 2 < n:
                    it = items[i - 2]
                    P.act(it[3], it[1][:, :], AF.Silu)
                    yield

            for g in range(ngroups):
                ps = psb[g % 2]
                for j in range(4):
                    c = 4 * g + j
                    P.mmk(ps[:, j * 128:(j + 1) * 128],
                          [((wt.col(kc, wcol0 + c * 128, 128) if hasattr(wt, "col")
                             else wt[:, kc, wcol0 + c * 128:wcol0 + (c + 1) * 128]), hT[:, kc, :]) for kc in range(8)])
                    yield
                if typ == 0:
                    xin = xinS_l[g % 2]
                    P.copy("pool", xin[:, :, :, 0:3], histS[:, 4 * g:4 * g + 4, :, :])
                    yield
                    P.copy("act", xin[:, :, :, 3:11], ps[:, :].rearrange("p (j s t) -> p j s t", j=4, s=16))
                    yield
                else:
                    xin = xinP_l[g % 2]
                    P.copy("pool", xin[:, :, 0:3], histP[:, 4 * g:4 * g + 4, :])
                    yield
                    P.copy("act", xin[:, :, 3:131], ps[:, :].rearrange("p (j t) -> p j t", j=4))
                    yield
                    P.copy("pool", histP[:, 4 * g:4 * g + 4, :], xin[:, :, 128:131])
                    yield
                for j in range(4):
                    c = 4 * g + j
                    a_ = accs[c % 4]
                    if typ == 0:
                        av = a_[:, :].rearrange("p (s t) -> p s t", s=16)
                        sh = [xin[:, j, :, k:k + 8] for k in range(4)]
                    else:
                        av = a_[:, :]
                        sh = [xin[:, j, k:k + 128] for k in range(4)]
                    items.append((av, a_, sh, dst_fn(c), c))
                    yield from slot(c)
            yield from slot(4 * ngroups)
            yield from slot(4 * ngroups + 1)

        def run_gen(g):
            for _ in g:
                pass

        w_in_v = w_in.rearrange("(kc p) n -> p kc n", p=128)
        if STOP_AFTER >= 1 and not SKIP1:
          with contextlib.ExitStack() as st1:
            wI = WBlocks(st1, "wI", 8, [0, 512, 1024, 1536, 2048, 2560, 2576])
            wI.load(w_in_v, 0, order=[2, 3, 4, 0, 1, 5])
            modv = sb(st1, "modv", [128, 3, D])
            cw = sb(st1, "cw", [128, 12, 5])
            nw = sb(st1, "ssdnw", [128, D])
            histS = sb(st1, "histS", [128, 12, 16, 3])
            histP = sb(st1, "histP", [128, 12, 3])
            P.dma("sp", cw[:, :, :], ssd_cw[:, :, :])
            P.dma("sp", nw[:, :], ssd_nw[:, :])
            P.dma("sp", histS[:, :, :, :], hist_ssd[:, :, :, :])
            P.memset("pool", histP[:, :, :], 0.0)
            xt = [sb(st1, "xt%d" % i, [128, D]) for i in range(2)]
            tmpA = sb(st1, "tmpA", [128, D])
            hb = sb(st1, "hb", [128, D], BF16)
            hT = sb(st1, "hT", [128, 8, 128], BF16)
            stt_ = sb(st1, "stt", [128, 4])
            xinS_l = [sb(st1, "xinS%d" % i, [128, 4, 16, 11]) for i in range(2)]
            xinP_l = [sb(st1, "xinP%d" % i, [128, 4, 131]) for i in range(2)]
            acc = [sb(st1, "acc%d" % i, [128, 128]) for i in range(4)]
            xc_2 = [sb(st1, "xc_%d" % i, [128, 12, 128], BF16) for i in range(2)]
            zs_2 = [sb(st1, "zs_%d" % i, [128, D], BF16) for i in range(2)]
            sm = sb(st1, "sm", [128, 48])
            af_2 = [sb(st1, "af_%d" % i, [128, 16]) for i in range(2)]
            ex_2 = [sb(st1, "ex_%d" % i, [128, 48]) for i in range(2)]
            xtm = sb(st1, "xtm", [128, D], BF16)
            xdt_2 = [sb(st1, "xdt_%d" % i, [128, D], BF16) for i in range(2)]
            xDs_2 = [sb(st1, "xDs_%d" % i, [128, D], BF16) for i in range(2)]
            xw_2 = [sb(st1, "xw_%d" % i, [128, D], BF16) for i in range(2)]
            Btm_2 = [sb(st1, "Btm_%d" % i, [128, 2, 128], BF16) for i in range(2)]
            CBm_2 = [sb(st1, "CBm_%d" % i, [128, 2, 128]) for i in range(2)]
            L4 = [sb(st1, "L4_%d" % i, [128, 4, 128]) for i in range(2)]
            E4 = [sb(st1, "E4_%d" % i, [128, 4, 128]) for i in range(2)]
            M4 = [sb(st1, "M4_%d" % i, [128, 4, 128], BF16) for i in range(2)]
            yo = sb(st1, "yo", [128, D])
            yy = sb(st1, "yy", [128, D])
            ss = sb(st1, "ss", [128, 8])
            ysb = [sb(st1, "ysb%d" % i, [128, D], BF16) for i in range(2)]
            hTf = sb(st1, "hTf", [128, D])
            hTb = sb(st1, "hTb", [128, D], BF16)
            lhs3 = sb(st1, "lhs3", [128, 8, 48], BF16)
            cv = sb(st1, "cv", [48, 1536])
            CmT = [sb(st1, "CmT%d" % g, [128, 16 * 128], BF16) for g in range(2)]
            h0 = [sb(st1, "h0_%d" % i, [128, 8, 128]) for i in range(2)]
            h0Tb = sb(st1, "h0Tb", [128, D], BF16)
            xwm = sb(st1, "xwm", [128, D], BF16)
            hn = [sb(st1, "hn%d" % i, [128, 8, 128]) for i in range(2)]
            rsel = sb(st1, "rsel", [128, 2, 128])
            decsel = sb(st1, "decsel", [128, 128])
            D2N = ("xc", "zs", "af", "ex", "xdt", "xDs", "xw", "Btm", "CBm")
            D2 = dict(xc=xc_2, zs=zs_2, af=af_2, ex=ex_2, xdt=xdt_2, xDs=xDs_2, xw=xw_2, Btm=Btm_2, CBm=CBm_2)
            P.memset("pool", hTf[:, :], 0.0)
            P.memset("pool", hTb[:, :], 0.0)
            for g in range(2):
                P.memset("pool", CmT[g][:, :], 0.0)

            def ssd_gen(ti):
                typ = 0 if ti == 0 else 1
                xc, zs, af, ex, xdt, xDs, xw, Btm, CBm = (D2[n][ti % 2] for n in D2N)
                PY = [PS[2], PS[3]] if typ == 1 else [PS[0], PS[1]]
                PYO = [PS[5], PS[6]] if typ == 1 else [PS[2], PS[3]]
                mk = masks[typ]
                if ti <= 1:
                    P.dma("sp", modv[:, :, :], modscr[typ].rearrange("p (s d) -> p s d", s=6)[:, 0:3, :])
                    yield
                x_ = xt[ti % 2]
                P.dma("sp", x_[:, :], xs_all[ti])
                yield
                yield from norm_to_T_g(x_, modv, 1, 0, tmpA, hb, hT, stt_)
                yield from proj_conv_gen(typ, 3, wI, 1024, hT, PS[0:2], xinS_l, xinP_l, histS, histP, cw, True, acc,
                                         lambda c: xc[:, c, :])
                P.sec("z (token-major) -> silu")
                for nb in range(2):
                    ps = PS[nb]
                    P.mmk(ps[:, :], [(hT[:, kc, :], wI.col(kc, nb * 512, 512)) for kc in range(8)])
                    yield
                    P.act(zs[:, nb * 512:(nb + 1) * 512], ps[:, :], AF.Silu)
                    yield
                P.sec("dt")
                pd = PS[4]
                P.mmk(pd[:, 0:16], [(hT[:, kc, :], wI.col(kc, 2560, 16)) for kc in range(8)])
                yield
                P.tt("dve", sm[:, 0:16], pd[:, 0:16], smv[:, 0:16], ALU.add)
                yield
                P.act(sm[:, 16:32], sm[:, 0:16], AF.Exp)
                yield
                P.act(sm[:, 32:48], sm[:, 16:32], AF.Ln, bias=cst[:, 1:2], scale=1.0)
                yield
                P.tt("dve", af[:, :], sm[:, 32:48], negA[:, 0:16], ALU.mult)
                yield
                yield from small_scan_terms_g(af[:, :], 16, typ, PS[4][:, 64:112], ex)
                P.sec("token-major x, xdt, xD, xw, B")
                P.trs([(PT[:, c * 128:(c + 1) * 128], xc[:, c, :]) for c in range(8)], identb[:, :])
                yield
                P.copy("act", xtm[:, :], PT[:, :])
                yield
                x3 = xtm[:, :].rearrange("p (h d) -> p h d", h=16)
                P.tt("dve", xdt[:, :].rearrange("p (h d) -> p h d", h=16), x3, bc(sm[:, 32:48], 2, 64), ALU.mult)
                yield
                P.tt("pool", xDs[:, :].rearrange("p (h d) -> p h d", h=16), x3, bc(smv[:, 32:48], 2, 64), ALU.mult)
                yield
                P.tt("pool", xw[:, :].rearrange("p (h d) -> p h d", h=16),
                     xdt[:, :].rearrange("p (h d) -> p h d", h=16), bc(ex[:, 16:32], 2, 64), ALU.mult)
                yield
                P.trs([(PT[:, g * 128:(g + 1) * 128], xc[:, 8 + g, :]) for g in range(2)], identb[:, :])
                yield
                P.copy("act", Btm[:, :, :], PT[:, 0:256].rearrange("p (g n) -> p g n", g=2))
                yield
                P.sec("CB^T masked")
                pc = PS[4][:, 256:512]
                for g in range(2):
                    P.mm(pc[:, g * 128:(g + 1) * 128], xc[:, 8 + g, :], xc[:, 10 + g, :])
                    yield
                P.tt("dve", CBm[:, :, :], pc[:, 0:256].rearrange("p (g l) -> p g l", g=2), bc(mk[:, MINC, :], 1, 2), ALU.mult)
                yield
                P.sec("y_off raw")
                if typ == 1:
                    yield "HALF"
                else:
                    for g in range(2):
                        base = CmT[g][:, :]
                        dst = bass.AP(base.tensor, base.offset, [[16 * 128, 128], [136, 16], [1, 8]])
                        P.op("pool", (lambda e, dst=dst, src=xc[:, 10 + g, :].rearrange("p (s t) -> p s t", s=16):
                                      e.tensor_copy(out=dst, in_=src)), ins=[xc[:, 0, :]], outs=[base])
                        yield
                    for hh in range(2):
                        P.tt("pool", rsel[:, hh, :].rearrange("p (j s) -> p j s", j=8),
                             bc(af[:, hh:16:2], 2, 16), bc(blk16[:, :], 1, 8), ALU.mult)
                        yield
                        P.mm(PS[5][:, 256 + hh * 128:256 + (hh + 1) * 128], onesf[:, :], rsel[:, hh, :])
                        yield
                    P.act(decsel[0:64, :], PS[5][0:64, 256:384], AF.Exp)
                    yield
                    P.act(decsel[64:128, :], PS[5][64:128, 384:512], AF.Exp)
                    yield
                    for s in range(16):
                        h0_ = h0[s % 2]
                        hn_ = hn[s % 2]
                        P.dma("sp", h0_[:, :, :], st_ssd[s].rearrange("(j p) n -> p j n", p=128))
                        yield
                        pa, pb = PS[0], PS[1]
                        P.trs([((pa if j < 4 else pb)[:, (j % 4) * 128:(j % 4 + 1) * 128], h0_[:, j, :]) for j in range(8)],
                              identf[:, :])
                        yield
                        P.copy("act", h0Tb[:, 0:512], pa[:, :])
                        yield
                        P.copy("dve", h0Tb[:, 512:1024], pb[:, :])
                        yield
                        for g in range(2):
                            P.op("pe", (lambda e, o=PS[2 + g][:, :], l=CmT[g][:, s * 128:(s + 1) * 128],
                                        r=h0Tb[:, g * 512:(g + 1) * 512], s=s:
                                        e.matmul(o, lhsT=l, rhs=r, start=(s == 0), stop=(s == 15))),
                                 ins=[CmT[g][:, :], h0Tb[:, :]], outs=[PS[2 + g][:, :]])
                            yield
                        P.act(xwm[:, :], xw[:, :], AF.Identity, scale=blk16[:, s:s + 1])
                        yield
                        for half in range(2):
                            pn = PS[half]
                            for jj in range(4):
                                j = half * 4 + jj
                                P.mm(pn[:, jj * 128:(jj + 1) * 128], xwm[:, j * 128:(j + 1) * 128], Btm[:, j // 4, :])
                                yield
                            for jj in range(4):
                                j = half * 4 + jj
                                P.stt("dve", hn_[:, j, :], h0_[:, j, :], decsel[:, j * 16 + s:j * 16 + s + 1],
                                      pn[:, jj * 128:(jj + 1) * 128], ALU.mult, ALU.add)
                                yield
                        P.dma("sp", o_ssd_s[s].rearrange("(j p) n -> p j n", p=128), hn_[:, :, :])
                        yield
                P.sec("per-head intra-chunk")
                for q in range(4):
                    g = q // 2
                    L_, E_, M_ = L4[q % 2], E4[q % 2], M4[q % 2]
                    P.tt("dve", L_[:, :, :], bc(mk[:, MSTT, :], 1, 4), bc(af[:, 4 * q:4 * q + 4], 2, 128), ALU.mult)
                    yield
                    pg = PS[5 + (q % 2)]
                    for j in range(4):
                        P.mm(pg[:, j * 128:(j + 1) * 128], L_[:, j, :], mk[:, MINC, :])
                        yield
                    P.act(E_[:, :, :], pg[:, :].rearrange("p (j l) -> p j l", j=4), AF.Exp)
                    yield
                    P.tt("dve", M_[:, :, :], E_[:, :, :], bc(CBm[:, g, :], 1, 4), ALU.mult)
                    yield
                    py = PY[g]
                    for j in range(4):
                        h = 4 * q + j
                        hh = h % 8
                        P.mmk(py[:, hh * 64:(hh + 1) * 64],
                              [(identb[:, :], xDs[:, h * 64:(h + 1) * 64]), (M_[:, j, :], xdt[:, h * 64:(h + 1) * 64])])
                        yield
                P.sec("combine y = y_diag + eacs * y_off")
                if typ == 1:
                    for g in range(2):
                        P.mm(PYO[g][:, :], xc[:, 10 + g, :], hTb[:, g * 512:(g + 1) * 512])
                        yield
                for g in range(2):
                    P.copy("act", yo[:, g * 512:(g + 1) * 512], PYO[g][:, :])
                    yield
                P.tt("dve", yo[:, :].rearrange("p (h d) -> p h d", h=16), yo[:, :].rearrange("p (h d) -> p h d", h=16),
                     bc(ex[:, 0:16], 2, 64), ALU.mult)
                yield
                for g in range(2):
                    P.tt("dve", yy[:, g * 512:(g + 1) * 512], yo[:, g * 512:(g + 1) * 512], PY[g][:, :], ALU.add)
                    yield
                P.sec("gate with silu(z), group rmsnorm")
                P.tt("dve", yy[:, :], yy[:, :], zs[:, :], ALU.mult)
                yield
                for g in range(2):
                    P.sqsum(yo[:, g * 512:(g + 1) * 512], yy[:, g * 512:(g + 1) * 512], ss[:, g:g + 1])
                    yield
                P.act(ss[:, 2:4], ss[:, 0:2], AF.Sqrt, bias=cst[:, 0:1], scale=1.0 / 512)
                yield
                P.recip(ss[:, 4:6], ss[:, 2:4])
                yield
                y_ = ysb[ti % 2]
                for g in range(2):
                    P.stt("dve", y_[:, g * 512:(g + 1) * 512], yy[:, g * 512:(g + 1) * 512], ss[:, 4 + g:5 + g],
                          nw[:, g * 512:(g + 1) * 512], ALU.mult, ALU.mult)
                    yield
                P.dma("sp", yssd_scr[ti], y_[:, :])
                yield
                P.sec("state update (prompt chunks)")
                if typ == 1:
                    for g in range(2):
                        P.mm(PYO[g][:, :], Btm[:, g, :], xw[:, g * 512:(g + 1) * 512])
                        yield
                    h3 = hTf[:, :].rearrange("p (h d) -> p h d", h=16)
                    P.tt("dve", h3, h3, bc(ex[:, 32:48], 2, 64), ALU.mult)
                    yield
                    for g in range(2):
                        P.tt("dve", hTf[:, g * 512:(g + 1) * 512], hTf[:, g * 512:(g + 1) * 512], PYO[g][:, :], ALU.add)
                        yield
                    P.copy("act", hTb[:, :], hTf[:, :])
                    yield
                P.sec("conv-state outputs (last 3 raw xbc rows)")
                if ti == 0 or ti == NT - 1:
                    M3 = 48 if typ == 0 else 3
                    if typ == 0:
                        P.copy("pool", lhs3[:, :, :].rearrange("p k (s t) -> p k s t", s=16),
                               hT[:, :, :].rearrange("p k (s t) -> p k s t", s=16)[:, :, :, 5:8])
                        yield
                    else:
                        P.copy("pool", lhs3[:, :, 0:3], hT[:, :, 125:128])
                        yield
                    for nb in range(3):
                        ps = PS[5 + (nb % 2)]
                        P.mmk(ps[0:M3, :], [(lhs3[:, kc, 0:M3], wI.col(kc, 1024 + nb * 512, 512))
                                            for kc in range(8)])
                        yield
                        P.copy("act", cv[0:M3, nb * 512:(nb + 1) * 512], ps[0:M3, :])
                        yield
                    P.dma("sp", (o_ssdc_s if typ == 0 else o_ssdc_p)[:, :], cv[0:M3, :])
                    yield

            def rolling(genf, first, last, yspeed=1):
                nxt_ = first + 1
                cur = genf(first)
                young = None
                hold = [False]
                while cur is not None:
                    try:
                        v = next(cur)
                        if v == "HALF" and young is None and nxt_ < last:
                            young = genf(nxt_)
                            nxt_ += 1
                    except StopIteration:
                        cur, young = young, None
                        if hold[0] and cur is not None and nxt_ < last:
                            young = genf(nxt_)
                            nxt_ += 1
                        hold[0] = False
                        if cur is None and nxt_ < last:
                            cur = genf(nxt_)
                            nxt_ += 1
                        continue
                    for _k in range(yspeed):
                        if young is not None and not hold[0]:
                            try:
                                if next(young) == "HALF":
                                    hold[0] = True
                            except StopIteration:
                                young = None

            run_gen(ssd_gen(0))
            rolling(ssd_gen, 1, NT, yspeed=2)
            for half in range(2):
                ps = PS[half]
                P.trs([(ps[:, jj * 128:(jj + 1) * 128], hTf[:, (half * 4 + jj) * 128:(half * 4 + jj + 1) * 128])
                       for jj in range(4)], identf[:, :])
                P.copy("act", hn[half][:, 0:4, :], ps[:, :].rearrange("p (j n) -> p j n", j=4))
                P.dma("sp", o_ssd_p[half * 512:(half + 1) * 512, :].rearrange("(j p) n -> p j n", p=128), hn[half][:, 0:4, :])
            P.barrier()
            P.emit()
        ps1.close()
        if STOP_AFTER >= 2:
          with contextlib.ExitStack() as st2:
            PA = [st2.enter_context(nc.psum_tensor("pa%d" % i, [128, 512], F32)) for i in range(2)]
            PM = st2.enter_context(nc.psum_tensor("pm", [128, 512], F32))
            LPB = [st2.enter_context(nc.psum_tensor("lpb%d" % i, [128, 512], F32)) for i in range(4)]
            LPS = [[LPB[ln][:, i * 128:(i + 1) * 128] for i in range(4)] for ln in range(4)]
            wII = WBlocks(st2, "wII", 8, [0, 512, 1024, 1536, 2048, 2560, 3072, 3584, 4096, 4112])
            wII.load(w_in_v, 2576)
            wO = WBlocks(st2, "wO", 16, [0, 512, 1024])
            w_out_v = w_out.rearrange("(kc p) n -> p kc n", p=128)
            wO.load(w_out_v, 0)
            modv = sb(st2, "modv2", [128, 3, D])
            cwg = sb(st2, "cwg", [128, 24, 4])
            gnw = sb(st2, "gnw", [128, 128])
            histP = sb(st2, "histPg", [128, 24, 3])
            P.dma("sp", cwg[:, :, :], gdn_cw[:, :, :])
            P.dma("sp", gnw[:, :], gdn_nw[:, :])
            P.memset("pool", histP[:, :, :], 0.0)
            tmpA = sb(st2, "tmpA2", [128, D])
            hT = sb(st2, "hT2", [128, 8, 128], BF16)
            stt_ = sb(st2, "stt2", [128, 4])
            xinP_l = [sb(st2, "xinP2_%d" % i, [128, 4, 131]) for i in range(2)]
            acc = [sb(st2, "acc2_%d" % i, [128, 128]) for i in range(4)]
            lhs3 = sb(st2, "lhs3g", [128, 8, 48], BF16)
            tmpT = sb(st2, "tmpT2", [128, D])
            sttT = sb(st2, "sttT2", [128, 4])
            ss8 = sb(st2, "ss8", [128, 24])
            otm = sb(st2, "otm", [128, D])
            mixed = sb(st2, "mixed", [128, 2 * D], BF16)
            mixedT = sb(st2, "mixedT", [128, 16, 128], BF16)

            class FB:
                pass

            def make_fb(i, stk):
                fb = FB()
                fb.xt = sb(stk, "f%d_xt" % i, [128, D])
                fb.hb = sb(stk, "f%d_hb" % i, [128, D], BF16)
                fb.qk = sb(stk, "f%d_qk" % i, [128, 16, 128], BF16)
                fb.vfm = sb(stk, "f%d_vfm" % i, [128, 8, 128], BF16)
                fb.ktm = sb(stk, "f%d_ktm" % i, [128, 8, 128], BF16)
                fb.vtm = sb(stk, "f%d_vtm" % i, [128, 8, 128], BF16)
                fb.gs = sb(stk, "f%d_gs" % i, [128, D], BF16)
                fb.sm = sb(stk, "f%d_sm" % i, [128, 64])
                fb.gf = sb(stk, "f%d_gf" % i, [128, 8])
                fb.ex = sb(stk, "f%d_ex" % i, [128, 24])
                return fb

            class Lane:
                pass

            lanes = []

            def make_lane(ln, stk):
                B = Lane()
                B.ps = LPS[ln]
                f = lambda nm, dt=F32: sb(stk, "ln%d_%s" % (ln, nm), [128, 128], dt)
                B.P = [f("P0"), f("P1")]
                B.PT = [f("PT0"), f("PT1")]
                B.X = [f("X0"), f("X1")]
                B.ot = f("ot")
                B.L, B.Dm, B.DmI, B.DmS = B.ot, B.X[1], B.PT[1], B.P[1]
                B.attnT, B.TTb, B.R, B.vn, B.kout = (f("attnT", BF16), f("TTb", BF16), f("R", BF16),
                                                     f("vn", BF16), f("kout", BF16))
                lanes.append(B)

            make_lane(0, st2)
            fbs = [make_fb(0, st2)]

            def front_gen(ti, typ, SB, fb):
                xt, hb, qk, vfm, sm, gf, ex = fb.xt, fb.hb, fb.qk, fb.vfm, fb.sm, fb.gf, fb.ex
                P.sec("F:load+norm")
                if ti <= 1:
                    P.dma("sp", modv[:, :, :], modscr[typ].rearrange("p (s d) -> p s d", s=6)[:, 0:3, :])
                    yield
                P.dma("sp", xt[:, :], xs_all[ti])
                yield
                yield from norm_to_T_g(xt, modv, 1, 0, tmpA, hb, hT, stt_)
                yield
                yield from proj_conv_gen(typ, 6, wII, 0, hT, PA, (SB.xinS_l if typ == 0 else None), xinP_l,
                                         (SB.histS if typ == 0 else None), histP, cwg, False, acc,
                                         lambda c: (qk[:, c, :] if c < 16 else vfm[:, c - 16, :]))
                if ti == 0 or ti == NT - 1:
                    P.sec("F:convstate")
                    M3 = 48 if typ == 0 else 3
                    if typ == 0:
                        P.copy("pool", lhs3[:, :, :].rearrange("p k (s t) -> p k s t", s=16),
                               hT[:, :, :].rearrange("p k (s t) -> p k s t", s=16)[:, :, :, 5:8])
                        yield
                    else:
                        P.copy("pool", lhs3[:, :, 0:3], hT[:, :, 125:128])
                        yield
                    for cc in range(3):
                        for nb in range(2):
                            c0 = cc * 1024 + nb * 512
                            P.mmk(PA[nb][0:M3, :], [(lhs3[:, kc, 0:M3], wII.col(kc, c0, 512)) for kc in range(8)])
                            yield
                            P.copy("act", tmpA[0:M3, nb * 512:(nb + 1) * 512], PA[nb][0:M3, :])
                            yield
                        P.dma("sp", (o_gdnc_s if typ == 0 else o_gdnc_p)[:, cc * 1024:(cc + 1) * 1024], tmpA[0:M3, :])
                        yield
                    yield
                P.sec("F:gate")
                for nb in range(2):
                    P.mmk(PA[nb][:, :], [(hT[:, kc, :], wII.col(kc, 3072 + nb * 512, 512))
                                         for kc in range(8)])
                    yield
                    P.act(fb.gs[:, nb * 512:(nb + 1) * 512], PA[nb][:, :], AF.Silu)
                    yield
                yield
                P.sec("F:beta/g")
                P.mmk(PM[:, 0:16], [(hT[:, kc, :], wII.col(kc, 4096, 16)) for kc in range(8)])
                yield
                P.act(sm[:, 0:8], PM[:, 0:8], AF.Exp, scale=-1.0)
                yield
                P.ts("dve", sm[:, 0:8], sm[:, 0:8], 1.0, None, ALU.add)
                yield
                P.recip(sm[:, 8:16], sm[:, 0:8])
                yield
                P.ts("dve", sm[:, 16:24], sm[:, 8:16], -1.0, None, ALU.mult)
                yield
                P.tt("dve", sm[:, 24:32], PM[:, 8:16], smv[:, 48:56], ALU.add)
                yield
                P.act(sm[:, 32:40], sm[:, 24:32], AF.Exp)
                yield
                P.act(sm[:, 40:48], sm[:, 32:40], AF.Ln, bias=cst[:, 1:2], scale=1.0)
                yield
                P.tt("dve", gf[:, :], sm[:, 40:48], negA[:, 16:24], ALU.mult)
                yield
                yield
                P.sec("F:beta/g")
                yield from small_scan_terms_g(gf[:, :], 8, typ, PM[:, 64:88], ex)
                P.ts("dve", sm[:, 48:56], ex[:, 0:8], -1.0, None, ALU.mult)
                yield
                yield
                for half in range(2):
                    P.sec("F:l2norm")
                    src = qk[:, half * 8:(half + 1) * 8, :]
                    P.tt("pool", hb[:, :].rearrange("p (c t) -> p c t", c=8), src, src, ALU.mult)
                    yield
                    for i in range(2):
                        rq = tmpA[:, i * 512:(i + 1) * 512]
                        P.mm(PA[i][:, :], onesb[:, :], hb[:, i * 512:(i + 1) * 512])
                        yield
                        if half == 0:
                            P.act(rq, PA[i][:, :], AF.Ln, bias=cst[:, 2:3], scale=128.0)
                            yield
                        else:
                            P.act(rq, PA[i][:, :], AF.Ln, bias=cst[:, 0:1], scale=1.0)
                            yield
                        P.act(rq, rq, AF.Exp, scale=-0.5)
                        yield
                        dst = qk[:, half * 8 + 4 * i:half * 8 + 4 * i + 4, :]
                        P.tt("dve", dst, dst, rq.rearrange("p (c t) -> p c t", c=4), ALU.mult)
                        yield
                    yield
                P.sec("F:transposes")
                P.copy("pool", hb[:, :].rearrange("p (c t) -> p c t", c=8), qk[:, 8:16, :])
                yield
                P.trs([(PT[:, j * 128:(j + 1) * 128], qk[:, 8 + j, :]) for j in range(8)], identb[:, :])
                yield
                P.copy("act", fb.ktm[:, :, :], PT[:, :].rearrange("p (h d) -> p h d", h=8))
                yield
                yield
                P.sec("F:transposes")
                P.trs([(PT[:, j * 128:(j + 1) * 128], vfm[:, j, :]) for j in range(8)], identb[:, :])
                yield
                P.copy("act", fb.vtm[:, :, :], PT[:, :].rearrange("p (h d) -> p h d", h=8))
                yield
                if typ == 0:
                    P.tt("pool", SB.rselg[:, :].rearrange("p (h s) -> p h s", h=8),
                         bc(gf[:, :], 2, 16), bc(blk16[:, :], 1, 8), ALU.mult)
                    yield
                    P.mm(PM[:, 128:256], onesf[:, :], SB.rselg[:, :])
                    yield
                    P.act(SB.gendsel[:, :], PM[:, 128:256], AF.Exp)
                    yield
                yield

            def head_gen(h, B, typ, SB, fb):
                mk = masks[typ]
                m_lev = 3 if typ == 0 else 7
                qk, hb, sm, gf, ex, ktm, vtm = fb.qk, fb.hb, fb.sm, fb.gf, fb.ex, fb.ktm, fb.vtm
                kT = qk[:, 8 + h, :]
                qT = qk[:, h, :]
                _st = ["H:decay"]
                P.sec(_st[0])
                P.act(B.L[:, :], mk[:, MSTT, :], AF.Identity, scale=gf[:, h:h + 1])
                yield
                P.mm(B.ps[3][:, :], B.L[:, :], mk[:, MINC, :])
                yield
                P.act(B.Dm[:, :], B.ps[3][:, :], AF.Exp)
                yield
                P.tt("dve", B.DmI[:, :], B.Dm[:, :], mk[:, MINC, :], ALU.mult)
                yield
                P.tt("dve", B.DmS[:, :], B.Dm[:, :], mk[:, MSTR, :], ALU.mult)
                yield
                P.mm(B.ps[0][:, :], kT, hb[:, h * 128:(h + 1) * 128])
                yield
                P.mm(B.ps[1][:, :], kT, qT)
                yield
                P.stt("dve", B.P[0][:, :], B.ps[0][:, :], sm[:, 16 + h:17 + h], B.DmS[:, :], ALU.mult, ALU.mult)
                yield
                P.tt("dve", B.attnT[:, :], B.ps[1][:, :], B.DmI[:, :], ALU.mult)
                yield
                yield
                _st[0] = "H:dbl"
                P.sec(_st[0])
                P.tr(B.ps[2][:, :], B.P[0][:, :], identf[:, :])
                yield
                P.copy("act", B.PT[0][:, :], B.ps[2][:, :])
                yield
                P.tt("dve", B.X[0][:, :], B.P[0][:, :], identf[:, :], ALU.add)
                yield
                yield
                P.sec(_st[0])
                if m_lev > 1:
                    P.mm(B.ps[0][:, :], B.P[0][:, :], B.PT[0][:, :])
                    yield
                    if m_lev > 2:
                        P.mm(B.ps[1][:, :], B.PT[0][:, :], B.P[0][:, :])
                        yield
                    P.copy("act", B.PT[1][:, :], B.ps[0][:, :])
                    yield
                    if m_lev > 2:
                        P.copy("act", B.P[1][:, :], B.ps[1][:, :])
                        yield
                    yield
                    P.sec(_st[0])
                xi = 0
                for j in range(1, m_lev):
                    cur, nx = j % 2, (j + 1) % 2
                    if j + 1 < m_lev:
                        P.mm(B.ps[0][:, :], B.P[cur][:, :], B.PT[cur][:, :])
                        yield
                        if j + 2 < m_lev:
                            P.mm(B.ps[1][:, :], B.PT[cur][:, :], B.P[cur][:, :])
                            yield
                    P.mm(B.ps[2][:, :], B.PT[cur][:, :], B.X[xi][:, :])
                    yield
                    if j + 1 < m_lev:
                        P.copy("act", B.PT[nx][:, :], B.ps[0][:, :])
                        yield
                        if j + 2 < m_lev:
                            P.copy("act", B.P[nx][:, :], B.ps[1][:, :])
                            yield
                    P.tt("dve", B.X[1 - xi][:, :], B.X[xi][:, :], B.ps[2][:, :], ALU.add)
                    yield
                    xi = 1 - xi
                    yield
                    P.sec(_st[0])
                _st[0] = "H:state"
                P.sec(_st[0])
                P.copy("act", B.TTb[:, :], B.X[xi][:, :])
                yield
                if typ == 1:
                    Sf_h, Sb_h = SB.Sf[h], SB.Sb[h]
                    P.mm(B.ps[3][:, :], kT, Sb_h[:, :])
                    yield
                else:
                    P.dma("sp", SB.S0f[:, :, :], st_gdn[:, h, :, :].rearrange("s d e -> d s e"))
                    yield
                    P.copy("act", SB.S0b[:, :, :], SB.S0f[:, :, :])
                    yield
                    for (dstT, srcT) in ((SB.kTm, kT), (SB.qTm, qT)):
                        base = dstT[:, :]
                        dap = bass.AP(base.tensor, base.offset, [[16 * 128, 128], [136, 16], [1, 8]])
                        P.op("pool", (lambda e, dap=dap, src=srcT.rearrange("p (s t) -> p s t", s=16):
                                      e.tensor_copy(out=dap, in_=src)), ins=[srcT], outs=[base])
                        yield
                    P.mmk(B.ps[3][:, :], [(SB.kTm[:, s * 128:(s + 1) * 128], SB.S0b[:, s, :]) for s in range(16)])
                    yield
                P.stt("dve", B.R[:, :], B.ps[3][:, :], sm[:, 48 + h:49 + h], vtm[:, h, :], ALU.mult, ALU.add)
                yield
                P.mm(B.ps[0][:, :], B.TTb[:, :], B.R[:, :])
                yield
                P.act(B.vn[:, :], B.ps[0][:, :], AF.Identity, scale=sm[:, 8 + h:9 + h])
                yield
                yield
                P.sec(_st[0])
                if typ == 1:
                    P.mm(B.ps[1][:, :], qT, Sb_h[:, :])
                    yield
                else:
                    P.mmk(B.ps[1][:, :], [(SB.qTm[:, s * 128:(s + 1) * 128], SB.S0b[:, s, :]) for s in range(16)])
                    yield
                P.mm(B.ps[2][:, :], B.attnT[:, :], B.vn[:, :])
                yield
                P.act(B.ot[:, :], B.ps[1][:, :], AF.Identity, scale=ex[:, h:h + 1])
                yield
                P.tt("dve", otm[:, h * 128:(h + 1) * 128], B.ot[:, :], B.ps[2][:, :], ALU.add)
                yield
                P.act(B.kout[:, :], ktm[:, h, :], AF.Identity, scale=ex[:, 8 + h:9 + h])
                yield
                yield
                P.sec(_st[0])
                if typ == 1:
                    P.mm(B.ps[3][:, :], B.kout[:, :], B.vn[:, :])
                    yield
                    P.stt("dve", Sf_h[:, :], Sf_h[:, :], ex[:, 16 + h:17 + h], B.ps[3][:, :], ALU.mult, ALU.add)
                    yield
                    P.copy("act", Sb_h[:, :], Sf_h[:, :])
                    yield
                else:
                    P.tt("dve", SB.koutm_all[:, :, :], bc(B.kout[:, :], 1, 16), bc(blk16[:, :], 2, 128), ALU.mult)
                    yield
                    for s in range(16):
                        P.mm(LPB[s // 4][:, (s % 4) * 128:(s % 4 + 1) * 128], SB.koutm_all[:, s, :], B.vn[:, :])
                    yield
                    for j4 in range(4):
                        S4 = SB.S0f[:, 4 * j4:4 * j4 + 4, :]
                        gsel = SB.gendsel[:, h * 16 + 4 * j4:h * 16 + 4 * j4 + 4]
                        P.tt("dve", S4, S4, bc(gsel, 2, 128), ALU.mult)
                        yield
                        P.tt("dve", S4, S4, LPB[j4][:, :].rearrange("p (s e) -> p s e", s=4), ALU.add)
                        yield
                    P.dma("sp", o_gdn_s[:, h, :, :].rearrange("s d e -> d s e"), SB.S0f[:, :, :])
                    yield
                yield

            def tail_gen(ti, fb):
                xt = fb.xt
                P.sec("T:onorm")
                P.dma("sp", mixed[:, 0:D], yssd_scr[ti])
                yield
                P.tt("pool", tmpT[:, :], otm[:, :], otm[:, :], ALU.mult)
                yield
                P.red("dve", ss8[:, 0:8], tmpT[:, :].rearrange("p (h d) -> p h d", h=8))
                yield
                P.act(ss8[:, 8:16], ss8[:, 0:8], AF.Sqrt, bias=cst[:, 0:1], scale=1.0 / 128)
                yield
                P.recip(ss8[:, 16:24], ss8[:, 8:16])
                yield
                o3 = otm[:, :].rearrange("p (h d) -> p h d", h=8)
                P.tt("dve", o3, o3, bc(ss8[:, 16:24], 2, 128), ALU.mult)
                yield
                P.tt("dve", o3, o3, bc(gnw[:, :], 1, 8), ALU.mult)
                yield
                P.tt("dve", mixed[:, D:2 * D], otm[:, :], fb.gs[:, :], ALU.mult)
                yield
                yield
                P.sec("T:outproj")
                for half in range(2):
                    P.trs([(PT[:, j * 128:(j + 1) * 128], mixed[:, (half * 8 + j) * 128:(half * 8 + j + 1) * 128])
                           for j in range(8)], identb[:, :])
                    yield
                    P.copy("act", mixedT[:, half * 8:(half + 1) * 8, :], PT[:, :].rearrange("p (c t) -> p c t", c=8))
                    yield
                yield
                P.sec("T:outproj")
                for nb in range(2):
                    P.mmk(PA[nb][:, :], [(mixedT[:, kc, :], wO.col(kc, nb * 512, 512)) for kc in range(16)])
                    yield
                    P.copy("act", tmpT[:, nb * 512:(nb + 1) * 512], PA[nb][:, :])
                    yield
                yield
                P.sec("T:resid")
                P.sqsum(otm[:, :], tmpT[:, :], sttT[:, 0:1])
                yield
                P.act(sttT[:, 1:2], sttT[:, 0:1], AF.Sqrt, bias=cst[:, 0:1], scale=1.0 / D)
                yield
                P.recip(sttT[:, 2:3], sttT[:, 1:2])
                yield
                P.stt("dve", tmpT[:, :], tmpT[:, :], sttT[:, 2:3], modv[:, 2, :], ALU.mult, ALU.mult)
                yield
                P.tt("dve", xt[:, :], tmpT[:, :], xt[:, :], ALU.add)
                yield
                P.dma("sp", x1_scr[ti], xt[:, :])
                yield
                yield

            def speed(g, k):
                while True:
                    for _ in range(k):
                        try:
                            next(g)
                        except StopIteration:
                            return
                    yield

            def run_rr(gens):
                alive = list(gens)
                while alive:
                    nxt = []
                    for gn in alive:
                        try:
                            next(gn)
                            nxt.append(gn)
                        except StopIteration:
                            pass
                    alive = nxt

            def back(ti, typ, SB, fb, nl, side, with_tail=True):
                side = [side] if side is not None else []
                for h0_ in range(0, 8, nl):
                    gens = [head_gen(h0_ + i, lanes[i], typ, SB, fb) for i in range(min(nl, 8 - h0_))]
                    alive = gens + side
                    while any(g in alive for g in gens):
                        nxt = []
                        for gn in alive:
                            try:
                                next(gn)
                                nxt.append(gn)
                            except StopIteration:
                                if gn in side:
                                    side = []
                        alive = nxt
                if with_tail:
                    run_rr([tail_gen(ti, fb)] + side)
                else:
                    run_rr(side)

            class SBufs:
                pass

            with contextlib.ExitStack() as st2s:
                SB = SBufs()
                SB.histS = sb(st2s, "histSg", [128, 24, 16, 3])
                SB.xinS_l = [sb(st2s, "xinS2_%d" % i, [128, 4, 16, 11]) for i in range(2)]
                SB.S0f = sb(st2s, "S0f", [128, 16, 128])
                SB.S0b = sb(st2s, "S0b", [128, 16, 128], BF16)
                SB.kTm = sb(st2s, "kTm", [128, 16 * 128], BF16)
                SB.qTm = sb(st2s, "qTm", [128, 16 * 128], BF16)
                SB.koutm_all = sb(st2s, "koutm_all", [128, 16, 128], BF16)
                SB.rselg = sb(st2s, "rselg", [128, 128])
                SB.gendsel = sb(st2s, "gendsel", [128, 128])
                P.dma("sp", SB.histS[:, :, :, :], hist_gdn[:, :, :, :])
                P.memset("pool", SB.kTm[:, :], 0.0)
                P.memset("pool", SB.qTm[:, :], 0.0)
                run_rr([front_gen(0, 0, SB, fbs[0])])
                back(0, 0, SB, fbs[0], 1, None)
                P.barrier()
                P.emit()
            with contextlib.ExitStack() as st2p:
                SB = SBufs()
                SB.Sf = [sb(st2p, "Sf%d" % h, [128, 128]) for h in range(8)]
                SB.Sb = [sb(st2p, "Sb%d" % h, [128, 128], BF16) for h in range(8)]
                for ln in range(1, P2_NL):
                    make_lane(ln, st2p)
                fbs.append(make_fb(1, st2p))
                for h in range(8):
                    P.memset("pool", SB.Sf[h][:, :], 0.0)
                    P.memset("pool", SB.Sb[h][:, :], 0.0)
                run_rr([front_gen(1, 1, SB, fbs[1])])
                pending_tail = None
                for ti in range(1, NT):
                    parts = []
                    if pending_tail is not None:
                        parts.append(pending_tail)
                    if ti + 1 < NT:
                        parts.append(speed(front_gen(ti + 1, 1, SB, fbs[(ti + 1) % 2]), 2))
                    side = itertools.chain(*parts) if parts else None
                    back(ti, 1, SB, fbs[ti % 2], P2_NL, side, with_tail=False)
                    pending_tail = tail_gen(ti, fbs[ti % 2])
                run_rr([pending_tail])
                for h in range(8):
                    P.dma("sp", o_gdn_p[h * 128:(h + 1) * 128, :], SB.Sf[h][:, :])
                P.barrier()
                P.emit()
        if STOP_AFTER >= 3:
          with contextlib.ExitStack() as st3:
            PB = [st3.enter_context(nc.psum_tensor("pb%d" % i, [128, 512], F32)) for i in range(7)]
            gb = [0, 512, 1024, 1536, 2048, 2560, 2816]
            wG = WBlocks(st3, "wG", 8, gb + [DFF + b for b in gb[1:]])
            w_gu_v = w_gu.rearrange("(kc p) n -> p kc n", p=128)
            wG.load(w_gu_v, 0, order=[j for i in range(6) for j in (i, 6 + i)])
            wD = WBlocks(st3, "wD", 22, [0, 512, 1024])
            w_dn_v = w_dn.rearrange("(kc p) n -> p kc n", p=128)
            wD.load(w_dn_v, 0)
            modv = sb(st3, "modv3", [128, 3, D])
            xt = [sb(st3, "xt3_%d" % i, [128, D]) for i in range(2)]
            tmpA = [sb(st3, "tmpA3_%d" % i, [128, D]) for i in range(2)]
            tmpB = sb(st3, "tmpB3", [128, D])
            hb = [sb(st3, "hb3_%d" % i, [128, D], BF16) for i in range(2)]
            hT = [sb(st3, "hT3_%d" % i, [128, 8, 128], BF16) for i in range(2)]
            stt_ = [sb(st3, "stt3_%d" % i, [128, 4]) for i in range(2)]
            sg = [sb(st3, "sg%d" % i, [128, 512]) for i in range(2)]
            hid = sb(st3, "hid", [128, DFF], BF16)
            hidT = sb(st3, "hidT", [128, 22, 128], BF16)

            def ffn_gen(ti):
                typ = 0 if ti == 0 else 1
                pr = ti % 2
                x_, tA, hb_, hT_, st_ = xt[pr], tmpA[pr], hb[pr], hT[pr], stt_[pr]
                if ti <= 1:
                    P.dma("sp", modv[:, :, :], modscr[typ].rearrange("p (s d) -> p s d", s=6)[:, 3:6, :])
                P.dma("sp", x_[:, :], x1_scr[ti])
                yield
                yield from norm_to_T_g(x_, modv, 1, 0, tA, hb_, hT_, st_)
                for i in range(6):
                    w = 512 if i < 5 else 256
                    pg_, pu_ = PB[(2 * i) % 6], PB[(2 * i + 1) % 6]
                    P.mmk(pg_[:, 0:w], [(hT_[:, kc, :], wG.col(kc, i * 512, w)) for kc in range(8)])
                    yield
                    P.mmk(pu_[:, 0:w], [(hT_[:, kc, :], wG.col(kc, DFF + i * 512, w)) for kc in range(8)])
                    yield
                    s_ = sg[i % 2]
                    P.act(s_[:, 0:w], pg_[:, 0:w], AF.Silu)
                    yield
                    P.tt("dve", hid[:, i * 512:i * 512 + w], s_[:, 0:w], pu_[:, 0:w], ALU.mult)
                    yield
                yield "HALF"
                for grp in range(3):
                    n = 8 if grp < 2 else 6
                    P.trs([(PT[:, j * 128:(j + 1) * 128], hid[:, (grp * 8 + j) * 128:(grp * 8 + j + 1) * 128])
                           for j in range(n)], identb[:, :])
                    yield
                    P.copy("act", hidT[:, grp * 8:grp * 8 + n, :],
                           PT[:, 0:n * 128].rearrange("p (c t) -> p c t", c=n))
                    yield
                for nb in range(2):
                    P.mmk(PB[6][:, :], [(hidT[:, kc, :], wD.col(kc, nb * 512, 512)) for kc in range(22)])
                    yield
                    P.copy("act", tA[:, nb * 512:(nb + 1) * 512], PB[6][:, :])
                    yield
                P.sqsum(tmpB[:, :], tA[:, :], st_[:, 0:1])
                yield
                P.act(st_[:, 1:2], st_[:, 0:1], AF.Sqrt, bias=cst[:, 0:1], scale=1.0 / D)
                yield
                P.recip(st_[:, 2:3], st_[:, 1:2])
                yield
                P.stt("dve", tA[:, :], tA[:, :], st_[:, 2:3], modv[:, 2, :], ALU.mult, ALU.mult)
                yield
                P.tt("dve", x_[:, :], tA[:, :], x_[:, :], ALU.add)
                yield
                P.dma("sp", y_all[ti], x_[:, :])
                yield

            run_gen(ffn_gen(0))
            nxt = 2
            cur = ffn_gen(1)
            young = None
            while cur is not None:
                try:
                    v = next(cur)
                    if v == "HALF" and young is None and nxt < NT:
                        young = ffn_gen(nxt)
                        nxt += 1
                except StopIteration:
                    cur, young = young, None
                    if cur is None and nxt < NT:
                        cur = ffn_gen(nxt)
                        nxt += 1
                    continue
                if young is not None:
                    try:
                        v2 = next(young)
                        if v2 == "HALF":
                            pass
                    except StopIteration:
                        young = None
            P.barrier()
            P.emit()
    return nc


def _prep_inputs(inp):
    f = lambda a: np.ascontiguousarray(np.asarray(a, dtype=np.float32))
    xp, xs = f(inp["x_prompt"]), f(inp["x_sample"])
    cp, cs = f(inp["c_prompt"]), f(inp["c_sample"])
    bcast = lambda v: np.ascontiguousarray(np.broadcast_to(np.asarray(v, np.float32).reshape(1, -1), (128, v.size)))
    shared = {}
    shared["w_ada"] = f(inp["w_ada"][0])
    shared["b_ada_b"] = bcast(inp["b_ada"][0])
    shared["normvecs"] = np.ascontiguousarray(np.stack(
        [bcast(inp[k][0]) for k in ("norm_mix_pre", "norm_mix_post", "norm_ffn_pre", "norm_ffn_post")], axis=1))
    shared["w_in"] = f(inp["w_in"][0])
    shared["w_out"] = f(inp["w_out"][0])
    shared["w_gu"] = f(inp["w_gate_up"][0])
    shared["w_dn"] = f(inp["w_down"][0])
    scw = np.concatenate([f(inp["ssd_conv_w"][0]), f(inp["ssd_conv_b"][0])[None, :]], axis=0)
    shared["ssd_cw"] = np.ascontiguousarray(scw.reshape(5, 12, 128).transpose(2, 1, 0))
    shared["gdn_cw"] = np.ascontiguousarray(f(inp["gdn_conv_w"][0]).reshape(4, 24, 128).transpose(2, 1, 0))
    sv = np.concatenate([f(inp["ssd_dt_bias"][0]), f(inp["ssd_A_log"][0]), f(inp["ssd_D"][0]),
                         f(inp["gdn_dt_bias"][0]), f(inp["gdn_A_log"][0])])
    shared["smallv"] = bcast(sv)
    shared["ssd_nw"] = bcast(inp["ssd_norm_w"][0])
    shared["gdn_nw"] = bcast(inp["gdn_norm_w"][0])
    shared["ident"] = np.eye(128, dtype=np.float32)
    idx = np.arange(128)
    m = np.zeros((2, 128, 4, 128), np.float32)
    for typ, bs in ((0, 8), (1, 128)):
        same = (idx[:, None] // bs) == (idx[None, :] // bs)
        m[typ, :, 0, :] = same & (idx[:, None] > idx[None, :])
        m[typ, :, 1, :] = same & (idx[:, None] <= idx[None, :])
        m[typ, :, 2, :] = same & (idx[:, None] < idx[None, :])
        m[typ, :, 3, :] = same
    shared["masks"] = m
    shared["blk16"] = np.ascontiguousarray(((idx[:, None] // 8) == np.arange(16)[None, :]).astype(np.float32))
    maps = []
    for i in range(NCORES):
        d = dict(shared)
        sl = slice(16 * i, 16 * (i + 1))
        d["xs_all"] = np.ascontiguousarray(np.concatenate(
            [xs[sl].reshape(1, 128, D), xp[i].reshape(16, 128, D)], axis=0))
        d["cexp"] = np.ascontiguousarray(np.stack(
            [np.repeat(cs[sl], 8, axis=0), np.broadcast_to(cp[i][None, :], (128, D))], axis=0))
        d["st_ssd"] = np.ascontiguousarray(f(inp["state_ssd"][0, sl]).reshape(16, 1024, 128))
        hs = f(inp["state_ssd_conv"][0, sl])
        d["hist_ssd"] = np.ascontiguousarray(hs.reshape(16, 3, 12, 128).transpose(3, 2, 0, 1))
        d["st_gdn"] = np.ascontiguousarray(f(inp["state_gdn"][0, sl]))
        hg = f(inp["state_gdn_conv"][0, sl])
        d["hist_gdn"] = np.ascontiguousarray(hg.reshape(16, 3, 24, 128).transpose(3, 2, 0, 1))
        maps.append(d)
    return maps


def kernel(**inp):
    maps = _prep_inputs(inp)
    nc = build_program()
    res = run_bass_kernel_spmd(nc, maps, core_ids=list(range(NCORES)))
    R = res.results
    cat = lambda k: np.stack([np.asarray(r[k]) for r in R], axis=0)
    y_all = cat("y_all")
    y_prompt = y_all[:, 1:].reshape(8, 2048, D)
    y_sample = y_all[:, 0].reshape(128, 8, D)
    ssd_p = cat("o_ssd_p").reshape(1, 8, 16, 64, 128)
    ssdc_p = cat("o_ssdc_p").reshape(1, 8, 3, 1536)
    gdn_p = cat("o_gdn_p").reshape(1, 8, 8, 128, 128)
    gdnc_p = cat("o_gdnc_p").reshape(1, 8, 3, 3072)
    ssd_s = cat("o_ssd_s").reshape(1, 128, 16, 64, 128)
    ssdc_s = cat("o_ssdc_s").reshape(1, 128, 3, 1536)
    gdn_s = cat("o_gdn_s").reshape(1, 128, 8, 128, 128)
    gdnc_s = cat("o_gdnc_s").reshape(1, 128, 3, 3072)
    outs = (y_prompt, y_sample, ssd_p, ssdc_p, gdn_p, gdnc_p, ssd_s, ssdc_s, gdn_s, gdnc_s)
    return tuple(np.ascontiguousarray(o, dtype=np.float32) for o in outs)
```
